# Optimizing a Trainium2 kernel written in Bass

```python
import math
import jax, jax.numpy as jnp
from jax import lax
import numpy as np

D_MODEL = 1024
BATCH = 4
SEQ = 4096
DEPTH = 2

N_AB_LAYERS = (DEPTH + 1) // 2
N_C_LAYERS = DEPTH // 2

RG_WIDTH = D_MODEL
RG_BLOCKS = 8
RG_BLOCK_DIM = RG_WIDTH // RG_BLOCKS
RG_CONV = 4
RG_C = 8.0

HG_WIDTH = D_MODEL
HG_HEAD_DIM = 128
HG_HEADS = HG_WIDTH // HG_HEAD_DIM
HG_CHUNK = 64

AB_IN_WIDTH = 2 * RG_WIDTH + 4 * HG_WIDTH
AB_MIX_WIDTH = RG_WIDTH + HG_WIDTH
AB_SPLITS = [RG_WIDTH, 2 * RG_WIDTH, 2 * RG_WIDTH + HG_WIDTH,
             2 * RG_WIDTH + 2 * HG_WIDTH, 2 * RG_WIDTH + 3 * HG_WIDTH]

RW_WIDTH = 2 * D_MODEL
RW_HEAD_DIM = 64
RW_HEADS = RW_WIDTH // RW_HEAD_DIM
RW_LORA = 64

RMS_EPS = 1e-6
GN_EPS = 64e-5

kernel_name = "hybrid_rglru_hgrn2_rwkv7_trunk"


def rms_norm(x, g):
    x32 = x.astype(jnp.float32)
    y = x32 * lax.rsqrt(jnp.mean(x32 * x32, axis=-1, keepdims=True) + RMS_EPS)
    return (y * g.astype(jnp.float32)).astype(x.dtype)


def causal_depthwise_conv(x, w, b):
    c = x.shape[-1]
    y = lax.conv_general_dilated(
        x, w.astype(x.dtype)[:, None, :], window_strides=(1,),
        padding=[(RG_CONV - 1, 0)], dimension_numbers=("NWC", "WIO", "NWC"),
        feature_group_count=c)
    return y + b.astype(x.dtype)


def rg_lru(x, w_a, b_a, w_x, b_x, lam):
    bsz, s, _ = x.shape
    xb = x.reshape(bsz, s, RG_BLOCKS, RG_BLOCK_DIM)
    gate_r = jax.nn.sigmoid(jnp.einsum("bsnc,ncd->bsnd", xb, w_a).reshape(bsz, s, RG_WIDTH) + b_a)
    gate_i = jax.nn.sigmoid(jnp.einsum("bsnc,ncd->bsnd", xb, w_x).reshape(bsz, s, RG_WIDTH) + b_x)
    log_a = -RG_C * gate_r * jax.nn.softplus(-lam)
    a = jnp.exp(log_a)
    mult = jnp.sqrt(-jnp.expm1(2.0 * log_a))
    u = mult * (gate_i * x)

    def combine(left, right):
        return (left[0] * right[0], right[0] * left[1] + right[1])

    _, h = lax.associative_scan(combine, (a, u), axis=1)
    return h


def hgrn2_mix(q, f_pre, v, lb):
    bsz, s, _ = q.shape
    n = s // HG_CHUNK
    log_f = jnp.log(lb + (1.0 - lb) * jax.nn.sigmoid(f_pre))
    k = (1.0 - lb) * jax.nn.sigmoid(-f_pre)

    def to_chunks(t):
        return t.reshape(bsz, n, HG_CHUNK, HG_HEADS, HG_HEAD_DIM).transpose(0, 3, 1, 2, 4)

    q, k, v, log_f = to_chunks(q), to_chunks(k), to_chunks(v), to_chunks(log_f)
    cum = jnp.cumsum(log_f, axis=3)
    total = cum[:, :, :, -1:, :]
    q_dec = q * jnp.exp(cum)
    k_inv = k * jnp.exp(-cum)
    k_end = k * jnp.exp(total - cum)
    causal = jnp.tril(jnp.ones((HG_CHUNK, HG_CHUNK), dtype=bool))
    scores = jnp.where(causal, jnp.einsum("bhnld,bhnmd->bhnlm", q_dec, k_inv), 0.0)
    o_intra = jnp.einsum("bhnlm,bhnme->bhnle", scores, v)

    def step(state, inp):
        q_c, k_c, v_c, dec_c = inp
        o_c = jnp.einsum("bhld,bhde->bhle", q_c, state)
        state = state * dec_c[..., None] + jnp.einsum("bhld,bhle->bhde", k_c, v_c)
        return state, o_c

    xs = (jnp.moveaxis(q_dec, 2, 0), jnp.moveaxis(k_end, 2, 0), jnp.moveaxis(v, 2, 0),
          jnp.moveaxis(jnp.exp(total[:, :, :, 0, :]), 2, 0))
    state0 = jnp.zeros((bsz, HG_HEADS, HG_HEAD_DIM, HG_HEAD_DIM), jnp.float32)
    _, o_inter = lax.scan(step, state0, xs)
    o = o_intra + jnp.moveaxis(o_inter, 0, 2)
    return o.transpose(0, 2, 3, 1, 4).reshape(bsz, s, HG_HEADS, HG_HEAD_DIM)


def rglru_hgrn2_layer(h, norm_g, w_in, conv_w, conv_b, w_a, b_a, w_x, b_x, lam, lb, hg_g, w_out):
    bsz, s, _ = h.shape
    u = rms_norm(h, norm_g).astype(jnp.float32)
    z = u @ w_in
    xa, ga, q, f_pre, iv, gb = jnp.split(z, AB_SPLITS, axis=-1)
    xa = causal_depthwise_conv(xa, conv_w, conv_b)
    ya = rg_lru(xa, w_a, b_a, w_x, b_x, lam) * jax.nn.silu(ga)
    o = hgrn2_mix(q, f_pre, iv, lb)
    o = o * lax.rsqrt(jnp.mean(o * o, axis=-1, keepdims=True) + RMS_EPS) * hg_g
    yb = o.reshape(bsz, s, HG_WIDTH) * jax.nn.silu(gb)
    out = jnp.concatenate([ya, yb], axis=-1) @ w_out
    return h + out.astype(h.dtype)


def rwkv7_scan(r, decay, k, v, kk, a):
    bsz = r.shape[0]

    def step(state, inp):
        r_t, w_t, k_t, v_t, kk_t, a_t = inp
        s_kk = jnp.einsum("bhvk,bhk->bhv", state, kk_t)
        state = (state * w_t[:, :, None, :]
                 - s_kk[..., None] * (kk_t * a_t)[:, :, None, :]
                 + v_t[..., None] * k_t[:, :, None, :])
        y_t = jnp.einsum("bhvk,bhk->bhv", state, r_t)
        return state, y_t

    xs = tuple(jnp.moveaxis(t, 1, 0) for t in (r, decay, k, v, kk, a))
    state0 = jnp.zeros((bsz, RW_HEADS, RW_HEAD_DIM, RW_HEAD_DIM), jnp.float32)
    _, y = lax.scan(step, state0, xs)
    return jnp.moveaxis(y, 0, 1)


def rwkv7_layer(h, norm_g, mu, w_r, w_k, w_v, w_g, w0, w1, w2, a0, a1, a2,
                k_k, k_a, r_k, lnx_g, lnx_b, w_o):
    bsz, s, _ = h.shape
    u = rms_norm(h, norm_g).astype(jnp.float32)
    delta = jnp.pad(u, ((0, 0), (1, 0), (0, 0)))[:, :-1] - u
    mu = mu.astype(jnp.float32)
    x_r, x_w, x_k, x_v, x_a, x_g = (u + delta * mu[i] for i in range(6))

    def heads(t):
        return t.reshape(bsz, s, RW_HEADS, RW_HEAD_DIM)

    r = x_r @ w_r
    k_raw = x_k @ w_k
    v = x_v @ w_v
    gate = jax.nn.silu(x_g @ w_g)
    w_log = -jax.nn.softplus(-(w0 + jnp.tanh(x_w @ w1) @ w2)) - 0.5
    decay = jnp.exp(-jnp.exp(w_log))
    a = jax.nn.sigmoid(a0 + (x_a @ a1) @ a2)
    kk = heads(k_raw * k_k)
    kk = kk / jnp.maximum(jnp.sqrt(jnp.sum(kk * kk, axis=-1, keepdims=True)), 1e-12)
    k = heads(k_raw * (1.0 + (a - 1.0) * k_a))
    r, v, decay, a = heads(r), heads(v), heads(decay), heads(a)
    y = rwkv7_scan(r, decay, k, v, kk, a)
    mean = jnp.mean(y, axis=-1, keepdims=True)
    var = jnp.mean(jnp.square(y - mean), axis=-1, keepdims=True)
    y = ((y - mean) * lax.rsqrt(var + GN_EPS)).reshape(bsz, s, RW_WIDTH) * lnx_g + lnx_b
    bonus = (jnp.sum(r * k * r_k, axis=-1, keepdims=True) * v).reshape(bsz, s, RW_WIDTH)
    out = ((y + bonus) * gate) @ w_o
    return h + out.astype(h.dtype)


def setup_inputs(seed: int = 0) -> dict:
    key = jax.random.key(seed)
    ks = iter(jax.random.split(key, 40))
    f32 = jnp.float32

    def nrm(shape, scale):
        return jax.random.normal(next(ks), shape, f32) * scale

    x = nrm((BATCH, SEQ, D_MODEL), 1.0)
    ab_norm_g = 1.0 + nrm((N_AB_LAYERS, D_MODEL), 0.02)
    ab_w_in = nrm((N_AB_LAYERS, D_MODEL, AB_IN_WIDTH), D_MODEL ** -0.5)
    rg_conv_w = nrm((N_AB_LAYERS, RG_CONV, RG_WIDTH), RG_CONV ** -0.5)
    rg_conv_b = nrm((N_AB_LAYERS, RG_WIDTH), 0.02)
    rg_w_a = nrm((N_AB_LAYERS, RG_BLOCKS, RG_BLOCK_DIM, RG_BLOCK_DIM), RG_BLOCK_DIM ** -0.5)
    rg_b_a = nrm((N_AB_LAYERS, RG_WIDTH), 0.02)
    rg_w_x = nrm((N_AB_LAYERS, RG_BLOCKS, RG_BLOCK_DIM, RG_BLOCK_DIM), RG_BLOCK_DIM ** -0.5)
    rg_b_x = nrm((N_AB_LAYERS, RG_WIDTH), 0.02)
    a_pow = jax.random.uniform(next(ks), (N_AB_LAYERS, RG_WIDTH), f32, minval=0.9, maxval=0.999)
    p = a_pow ** (1.0 / RG_C)
    rg_lambda = jnp.log(p) - jnp.log1p(-p)
    hg_lb_logits = nrm((N_AB_LAYERS + 1, HG_WIDTH), 0.1)
    hg_norm_g = 1.0 + nrm((N_AB_LAYERS, HG_HEAD_DIM), 0.02)
    ab_w_out = nrm((N_AB_LAYERS, AB_MIX_WIDTH, D_MODEL), AB_MIX_WIDTH ** -0.5)

    c_norm_g = 1.0 + nrm((N_C_LAYERS, D_MODEL), 0.02)
    c_mu = jax.random.uniform(next(ks), (N_C_LAYERS, 6, D_MODEL), f32)
    c_w_r = nrm((N_C_LAYERS, D_MODEL, RW_WIDTH), D_MODEL ** -0.5)
    c_w_k = nrm((N_C_LAYERS, D_MODEL, RW_WIDTH), D_MODEL ** -0.5)
    c_w_v = nrm((N_C_LAYERS, D_MODEL, RW_WIDTH), D_MODEL ** -0.5)
    c_w_g = nrm((N_C_LAYERS, D_MODEL, RW_WIDTH), D_MODEL ** -0.5)
    c_w0 = jnp.linspace(-6.0, -1.0, RW_WIDTH, dtype=f32)[None, :] + nrm((N_C_LAYERS, RW_WIDTH), 0.1)
    c_w1 = nrm((N_C_LAYERS, D_MODEL, RW_LORA), D_MODEL ** -0.5)
    c_w2 = nrm((N_C_LAYERS, RW_LORA, RW_WIDTH), 0.5 * RW_LORA ** -0.5)
    c_a0 = nrm((N_C_LAYERS, RW_WIDTH), 0.1)
    c_a1 = nrm((N_C_LAYERS, D_MODEL, RW_LORA), D_MODEL ** -0.5)
    c_a2 = nrm((N_C_LAYERS, RW_LORA, RW_WIDTH), RW_LORA ** -0.5)
    c_k_k = 0.85 + nrm((N_C_LAYERS, RW_WIDTH), 0.02)
    c_k_a = 1.0 + nrm((N_C_LAYERS, RW_WIDTH), 0.02)
    c_r_k = nrm((N_C_LAYERS, RW_HEADS, RW_HEAD_DIM), 0.1)
    c_lnx_g = 1.0 + nrm((N_C_LAYERS, RW_WIDTH), 0.02)
    c_lnx_b = nrm((N_C_LAYERS, RW_WIDTH), 0.02)
    c_w_o = nrm((N_C_LAYERS, RW_WIDTH, D_MODEL), RW_WIDTH ** -0.5)
    final_g = 1.0 + nrm((D_MODEL,), 0.02)
    return {
        "x": x, "ab_norm_g": ab_norm_g, "ab_w_in": ab_w_in, "rg_conv_w": rg_conv_w,
        "rg_conv_b": rg_conv_b, "rg_w_a": rg_w_a, "rg_b_a": rg_b_a, "rg_w_x": rg_w_x,
        "rg_b_x": rg_b_x, "rg_lambda": rg_lambda, "hg_lb_logits": hg_lb_logits,
        "hg_norm_g": hg_norm_g, "ab_w_out": ab_w_out, "c_norm_g": c_norm_g, "c_mu": c_mu,
        "c_w_r": c_w_r, "c_w_k": c_w_k, "c_w_v": c_w_v, "c_w_g": c_w_g, "c_w0": c_w0,
        "c_w1": c_w1, "c_w2": c_w2, "c_a0": c_a0, "c_a1": c_a1, "c_a2": c_a2,
        "c_k_k": c_k_k, "c_k_a": c_k_a, "c_r_k": c_r_k, "c_lnx_g": c_lnx_g,
        "c_lnx_b": c_lnx_b, "c_w_o": c_w_o, "final_g": final_g,
    }


def reference(x, ab_norm_g, ab_w_in, rg_conv_w, rg_conv_b, rg_w_a, rg_b_a, rg_w_x, rg_b_x,
              rg_lambda, hg_lb_logits, hg_norm_g, ab_w_out, c_norm_g, c_mu, c_w_r, c_w_k,
              c_w_v, c_w_g, c_w0, c_w1, c_w2, c_a0, c_a1, c_a2, c_k_k, c_k_a, c_r_k,
              c_lnx_g, c_lnx_b, c_w_o, final_g):
    lb_table = jnp.cumsum(jax.nn.softmax(hg_lb_logits.astype(jnp.float32), axis=0), axis=0)
    h = x
    for layer in range(DEPTH):
        j = layer // 2
        if layer % 2 == 0:
            h = rglru_hgrn2_layer(h, ab_norm_g[j], ab_w_in[j], rg_conv_w[j], rg_conv_b[j],
                                  rg_w_a[j], rg_b_a[j], rg_w_x[j], rg_b_x[j], rg_lambda[j],
                                  lb_table[j], hg_norm_g[j], ab_w_out[j])
        else:
            h = rwkv7_layer(h, c_norm_g[j], c_mu[j], c_w_r[j], c_w_k[j], c_w_v[j], c_w_g[j],
                            c_w0[j], c_w1[j], c_w2[j], c_a0[j], c_a1[j], c_a2[j], c_k_k[j],
                            c_k_a[j], c_r_k[j], c_lnx_g[j], c_lnx_b[j], c_w_o[j])
    return rms_norm(h, final_g)
```

```python
import contextlib
import numpy as np
import concourse.bass as bass
import concourse.mybir as mybir
from concourse.bass_utils import run_bass_kernel_spmd
from concourse.alu_op_type import AluOpType as ALU

F32 = mybir.dt.float32
BF16 = mybir.dt.bfloat16
AF = mybir.ActivationFunctionType

S = 4096
D = 1024
T = 256
NT = S // T
RMS_EPS = 1e-6
GN_EPS = 64e-5
DEC = 0.6065306597126334
SAME_SYNC = True
import os
BSTOP = float(os.environ.get('BSTOP', '99'))
RSTOP = float(os.environ.get('RSTOP', '99'))

_off = {}
_n = 0
for _name, _w in [("ab_g", 8), ("c_g", 8), ("fin_g", 8), ("conv_w", 32), ("conv_b", 8), ("b_a", 8), ("b_x", 8),
                  ("lam", 8), ("lb0", 8), ("lb1", 8), ("hg_g", 1), ("mu", 48), ("w0", 16), ("a0", 16),
                  ("k_k", 16), ("k_a", 16), ("r_k", 16), ("lnx_g", 16), ("lnx_b", 16)]:
    _off[_name] = _n
    _n += _w
NCONST = _n
M_ID = 0
M_ONES = 128
M_OBLK = 256
M_HG = 384
M_RW = 512
M_ID2 = 832
NMASK = 960


class Buf:
    __slots__ = ("t", "name", "w", "r", "busy")

    def __init__(self, t, name):
        self.t = t
        self.name = name
        self.w = None
        self.r = {}
        self.busy = False

    def __getitem__(self, idx):
        return self.t[idx]


class Stream:
    def __init__(self, sem, key):
        self.sem = sem
        self.key = key
        self.count = 0


class KB:
    def __init__(self, nc, es):
        self.nc = nc
        self.es = es
        self.eng = {"pe": nc.tensor, "act": nc.scalar, "dve": nc.vector, "pool": nc.gpsimd, "sp": nc.sync}
        self.st = {k: Stream(es.enter_context(nc.semaphore(k + "_s")), k) for k in self.eng}
        self.waited = {k: {} for k in self.eng}
        self.dstreams = []
        self.banks = []
        self.bank_i = 0
        self.nins = 0

    def dma_stream(self, name):
        s = Stream(self.es.enter_context(self.nc.semaphore(name)), name)
        self.dstreams.append(s)
        return s

    def sbuf(self, name, shape, dtype, es=None):
        es = es or self.es
        self.nins += 0
        self.uid = getattr(self, "uid", 0) + 1
        name = "%s_u%d" % (name, self.uid)
        return Buf(es.enter_context(self.nc.sbuf_tensor(name, list(shape), dtype)), name)

    def init_psum(self):
        for i in range(8):
            self.banks.append(Buf(self.es.enter_context(self.nc.psum_tensor("bank%d" % i, [128, 512], F32)), "bank%d" % i))

    def psum(self):
        b = self.banks[2 + self.bank_i % 6]
        self.bank_i += 1
        assert not b.busy, "psum bank still in use: " + b.name
        b.busy = True
        return b

    def _wait(self, e, sv):
        s, v = sv
        w = self.waited[e]
        if w.get(s.key, 0) >= v:
            return
        w[s.key] = v
        self.eng[e].wait_ge(s.sem, v)

    def _deps(self, e, reads, writes):
        for b in reads:
            if b.name.startswith("bank"):
                for sv in b.r.values():
                    if sv[0].key != e:
                        self._wait(e, sv)
            if b.w is not None:
                if b.w[0].key == e:
                    if SAME_SYNC and e != "pe":
                        self._wait(e, b.w)
                else:
                    self._wait(e, b.w)
        for b in writes:
            if b.w is not None and b.w[0].key != e:
                self._wait(e, b.w)
            for sv in b.r.values():
                if sv[0].key != e:
                    self._wait(e, sv)

    def op(self, e, fn, reads=(), writes=()):
        self._deps(e, reads, writes)
        ins = fn(self.eng[e])
        s = self.st[e]
        s.count += 1
        ins.then_inc(s.sem, 1)
        self.nins += 1
        for b in reads:
            b.r[s.key] = (s, s.count)
        for b in writes:
            b.w = (s, s.count)
            b.r = {}

    def dma(self, q, stream, out_ap, in_ap, reads=(), writes=()):
        self._deps(q, reads, writes)
        ins = self.eng[q].dma_start(out=out_ap, in_=in_ap)
        stream.count += 16
        ins.then_inc(stream.sem, 16)
        self.nins += 1
        for b in reads:
            b.r[stream.key] = (stream, stream.count)
        for b in writes:
            b.w = (stream, stream.count)
            b.r = {}

    def seal(self, stream, bufs):
        for b in bufs:
            b.w = (stream, stream.count)

    def barrier(self):
        allst = list(self.st.values()) + self.dstreams
        for e in self.eng:
            for s in allst:
                if s.key != e and s.count > 0:
                    self._wait(e, (s, s.count))

    def mm(self, bank, out_ap, lhsT, rhs, start, stop, reads):
        self.op("pe", lambda pe: pe.matmul(out_ap, lhsT=lhsT, rhs=rhs, start=start, stop=stop), reads=reads, writes=[bank])

    def act(self, out_ap, in_ap, func, reads, writes, bias=None, scale=None):
        kw = {}
        if bias is not None:
            kw["bias"] = bias
        if scale is not None:
            kw["scale"] = scale
        self.op("act", lambda a: a.activation(out=out_ap, in_=in_ap, func=func, **kw), reads=reads, writes=writes)

    def tt(self, e, out_ap, in0, in1, op, reads, writes):
        self.op(e, lambda v: v.tensor_tensor(out=out_ap, in0=in0, in1=in1, op=op), reads=reads, writes=writes)

    def ts(self, e, out_ap, in0, s1, s2, op0, op1, reads, writes):
        if op1 is None:
            self.op(e, lambda v: v.tensor_scalar(out=out_ap, in0=in0, scalar1=s1, scalar2=None, op0=op0), reads=reads, writes=writes)
        else:
            self.op(e, lambda v: v.tensor_scalar(out=out_ap, in0=in0, scalar1=s1, scalar2=s2, op0=op0, op1=op1), reads=reads, writes=writes)

    def stt(self, out_ap, in0, scalar, in1, op0, op1, reads, writes):
        self.op("dve", lambda v: v.scalar_tensor_tensor(out=out_ap, in0=in0, scalar=scalar, in1=in1, op0=op0, op1=op1), reads=reads, writes=writes)

    def copy(self, e, out_ap, in_ap, reads, writes):
        if e == "act":
            self.op("act", lambda a: a.copy(out=out_ap, in_=in_ap), reads=reads, writes=writes)
        else:
            self.op(e, lambda v: v.tensor_copy(out=out_ap, in_=in_ap), reads=reads, writes=writes)


def build(nt=NT, passes="ABCD", debug=False):
    nc = bass.Bass("TRN2", target_bir_lowering=False)

    def din(name, shape):
        return nc.dram_tensor(name, list(shape), F32, kind="ExternalInput").ap()

    x_d = din("x", [S, D])
    consts_d = din("consts", [128, NCONST])
    masks_d = din("masks", [128, NMASK])
    wAin_d = din("wAin", [128, 8, 2048])
    rgwa_d = din("rgwa", [128, 8, 128])
    rgwx_d = din("rgwx", [128, 8, 128])
    wAout_d = din("wAout", [128, 8, 1024])
    wBin_d = din("wBin", [128, 8, 4096])
    wBout_d = din("wBout", [128, 8, 1024])
    wr_d = [din("wr%d" % h, [128, 8, 1024]) for h in range(2)]
    wk_d = [din("wk%d" % h, [128, 8, 1024]) for h in range(2)]
    wv_d = [din("wv%d" % h, [128, 8, 1024]) for h in range(2)]
    wg_d = [din("wg%d" % h, [128, 8, 1024]) for h in range(2)]
    wo_d = [din("wo%d" % h, [128, 8, 1024]) for h in range(2)]
    w1_d = din("w1", [128, 8, 64])
    a1_d = din("a1", [128, 8, 64])
    w2_d = [din("w2_%d" % h, [64, 1024]) for h in range(2)]
    a2_d = [din("a2_%d" % h, [64, 1024]) for h in range(2)]
    out_d = nc.dram_tensor("out", [S, D], F32, kind="ExternalOutput").ap()
    skind = "ExternalOutput" if debug else "Internal"
    h0_d = nc.dram_tensor("h0fm", [128, 8, S], F32, kind=skind).ap()
    hA_d = nc.dram_tensor("hAfm", [128, 8, S], F32, kind=skind).ap()
    h1_d = nc.dram_tensor("h1fm", [128, 8, S], F32, kind=skind).ap()
    hC_d = nc.dram_tensor("hCfm", [128, 8, S], F32, kind=skind).ap()

    es = contextlib.ExitStack()
    with es:
        k = KB(nc, es)
        k.init_psum()
        cst = k.sbuf("cst", [128, NCONST], F32)
        der = k.sbuf("der", [128, 64], F32)
        mk_f = k.sbuf("mk_f", [128, 128], F32)
        mk_b = k.sbuf("mk_b", [128, NMASK], BF16)
        mk_rw = k.sbuf("mk_rw", [128, 320], F32)
        mk_hg = k.sbuf("mk_hg", [128, 128], F32)
        zeros = k.sbuf("zeros", [128, 64], F32)
        onesf = k.sbuf("onesf", [128, 64], F32)
        cs = k.dma_stream("cstream")
        k.dma("sp", cs, cst[:, :], consts_d[:, :], writes=[cst])
        k.dma("sp", cs, mk_f[:, :], masks_d[:, M_ID:M_ID + 128], writes=[mk_f])
        k.dma("sp", cs, mk_rw[:, :], masks_d[:, M_RW:M_RW + 320], writes=[mk_rw])
        k.dma("sp", cs, mk_hg[:, :], masks_d[:, M_HG:M_HG + 128], writes=[mk_hg])
        cs2 = k.dma_stream("cstream2")
        k.dma("pool", cs2, mk_b[:, :], masks_d[:, :], writes=[mk_b])
        k.seal(cs, [cst, mk_f, mk_rw, mk_hg])
        k.op("pool", lambda g: g.memset(zeros[:, :], 0.0), writes=[zeros])
        k.op("pool", lambda g: g.memset(onesf[:, :], 1.0), writes=[onesf])
        ident_b = mk_b[:, M_ID:M_ID + 128]
        ones_b = mk_b[:, M_ONES:M_ONES + 128]
        oblk_b = mk_b[:, M_OBLK:M_OBLK + 128]

        def C(name, j=0, w=1):
            o = _off[name] + j
            return cst[:, o:o + w]

        k.act(der[:, 0:8], C("lam", 0, 8), AF.Exp, [cst], [der], scale=-1.0)
        k.act(der[:, 0:8], der[:, 0:8], AF.Ln, [der, onesf], [der], bias=onesf[:, 0:1])
        k.ts("dve", der[:, 8:16], der[:, 0:8], -16.0, None, ALU.mult, None, [der], [der])
        k.ts("dve", der[:, 0:8], der[:, 0:8], -8.0, None, ALU.mult, None, [der], [der])
        k.tt("dve", der[:, 16:24], C("lb0", 0, 8), C("lb1", 0, 8), ALU.subtract, [cst], [der])
        k.act(der[:, 16:24], der[:, 16:24], AF.Sigmoid, [der], [der])
        k.ts("dve", der[:, 24:32], der[:, 16:24], -1.0, 1.0, ALU.mult, ALU.add, [der], [der])
        k.ts("dve", der[:, 32:48], C("k_a", 0, 16), -1.0, 1.0, ALU.mult, ALU.add, [cst], [der])
        k.op("pool", lambda g: g.memset(der[:, 48:49], RMS_EPS), writes=[der])
        k.op("pool", lambda g: g.memset(der[:, 49:50], GN_EPS), writes=[der])
        eps_rms = der[:, 48:49]
        eps_gn = der[:, 49:50]

        ws = k.dma_stream("wstream")
        ldS = [k.dma_stream("ldU0"), k.dma_stream("ldU1")]
        ldR = [k.dma_stream("ldR0"), k.dma_stream("ldR1")]
        stS = [k.dma_stream("st0"), k.dma_stream("st1")]
        stS2 = [k.dma_stream("st2_0"), k.dma_stream("st2_1")]
        dr = {nm: [Buf(None, "%s_%d" % (nm, i)) for i in range(NT)] for nm in ("h0", "hA", "h1", "hC")}

        def loadw(buf, dram, nk=8, ncol=None, dcol0=0, dk0=0):
            ncol = ncol or dram.shape[2]
            for kc in range(nk):
                for c0 in range(0, ncol, 1024):
                    c1 = min(ncol, c0 + 1024)
                    k.dma("pool", ws, buf[:, kc, c0:c1], dram[:, dk0 + kc, dcol0 + c0:dcol0 + c1], writes=[buf])

        def emit_norm(pes_bufs, hU, gname, outbuf, col0, fp32_out):
            sq, sd = pes_bufs
            k.act(sq[:, :, :], hU[:, :, :], AF.Square, [hU], [sq])
            bank = k.psum()
            for c in range(8):
                k.mm(bank, bank[:, 0:T], ones_b, sq[:, c, :], c == 0, c == 7, [sq, mk_b])
            k.act(sd[:, :], bank[:, 0:T], AF.Sqrt, [bank, der], [sd], bias=eps_rms, scale=1.0 / D)
            bank.busy = False
            k.op("dve", lambda v: v.reciprocal(out=sd[:, :], in_=sd[:, :]), reads=[sd], writes=[sd])
            for c in range(8):
                k.stt(outbuf[:, c, col0:col0 + T], hU[:, c, :], C(gname, c), sd[:, :], ALU.mult, ALU.mult,
                      [hU, cst, sd], [outbuf])

        def out_proj(Wout, y, hR):
            for dc in range(8):
                bank = k.psum()
                for m in range(8):
                    k.mm(bank, bank[:, 0:T], Wout[:, m, dc * 128:(dc + 1) * 128], y[:, m, :], m == 0, m == 7, [Wout, y])
                k.tt("dve", hR[:, dc, :], hR[:, dc, :], bank[:, 0:T], ALU.add, [hR, bank], [hR])
                bank.busy = False

        def pass_A():
            with contextlib.ExitStack() as pes:
                Win = k.sbuf("A_Win", [128, 8, 2048], BF16, pes)
                Wa = k.sbuf("A_Wa", [128, 8, 128], BF16, pes)
                Wx = k.sbuf("A_Wx", [128, 8, 128], BF16, pes)
                Wout = k.sbuf("A_Wout", [128, 8, 1024], BF16, pes)
                loadw(Win, wAin_d)
                k.dma("pool", ws, Wa[:, :, :], rgwa_d[:, :, :], writes=[Wa])
                k.dma("pool", ws, Wx[:, :, :], rgwx_d[:, :, :], writes=[Wx])
                loadw(Wout, wAout_d)
                k.seal(ws, [Win, Wa, Wx, Wout])
                xts = [k.sbuf("A_xt%d" % i, [128, 2, 1024], F32, pes) for i in range(2)]
                hUs = [k.sbuf("A_hU%d" % i, [128, 8, T], F32, pes) for i in range(2)]
                sq = k.sbuf("A_sq", [128, 8, T], BF16, pes)
                sd = k.sbuf("A_sd", [128, T], F32, pes)
                u = k.sbuf("A_u", [128, 8, T], BF16, pes)
                y = k.sbuf("A_y", [128, 8, T], BF16, pes)
                xaext = [k.sbuf("A_xa%d" % c, [128, T + 3], F32, pes) for c in range(8)]
                carry = [k.sbuf("A_cy%d" % c, [128, 1], F32, pes) for c in range(8)]
                xc = [k.sbuf("A_xc%d" % i, [128, T], F32, pes) for i in range(2)]
                xcb = [k.sbuf("A_xcb%d" % i, [128, T], BF16, pes) for i in range(2)]
                sr = [k.sbuf("A_sr%d" % i, [128, T], F32, pes) for i in range(2)]
                si = [k.sbuf("A_si%d" % i, [128, T], F32, pes) for i in range(2)]
                av = [k.sbuf("A_av%d" % i, [128, T], F32, pes) for i in range(2)]
                mv = [k.sbuf("A_mv%d" % i, [128, T], F32, pes) for i in range(2)]
                uu = [k.sbuf("A_uu%d" % i, [128, T], F32, pes) for i in range(2)]
                hh = [k.sbuf("A_hh%d" % i, [128, T], F32, pes) for i in range(2)]
                sg = [k.sbuf("A_sg%d" % i, [128, T], F32, pes) for i in range(2)]
                for c in range(8):
                    k.op("pool", lambda g, c=c: g.memset(xaext[c][:, :], 0.0), writes=[xaext[c]])
                    k.op("pool", lambda g, c=c: g.memset(carry[c][:, :], 0.0), writes=[carry[c]])

                for ti in range(nt):
                    t0 = ti * T
                    xt = xts[ti % 2]
                    hU = hUs[ti % 2]
                    k.dma("sp", ldS[ti % 2], xt[:, :, :], x_d[t0:t0 + T, :].rearrange("(g p) d -> p g d", p=128), writes=[xt])
                    for cp in range(4):
                        bank = k.psum()
                        for cc in range(2):
                            c = cp * 2 + cc
                            for tg in range(2):
                                o = cc * 256 + tg * 128
                                k.op("pe", lambda pe, o=o, c=c, tg=tg, bank=bank: pe.transpose(
                                    out=bank[:, o:o + 128], in_=xt[:, tg, c * 128:(c + 1) * 128], identity=mk_f[:, :]),
                                    reads=[xt, mk_f], writes=[bank])
                        for cc in range(2):
                            c = cp * 2 + cc
                            k.copy("act" if cc == 0 else "dve", hU[:, c, :], bank[:, cc * 256:cc * 256 + 256], [bank], [hU])
                        bank.busy = False
                    k.dma("sp", stS2[ti % 2], h0_d[:, :, t0:t0 + T], hU[:, :, :], reads=[hU], writes=[dr["h0"][ti]])
                    emit_norm((sq, sd), hU, "ab_g", u, 0, False)
                    for c in range(8):
                        i2 = c % 2
                        b1 = k.psum()
                        for kc in range(8):
                            k.mm(b1, b1[:, 0:T], Win[:, kc, c * 128:(c + 1) * 128], u[:, kc, :], kc == 0, kc == 7, [Win, u])
                        for kc in range(8):
                            k.mm(b1, b1[:, T:2 * T], Win[:, kc, 1024 + c * 128:1024 + (c + 1) * 128], u[:, kc, :], kc == 0, kc == 7, [Win, u])
                        xe = xaext[c]
                        k.copy("pool", xe[:, 0:3], xe[:, T:T + 3], [xe], [xe])
                        k.copy("act", xe[:, 3:T + 3], b1[:, 0:T], [b1], [xe])
                        cw = _off["conv_w"] + c * 4
                        k.ts("dve", xc[i2][:, :], xe[:, 3:T + 3], cst[:, cw + 3:cw + 4], C("conv_b", c), ALU.mult, ALU.add, [xe, cst], [xc[i2]])
                        for j in (2, 1, 0):
                            k.stt(xc[i2][:, :], xe[:, j:j + T], cst[:, cw + j:cw + j + 1], xc[i2][:, :], ALU.mult, ALU.add, [xe, cst, xc[i2]], [xc[i2]])
                        k.copy("pool", xcb[i2][:, :], xc[i2][:, :], [xc[i2]], [xcb[i2]])
                        b2 = k.psum()
                        k.mm(b2, b2[:, 0:T], Wa[:, c, :], xcb[i2][:, :], True, True, [Wa, xcb[i2]])
                        k.mm(b2, b2[:, T:2 * T], Wx[:, c, :], xcb[i2][:, :], True, True, [Wx, xcb[i2]])
                        k.act(sr[i2][:, :], b2[:, 0:T], AF.Sigmoid, [b2, cst], [sr[i2]], bias=C("b_a", c))
                        k.act(si[i2][:, :], b2[:, T:2 * T], AF.Sigmoid, [b2, cst], [si[i2]], bias=C("b_x", c))
                        b2.busy = False
                        k.act(sg[i2][:, :], b1[:, T:2 * T], AF.Silu, [b1], [sg[i2]])
                        b1.busy = False
                        k.act(av[i2][:, :], sr[i2][:, :], AF.Exp, [sr[i2], der], [av[i2]], scale=der[:, c:c + 1])
                        k.act(mv[i2][:, :], sr[i2][:, :], AF.Exp, [sr[i2], der], [mv[i2]], scale=der[:, 8 + c:9 + c])
                        k.act(mv[i2][:, :], mv[i2][:, :], AF.Sqrt, [mv[i2], onesf], [mv[i2]], bias=onesf[:, 0:1], scale=-1.0)
                        k.tt("pool", uu[i2][:, :], si[i2][:, :], xc[i2][:, :], ALU.mult, [si[i2], xc[i2]], [uu[i2]])
                        k.tt("dve", uu[i2][:, :], uu[i2][:, :], mv[i2][:, :], ALU.mult, [uu[i2], mv[i2]], [uu[i2]])
                        k.op("dve", lambda v, i2=i2, c=c: v.tensor_tensor_scan(
                            out=hh[i2][:, :], data0=av[i2][:, :], data1=uu[i2][:, :], initial=carry[c][:, 0:1],
                            op0=ALU.mult, op1=ALU.add), reads=[av[i2], uu[i2], carry[c]], writes=[hh[i2]])
                        k.copy("pool", carry[c][:, 0:1], hh[i2][:, T - 1:T], [hh[i2]], [carry[c]])
                        k.tt("dve", y[:, c, :], hh[i2][:, :], sg[i2][:, :], ALU.mult, [hh[i2], sg[i2]], [y])
                    out_proj(Wout, y, hU)
                    k.dma("sp", stS[ti % 2], hA_d[:, :, t0:t0 + T], hU[:, :, :], reads=[hU], writes=[dr["hA"][ti]])
                k.barrier()

        def pass_B():
            with contextlib.ExitStack() as pes:
                Win = k.sbuf("B_Win", [128, 8, 4096], BF16, pes)
                Wout = k.sbuf("B_Wout", [128, 8, 1024], BF16, pes)
                loadw(Win, wBin_d)
                loadw(Wout, wBout_d)
                k.seal(ws, [Win, Wout])
                hUs = [k.sbuf("B_hU%d" % i, [128, 8, T], F32, pes) for i in range(2)]
                hRs = [k.sbuf("B_hR%d" % i, [128, 8, T], F32, pes) for i in range(2)]
                sq = k.sbuf("B_sq", [128, 8, T], BF16, pes)
                sd = k.sbuf("B_sd", [128, T], F32, pes)
                u = k.sbuf("B_u", [128, 8, T], BF16, pes)
                y = k.sbuf("B_y", [128, 8, T], BF16, pes)
                vtok = k.sbuf("B_vtok", [128, 2, 1024], BF16, pes)
                stf = [k.sbuf("B_stf%d" % h, [128, 128], F32, pes) for h in range(8)]
                stb = [k.sbuf("B_stb%d" % h, [128, 128], BF16, pes) for h in range(8)]
                NB = 2
                sig = [k.sbuf("B_sig%d" % i, [128, T], F32, pes) for i in range(NB)]
                ff = [k.sbuf("B_f%d" % i, [128, T], F32, pes) for i in range(NB)]
                kf = [k.sbuf("B_k%d" % i, [128, T], F32, pes) for i in range(NB)]
                Pc = [k.sbuf("B_P%d" % i, [128, T], F32, pes) for i in range(NB)]
                Pi = [k.sbuf("B_Pi%d" % i, [128, T], F32, pes) for i in range(NB)]
                qd = [k.sbuf("B_qd%d" % i, [128, T], BF16, pes) for i in range(NB)]
                kif = [k.sbuf("B_kif%d" % i, [128, T], F32, pes) for i in range(NB)]
                kib = [k.sbuf("B_kib%d" % i, [128, T], BF16, pes) for i in range(NB)]
                keb = [k.sbuf("B_keb%d" % i, [128, T], BF16, pes) for i in range(NB)]
                scm = [k.sbuf("B_scm%d" % i, [128, 128], BF16, pes) for i in range(NB)]
                ket = [k.sbuf("B_ket%d" % i, [128, 128], BF16, pes) for i in range(NB)]
                osq = [k.sbuf("B_osq%d" % i, [128, T], BF16, pes) for i in range(NB)]
                ors = [k.sbuf("B_ors%d" % i, [128, T], F32, pes) for i in range(NB)]
                o1 = [k.sbuf("B_o1%d" % i, [128, T], F32, pes) for i in range(NB)]
                sgb = [k.sbuf("B_sg%d" % i, [128, T], F32, pes) for i in range(NB)]
                for h in range(8):
                    k.op("pool", lambda g, h=h: g.memset(stf[h][:, :], 0.0), writes=[stf[h]])
                    k.op("pool", lambda g, h=h: g.memset(stb[h][:, :], 0.0), writes=[stb[h]])
                accb = [k.banks[0], k.banks[1]]
                for ti in range(nt):
                    t0 = ti * T
                    hU = hUs[ti % 2]
                    hR = hRs[ti % 2]
                    k.dma("sp", ldS[ti % 2], hU[:, :, :], h0_d[:, :, t0:t0 + T], reads=[dr["h0"][ti]], writes=[hU])
                    k.dma("sp", ldR[ti % 2], hR[:, :, :], hA_d[:, :, t0:t0 + T], reads=[dr["hA"][ti]], writes=[hR])
                    emit_norm((sq, sd), hU, "ab_g", u, 0, False)
                    for tg in range(2):
                        for cg in range(2):
                            bank = k.psum()
                            for kc in range(8):
                                k.mm(bank, bank[:, 0:512], u[:, kc, tg * 128:(tg + 1) * 128],
                                     Win[:, kc, 2048 + cg * 512:2048 + (cg + 1) * 512], kc == 0, kc == 7, [u, Win])
                            k.copy("act" if cg == 0 else "dve", vtok[:, tg, cg * 512:(cg + 1) * 512], bank[:, 0:512], [bank], [vtok])
                            bank.busy = False
                    for h in range(8 if BSTOP > 1 else 0):
                        i2 = h % NB
                        bq = k.psum()
                        for kc in range(8):
                            k.mm(bq, bq[:, 0:T], Win[:, kc, h * 128:(h + 1) * 128], u[:, kc, :], kc == 0, kc == 7, [Win, u])
                        for kc in range(8):
                            k.mm(bq, bq[:, T:2 * T], Win[:, kc, 1024 + h * 128:1024 + (h + 1) * 128], u[:, kc, :], kc == 0, kc == 7, [Win, u])
                        k.act(sig[i2][:, :], bq[:, T:2 * T], AF.Sigmoid, [bq], [sig[i2]])
                        k.ts("dve", ff[i2][:, :], sig[i2][:, :], der[:, 24 + h:25 + h], der[:, 16 + h:17 + h], ALU.mult, ALU.add, [sig[i2], der], [ff[i2]])
                        k.ts("pool", kf[i2][:, :], ff[i2][:, :], -1.0, 1.0, ALU.mult, ALU.add, [ff[i2]], [kf[i2]])
                        for j in range(T // 64):
                            k.op("dve", lambda v, i2=i2, j=j: v.tensor_tensor_scan(
                                out=Pc[i2][:, j * 64:(j + 1) * 64], data0=ff[i2][:, j * 64:(j + 1) * 64], data1=zeros[:, 0:64],
                                initial=1.0, op0=ALU.mult, op1=ALU.add), reads=[ff[i2], zeros], writes=[Pc[i2]])
                        k.op("dve", lambda v, i2=i2: v.reciprocal(out=Pi[i2][:, :], in_=Pc[i2][:, :]), reads=[Pc[i2]], writes=[Pi[i2]])
                        k.tt("dve", qd[i2][:, :], bq[:, 0:T], Pc[i2][:, :], ALU.mult, [bq, Pc[i2]], [qd[i2]])
                        bq.busy = False
                        k.tt("pool", kif[i2][:, :], kf[i2][:, :], Pi[i2][:, :], ALU.mult, [kf[i2], Pi[i2]], [kif[i2]])
                        k.copy("pool", kib[i2][:, :], kif[i2][:, :], [kif[i2]], [kib[i2]])
                        for j in range(T // 64):
                            k.ts("dve", keb[i2][:, j * 64:(j + 1) * 64], kif[i2][:, j * 64:(j + 1) * 64],
                                 Pc[i2][:, j * 64 + 63:j * 64 + 64], None, ALU.mult, None, [kif[i2], Pc[i2]], [keb[i2]])
                        bo = accb[h % 2]
                        for tg in range(2 if BSTOP > 2 else 0):
                            c0 = tg * 128
                            bs = k.psum()
                            if BSTOP != 2.6:
                                k.mm(bs, bs[:, 0:128], kib[i2][:, c0:c0 + 128], qd[i2][:, c0:c0 + 128], True, True, [kib[i2], qd[i2]])
                            bs2 = bs
                            if BSTOP != 2.3:
                                k.mm(bs2, bs2[:, 128:256], keb[i2][:, c0:c0 + 128], ident_b, True, True, [keb[i2], mk_b])
                            if BSTOP != 2.6:
                                k.tt("dve", scm[i2][:, :], bs[:, 0:128], mk_hg[:, :], ALU.mult, [bs, mk_hg], [scm[i2]])
                            if BSTOP != 2.3:
                                k.copy("act", ket[i2][:, :], bs2[:, 128:256], [bs2], [ket[i2]])
                            bs.busy = False
                            for jp in range(2 if BSTOP > 3 else 0):
                                cj = c0 + jp * 64
                                r0 = jp * 64
                                k.mm(bo, bo[:, cj:cj + 64], vtok[:, tg, h * 128:(h + 1) * 128], scm[i2][:, r0:r0 + 64], True, False, [vtok, scm[i2]])
                                k.mm(bo, bo[:, cj:cj + 64], stb[h][:, :], qd[i2][:, cj:cj + 64], False, True, [stb[h], qd[i2]])
                                bst = k.psum()
                                k.mm(bst, bst[:, 0:128], ket[i2][r0:r0 + 64, :], vtok[r0:r0 + 64, tg, h * 128:(h + 1) * 128], True, True, [ket[i2], vtok])
                                k.stt(stf[h][:, :], stf[h][:, :], Pc[i2][:, cj + 63:cj + 64], bst[:, 0:128], ALU.mult, ALU.add, [stf[h], Pc[i2], bst], [stf[h]])
                                bst.busy = False
                                k.copy("act", stb[h][:, :], stf[h][:, :], [stf[h]], [stb[h]])
                        bg = k.psum()
                        for kc in range(8):
                            k.mm(bg, bg[:, 0:T], Win[:, kc, 3072 + h * 128:3072 + (h + 1) * 128], u[:, kc, :], kc == 0, kc == 7, [Win, u])
                        k.act(sgb[i2][:, :], bg[:, 0:T], AF.Silu, [bg], [sgb[i2]])
                        k.act(osq[i2][:, :], bo[:, 0:T], AF.Square, [bo], [osq[i2]])
                        k.mm(bg, bg[:, T:2 * T], ones_b, osq[i2][:, :], True, True, [mk_b, osq[i2]])
                        k.act(ors[i2][:, :], bg[:, T:2 * T], AF.Sqrt, [bg, der], [ors[i2]], bias=eps_rms, scale=1.0 / 128)
                        bg.busy = False
                        k.op("dve", lambda v, i2=i2: v.reciprocal(out=ors[i2][:, :], in_=ors[i2][:, :]), reads=[ors[i2]], writes=[ors[i2]])
                        k.tt("dve", o1[i2][:, :], bo[:, 0:T], ors[i2][:, :], ALU.mult, [bo, ors[i2]], [o1[i2]])
                        k.stt(y[:, h, :], o1[i2][:, :], C("hg_g"), sgb[i2][:, :], ALU.mult, ALU.mult, [o1[i2], cst, sgb[i2]], [y])
                    out_proj(Wout, y, hR)
                    k.dma("sp", stS[ti % 2], h1_d[:, :, t0:t0 + T], hR[:, :, :], reads=[hR], writes=[dr["h1"][ti]])
                k.barrier()


        def pass_R(q, res_d, res_tr, dst_d, dst_tr, last):
            hf, qo = q // 2, (q % 2) * 512
            with contextlib.ExitStack() as pes:
                Wr = k.sbuf("R_Wr", [128, 8, 512], BF16, pes)
                Wk = k.sbuf("R_Wk", [128, 8, 512], BF16, pes)
                Wv = k.sbuf("R_Wv", [128, 8, 512], BF16, pes)
                Wg = k.sbuf("R_Wg", [128, 8, 512], BF16, pes)
                W1 = k.sbuf("R_W1", [128, 8, 64], BF16, pes)
                A1 = k.sbuf("R_A1", [128, 8, 64], BF16, pes)
                W2 = k.sbuf("R_W2", [64, 512], BF16, pes)
                A2 = k.sbuf("R_A2", [64, 512], BF16, pes)
                Wo = k.sbuf("R_Wo", [128, 4, 1024], BF16, pes)
                loadw(Wr, wr_d[hf], ncol=512, dcol0=qo)
                loadw(Wk, wk_d[hf], ncol=512, dcol0=qo)
                loadw(Wv, wv_d[hf], ncol=512, dcol0=qo)
                loadw(Wg, wg_d[hf], ncol=512, dcol0=qo)
                k.dma("pool", ws, W1[:, :, :], w1_d[:, :, :], writes=[W1])
                k.dma("pool", ws, A1[:, :, :], a1_d[:, :, :], writes=[A1])
                k.dma("pool", ws, W2[:, :], w2_d[hf][:, qo:qo + 512], writes=[W2])
                k.dma("pool", ws, A2[:, :], a2_d[hf][:, qo:qo + 512], writes=[A2])
                loadw(Wo, wo_d[hf], nk=4, dk0=(q % 2) * 4)
                k.seal(ws, [Wr, Wk, Wv, Wg, W1, A1, W2, A2, Wo])
                hUs = [k.sbuf("R_hU%d" % i, [128, 8, T], F32, pes) for i in range(2)]
                hRs = [k.sbuf("R_hR%d" % i, [128, 8, T], F32, pes) for i in range(2)] if q > 0 else hUs
                sq = k.sbuf("R_sq", [128, 8, T], BF16, pes)
                sd = k.sbuf("R_sd", [128, T], F32, pes)
                ufp = k.sbuf("R_ufp", [128, 8, T + 1], F32, pes)
                delta = k.sbuf("R_delta", [128, 8, T], F32, pes)
                xj = [k.sbuf("R_x%d" % j, [128, 8, T], BF16, pes) for j in range(6)]
                y = k.sbuf("R_y", [128, 4, T], BF16, pes)
                vtok = k.sbuf("R_vtok", [128, 2, 512], BF16, pes)
                tw = k.sbuf("R_tw", [64, T], BF16, pes)
                al = k.sbuf("R_al", [64, T], BF16, pes)
                Tf = [k.sbuf("R_Tf%d" % p, [128, 64], F32, pes) for p in range(4)]
                Tb = [k.sbuf("R_Tb%d" % p, [128, 64], BF16, pes) for p in range(4)]
                ot = k.sbuf("R_ot", [128, 2, 1024], F32, pes) if last else None
                f32n = ["sw", "av", "rf", "gate", "kkp", "nk", "kf", "bb", "cum", "cm", "g", "gi", "vfm", "bonus", "yf", "mean", "var"]
                t32 = {n_: k.sbuf("R_" + n_, [128, T], F32, pes) for n_ in f32n}
                b16n = ["kk2", "rk2", "ktb", "btb", "kend", "bend", "yb16", "ysq"]
                t16 = {n_: k.sbuf("R_" + n_, [128, T], BF16, pes) for n_ in b16n}
                krt = k.sbuf("R_krt", [128, 2, T], BF16, pes)
                tokA = k.sbuf("R_tokA", [128, 512], BF16, pes)
                tokB = k.sbuf("R_tokB", [128, 256], BF16, pes)
                GZ = [[k.sbuf("R_Gz%d_%d" % (i, j), [128, 2, 320], BF16, pes) for j in range(2)] for i in range(2)]
                QZ = [[k.sbuf("R_Qz%d_%d" % (i, j), [128, 128], BF16, pes) for j in range(2)] for i in range(2)]
                NUZ = [[k.sbuf("R_nUz%d_%d" % (i, j), [128, 128], BF16, pes) for j in range(2)] for i in range(2)]
                WTD = [[k.sbuf("R_WTd%d_%d" % (i, j), [128, 64], BF16, pes) for j in range(2)] for i in range(2)]
                KEZ = [k.sbuf("R_kez%d" % j, [128, 256], BF16, pes) for j in range(2)]
                AKV = k.sbuf("R_akv", [128, 128], BF16, pes)
                UP = k.sbuf("R_up", [128, 128], F32, pes)
                Tbk = [k.sbuf("R_Tbk%d" % p, [128, 128], BF16, pes) for p in range(4)]
                for bl_ in [b_ for i in range(2) for b_ in GZ[i] + QZ[i] + NUZ[i]] + KEZ + Tbk:
                    k.op("pool", lambda g_, bl_=bl_: g_.memset(bl_[:, :, :] if len(bl_.t.shape) == 3 else bl_[:, :], 0.0), writes=[bl_])
                Qs = [k.sbuf("R_Q%d" % i, [128, 128], BF16, pes) for i in range(2)]
                PPs = [k.sbuf("R_PP%d" % i, [128, 256], BF16, pes) for i in range(2)]
                IP = k.sbuf("R_IP", [128, 2, 64], BF16, pes)
                WA = [k.sbuf("R_WA%d" % i, [128, 256], BF16, pes) for i in range(2)]
                nU = [k.sbuf("R_nU%d" % i, [128, 128], BF16, pes) for i in range(2)]
                id2 = mk_b[:, M_ID2:M_ID2 + 128]
                k.op("pool", lambda g_: g_.memset(ufp[:, :, :], 0.0), writes=[ufp])
                for p in range(4):
                    k.op("pool", lambda g_, p=p: g_.memset(Tf[p][:, :], 0.0), writes=[Tf[p]])
                    k.op("pool", lambda g_, p=p: g_.memset(Tb[p][:, :], 0.0), writes=[Tb[p]])
                accb = [k.banks[0], k.banks[1]]
                ucount = 0
                for ti in range(nt):
                    t0 = ti * T
                    hU = hUs[ti % 2]
                    hR = hRs[ti % 2]
                    k.dma("sp", ldS[ti % 2], hU[:, :, :], h1_d[:, :, t0:t0 + T], reads=[dr["h1"][ti]], writes=[hU])
                    if q > 0:
                        k.dma("sp", ldR[ti % 2], hR[:, :, :], res_d[:, :, t0:t0 + T], reads=[res_tr[ti]], writes=[hR])
                    k.copy("pool", ufp[:, :, 0:1], ufp[:, :, T:T + 1], [ufp], [ufp])
                    emit_norm((sq, sd), hU, "c_g", ufp, 1, True)
                    k.tt("pool", delta[:, :, :], ufp[:, :, 0:T], ufp[:, :, 1:T + 1], ALU.subtract, [ufp], [delta])
                    for j in range(6):
                        for c in range(8):
                            mo = _off["mu"] + j * 8 + c
                            k.stt(xj[j][:, c, :], delta[:, c, :], cst[:, mo:mo + 1], ufp[:, c, 1:T + 1], ALU.mult, ALU.add,
                                  [delta, cst, ufp], [xj[j]])
                    xr, xw, xk, xv, xa, xg = xj
                    bl = k.psum()
                    for kc in range(8):
                        k.mm(bl, bl[0:64, 0:T], W1[:, kc, :], xw[:, kc, :], kc == 0, kc == 7, [W1, xw])
                    for kc in range(8):
                        k.mm(bl, bl[0:64, T:2 * T], A1[:, kc, :], xa[:, kc, :], kc == 0, kc == 7, [A1, xa])
                    k.act(tw[:, :], bl[0:64, 0:T], AF.Tanh, [bl], [tw])
                    k.copy("act", al[:, :], bl[0:64, T:2 * T], [bl], [al])
                    bl.busy = False
                    for tg in range(2):
                        bank = k.psum()
                        for kc in range(8):
                            k.mm(bank, bank[:, 0:512], xv[:, kc, tg * 128:(tg + 1) * 128], Wv[:, kc, :], kc == 0, kc == 7, [xv, Wv])
                        k.copy("act" if tg == 0 else "dve", vtok[:, tg, :], bank[:, 0:512], [bank], [vtok])
                        bank.busy = False
                    for p in range(4 if RSTOP > 1 else 0):
                        pg = q * 4 + p
                        cols = slice(p * 128, (p + 1) * 128)
                        X = t32
                        b_rk = k.psum()
                        for kc in range(8):
                            k.mm(b_rk, b_rk[:, 0:T], Wr[:, kc, cols], xr[:, kc, :], kc == 0, kc == 7, [Wr, xr])
                        for kc in range(8):
                            k.mm(b_rk, b_rk[:, T:2 * T], Wk[:, kc, cols], xk[:, kc, :], kc == 0, kc == 7, [Wk, xk])
                        b_gw = k.psum()
                        for kc in range(8):
                            k.mm(b_gw, b_gw[:, 0:T], Wg[:, kc, cols], xg[:, kc, :], kc == 0, kc == 7, [Wg, xg])
                        k.mm(b_gw, b_gw[:, T:2 * T], W2[:, cols], tw[:, :], True, True, [W2, tw])
                        b_av = k.psum()
                        k.mm(b_av, b_av[:, 0:T], A2[:, cols], al[:, :], True, True, [A2, al])
                        for tg in range(2):
                            k.mm(b_av, b_av[:, T + tg * 128:T + (tg + 1) * 128], vtok[:, tg, cols], ident_b, True, True, [vtok, mk_b])
                        k.act(X["sw"][:, :], b_gw[:, T:2 * T], AF.Sigmoid, [b_gw, cst], [X["sw"]], bias=C("w0", pg))
                        k.act(X["gate"][:, :], b_gw[:, 0:T], AF.Silu, [b_gw], [X["gate"]])
                        b_gw.busy = False
                        k.act(X["av"][:, :], b_av[:, 0:T], AF.Sigmoid, [b_av, cst], [X["av"]], bias=C("a0", pg))
                        k.copy("act", X["vfm"][:, :], b_av[:, T:2 * T], [b_av], [X["vfm"]])
                        b_av.busy = False
                        k.copy("act", X["rf"][:, :], b_rk[:, 0:T], [b_rk], [X["rf"]])
                        k.ts("dve", X["kkp"][:, :], b_rk[:, T:2 * T], C("k_k", pg), None, ALU.mult, None, [b_rk, cst], [X["kkp"]])
                        k.ts("dve", X["kf"][:, :], X["av"][:, :], C("k_a", pg), der[:, 32 + pg:33 + pg], ALU.mult, ALU.add, [X["av"], cst, der], [X["kf"]])
                        k.tt("dve", X["kf"][:, :], b_rk[:, T:2 * T], X["kf"][:, :], ALU.mult, [b_rk, X["kf"]], [X["kf"]])
                        b_rk.busy = False
                        k.act(t16["kk2"][:, :], X["kkp"][:, :], AF.Square, [X["kkp"]], [t16["kk2"]])
                        b_n = k.psum()
                        k.mm(b_n, b_n[:, 0:T], oblk_b, t16["kk2"][:, :], True, True, [mk_b, t16["kk2"]])
                        k.act(X["nk"][:, :], b_n[:, 0:T], AF.Sqrt, [b_n], [X["nk"]])
                        k.ts("dve", X["nk"][:, :], X["nk"][:, :], 1e-12, None, ALU.max, None, [X["nk"]], [X["nk"]])
                        k.op("dve", lambda v: v.reciprocal(out=X["nk"][:, :], in_=X["nk"][:, :]), reads=[X["nk"]], writes=[X["nk"]])
                        k.tt("dve", X["kkp"][:, :], X["kkp"][:, :], X["nk"][:, :], ALU.mult, [X["kkp"], X["nk"]], [X["kkp"]])
                        k.tt("pool", X["bb"][:, :], X["kkp"][:, :], X["av"][:, :], ALU.mult, [X["kkp"], X["av"]], [X["bb"]])
                        k.stt(t16["rk2"][:, :], X["rf"][:, :], C("r_k", pg), X["kf"][:, :], ALU.mult, ALU.mult, [X["rf"], cst, X["kf"]], [t16["rk2"]])
                        k.mm(b_n, b_n[:, T:2 * T], oblk_b, t16["rk2"][:, :], True, True, [mk_b, t16["rk2"]])
                        k.tt("dve", X["bonus"][:, :], b_n[:, T:2 * T], X["vfm"][:, :], ALU.mult, [b_n, X["vfm"]], [X["bonus"]])
                        b_n.busy = False
                        for j in range(T // 64):
                            k.op("dve", lambda v, j=j: v.tensor_tensor_scan(
                                out=X["cum"][:, j * 64:(j + 1) * 64], data0=onesf[:, 0:64], data1=X["sw"][:, j * 64:(j + 1) * 64],
                                initial=0.0, op0=ALU.mult, op1=ALU.add), reads=[onesf, X["sw"]], writes=[X["cum"]])
                        k.tt("pool", X["cm"][:, :], X["cum"][:, :], X["sw"][:, :], ALU.subtract, [X["cum"], X["sw"]], [X["cm"]])
                        k.act(X["g"][:, :], X["cum"][:, :], AF.Exp, [X["cum"]], [X["g"]], scale=-DEC)
                        k.act(X["gi"][:, :], X["cum"][:, :], AF.Exp, [X["cum"]], [X["gi"]], scale=DEC)
                        k.act(X["cm"][:, :], X["cm"][:, :], AF.Exp, [X["cm"]], [X["cm"]], scale=-DEC)
                        k.tt("pool", krt[:, 0, :], X["kkp"][:, :], X["cm"][:, :], ALU.mult, [X["kkp"], X["cm"]], [krt])
                        k.tt("dve", krt[:, 1, :], X["rf"][:, :], X["g"][:, :], ALU.mult, [X["rf"], X["g"]], [krt])
                        k.tt("dve", X["kf"][:, :], X["kf"][:, :], X["gi"][:, :], ALU.mult, [X["kf"], X["gi"]], [X["kf"]])
                        k.tt("pool", X["bb"][:, :], X["bb"][:, :], X["gi"][:, :], ALU.mult, [X["bb"], X["gi"]], [X["bb"]])
                        k.copy("pool", t16["ktb"][:, :], X["kf"][:, :], [X["kf"]], [t16["ktb"]])
                        k.copy("pool", t16["btb"][:, :], X["bb"][:, :], [X["bb"]], [t16["btb"]])
                        for j in range(T // 64):
                            sl = slice(j * 64, (j + 1) * 64)
                            ge = X["g"][:, j * 64 + 63:j * 64 + 64]
                            k.ts("dve", t16["kend"][:, sl], X["kf"][:, sl], ge, None, ALU.mult, None, [X["kf"], X["g"]], [t16["kend"]])
                            k.ts("dve", t16["bend"][:, sl], X["bb"][:, sl], ge, None, ALU.mult, None, [X["bb"], X["g"]], [t16["bend"]])
                        b_t = k.psum()
                        for tg in range(2):
                            k.mm(b_t, b_t[:, tg * 128:(tg + 1) * 128], krt[:, 0, tg * 128:(tg + 1) * 128], ident_b, True, True, [krt, mk_b])
                        for tg in range(2):
                            k.mm(b_t, b_t[:, 256 + tg * 128:256 + (tg + 1) * 128], t16["kend"][:, tg * 128:(tg + 1) * 128], ident_b, True, True, [t16["kend"], mk_b])
                        k.copy("act", tokA[:, 0:256], b_t[:, 0:256], [b_t], [tokA])
                        for jp in range(2):
                            ro = slice(64 * jp, 64 * jp + 64)
                            k.copy("act", KEZ[jp][ro, :], b_t[ro, 256:512], [b_t], [KEZ[jp]])
                        b_t.busy = False
                        b_t2 = k.psum()
                        for tg in range(2):
                            k.mm(b_t2, b_t2[:, tg * 128:(tg + 1) * 128], t16["bend"][:, tg * 128:(tg + 1) * 128], ident_b, True, True, [t16["bend"], mk_b])
                        k.copy("dve", tokB[:, :], b_t2[:, 0:256], [b_t2], [tokB])
                        b_t2.busy = False
                        psY = accb[p % 2]
                        for tg in range(2 if RSTOP > 2 else 0):
                            Gz = GZ[ucount % 2]
                            Qz = QZ[ucount % 2]
                            nUz = NUZ[ucount % 2]
                            WTd = WTD[ucount % 2]
                            ucount += 1
                            for hp in range(2):
                                rs_ = slice(64 * hp, 64 * hp + 64)
                                bG = k.psum()
                                for jp in range(2):
                                    cs_ = slice(tg * 128 + jp * 64, tg * 128 + jp * 64 + 64)
                                    ro = slice(64 * jp, 64 * jp + 64)
                                    k.mm(bG, bG[ro, 0:64], t16["ktb"][rs_, cs_], krt[rs_, 0, cs_], True, True, [t16["ktb"], krt])
                                    k.mm(bG, bG[ro, 64:128], t16["ktb"][rs_, cs_], krt[rs_, 1, cs_], True, True, [t16["ktb"], krt])
                                    k.mm(bG, bG[ro, 128:192], t16["btb"][rs_, cs_], krt[rs_, 0, cs_], True, True, [t16["btb"], krt])
                                    k.mm(bG, bG[ro, 192:256], t16["btb"][rs_, cs_], krt[rs_, 1, cs_], True, True, [t16["btb"], krt])
                                    k.mm(bG, bG[ro, 256:320], krt[rs_, 0, cs_], t16["btb"][rs_, cs_], True, True, [t16["btb"], krt])
                                for jp in range(2):
                                    ro = slice(64 * jp, 64 * jp + 64)
                                    k.tt("dve", Gz[jp][ro, hp, :], bG[ro, 0:320], mk_rw[ro, :], ALU.mult, [bG, mk_rw], [Gz[jp]])
                                bG.busy = False
                            Qc = Qs[0]
                            for jp in range(2):
                                ro = slice(64 * jp, 64 * jp + 64)
                                for hp in range(2):
                                    k.tt("pool", Qc[ro, 64 * hp:64 * hp + 64], id2[ro, 0:64], Gz[jp][ro, hp, 128:192], ALU.subtract, [mk_b, Gz[jp]], [Qc])
                            qi = 0
                            for lvl in range(1, 6):
                                bP = k.psum()
                                for hp in range(2):
                                    for jp in range(2):
                                        ro = slice(64 * jp, 64 * jp + 64)
                                        if lvl == 1:
                                            Pm, PTm, Pb = Gz[jp][ro, hp, 256:320], Gz[jp][ro, hp, 128:192], Gz[jp]
                                        else:
                                            PPc = PPs[lvl % 2]
                                            Pm, PTm, Pb = PPc[ro, hp * 128:hp * 128 + 64], PPc[ro, hp * 128 + 64:hp * 128 + 128], PPc
                                        k.mm(bP, bP[ro, hp * 128:hp * 128 + 64], PTm, Pm, True, True, [Pb])
                                        if lvl < 5:
                                            k.mm(bP, bP[ro, hp * 128 + 64:hp * 128 + 128], Pm, PTm, True, True, [Pb])
                                for hp in range(2):
                                    k.tt("dve", IP[:, hp, :], bP[:, hp * 128:hp * 128 + 64], id2[:, 0:64], ALU.add, [bP, mk_b], [IP])
                                if lvl < 5:
                                    PPn = PPs[(lvl + 1) % 2]
                                    k.copy("dve", PPn[:, :], bP[:, 0:256], [bP], [PPn])
                                bP.busy = False
                                bQ = k.psum()
                                for hp in range(2):
                                    for jp in range(2):
                                        ro = slice(64 * jp, 64 * jp + 64)
                                        k.mm(bQ, bQ[ro, hp * 64:hp * 64 + 64], IP[ro, hp, :], Qs[qi][ro, hp * 64:hp * 64 + 64], True, True, [IP, Qs[qi]])
                                if lvl < 5:
                                    Qn = Qs[1 - qi]
                                    k.copy("act", Qn[:, :], bQ[:, 0:128], [bQ], [Qn])
                                else:
                                    for jp in range(2):
                                        ro = slice(64 * jp, 64 * jp + 64)
                                        k.copy("act", Qz[jp][ro, :], bQ[ro, 0:128], [bQ], [Qz[jp]])
                                bQ.busy = False
                                qi = 1 - qi
                            if RSTOP <= 3:
                                continue
                            bA = k.psum()
                            for hp in range(2):
                                for jp in range(2):
                                    ro = slice(64 * jp, 64 * jp + 64)
                                    vc = slice(p * 128 + 64 * hp, p * 128 + 64 * hp + 64)
                                    k.mm(bA, bA[ro, 64 * hp:64 * hp + 64], Gz[jp][ro, hp, 0:64], vtok[ro, tg, vc], True, True, [Gz[jp], vtok])
                            k.copy("act", AKV[:, :], bA[:, 0:128], [bA], [AKV])
                            bA.busy = False
                            bX = k.psum()
                            for hp in range(2):
                                for jp in range(2):
                                    ro = slice(64 * jp, 64 * jp + 64)
                                    k.mm(bX, bX[ro, 64 * hp:64 * hp + 64], Qz[jp][ro, hp * 64:hp * 64 + 64], AKV[ro, 64 * hp:64 * hp + 64], True, True, [Qz[jp], AKV])
                            k.copy("dve", UP[:, :], bX[:, 0:128], [bX], [UP])
                            bX.busy = False
                            bW = k.psum()
                            for jp in range(2):
                                k.mm(bW, bW[:, jp * 128:(jp + 1) * 128], tokA[:, tg * 128:(tg + 1) * 128], Qz[jp][:, :], True, True, [tokA, Qz[jp]])
                            for jp in range(2):
                                for hp in range(2):
                                    rs_ = slice(64 * hp, 64 * hp + 64)
                                    k.copy("act" if hp == 0 else "dve", WTd[jp][rs_, :], bW[rs_, jp * 128 + 64 * hp:jp * 128 + 64 * hp + 64], [bW], [WTd[jp]])
                            bW.busy = False
                            for jp in range(2 if RSTOP > 3.3 else 0):
                                ro = slice(64 * jp, 64 * jp + 64)
                                c0 = tg * 128 + jp * 64
                                cs_ = slice(c0, c0 + 64)
                                bU = k.psum()
                                k.mm(bU, bU[ro, 0:128], WTd[jp][:, :], Tbk[p][:, :], True, True, [WTd[jp], Tbk[p]])
                                k.stt(nUz[jp][ro, :], bU[ro, 0:128], -1.0, UP[ro, :], ALU.mult, ALU.subtract, [bU, UP], [nUz[jp]])
                                bU.busy = False
                                if RSTOP <= 3.6:
                                    continue
                                k.mm(psY, psY[:, cs_], Tbk[p][:, :], krt[:, 1, cs_], True, False, [Tbk[p], krt])
                                for hp in range(2):
                                    rs_ = slice(64 * hp, 64 * hp + 64)
                                    vc = slice(p * 128 + 64 * hp, p * 128 + 64 * hp + 64)
                                    k.mm(psY, psY[rs_, cs_], vtok[:, tg, vc], Gz[jp][:, hp, 64:128], False, False, [vtok, Gz[jp]])
                                    k.mm(psY, psY[rs_, cs_], nUz[jp][:, 64 * hp:64 * hp + 64], Gz[jp][:, hp, 192:256], False, True, [nUz[jp], Gz[jp]])
                                if RSTOP <= 3.8:
                                    continue
                                bS = k.psum()
                                for hp in range(2):
                                    rs_ = slice(64 * hp, 64 * hp + 64)
                                    vc = slice(p * 128 + 64 * hp, p * 128 + 64 * hp + 64)
                                    hc = slice(tg * 128 + 64 * hp, tg * 128 + 64 * hp + 64)
                                    k.mm(bS, bS[rs_, 0:64], KEZ[jp][:, hc], vtok[:, tg, vc], True, False, [KEZ[jp], vtok])
                                    k.mm(bS, bS[rs_, 0:64], tokB[:, hc], nUz[jp][:, 64 * hp:64 * hp + 64], False, True, [tokB, nUz[jp]])
                                k.stt(Tf[p][:, :], Tf[p][:, :], X["g"][:, c0 + 63:c0 + 64], bS[:, 0:64], ALU.mult, ALU.add, [Tf[p], X["g"], bS], [Tf[p]])
                                bS.busy = False
                                for hp in range(2 if RSTOP > 3.9 else 0):
                                    rs_ = slice(64 * hp, 64 * hp + 64)
                                    k.copy("act", Tbk[p][rs_, 64 * hp:64 * hp + 64], Tf[p][rs_, :], [Tf[p]], [Tbk[p]])
                        k.copy("act", X["yf"][:, :], psY[:, 0:T], [psY], [X["yf"]])
                        k.act(t16["ysq"][:, :], psY[:, 0:T], AF.Square, [psY], [t16["ysq"]])
                        k.copy("pool", t16["yb16"][:, :], X["yf"][:, :], [X["yf"]], [t16["yb16"]])
                        bM = k.psum()
                        k.mm(bM, bM[:, 0:T], oblk_b, t16["yb16"][:, :], True, True, [mk_b, t16["yb16"]])
                        k.mm(bM, bM[:, T:2 * T], oblk_b, t16["ysq"][:, :], True, True, [mk_b, t16["ysq"]])
                        k.ts("dve", X["mean"][:, :], bM[:, 0:T], 1.0 / 64, None, ALU.mult, None, [bM], [X["mean"]])
                        k.tt("dve", X["var"][:, :], X["mean"][:, :], X["mean"][:, :], ALU.mult, [X["mean"]], [X["var"]])
                        k.stt(X["var"][:, :], bM[:, T:2 * T], 1.0 / 64, X["var"][:, :], ALU.mult, ALU.subtract, [bM, X["var"]], [X["var"]])
                        bM.busy = False
                        k.act(X["var"][:, :], X["var"][:, :], AF.Sqrt, [X["var"], der], [X["var"]], bias=eps_gn)
                        k.op("dve", lambda v: v.reciprocal(out=X["var"][:, :], in_=X["var"][:, :]), reads=[X["var"]], writes=[X["var"]])
                        k.tt("dve", X["yf"][:, :], X["yf"][:, :], X["mean"][:, :], ALU.subtract, [X["yf"], X["mean"]], [X["yf"]])
                        k.tt("dve", X["yf"][:, :], X["yf"][:, :], X["var"][:, :], ALU.mult, [X["yf"], X["var"]], [X["yf"]])
                        k.ts("dve", X["yf"][:, :], X["yf"][:, :], C("lnx_g", pg), C("lnx_b", pg), ALU.mult, ALU.add, [X["yf"], cst], [X["yf"]])
                        k.tt("pool", X["yf"][:, :], X["yf"][:, :], X["bonus"][:, :], ALU.add, [X["yf"], X["bonus"]], [X["yf"]])
                        k.tt("dve", y[:, p, :], X["yf"][:, :], X["gate"][:, :], ALU.mult, [X["yf"], X["gate"]], [y])
                    for dc in range(8):
                        bank = k.psum()
                        for m in range(4):
                            k.mm(bank, bank[:, 0:T], Wo[:, m, dc * 128:(dc + 1) * 128], y[:, m, :], m == 0, m == 3, [Wo, y])
                        k.tt("dve", hR[:, dc, :], hR[:, dc, :], bank[:, 0:T], ALU.add, [hR, bank], [hR])
                        bank.busy = False
                    if not last:
                        k.dma("sp", stS[ti % 2], dst_d[:, :, t0:t0 + T], hR[:, :, :], reads=[hR], writes=[dst_tr[ti]])
                    else:
                        emit_norm((sq, sd), hR, "fin_g", delta, 0, True)
                        for tg in range(2):
                            for cq in range(2):
                                bank = k.psum()
                                for c4 in range(4):
                                    c = cq * 4 + c4
                                    k.op("pe", lambda pe, bank=bank, c4=c4, c=c, tg=tg: pe.transpose(
                                        out=bank[:, c4 * 128:(c4 + 1) * 128], in_=delta[:, c, tg * 128:(tg + 1) * 128], identity=mk_f[:, :]),
                                        reads=[delta, mk_f], writes=[bank])
                                k.copy("act" if cq == 0 else "dve", ot[:, tg, cq * 512:(cq + 1) * 512], bank[:, 0:512], [bank], [ot])
                                bank.busy = False
                        k.dma("sp", stS[ti % 2], out_d[t0:t0 + T, :].rearrange("(g p) d -> p g d", p=128), ot[:, :, :], reads=[ot], writes=[])
                k.barrier()

        if "A" in passes:
            pass_A()
        if "B" in passes:
            pass_B()
        if "C" in passes:
            trs = [[Buf(None, "tr%d_%d" % (i, j)) for j in range(NT)] for i in range(3)]
            pass_R(0, None, None, hC_d, trs[0], False)
            pass_R(1, hC_d, trs[0], hA_d, trs[1], False)
            pass_R(2, hA_d, trs[1], hC_d, trs[2], False)
            pass_R(3, hC_d, trs[2], None, None, True)
        k.barrier()
        print("instructions:", k.nins)
    return nc


def _fm(w):
    n = w.shape[1]
    return np.ascontiguousarray(w.reshape(8, 128, n).transpose(1, 0, 2))


def _cv(v):
    return np.ascontiguousarray(v.reshape(-1, 128).T)


def make_masks():
    m = np.zeros((128, NMASK), np.float32)
    p = np.arange(128)[:, None]
    c = np.arange(128)[None, :]
    m[:, M_ID:M_ID + 128] = (p == c)
    m[:, M_ONES:M_ONES + 128] = 1.0
    m[:, M_OBLK:M_OBLK + 128] = (p // 64 == c // 64)
    m[:, M_HG:M_HG + 128] = (p // 64 == c // 64) & (p <= c)
    s = np.arange(128)[:, None] % 64
    t = np.arange(64)[None, :]
    strict = (s < t).astype(np.float32)
    incl = (s <= t).astype(np.float32)
    m[:, M_RW:M_RW + 64] = strict
    m[:, M_RW + 64:M_RW + 128] = incl
    m[:, M_RW + 128:M_RW + 192] = strict
    m[:, M_RW + 192:M_RW + 256] = incl
    m[:, M_RW + 256:M_RW + 320] = (t < s)
    eye = (s == t).astype(np.float32)
    m[:, M_ID2:M_ID2 + 64] = eye
    m[:, M_ID2 + 64:M_ID2 + 128] = eye
    return m


def prep_inputs(inp, b):
    f = lambda a: np.asarray(a, np.float32)
    cst = np.zeros((128, NCONST), np.float32)

    def put(name, arr):
        cst[:, _off[name]:_off[name] + arr.shape[1]] = arr

    put("ab_g", _cv(f(inp["ab_norm_g"])[0]))
    put("c_g", _cv(f(inp["c_norm_g"])[0]))
    put("fin_g", _cv(f(inp["final_g"])))
    cw = f(inp["rg_conv_w"])[0]
    put("conv_w", np.ascontiguousarray(cw.reshape(4, 8, 128).transpose(2, 1, 0).reshape(128, 32)))
    put("conv_b", _cv(f(inp["rg_conv_b"])[0]))
    put("b_a", _cv(f(inp["rg_b_a"])[0]))
    put("b_x", _cv(f(inp["rg_b_x"])[0]))
    put("lam", _cv(f(inp["rg_lambda"])[0]))
    put("lb0", _cv(f(inp["hg_lb_logits"])[0]))
    put("lb1", _cv(f(inp["hg_lb_logits"])[1]))
    put("hg_g", f(inp["hg_norm_g"])[0].reshape(128, 1))
    mu = f(inp["c_mu"])[0]
    put("mu", np.ascontiguousarray(mu.reshape(6, 8, 128).transpose(2, 0, 1).reshape(128, 48)))
    put("w0", _cv(f(inp["c_w0"])[0]))
    put("a0", _cv(f(inp["c_a0"])[0]))
    put("k_k", _cv(f(inp["c_k_k"])[0]))
    put("k_a", _cv(f(inp["c_k_a"])[0]))
    put("r_k", _cv(f(inp["c_r_k"])[0].reshape(-1)))
    put("lnx_g", _cv(f(inp["c_lnx_g"])[0]))
    put("lnx_b", _cv(f(inp["c_lnx_b"])[0]))
    win = f(inp["ab_w_in"])[0]
    wout = f(inp["ab_w_out"])[0]
    m = {
        "x": np.ascontiguousarray(f(inp["x"])[b]),
        "consts": cst,
        "masks": make_masks(),
        "wAin": _fm(win[:, 0:2048]),
        "rgwa": np.ascontiguousarray(f(inp["rg_w_a"])[0].transpose(1, 0, 2)),
        "rgwx": np.ascontiguousarray(f(inp["rg_w_x"])[0].transpose(1, 0, 2)),
        "wAout": _fm(wout[0:1024]),
        "wBin": _fm(win[:, 2048:6144]),
        "wBout": _fm(wout[1024:2048]),
        "w1": _fm(f(inp["c_w1"])[0]),
        "a1": _fm(f(inp["c_a1"])[0]),
    }
    for h in range(2):
        sl = slice(h * 1024, (h + 1) * 1024)
        m["wr%d" % h] = _fm(f(inp["c_w_r"])[0][:, sl])
        m["wk%d" % h] = _fm(f(inp["c_w_k"])[0][:, sl])
        m["wv%d" % h] = _fm(f(inp["c_w_v"])[0][:, sl])
        m["wg%d" % h] = _fm(f(inp["c_w_g"])[0][:, sl])
        m["wo%d" % h] = _fm(f(inp["c_w_o"])[0][sl, :])
        m["w2_%d" % h] = np.ascontiguousarray(f(inp["c_w2"])[0][:, sl])
        m["a2_%d" % h] = np.ascontiguousarray(f(inp["c_a2"])[0][:, sl])
    return m


def kernel(**inputs):
    nc = build()
    in_maps = [prep_inputs(inputs, i % 4) for i in range(8)]
    res = run_bass_kernel_spmd(nc, in_maps, core_ids=list(range(8)))
    out = np.stack([np.asarray(res.results[i]["out"], np.float32) for i in range(4)], axis=0)
    return out
```

```python
import contextlib
import numpy as np
import concourse.bass as bass
import concourse.mybir as mybir
from concourse.bass_utils import run_bass_kernel_spmd
from concourse.alu_op_type import AluOpType as ALU

F32 = mybir.dt.float32
BF16 = mybir.dt.bfloat16
AF = mybir.ActivationFunctionType

S = 4096
D = 1024
T = 256
NT = S // T
RMS_EPS = 1e-6
GN_EPS = 64e-5
DEC = 0.6065306597126334
SAME_SYNC = True
import os
BSTOP = float(os.environ.get('BSTOP', '99'))
RSTOP = float(os.environ.get('RSTOP', '99'))

_off = {}
_n = 0
for _name, _w in [("ab_g", 8), ("c_g", 8), ("fin_g", 8), ("conv_w", 32), ("conv_b", 8), ("b_a", 8), ("b_x", 8),
                  ("lam", 8), ("lb0", 8), ("lb1", 8), ("hg_g", 1), ("mu", 48), ("w0", 16), ("a0", 16),
                  ("k_k", 16), ("k_a", 16), ("r_k", 16), ("lnx_g", 16), ("lnx_b", 16)]:
    _off[_name] = _n
    _n += _w
NCONST = _n
M_ID = 0
M_ONES = 128
M_OBLK = 256
M_HG = 384
M_RW = 512
M_ID2 = 832
NMASK = 960


class Buf:
    __slots__ = ("t", "name", "w", "r", "busy")

    def __init__(self, t, name):
        self.t = t
        self.name = name
        self.w = None
        self.r = {}
        self.busy = False

    def __getitem__(self, idx):
        return self.t[idx]


class Stream:
    def __init__(self, sem, key):
        self.sem = sem
        self.key = key
        self.count = 0


class KB:
    def __init__(self, nc, es):
        self.nc = nc
        self.es = es
        self.eng = {"pe": nc.tensor, "act": nc.scalar, "dve": nc.vector, "pool": nc.gpsimd, "sp": nc.sync}
        self.st = {k: Stream(es.enter_context(nc.semaphore(k + "_s")), k) for k in self.eng}
        self.waited = {k: {} for k in self.eng}
        self.dstreams = []
        self.banks = []
        self.bank_i = 0
        self.nins = 0

    def dma_stream(self, name):
        s = Stream(self.es.enter_context(self.nc.semaphore(name)), name)
        self.dstreams.append(s)
        return s

    def sbuf(self, name, shape, dtype, es=None):
        es = es or self.es
        self.nins += 0
        self.uid = getattr(self, "uid", 0) + 1
        name = "%s_u%d" % (name, self.uid)
        return Buf(es.enter_context(self.nc.sbuf_tensor(name, list(shape), dtype)), name)

    def init_psum(self):
        for i in range(8):
            self.banks.append(Buf(self.es.enter_context(self.nc.psum_tensor("bank%d" % i, [128, 512], F32)), "bank%d" % i))

    def psum(self):
        b = self.banks[2 + self.bank_i % 6]
        self.bank_i += 1
        assert not b.busy, "psum bank still in use: " + b.name
        b.busy = True
        return b

    def _wait(self, e, sv):
        s, v = sv
        w = self.waited[e]
        if w.get(s.key, 0) >= v:
            return
        w[s.key] = v
        self.eng[e].wait_ge(s.sem, v)

    def _deps(self, e, reads, writes):
        for b in reads:
            if b.name.startswith("bank"):
                for sv in b.r.values():
                    if sv[0].key != e:
                        self._wait(e, sv)
            if b.w is not None:
                if b.w[0].key == e:
                    if SAME_SYNC and e != "pe":
                        self._wait(e, b.w)
                else:
                    self._wait(e, b.w)
        for b in writes:
            if b.w is not None and b.w[0].key != e:
                self._wait(e, b.w)
            for sv in b.r.values():
                if sv[0].key != e:
                    self._wait(e, sv)

    def op(self, e, fn, reads=(), writes=()):
        self._deps(e, reads, writes)
        ins = fn(self.eng[e])
        s = self.st[e]
        s.count += 1
        ins.then_inc(s.sem, 1)
        self.nins += 1
        for b in reads:
            b.r[s.key] = (s, s.count)
        for b in writes:
            b.w = (s, s.count)
            b.r = {}

    def dma(self, q, stream, out_ap, in_ap, reads=(), writes=()):
        self._deps(q, reads, writes)
        ins = self.eng[q].dma_start(out=out_ap, in_=in_ap)
        stream.count += 16
        ins.then_inc(stream.sem, 16)
        self.nins += 1
        for b in reads:
            b.r[stream.key] = (stream, stream.count)
        for b in writes:
            b.w = (stream, stream.count)
            b.r = {}

    def seal(self, stream, bufs):
        for b in bufs:
            b.w = (stream, stream.count)

    def barrier(self):
        allst = list(self.st.values()) + self.dstreams
        for e in self.eng:
            for s in allst:
                if s.key != e and s.count > 0:
                    self._wait(e, (s, s.count))

    def mm(self, bank, out_ap, lhsT, rhs, start, stop, reads):
        self.op("pe", lambda pe: pe.matmul(out_ap, lhsT=lhsT, rhs=rhs, start=start, stop=stop), reads=reads, writes=[bank])

    def act(self, out_ap, in_ap, func, reads, writes, bias=None, scale=None):
        kw = {}
        if bias is not None:
            kw["bias"] = bias
        if scale is not None:
            kw["scale"] = scale
        self.op("act", lambda a: a.activation(out=out_ap, in_=in_ap, func=func, **kw), reads=reads, writes=writes)

    def tt(self, e, out_ap, in0, in1, op, reads, writes):
        self.op(e, lambda v: v.tensor_tensor(out=out_ap, in0=in0, in1=in1, op=op), reads=reads, writes=writes)

    def ts(self, e, out_ap, in0, s1, s2, op0, op1, reads, writes):
        if op1 is None:
            self.op(e, lambda v: v.tensor_scalar(out=out_ap, in0=in0, scalar1=s1, scalar2=None, op0=op0), reads=reads, writes=writes)
        else:
            self.op(e, lambda v: v.tensor_scalar(out=out_ap, in0=in0, scalar1=s1, scalar2=s2, op0=op0, op1=op1), reads=reads, writes=writes)

    def stt(self, out_ap, in0, scalar, in1, op0, op1, reads, writes):
        self.op("dve", lambda v: v.scalar_tensor_tensor(out=out_ap, in0=in0, scalar=scalar, in1=in1, op0=op0, op1=op1), reads=reads, writes=writes)

    def copy(self, e, out_ap, in_ap, reads, writes):
        if e == "act":
            self.op("act", lambda a: a.copy(out=out_ap, in_=in_ap), reads=reads, writes=writes)
        else:
            self.op(e, lambda v: v.tensor_copy(out=out_ap, in_=in_ap), reads=reads, writes=writes)


def build(nt=NT, passes="ABCD", debug=False):
    nc = bass.Bass("TRN2", target_bir_lowering=False)

    def din(name, shape):
        return nc.dram_tensor(name, list(shape), F32, kind="ExternalInput").ap()

    x_d = din("x", [S, D])
    consts_d = din("consts", [128, NCONST])
    masks_d = din("masks", [128, NMASK])
    wAin_d = din("wAin", [128, 8, 2048])
    rgwa_d = din("rgwa", [128, 8, 128])
    rgwx_d = din("rgwx", [128, 8, 128])
    wAout_d = din("wAout", [128, 8, 1024])
    wBin_d = din("wBin", [128, 8, 4096])
    wBout_d = din("wBout", [128, 8, 1024])
    wr_d = [din("wr%d" % h, [128, 8, 1024]) for h in range(2)]
    wk_d = [din("wk%d" % h, [128, 8, 1024]) for h in range(2)]
    wv_d = [din("wv%d" % h, [128, 8, 1024]) for h in range(2)]
    wg_d = [din("wg%d" % h, [128, 8, 1024]) for h in range(2)]
    wo_d = [din("wo%d" % h, [128, 8, 1024]) for h in range(2)]
    w1_d = din("w1", [128, 8, 64])
    a1_d = din("a1", [128, 8, 64])
    w2_d = [din("w2_%d" % h, [64, 1024]) for h in range(2)]
    a2_d = [din("a2_%d" % h, [64, 1024]) for h in range(2)]
    out_d = nc.dram_tensor("out", [S, D], F32, kind="ExternalOutput").ap()
    skind = "ExternalOutput" if debug else "Internal"
    h0_d = nc.dram_tensor("h0fm", [128, 8, S], F32, kind=skind).ap()
    hA_d = nc.dram_tensor("hAfm", [128, 8, S], F32, kind=skind).ap()
    h1_d = nc.dram_tensor("h1fm", [128, 8, S], F32, kind=skind).ap()
    hC_d = nc.dram_tensor("hCfm", [128, 8, S], F32, kind=skind).ap()

    es = contextlib.ExitStack()
    with es:
        k = KB(nc, es)
        k.init_psum()
        cst = k.sbuf("cst", [128, NCONST], F32)
        der = k.sbuf("der", [128, 64], F32)
        mk_f = k.sbuf("mk_f", [128, 128], F32)
        mk_b = k.sbuf("mk_b", [128, NMASK], BF16)
        mk_rw = k.sbuf("mk_rw", [128, 320], F32)
        mk_hg = k.sbuf("mk_hg", [128, 128], F32)
        zeros = k.sbuf("zeros", [128, 64], F32)
        onesf = k.sbuf("onesf", [128, 64], F32)
        cs = k.dma_stream("cstream")
        k.dma("sp", cs, cst[:, :], consts_d[:, :], writes=[cst])
        k.dma("sp", cs, mk_f[:, :], masks_d[:, M_ID:M_ID + 128], writes=[mk_f])
        k.dma("sp", cs, mk_rw[:, :], masks_d[:, M_RW:M_RW + 320], writes=[mk_rw])
        k.dma("sp", cs, mk_hg[:, :], masks_d[:, M_HG:M_HG + 128], writes=[mk_hg])
        cs2 = k.dma_stream("cstream2")
        k.dma("pool", cs2, mk_b[:, :], masks_d[:, :], writes=[mk_b])
        k.seal(cs, [cst, mk_f, mk_rw, mk_hg])
        k.op("pool", lambda g: g.memset(zeros[:, :], 0.0), writes=[zeros])
        k.op("pool", lambda g: g.memset(onesf[:, :], 1.0), writes=[onesf])
        ident_b = mk_b[:, M_ID:M_ID + 128]
        ones_b = mk_b[:, M_ONES:M_ONES + 128]
        oblk_b = mk_b[:, M_OBLK:M_OBLK + 128]

        def C(name, j=0, w=1):
            o = _off[name] + j
            return cst[:, o:o + w]

        k.act(der[:, 0:8], C("lam", 0, 8), AF.Exp, [cst], [der], scale=-1.0)
        k.act(der[:, 0:8], der[:, 0:8], AF.Ln, [der, onesf], [der], bias=onesf[:, 0:1])
        k.ts("dve", der[:, 8:16], der[:, 0:8], -16.0, None, ALU.mult, None, [der], [der])
        k.ts("dve", der[:, 0:8], der[:, 0:8], -8.0, None, ALU.mult, None, [der], [der])
        k.tt("dve", der[:, 16:24], C("lb0", 0, 8), C("lb1", 0, 8), ALU.subtract, [cst], [der])
        k.act(der[:, 16:24], der[:, 16:24], AF.Sigmoid, [der], [der])
        k.ts("dve", der[:, 24:32], der[:, 16:24], -1.0, 1.0, ALU.mult, ALU.add, [der], [der])
        k.ts("dve", der[:, 32:48], C("k_a", 0, 16), -1.0, 1.0, ALU.mult, ALU.add, [cst], [der])
        k.op("pool", lambda g: g.memset(der[:, 48:49], RMS_EPS), writes=[der])
        k.op("pool", lambda g: g.memset(der[:, 49:50], GN_EPS), writes=[der])
        eps_rms = der[:, 48:49]
        eps_gn = der[:, 49:50]

        ws = k.dma_stream("wstream")
        ldS = [k.dma_stream("ldU0"), k.dma_stream("ldU1")]
        ldR = [k.dma_stream("ldR0"), k.dma_stream("ldR1")]
        stS = [k.dma_stream("st0"), k.dma_stream("st1")]
        stS2 = [k.dma_stream("st2_0"), k.dma_stream("st2_1")]
        dr = {nm: [Buf(None, "%s_%d" % (nm, i)) for i in range(NT)] for nm in ("h0", "hA", "h1", "hC")}

        def loadw(buf, dram, nk=8, ncol=None, dcol0=0, dk0=0):
            ncol = ncol or dram.shape[2]
            for kc in range(nk):
                for c0 in range(0, ncol, 1024):
                    c1 = min(ncol, c0 + 1024)
                    k.dma("pool", ws, buf[:, kc, c0:c1], dram[:, dk0 + kc, dcol0 + c0:dcol0 + c1], writes=[buf])

        def emit_norm(pes_bufs, hU, gname, outbuf, col0, fp32_out):
            sq, sd = pes_bufs
            k.act(sq[:, :, :], hU[:, :, :], AF.Square, [hU], [sq])
            bank = k.psum()
            for c in range(8):
                k.mm(bank, bank[:, 0:T], ones_b, sq[:, c, :], c == 0, c == 7, [sq, mk_b])
            k.act(sd[:, :], bank[:, 0:T], AF.Sqrt, [bank, der], [sd], bias=eps_rms, scale=1.0 / D)
            bank.busy = False
            k.op("dve", lambda v: v.reciprocal(out=sd[:, :], in_=sd[:, :]), reads=[sd], writes=[sd])
            for c in range(8):
                k.stt(outbuf[:, c, col0:col0 + T], hU[:, c, :], C(gname, c), sd[:, :], ALU.mult, ALU.mult,
                      [hU, cst, sd], [outbuf])

        def out_proj(Wout, y, hR):
            for dc in range(8):
                bank = k.psum()
                for m in range(8):
                    k.mm(bank, bank[:, 0:T], Wout[:, m, dc * 128:(dc + 1) * 128], y[:, m, :], m == 0, m == 7, [Wout, y])
                k.tt("dve", hR[:, dc, :], hR[:, dc, :], bank[:, 0:T], ALU.add, [hR, bank], [hR])
                bank.busy = False

        def pass_A():
            with contextlib.ExitStack() as pes:
                Win = k.sbuf("A_Win", [128, 8, 2048], BF16, pes)
                Wa = k.sbuf("A_Wa", [128, 8, 128], BF16, pes)
                Wx = k.sbuf("A_Wx", [128, 8, 128], BF16, pes)
                Wout = k.sbuf("A_Wout", [128, 8, 1024], BF16, pes)
                loadw(Win, wAin_d)
                k.dma("pool", ws, Wa[:, :, :], rgwa_d[:, :, :], writes=[Wa])
                k.dma("pool", ws, Wx[:, :, :], rgwx_d[:, :, :], writes=[Wx])
                loadw(Wout, wAout_d)
                k.seal(ws, [Win, Wa, Wx, Wout])
                xts = [k.sbuf("A_xt%d" % i, [128, 2, 1024], F32, pes) for i in range(2)]
                hUs = [k.sbuf("A_hU%d" % i, [128, 8, T], F32, pes) for i in range(2)]
                sq = k.sbuf("A_sq", [128, 8, T], BF16, pes)
                sd = k.sbuf("A_sd", [128, T], F32, pes)
                u = k.sbuf("A_u", [128, 8, T], BF16, pes)
                y = k.sbuf("A_y", [128, 8, T], BF16, pes)
                xaext = [k.sbuf("A_xa%d" % c, [128, T + 3], F32, pes) for c in range(8)]
                carry = [k.sbuf("A_cy%d" % c, [128, 1], F32, pes) for c in range(8)]
                xc = [k.sbuf("A_xc%d" % i, [128, T], F32, pes) for i in range(2)]
                xcb = [k.sbuf("A_xcb%d" % i, [128, T], BF16, pes) for i in range(2)]
                sr = [k.sbuf("A_sr%d" % i, [128, T], F32, pes) for i in range(2)]
                si = [k.sbuf("A_si%d" % i, [128, T], F32, pes) for i in range(2)]
                av = [k.sbuf("A_av%d" % i, [128, T], F32, pes) for i in range(2)]
                mv = [k.sbuf("A_mv%d" % i, [128, T], F32, pes) for i in range(2)]
                uu = [k.sbuf("A_uu%d" % i, [128, T], F32, pes) for i in range(2)]
                hh = [k.sbuf("A_hh%d" % i, [128, T], F32, pes) for i in range(2)]
                sg = [k.sbuf("A_sg%d" % i, [128, T], F32, pes) for i in range(2)]
                for c in range(8):
                    k.op("pool", lambda g, c=c: g.memset(xaext[c][:, :], 0.0), writes=[xaext[c]])
                    k.op("pool", lambda g, c=c: g.memset(carry[c][:, :], 0.0), writes=[carry[c]])

                for ti in range(nt):
                    t0 = ti * T
                    xt = xts[ti % 2]
                    hU = hUs[ti % 2]
                    k.dma("sp", ldS[ti % 2], xt[:, :, :], x_d[t0:t0 + T, :].rearrange("(g p) d -> p g d", p=128), writes=[xt])
                    for cp in range(4):
                        bank = k.psum()
                        for cc in range(2):
                            c = cp * 2 + cc
                            for tg in range(2):
                                o = cc * 256 + tg * 128
                                k.op("pe", lambda pe, o=o, c=c, tg=tg, bank=bank: pe.transpose(
                                    out=bank[:, o:o + 128], in_=xt[:, tg, c * 128:(c + 1) * 128], identity=mk_f[:, :]),
                                    reads=[xt, mk_f], writes=[bank])
                        for cc in range(2):
                            c = cp * 2 + cc
                            k.copy("act" if cc == 0 else "dve", hU[:, c, :], bank[:, cc * 256:cc * 256 + 256], [bank], [hU])
                        bank.busy = False
                    k.dma("sp", stS2[ti % 2], h0_d[:, :, t0:t0 + T], hU[:, :, :], reads=[hU], writes=[dr["h0"][ti]])
                    emit_norm((sq, sd), hU, "ab_g", u, 0, False)
                    for c in range(8):
                        i2 = c % 2
                        b1 = k.psum()
                        for kc in range(8):
                            k.mm(b1, b1[:, 0:T], Win[:, kc, c * 128:(c + 1) * 128], u[:, kc, :], kc == 0, kc == 7, [Win, u])
                        for kc in range(8):
                            k.mm(b1, b1[:, T:2 * T], Win[:, kc, 1024 + c * 128:1024 + (c + 1) * 128], u[:, kc, :], kc == 0, kc == 7, [Win, u])
                        xe = xaext[c]
                        k.copy("pool", xe[:, 0:3], xe[:, T:T + 3], [xe], [xe])
                        k.copy("act", xe[:, 3:T + 3], b1[:, 0:T], [b1], [xe])
                        cw = _off["conv_w"] + c * 4
                        k.ts("dve", xc[i2][:, :], xe[:, 3:T + 3], cst[:, cw + 3:cw + 4], C("conv_b", c), ALU.mult, ALU.add, [xe, cst], [xc[i2]])
                        for j in (2, 1, 0):
                            k.stt(xc[i2][:, :], xe[:, j:j + T], cst[:, cw + j:cw + j + 1], xc[i2][:, :], ALU.mult, ALU.add, [xe, cst, xc[i2]], [xc[i2]])
                        k.copy("pool", xcb[i2][:, :], xc[i2][:, :], [xc[i2]], [xcb[i2]])
                        b2 = k.psum()
                        k.mm(b2, b2[:, 0:T], Wa[:, c, :], xcb[i2][:, :], True, True, [Wa, xcb[i2]])
                        k.mm(b2, b2[:, T:2 * T], Wx[:, c, :], xcb[i2][:, :], True, True, [Wx, xcb[i2]])
                        k.act(sr[i2][:, :], b2[:, 0:T], AF.Sigmoid, [b2, cst], [sr[i2]], bias=C("b_a", c))
                        k.act(si[i2][:, :], b2[:, T:2 * T], AF.Sigmoid, [b2, cst], [si[i2]], bias=C("b_x", c))
                        b2.busy = False
                        k.act(sg[i2][:, :], b1[:, T:2 * T], AF.Silu, [b1], [sg[i2]])
                        b1.busy = False
                        k.act(av[i2][:, :], sr[i2][:, :], AF.Exp, [sr[i2], der], [av[i2]], scale=der[:, c:c + 1])
                        k.act(mv[i2][:, :], sr[i2][:, :], AF.Exp, [sr[i2], der], [mv[i2]], scale=der[:, 8 + c:9 + c])
                        k.act(mv[i2][:, :], mv[i2][:, :], AF.Sqrt, [mv[i2], onesf], [mv[i2]], bias=onesf[:, 0:1], scale=-1.0)
                        k.tt("pool", uu[i2][:, :], si[i2][:, :], xc[i2][:, :], ALU.mult, [si[i2], xc[i2]], [uu[i2]])
                        k.tt("dve", uu[i2][:, :], uu[i2][:, :], mv[i2][:, :], ALU.mult, [uu[i2], mv[i2]], [uu[i2]])
                        k.op("dve", lambda v, i2=i2, c=c: v.tensor_tensor_scan(
                            out=hh[i2][:, :], data0=av[i2][:, :], data1=uu[i2][:, :], initial=carry[c][:, 0:1],
                            op0=ALU.mult, op1=ALU.add), reads=[av[i2], uu[i2], carry[c]], writes=[hh[i2]])
                        k.copy("pool", carry[c][:, 0:1], hh[i2][:, T - 1:T], [hh[i2]], [carry[c]])
                        k.tt("dve", y[:, c, :], hh[i2][:, :], sg[i2][:, :], ALU.mult, [hh[i2], sg[i2]], [y])
                    out_proj(Wout, y, hU)
                    k.dma("sp", stS[ti % 2], hA_d[:, :, t0:t0 + T], hU[:, :, :], reads=[hU], writes=[dr["hA"][ti]])
                k.barrier()

        def pass_B():
            with contextlib.ExitStack() as pes:
                Win = k.sbuf("B_Win", [128, 8, 4096], BF16, pes)
                Wout = k.sbuf("B_Wout", [128, 8, 1024], BF16, pes)
                loadw(Win, wBin_d)
                loadw(Wout, wBout_d)
                k.seal(ws, [Win, Wout])
                hUs = [k.sbuf("B_hU%d" % i, [128, 8, T], F32, pes) for i in range(2)]
                hRs = [k.sbuf("B_hR%d" % i, [128, 8, T], F32, pes) for i in range(2)]
                sq = k.sbuf("B_sq", [128, 8, T], BF16, pes)
                sd = k.sbuf("B_sd", [128, T], F32, pes)
                u = k.sbuf("B_u", [128, 8, T], BF16, pes)
                y = k.sbuf("B_y", [128, 8, T], BF16, pes)
                vtok = k.sbuf("B_vtok", [128, 2, 1024], BF16, pes)
                stf = [k.sbuf("B_stf%d" % h, [128, 128], F32, pes) for h in range(8)]
                stb = [k.sbuf("B_stb%d" % h, [128, 128], BF16, pes) for h in range(8)]
                NB = 2
                sig = [k.sbuf("B_sig%d" % i, [128, T], F32, pes) for i in range(NB)]
                ff = [k.sbuf("B_f%d" % i, [128, T], F32, pes) for i in range(NB)]
                kf = [k.sbuf("B_k%d" % i, [128, T], F32, pes) for i in range(NB)]
                Pc = [k.sbuf("B_P%d" % i, [128, T], F32, pes) for i in range(NB)]
                Pi = [k.sbuf("B_Pi%d" % i, [128, T], F32, pes) for i in range(NB)]
                qd = [k.sbuf("B_qd%d" % i, [128, T], BF16, pes) for i in range(NB)]
                kif = [k.sbuf("B_kif%d" % i, [128, T], F32, pes) for i in range(NB)]
                kib = [k.sbuf("B_kib%d" % i, [128, T], BF16, pes) for i in range(NB)]
                keb = [k.sbuf("B_keb%d" % i, [128, T], BF16, pes) for i in range(NB)]
                scm = [k.sbuf("B_scm%d" % i, [128, 128], BF16, pes) for i in range(NB)]
                ket = [k.sbuf("B_ket%d" % i, [128, 128], BF16, pes) for i in range(NB)]
                osq = [k.sbuf("B_osq%d" % i, [128, T], BF16, pes) for i in range(NB)]
                ors = [k.sbuf("B_ors%d" % i, [128, T], F32, pes) for i in range(NB)]
                o1 = [k.sbuf("B_o1%d" % i, [128, T], F32, pes) for i in range(NB)]
                sgb = [k.sbuf("B_sg%d" % i, [128, T], F32, pes) for i in range(NB)]
                for h in range(8):
                    k.op("pool", lambda g, h=h: g.memset(stf[h][:, :], 0.0), writes=[stf[h]])
                    k.op("pool", lambda g, h=h: g.memset(stb[h][:, :], 0.0), writes=[stb[h]])
                accb = [k.banks[0], k.banks[1]]
                for ti in range(nt):
                    t0 = ti * T
                    hU = hUs[ti % 2]
                    hR = hRs[ti % 2]
                    k.dma("sp", ldS[ti % 2], hU[:, :, :], h0_d[:, :, t0:t0 + T], reads=[dr["h0"][ti]], writes=[hU])
                    k.dma("sp", ldR[ti % 2], hR[:, :, :], hA_d[:, :, t0:t0 + T], reads=[dr["hA"][ti]], writes=[hR])
                    emit_norm((sq, sd), hU, "ab_g", u, 0, False)
                    for tg in range(2):
                        for cg in range(2):
                            bank = k.psum()
                            for kc in range(8):
                                k.mm(bank, bank[:, 0:512], u[:, kc, tg * 128:(tg + 1) * 128],
                                     Win[:, kc, 2048 + cg * 512:2048 + (cg + 1) * 512], kc == 0, kc == 7, [u, Win])
                            k.copy("act" if cg == 0 else "dve", vtok[:, tg, cg * 512:(cg + 1) * 512], bank[:, 0:512], [bank], [vtok])
                            bank.busy = False
                    for h in range(8 if BSTOP > 1 else 0):
                        i2 = h % NB
                        bq = k.psum()
                        for kc in range(8):
                            k.mm(bq, bq[:, 0:T], Win[:, kc, h * 128:(h + 1) * 128], u[:, kc, :], kc == 0, kc == 7, [Win, u])
                        for kc in range(8):
                            k.mm(bq, bq[:, T:2 * T], Win[:, kc, 1024 + h * 128:1024 + (h + 1) * 128], u[:, kc, :], kc == 0, kc == 7, [Win, u])
                        k.act(sig[i2][:, :], bq[:, T:2 * T], AF.Sigmoid, [bq], [sig[i2]])
                        k.ts("dve", ff[i2][:, :], sig[i2][:, :], der[:, 24 + h:25 + h], der[:, 16 + h:17 + h], ALU.mult, ALU.add, [sig[i2], der], [ff[i2]])
                        k.ts("pool", kf[i2][:, :], ff[i2][:, :], -1.0, 1.0, ALU.mult, ALU.add, [ff[i2]], [kf[i2]])
                        for j in range(T // 64):
                            k.op("dve", lambda v, i2=i2, j=j: v.tensor_tensor_scan(
                                out=Pc[i2][:, j * 64:(j + 1) * 64], data0=ff[i2][:, j * 64:(j + 1) * 64], data1=zeros[:, 0:64],
                                initial=1.0, op0=ALU.mult, op1=ALU.add), reads=[ff[i2], zeros], writes=[Pc[i2]])
                        k.op("dve", lambda v, i2=i2: v.reciprocal(out=Pi[i2][:, :], in_=Pc[i2][:, :]), reads=[Pc[i2]], writes=[Pi[i2]])
                        k.tt("dve", qd[i2][:, :], bq[:, 0:T], Pc[i2][:, :], ALU.mult, [bq, Pc[i2]], [qd[i2]])
                        bq.busy = False
                        k.tt("pool", kif[i2][:, :], kf[i2][:, :], Pi[i2][:, :], ALU.mult, [kf[i2], Pi[i2]], [kif[i2]])
                        k.copy("pool", kib[i2][:, :], kif[i2][:, :], [kif[i2]], [kib[i2]])
                        for j in range(T // 64):
                            k.ts("dve", keb[i2][:, j * 64:(j + 1) * 64], kif[i2][:, j * 64:(j + 1) * 64],
                                 Pc[i2][:, j * 64 + 63:j * 64 + 64], None, ALU.mult, None, [kif[i2], Pc[i2]], [keb[i2]])
                        bo = accb[h % 2]
                        for tg in range(2 if BSTOP > 2 else 0):
                            c0 = tg * 128
                            bs = k.psum()
                            if BSTOP != 2.6:
                                k.mm(bs, bs[:, 0:128], kib[i2][:, c0:c0 + 128], qd[i2][:, c0:c0 + 128], True, True, [kib[i2], qd[i2]])
                            bs2 = bs
                            if BSTOP != 2.3:
                                k.mm(bs2, bs2[:, 128:256], keb[i2][:, c0:c0 + 128], ident_b, True, True, [keb[i2], mk_b])
                            if BSTOP != 2.6:
                                k.tt("dve", scm[i2][:, :], bs[:, 0:128], mk_hg[:, :], ALU.mult, [bs, mk_hg], [scm[i2]])
                            if BSTOP != 2.3:
                                k.copy("act", ket[i2][:, :], bs2[:, 128:256], [bs2], [ket[i2]])
                            bs.busy = False
                            for jp in range(2 if BSTOP > 3 else 0):
                                cj = c0 + jp * 64
                                r0 = jp * 64
                                k.mm(bo, bo[:, cj:cj + 64], vtok[:, tg, h * 128:(h + 1) * 128], scm[i2][:, r0:r0 + 64], True, False, [vtok, scm[i2]])
                                k.mm(bo, bo[:, cj:cj + 64], stb[h][:, :], qd[i2][:, cj:cj + 64], False, True, [stb[h], qd[i2]])
                                bst = k.psum()
                                k.mm(bst, bst[:, 0:128], ket[i2][r0:r0 + 64, :], vtok[r0:r0 + 64, tg, h * 128:(h + 1) * 128], True, True, [ket[i2], vtok])
                                k.stt(stf[h][:, :], stf[h][:, :], Pc[i2][:, cj + 63:cj + 64], bst[:, 0:128], ALU.mult, ALU.add, [stf[h], Pc[i2], bst], [stf[h]])
                                bst.busy = False
                                k.copy("act", stb[h][:, :], stf[h][:, :], [stf[h]], [stb[h]])
                        bg = k.psum()
                        for kc in range(8):
                            k.mm(bg, bg[:, 0:T], Win[:, kc, 3072 + h * 128:3072 + (h + 1) * 128], u[:, kc, :], kc == 0, kc == 7, [Win, u])
                        k.act(sgb[i2][:, :], bg[:, 0:T], AF.Silu, [bg], [sgb[i2]])
                        k.act(osq[i2][:, :], bo[:, 0:T], AF.Square, [bo], [osq[i2]])
                        k.mm(bg, bg[:, T:2 * T], ones_b, osq[i2][:, :], True, True, [mk_b, osq[i2]])
                        k.act(ors[i2][:, :], bg[:, T:2 * T], AF.Sqrt, [bg, der], [ors[i2]], bias=eps_rms, scale=1.0 / 128)
                        bg.busy = False
                        k.op("dve", lambda v, i2=i2: v.reciprocal(out=ors[i2][:, :], in_=ors[i2][:, :]), reads=[ors[i2]], writes=[ors[i2]])
                        k.tt("dve", o1[i2][:, :], bo[:, 0:T], ors[i2][:, :], ALU.mult, [bo, ors[i2]], [o1[i2]])
                        k.stt(y[:, h, :], o1[i2][:, :], C("hg_g"), sgb[i2][:, :], ALU.mult, ALU.mult, [o1[i2], cst, sgb[i2]], [y])
                    out_proj(Wout, y, hR)
                    k.dma("sp", stS[ti % 2], h1_d[:, :, t0:t0 + T], hR[:, :, :], reads=[hR], writes=[dr["h1"][ti]])
                k.barrier()


        def rr(gens):
            gens = list(gens)
            while gens:
                for g_ in list(gens):
                    try:
                        next(g_)
                    except StopIteration:
                        gens.remove(g_)
                yield

        def pass_R(q, res_d, res_tr, dst_d, dst_tr, last):
            hf, qo = q // 2, (q % 2) * 512
            with contextlib.ExitStack() as pes:
                Wr = k.sbuf("R_Wr", [128, 8, 512], BF16, pes)
                Wk = k.sbuf("R_Wk", [128, 8, 512], BF16, pes)
                Wv = k.sbuf("R_Wv", [128, 8, 512], BF16, pes)
                Wg = k.sbuf("R_Wg", [128, 8, 512], BF16, pes)
                W1 = k.sbuf("R_W1", [128, 8, 64], BF16, pes)
                A1 = k.sbuf("R_A1", [128, 8, 64], BF16, pes)
                W2 = k.sbuf("R_W2", [64, 512], BF16, pes)
                A2 = k.sbuf("R_A2", [64, 512], BF16, pes)
                Wo = k.sbuf("R_Wo", [128, 4, 1024], BF16, pes)
                loadw(Wr, wr_d[hf], ncol=512, dcol0=qo)
                loadw(Wk, wk_d[hf], ncol=512, dcol0=qo)
                loadw(Wv, wv_d[hf], ncol=512, dcol0=qo)
                loadw(Wg, wg_d[hf], ncol=512, dcol0=qo)
                k.dma("pool", ws, W1[:, :, :], w1_d[:, :, :], writes=[W1])
                k.dma("pool", ws, A1[:, :, :], a1_d[:, :, :], writes=[A1])
                k.dma("pool", ws, W2[:, :], w2_d[hf][:, qo:qo + 512], writes=[W2])
                k.dma("pool", ws, A2[:, :], a2_d[hf][:, qo:qo + 512], writes=[A2])
                loadw(Wo, wo_d[hf], nk=4, dk0=(q % 2) * 4)
                k.seal(ws, [Wr, Wk, Wv, Wg, W1, A1, W2, A2, Wo])
                hUs = [k.sbuf("R_hU%d" % i, [128, 8, T], F32, pes) for i in range(2)]
                hRs = [k.sbuf("R_hR%d" % i, [128, 8, T], F32, pes) for i in range(1)] if q > 0 else hUs
                sq = k.sbuf("R_sq", [128, 8, T], BF16, pes)
                sd = k.sbuf("R_sd", [128, T], F32, pes)
                ufp = k.sbuf("R_ufp", [128, 8, T + 1], F32, pes)
                delta = k.sbuf("R_delta", [128, 8, T], F32, pes)
                xj = [k.sbuf("R_x%d" % j, [128, 8, T], BF16, pes) for j in range(6)]
                y = k.sbuf("R_y", [128, 4, T], BF16, pes)
                vtok = k.sbuf("R_vtok", [128, 2, 512], BF16, pes)
                tw = k.sbuf("R_tw", [64, T], BF16, pes)
                al = k.sbuf("R_al", [64, T], BF16, pes)
                Tf = [k.sbuf("R_Tf%d" % p, [128, 64], F32, pes) for p in range(4)]
                Tbk = [k.sbuf("R_Tbk%d" % p, [128, 128], BF16, pes) for p in range(4)]
                ot = k.sbuf("R_ot", [128, 2, 1024], F32, pes) if last else None
                f32n = ["sw", "av", "rf", "gate", "kkp", "nk", "kf", "bb", "cum", "cm", "g", "gi", "vfm", "bonus"]
                b16n = ["kk2", "rk2", "ktb", "btb", "kend", "bend", "yb16", "ysq"]
                zl = list(Tbk)
                PB = []
                for si in range(2):
                    d_ = {}
                    d_["X"] = {n_: k.sbuf("R_" + n_, [128, T], F32, pes) for n_ in f32n}
                    d_["X"]["yf"] = d_["X"]["sw"]
                    d_["X"]["mean"] = d_["X"]["cum"]
                    d_["X"]["var"] = d_["X"]["gi"]
                    d_["t16"] = {n_: k.sbuf("R_" + n_, [128, T], BF16, pes) for n_ in b16n}
                    d_["krt"] = k.sbuf("R_krt", [128, 2, T], BF16, pes)
                    d_["tokA"] = k.sbuf("R_tokA", [128, 256], BF16, pes)
                    d_["tokB"] = k.sbuf("R_tokB", [128, 256], BF16, pes)
                    d_["KEZ"] = [k.sbuf("R_kez", [128, 256], BF16, pes) for j in range(2)]
                    zl += d_["KEZ"]
                    d_["ch"] = []
                    for tg in range(2):
                        c_ = {}
                        c_["Gz"] = [k.sbuf("R_Gz", [128, 2, 320], BF16, pes) for j in range(2)]
                        c_["Qz"] = [k.sbuf("R_Qz", [128, 128], BF16, pes) for j in range(2)]
                        c_["nUz"] = [k.sbuf("R_nUz", [128, 128], BF16, pes) for j in range(2)]
                        c_["WTd"] = [k.sbuf("R_WTd", [128, 64], BF16, pes) for j in range(2)]
                        c_["Qs"] = [k.sbuf("R_Q", [128, 128], BF16, pes) for j in range(2)]
                        c_["PPs"] = [k.sbuf("R_PP", [128, 256], BF16, pes) for j in range(2)]
                        c_["IP"] = k.sbuf("R_IP", [128, 2, 64], BF16, pes)
                        c_["AKV"] = k.sbuf("R_akv", [128, 128], BF16, pes)
                        c_["UP"] = k.sbuf("R_up", [128, 128], F32, pes)
                        zl += c_["Gz"] + c_["Qz"] + c_["nUz"]
                        d_["ch"].append(c_)
                    PB.append(d_)
                for bl_ in zl:
                    k.op("pool", lambda g_, bl_=bl_: g_.memset(bl_[:, :, :] if len(bl_.t.shape) == 3 else bl_[:, :], 0.0), writes=[bl_])
                id2 = mk_b[:, M_ID2:M_ID2 + 128]
                k.op("pool", lambda g_: g_.memset(ufp[:, :, :], 0.0), writes=[ufp])
                for p in range(4):
                    k.op("pool", lambda g_, p=p: g_.memset(Tf[p][:, :], 0.0), writes=[Tf[p]])
                accb = [k.banks[0], k.banks[1]]
                xr, xw, xk, xv, xa, xg = xj

                def chain_gen(p, tg, S_):
                    c_ = S_["ch"][tg]
                    t16, krt, tokA = S_["t16"], S_["krt"], S_["tokA"]
                    Gz, Qz, WTd, Qs, PPs, IP, AKV, UP = c_["Gz"], c_["Qz"], c_["WTd"], c_["Qs"], c_["PPs"], c_["IP"], c_["AKV"], c_["UP"]
                    for hp in range(2):
                        rs_ = slice(64 * hp, 64 * hp + 64)
                        bG = k.psum()
                        for jp in range(2):
                            cs_ = slice(tg * 128 + jp * 64, tg * 128 + jp * 64 + 64)
                            ro = slice(64 * jp, 64 * jp + 64)
                            k.mm(bG, bG[ro, 0:64], t16["ktb"][rs_, cs_], krt[rs_, 0, cs_], True, True, [t16["ktb"], krt])
                            k.mm(bG, bG[ro, 64:128], t16["ktb"][rs_, cs_], krt[rs_, 1, cs_], True, True, [t16["ktb"], krt])
                            k.mm(bG, bG[ro, 128:192], t16["btb"][rs_, cs_], krt[rs_, 0, cs_], True, True, [t16["btb"], krt])
                            k.mm(bG, bG[ro, 192:256], t16["btb"][rs_, cs_], krt[rs_, 1, cs_], True, True, [t16["btb"], krt])
                            k.mm(bG, bG[ro, 256:320], krt[rs_, 0, cs_], t16["btb"][rs_, cs_], True, True, [t16["btb"], krt])
                        for jp in range(2):
                            ro = slice(64 * jp, 64 * jp + 64)
                            k.tt("dve", Gz[jp][ro, hp, :], bG[ro, 0:320], mk_rw[ro, :], ALU.mult, [bG, mk_rw], [Gz[jp]])
                        bG.busy = False
                        yield
                    Qc = Qs[0]
                    for jp in range(2):
                        ro = slice(64 * jp, 64 * jp + 64)
                        for hp in range(2):
                            k.tt("pool", Qc[ro, 64 * hp:64 * hp + 64], id2[ro, 0:64], Gz[jp][ro, hp, 128:192], ALU.subtract, [mk_b, Gz[jp]], [Qc])
                    qi = 0
                    for lvl in range(1, 6):
                        bP = k.psum()
                        for hp in range(2):
                            for jp in range(2):
                                ro = slice(64 * jp, 64 * jp + 64)
                                if lvl == 1:
                                    Pm, PTm, Pb = Gz[jp][ro, hp, 256:320], Gz[jp][ro, hp, 128:192], Gz[jp]
                                else:
                                    PPc = PPs[lvl % 2]
                                    Pm, PTm, Pb = PPc[ro, hp * 128:hp * 128 + 64], PPc[ro, hp * 128 + 64:hp * 128 + 128], PPc
                                k.mm(bP, bP[ro, hp * 128:hp * 128 + 64], PTm, Pm, True, True, [Pb])
                                if lvl < 5:
                                    k.mm(bP, bP[ro, hp * 128 + 64:hp * 128 + 128], Pm, PTm, True, True, [Pb])
                        for hp in range(2):
                            k.tt("dve", IP[:, hp, :], bP[:, hp * 128:hp * 128 + 64], id2[:, 0:64], ALU.add, [bP, mk_b], [IP])
                        if lvl < 5:
                            PPn = PPs[(lvl + 1) % 2]
                            k.copy("act", PPn[:, :], bP[:, 0:256], [bP], [PPn])
                        bP.busy = False
                        yield
                        bQ = k.psum()
                        for hp in range(2):
                            for jp in range(2):
                                ro = slice(64 * jp, 64 * jp + 64)
                                k.mm(bQ, bQ[ro, hp * 64:hp * 64 + 64], IP[ro, hp, :], Qs[qi][ro, hp * 64:hp * 64 + 64], True, True, [IP, Qs[qi]])
                        if lvl < 5:
                            Qn = Qs[1 - qi]
                            k.copy("act", Qn[:, :], bQ[:, 0:128], [bQ], [Qn])
                        else:
                            for jp in range(2):
                                ro = slice(64 * jp, 64 * jp + 64)
                                k.copy("act", Qz[jp][ro, :], bQ[ro, 0:128], [bQ], [Qz[jp]])
                        bQ.busy = False
                        qi = 1 - qi
                        yield
                    bA = k.psum()
                    for hp in range(2):
                        for jp in range(2):
                            ro = slice(64 * jp, 64 * jp + 64)
                            vc = slice(p * 128 + 64 * hp, p * 128 + 64 * hp + 64)
                            k.mm(bA, bA[ro, 64 * hp:64 * hp + 64], Gz[jp][ro, hp, 0:64], vtok[ro, tg, vc], True, True, [Gz[jp], vtok])
                    k.copy("act", AKV[:, :], bA[:, 0:128], [bA], [AKV])
                    bA.busy = False
                    bW = k.psum()
                    for jp in range(2):
                        k.mm(bW, bW[:, jp * 128:(jp + 1) * 128], tokA[:, tg * 128:(tg + 1) * 128], Qz[jp][:, :], True, True, [tokA, Qz[jp]])
                    for jp in range(2):
                        for hp in range(2):
                            rs_ = slice(64 * hp, 64 * hp + 64)
                            k.copy("dve", WTd[jp][rs_, :], bW[rs_, jp * 128 + 64 * hp:jp * 128 + 64 * hp + 64], [bW], [WTd[jp]])
                    bW.busy = False
                    yield
                    bX = k.psum()
                    for hp in range(2):
                        for jp in range(2):
                            ro = slice(64 * jp, 64 * jp + 64)
                            k.mm(bX, bX[ro, 64 * hp:64 * hp + 64], Qz[jp][ro, hp * 64:hp * 64 + 64], AKV[ro, 64 * hp:64 * hp + 64], True, True, [Qz[jp], AKV])
                    k.copy("act", UP[:, :], bX[:, 0:128], [bX], [UP])
                    bX.busy = False
                    yield

                def seq_gen(p, tg, S_, psY):
                    c_ = S_["ch"][tg]
                    X, krt, tokB, KEZ = S_["X"], S_["krt"], S_["tokB"], S_["KEZ"]
                    Gz, nUz, WTd, UP = c_["Gz"], c_["nUz"], c_["WTd"], c_["UP"]
                    for jp in range(2):
                        ro = slice(64 * jp, 64 * jp + 64)
                        c0 = tg * 128 + jp * 64
                        cs_ = slice(c0, c0 + 64)
                        bU = k.psum()
                        k.mm(bU, bU[ro, 0:128], WTd[jp][:, :], Tbk[p][:, :], True, True, [WTd[jp], Tbk[p]])
                        k.stt(nUz[jp][ro, :], bU[ro, 0:128], -1.0, UP[ro, :], ALU.mult, ALU.subtract, [bU, UP], [nUz[jp]])
                        bU.busy = False
                        k.mm(psY, psY[:, cs_], Tbk[p][:, :], krt[:, 1, cs_], True, False, [Tbk[p], krt])
                        for hp in range(2):
                            rs_ = slice(64 * hp, 64 * hp + 64)
                            vc = slice(p * 128 + 64 * hp, p * 128 + 64 * hp + 64)
                            k.mm(psY, psY[rs_, cs_], vtok[:, tg, vc], Gz[jp][:, hp, 64:128], False, False, [vtok, Gz[jp]])
                        yield
                        for hp in range(2):
                            rs_ = slice(64 * hp, 64 * hp + 64)
                            k.mm(psY, psY[rs_, cs_], nUz[jp][:, 64 * hp:64 * hp + 64], Gz[jp][:, hp, 192:256], False, True, [nUz[jp], Gz[jp]])
                        bS = k.psum()
                        for hp in range(2):
                            rs_ = slice(64 * hp, 64 * hp + 64)
                            vc = slice(p * 128 + 64 * hp, p * 128 + 64 * hp + 64)
                            hc = slice(tg * 128 + 64 * hp, tg * 128 + 64 * hp + 64)
                            k.mm(bS, bS[rs_, 0:64], KEZ[jp][:, hc], vtok[:, tg, vc], True, False, [KEZ[jp], vtok])
                            k.mm(bS, bS[rs_, 0:64], tokB[:, hc], nUz[jp][:, 64 * hp:64 * hp + 64], False, True, [tokB, nUz[jp]])
                        k.stt(Tf[p][:, :], Tf[p][:, :], X["g"][:, c0 + 63:c0 + 64], bS[:, 0:64], ALU.mult, ALU.add, [Tf[p], X["g"], bS], [Tf[p]])
                        bS.busy = False
                        for hp in range(2):
                            rs_ = slice(64 * hp, 64 * hp + 64)
                            k.copy("act", Tbk[p][rs_, 64 * hp:64 * hp + 64], Tf[p][rs_, :], [Tf[p]], [Tbk[p]])
                        yield

                def pair_gen(p, S_):
                    pg = q * 4 + p
                    cols = slice(p * 128, (p + 1) * 128)
                    X, t16, krt, tokA, tokB, KEZ = S_["X"], S_["t16"], S_["krt"], S_["tokA"], S_["tokB"], S_["KEZ"]
                    b_rk = k.psum()
                    for kc in range(8):
                        k.mm(b_rk, b_rk[:, 0:T], Wr[:, kc, cols], xr[:, kc, :], kc == 0, kc == 7, [Wr, xr])
                    for kc in range(8):
                        k.mm(b_rk, b_rk[:, T:2 * T], Wk[:, kc, cols], xk[:, kc, :], kc == 0, kc == 7, [Wk, xk])
                    k.copy("act", X["rf"][:, :], b_rk[:, 0:T], [b_rk], [X["rf"]])
                    k.ts("dve", X["kkp"][:, :], b_rk[:, T:2 * T], C("k_k", pg), None, ALU.mult, None, [b_rk, cst], [X["kkp"]])
                    k.copy("act", X["kf"][:, :], b_rk[:, T:2 * T], [b_rk], [X["kf"]])
                    b_rk.busy = False
                    yield
                    b_gw = k.psum()
                    for kc in range(8):
                        k.mm(b_gw, b_gw[:, 0:T], Wg[:, kc, cols], xg[:, kc, :], kc == 0, kc == 7, [Wg, xg])
                    k.mm(b_gw, b_gw[:, T:2 * T], W2[:, cols], tw[:, :], True, True, [W2, tw])
                    k.act(X["sw"][:, :], b_gw[:, T:2 * T], AF.Sigmoid, [b_gw, cst], [X["sw"]], bias=C("w0", pg))
                    k.act(X["gate"][:, :], b_gw[:, 0:T], AF.Silu, [b_gw], [X["gate"]])
                    b_gw.busy = False
                    yield
                    b_av = k.psum()
                    k.mm(b_av, b_av[:, 0:T], A2[:, cols], al[:, :], True, True, [A2, al])
                    for tg in range(2):
                        k.mm(b_av, b_av[:, T + tg * 128:T + (tg + 1) * 128], vtok[:, tg, cols], ident_b, True, True, [vtok, mk_b])
                    k.act(X["av"][:, :], b_av[:, 0:T], AF.Sigmoid, [b_av, cst], [X["av"]], bias=C("a0", pg))
                    k.copy("act", X["vfm"][:, :], b_av[:, T:2 * T], [b_av], [X["vfm"]])
                    b_av.busy = False
                    yield
                    k.ts("pool", X["nk"][:, :], X["av"][:, :], C("k_a", pg), der[:, 32 + pg:33 + pg], ALU.mult, ALU.add, [X["av"], cst, der], [X["nk"]])
                    k.tt("pool", X["kf"][:, :], X["kf"][:, :], X["nk"][:, :], ALU.mult, [X["kf"], X["nk"]], [X["kf"]])
                    k.act(t16["kk2"][:, :], X["kkp"][:, :], AF.Square, [X["kkp"]], [t16["kk2"]])
                    b_n = k.psum()
                    k.mm(b_n, b_n[:, 0:T], oblk_b, t16["kk2"][:, :], True, True, [mk_b, t16["kk2"]])
                    k.act(X["nk"][:, :], b_n[:, 0:T], AF.Sqrt, [b_n], [X["nk"]])
                    k.ts("dve", X["nk"][:, :], X["nk"][:, :], 1e-12, None, ALU.max, None, [X["nk"]], [X["nk"]])
                    k.op("dve", lambda v: v.reciprocal(out=X["nk"][:, :], in_=X["nk"][:, :]), reads=[X["nk"]], writes=[X["nk"]])
                    k.tt("dve", X["kkp"][:, :], X["kkp"][:, :], X["nk"][:, :], ALU.mult, [X["kkp"], X["nk"]], [X["kkp"]])
                    k.tt("pool", X["bb"][:, :], X["kkp"][:, :], X["av"][:, :], ALU.mult, [X["kkp"], X["av"]], [X["bb"]])
                    k.stt(t16["rk2"][:, :], X["rf"][:, :], C("r_k", pg), X["kf"][:, :], ALU.mult, ALU.mult, [X["rf"], cst, X["kf"]], [t16["rk2"]])
                    k.mm(b_n, b_n[:, T:2 * T], oblk_b, t16["rk2"][:, :], True, True, [mk_b, t16["rk2"]])
                    k.tt("dve", X["bonus"][:, :], b_n[:, T:2 * T], X["vfm"][:, :], ALU.mult, [b_n, X["vfm"]], [X["bonus"]])
                    b_n.busy = False
                    yield
                    for j in range(T // 64):
                        k.op("dve", lambda v, j=j: v.tensor_tensor_scan(
                            out=X["cum"][:, j * 64:(j + 1) * 64], data0=onesf[:, 0:64], data1=X["sw"][:, j * 64:(j + 1) * 64],
                            initial=0.0, op0=ALU.mult, op1=ALU.add), reads=[onesf, X["sw"]], writes=[X["cum"]])
                    k.tt("pool", X["cm"][:, :], X["cum"][:, :], X["sw"][:, :], ALU.subtract, [X["cum"], X["sw"]], [X["cm"]])
                    k.act(X["g"][:, :], X["cum"][:, :], AF.Exp, [X["cum"]], [X["g"]], scale=-DEC)
                    k.act(X["gi"][:, :], X["cum"][:, :], AF.Exp, [X["cum"]], [X["gi"]], scale=DEC)
                    k.act(X["cm"][:, :], X["cm"][:, :], AF.Exp, [X["cm"]], [X["cm"]], scale=-DEC)
                    yield
                    k.tt("pool", krt[:, 0, :], X["kkp"][:, :], X["cm"][:, :], ALU.mult, [X["kkp"], X["cm"]], [krt])
                    k.tt("dve", krt[:, 1, :], X["rf"][:, :], X["g"][:, :], ALU.mult, [X["rf"], X["g"]], [krt])
                    k.tt("dve", X["kf"][:, :], X["kf"][:, :], X["gi"][:, :], ALU.mult, [X["kf"], X["gi"]], [X["kf"]])
                    k.tt("pool", X["bb"][:, :], X["bb"][:, :], X["gi"][:, :], ALU.mult, [X["bb"], X["gi"]], [X["bb"]])
                    k.copy("act", t16["ktb"][:, :], X["kf"][:, :], [X["kf"]], [t16["ktb"]])
                    k.copy("pool", t16["btb"][:, :], X["bb"][:, :], [X["bb"]], [t16["btb"]])
                    for j in range(T // 64):
                        sl = slice(j * 64, (j + 1) * 64)
                        ge = X["g"][:, j * 64 + 63:j * 64 + 64]
                        k.ts("dve", t16["kend"][:, sl], X["kf"][:, sl], ge, None, ALU.mult, None, [X["kf"], X["g"]], [t16["kend"]])
                        k.ts("pool", t16["bend"][:, sl], X["bb"][:, sl], ge, None, ALU.mult, None, [X["bb"], X["g"]], [t16["bend"]])
                    yield
                    b_t = k.psum()
                    for tg in range(2):
                        k.mm(b_t, b_t[:, tg * 128:(tg + 1) * 128], krt[:, 0, tg * 128:(tg + 1) * 128], ident_b, True, True, [krt, mk_b])
                    for tg in range(2):
                        k.mm(b_t, b_t[:, 256 + tg * 128:256 + (tg + 1) * 128], t16["kend"][:, tg * 128:(tg + 1) * 128], ident_b, True, True, [t16["kend"], mk_b])
                    k.copy("act", tokA[:, 0:256], b_t[:, 0:256], [b_t], [tokA])
                    for jp in range(2):
                        ro = slice(64 * jp, 64 * jp + 64)
                        k.copy("act", KEZ[jp][ro, :], b_t[ro, 256:512], [b_t], [KEZ[jp]])
                    b_t.busy = False
                    b_t2 = k.psum()
                    for tg in range(2):
                        k.mm(b_t2, b_t2[:, tg * 128:(tg + 1) * 128], t16["bend"][:, tg * 128:(tg + 1) * 128], ident_b, True, True, [t16["bend"], mk_b])
                    k.copy("dve", tokB[:, :], b_t2[:, 0:256], [b_t2], [tokB])
                    b_t2.busy = False
                    yield
                    psY = accb[p % 2]
                    yield from rr([chain_gen(p, 0, S_), chain_gen(p, 1, S_)])
                    yield from seq_gen(p, 0, S_, psY)
                    yield from seq_gen(p, 1, S_, psY)
                    k.copy("act", X["yf"][:, :], psY[:, 0:T], [psY], [X["yf"]])
                    k.act(t16["ysq"][:, :], psY[:, 0:T], AF.Square, [psY], [t16["ysq"]])
                    k.copy("pool", t16["yb16"][:, :], X["yf"][:, :], [X["yf"]], [t16["yb16"]])
                    bM = k.psum()
                    k.mm(bM, bM[:, 0:T], oblk_b, t16["yb16"][:, :], True, True, [mk_b, t16["yb16"]])
                    k.mm(bM, bM[:, T:2 * T], oblk_b, t16["ysq"][:, :], True, True, [mk_b, t16["ysq"]])
                    k.ts("dve", X["mean"][:, :], bM[:, 0:T], 1.0 / 64, None, ALU.mult, None, [bM], [X["mean"]])
                    k.tt("pool", X["var"][:, :], X["mean"][:, :], X["mean"][:, :], ALU.mult, [X["mean"]], [X["var"]])
                    k.stt(X["var"][:, :], bM[:, T:2 * T], 1.0 / 64, X["var"][:, :], ALU.mult, ALU.subtract, [bM, X["var"]], [X["var"]])
                    bM.busy = False
                    yield
                    k.act(X["var"][:, :], X["var"][:, :], AF.Sqrt, [X["var"], der], [X["var"]], bias=eps_gn)
                    k.op("dve", lambda v: v.reciprocal(out=X["var"][:, :], in_=X["var"][:, :]), reads=[X["var"]], writes=[X["var"]])
                    k.tt("pool", X["yf"][:, :], X["yf"][:, :], X["mean"][:, :], ALU.subtract, [X["yf"], X["mean"]], [X["yf"]])
                    k.tt("dve", X["yf"][:, :], X["yf"][:, :], X["var"][:, :], ALU.mult, [X["yf"], X["var"]], [X["yf"]])
                    k.ts("pool", X["yf"][:, :], X["yf"][:, :], C("lnx_g", pg), C("lnx_b", pg), ALU.mult, ALU.add, [X["yf"], cst], [X["yf"]])
                    k.tt("pool", X["yf"][:, :], X["yf"][:, :], X["bonus"][:, :], ALU.add, [X["yf"], X["bonus"]], [X["yf"]])
                    k.tt("dve", y[:, p, :], X["yf"][:, :], X["gate"][:, :], ALU.mult, [X["yf"], X["gate"]], [y])
                    yield

                for ti in range(nt):
                    t0 = ti * T
                    hU = hUs[ti % 2]
                    hR = hRs[ti % len(hRs)]
                    k.dma("sp", ldS[ti % 2], hU[:, :, :], h1_d[:, :, t0:t0 + T], reads=[dr["h1"][ti]], writes=[hU])
                    if q > 0:
                        k.dma("sp", ldR[ti % 2], hR[:, :, :], res_d[:, :, t0:t0 + T], reads=[res_tr[ti]], writes=[hR])
                    k.copy("pool", ufp[:, :, 0:1], ufp[:, :, T:T + 1], [ufp], [ufp])
                    emit_norm((sq, sd), hU, "c_g", ufp, 1, True)
                    k.tt("pool", delta[:, :, :], ufp[:, :, 0:T], ufp[:, :, 1:T + 1], ALU.subtract, [ufp], [delta])
                    for j in range(6):
                        for c in range(8):
                            mo = _off["mu"] + j * 8 + c
                            k.stt(xj[j][:, c, :], delta[:, c, :], cst[:, mo:mo + 1], ufp[:, c, 1:T + 1], ALU.mult, ALU.add,
                                  [delta, cst, ufp], [xj[j]])
                    bl = k.psum()
                    for kc in range(8):
                        k.mm(bl, bl[0:64, 0:T], W1[:, kc, :], xw[:, kc, :], kc == 0, kc == 7, [W1, xw])
                    for kc in range(8):
                        k.mm(bl, bl[0:64, T:2 * T], A1[:, kc, :], xa[:, kc, :], kc == 0, kc == 7, [A1, xa])
                    k.act(tw[:, :], bl[0:64, 0:T], AF.Tanh, [bl], [tw])
                    k.copy("act", al[:, :], bl[0:64, T:2 * T], [bl], [al])
                    bl.busy = False
                    for tg in range(2):
                        bank = k.psum()
                        for kc in range(8):
                            k.mm(bank, bank[:, 0:512], xv[:, kc, tg * 128:(tg + 1) * 128], Wv[:, kc, :], kc == 0, kc == 7, [xv, Wv])
                        k.copy("act" if tg == 0 else "dve", vtok[:, tg, :], bank[:, 0:512], [bank], [vtok])
                        bank.busy = False
                    for pp in range(2):
                        for _ in rr([pair_gen(2 * pp, PB[0]), pair_gen(2 * pp + 1, PB[1])]):
                            pass
                    for dc in range(8):
                        bank = k.psum()
                        for m in range(4):
                            k.mm(bank, bank[:, 0:T], Wo[:, m, dc * 128:(dc + 1) * 128], y[:, m, :], m == 0, m == 3, [Wo, y])
                        k.tt("dve", hR[:, dc, :], hR[:, dc, :], bank[:, 0:T], ALU.add, [hR, bank], [hR])
                        bank.busy = False
                    if not last:
                        k.dma("sp", stS[ti % 2], dst_d[:, :, t0:t0 + T], hR[:, :, :], reads=[hR], writes=[dst_tr[ti]])
                    else:
                        emit_norm((sq, sd), hR, "fin_g", delta, 0, True)
                        for tg in range(2):
                            for cq in range(2):
                                bank = k.psum()
                                for c4 in range(4):
                                    c = cq * 4 + c4
                                    k.op("pe", lambda pe, bank=bank, c4=c4, c=c, tg=tg: pe.transpose(
                                        out=bank[:, c4 * 128:(c4 + 1) * 128], in_=delta[:, c, tg * 128:(tg + 1) * 128], identity=mk_f[:, :]),
                                        reads=[delta, mk_f], writes=[bank])
                                k.copy("act" if cq == 0 else "dve", ot[:, tg, cq * 512:(cq + 1) * 512], bank[:, 0:512], [bank], [ot])
                                bank.busy = False
                        k.dma("sp", stS[ti % 2], out_d[t0:t0 + T, :].rearrange("(g p) d -> p g d", p=128), ot[:, :, :], reads=[ot], writes=[])
                k.barrier()

        if "A" in passes:
            pass_A()
        if "B" in passes:
            pass_B()
        if "C" in passes:
            trs = [[Buf(None, "tr%d_%d" % (i, j)) for j in range(NT)] for i in range(3)]
            pass_R(0, None, None, hC_d, trs[0], False)
            pass_R(1, hC_d, trs[0], hA_d, trs[1], False)
            pass_R(2, hA_d, trs[1], hC_d, trs[2], False)
            pass_R(3, hC_d, trs[2], None, None, True)
        k.barrier()
        print("instructions:", k.nins)
    return nc


def _fm(w):
    n = w.shape[1]
    return np.ascontiguousarray(w.reshape(8, 128, n).transpose(1, 0, 2))


def _cv(v):
    return np.ascontiguousarray(v.reshape(-1, 128).T)


def make_masks():
    m = np.zeros((128, NMASK), np.float32)
    p = np.arange(128)[:, None]
    c = np.arange(128)[None, :]
    m[:, M_ID:M_ID + 128] = (p == c)
    m[:, M_ONES:M_ONES + 128] = 1.0
    m[:, M_OBLK:M_OBLK + 128] = (p // 64 == c // 64)
    m[:, M_HG:M_HG + 128] = (p // 64 == c // 64) & (p <= c)
    s = np.arange(128)[:, None] % 64
    t = np.arange(64)[None, :]
    strict = (s < t).astype(np.float32)
    incl = (s <= t).astype(np.float32)
    m[:, M_RW:M_RW + 64] = strict
    m[:, M_RW + 64:M_RW + 128] = incl
    m[:, M_RW + 128:M_RW + 192] = strict
    m[:, M_RW + 192:M_RW + 256] = incl
    m[:, M_RW + 256:M_RW + 320] = (t < s)
    eye = (s == t).astype(np.float32)
    m[:, M_ID2:M_ID2 + 64] = eye
    m[:, M_ID2 + 64:M_ID2 + 128] = eye
    return m


def prep_inputs(inp, b):
    f = lambda a: np.asarray(a, np.float32)
    cst = np.zeros((128, NCONST), np.float32)

    def put(name, arr):
        cst[:, _off[name]:_off[name] + arr.shape[1]] = arr

    put("ab_g", _cv(f(inp["ab_norm_g"])[0]))
    put("c_g", _cv(f(inp["c_norm_g"])[0]))
    put("fin_g", _cv(f(inp["final_g"])))
    cw = f(inp["rg_conv_w"])[0]
    put("conv_w", np.ascontiguousarray(cw.reshape(4, 8, 128).transpose(2, 1, 0).reshape(128, 32)))
    put("conv_b", _cv(f(inp["rg_conv_b"])[0]))
    put("b_a", _cv(f(inp["rg_b_a"])[0]))
    put("b_x", _cv(f(inp["rg_b_x"])[0]))
    put("lam", _cv(f(inp["rg_lambda"])[0]))
    put("lb0", _cv(f(inp["hg_lb_logits"])[0]))
    put("lb1", _cv(f(inp["hg_lb_logits"])[1]))
    put("hg_g", f(inp["hg_norm_g"])[0].reshape(128, 1))
    mu = f(inp["c_mu"])[0]
    put("mu", np.ascontiguousarray(mu.reshape(6, 8, 128).transpose(2, 0, 1).reshape(128, 48)))
    put("w0", _cv(f(inp["c_w0"])[0]))
    put("a0", _cv(f(inp["c_a0"])[0]))
    put("k_k", _cv(f(inp["c_k_k"])[0]))
    put("k_a", _cv(f(inp["c_k_a"])[0]))
    put("r_k", _cv(f(inp["c_r_k"])[0].reshape(-1)))
    put("lnx_g", _cv(f(inp["c_lnx_g"])[0]))
    put("lnx_b", _cv(f(inp["c_lnx_b"])[0]))
    win = f(inp["ab_w_in"])[0]
    wout = f(inp["ab_w_out"])[0]
    m = {
        "x": np.ascontiguousarray(f(inp["x"])[b]),
        "consts": cst,
        "masks": make_masks(),
        "wAin": _fm(win[:, 0:2048]),
        "rgwa": np.ascontiguousarray(f(inp["rg_w_a"])[0].transpose(1, 0, 2)),
        "rgwx": np.ascontiguousarray(f(inp["rg_w_x"])[0].transpose(1, 0, 2)),
        "wAout": _fm(wout[0:1024]),
        "wBin": _fm(win[:, 2048:6144]),
        "wBout": _fm(wout[1024:2048]),
        "w1": _fm(f(inp["c_w1"])[0]),
        "a1": _fm(f(inp["c_a1"])[0]),
    }
    for h in range(2):
        sl = slice(h * 1024, (h + 1) * 1024)
        m["wr%d" % h] = _fm(f(inp["c_w_r"])[0][:, sl])
        m["wk%d" % h] = _fm(f(inp["c_w_k"])[0][:, sl])
        m["wv%d" % h] = _fm(f(inp["c_w_v"])[0][:, sl])
        m["wg%d" % h] = _fm(f(inp["c_w_g"])[0][:, sl])
        m["wo%d" % h] = _fm(f(inp["c_w_o"])[0][sl, :])
        m["w2_%d" % h] = np.ascontiguousarray(f(inp["c_w2"])[0][:, sl])
        m["a2_%d" % h] = np.ascontiguousarray(f(inp["c_a2"])[0][:, sl])
    return m


def kernel(**inputs):
    nc = build()
    in_maps = [prep_inputs(inputs, i % 4) for i in range(8)]
    res = run_bass_kernel_spmd(nc, in_maps, core_ids=list(range(8)))
    out = np.stack([np.asarray(res.results[i]["out"], np.float32) for i in range(4)], axis=0)
    return out
```

```python
import contextlib
import numpy as np
import concourse.bass as bass
import concourse.mybir as mybir
from concourse.bass_utils import run_bass_kernel_spmd
from concourse.alu_op_type import AluOpType as ALU

F32 = mybir.dt.float32
BF16 = mybir.dt.bfloat16
AF = mybir.ActivationFunctionType

S = 4096
D = 1024
T = 256
NT = S // T
RMS_EPS = 1e-6
GN_EPS = 64e-5
DEC = 0.6065306597126334
SAME_SYNC = "pool"
import os
BSTOP = float(os.environ.get('BSTOP', '99'))
RSTOP = float(os.environ.get('RSTOP', '99'))

_off = {}
_n = 0
for _name, _w in [("ab_g", 8), ("c_g", 8), ("fin_g", 8), ("conv_w", 32), ("conv_b", 8), ("b_a", 8), ("b_x", 8),
                  ("lam", 8), ("lb0", 8), ("lb1", 8), ("hg_g", 1), ("mu", 48), ("w0", 16), ("a0", 16),
                  ("k_k", 16), ("k_a", 16), ("r_k", 16), ("lnx_g", 16), ("lnx_b", 16)]:
    _off[_name] = _n
    _n += _w
NCONST = _n
M_ID = 0
M_ONES = 128
M_OBLK = 256
M_HG = 384
M_RW = 512
M_ID2 = 832
NMASK = 960


class Buf:
    __slots__ = ("t", "name", "w", "r", "busy")

    def __init__(self, t, name):
        self.t = t
        self.name = name
        self.w = None
        self.r = {}
        self.busy = False

    def __getitem__(self, idx):
        return self.t[idx]


class Stream:
    def __init__(self, sem, key):
        self.sem = sem
        self.key = key
        self.count = 0


class KB:
    def __init__(self, nc, es):
        self.nc = nc
        self.es = es
        self.eng = {"pe": nc.tensor, "act": nc.scalar, "dve": nc.vector, "pool": nc.gpsimd, "sp": nc.sync}
        self.st = {k: Stream(es.enter_context(nc.semaphore(k + "_s")), k) for k in self.eng}
        self.waited = {k: {} for k in self.eng}
        self.dstreams = []
        self.banks = []
        self.bank_i = 0
        self.nins = 0

    def dma_stream(self, name):
        s = Stream(self.es.enter_context(self.nc.semaphore(name)), name)
        self.dstreams.append(s)
        return s

    def sbuf(self, name, shape, dtype, es=None):
        es = es or self.es
        self.nins += 0
        self.uid = getattr(self, "uid", 0) + 1
        name = "%s_u%d" % (name, self.uid)
        return Buf(es.enter_context(self.nc.sbuf_tensor(name, list(shape), dtype)), name)

    def init_psum(self):
        for i in range(8):
            self.banks.append(Buf(self.es.enter_context(self.nc.psum_tensor("bank%d" % i, [128, 512], F32)), "bank%d" % i))

    def psum(self):
        b = self.banks[2 + self.bank_i % 6]
        self.bank_i += 1
        assert not b.busy, "psum bank still in use: " + b.name
        b.busy = True
        return b

    def _wait(self, e, sv):
        s, v = sv
        w = self.waited[e]
        if w.get(s.key, 0) >= v:
            return
        w[s.key] = v
        self.eng[e].wait_ge(s.sem, v)

    def _deps(self, e, reads, writes):
        for b in reads:
            if b.name.startswith("bank"):
                for sv in b.r.values():
                    if sv[0].key != e:
                        self._wait(e, sv)
            if b.w is not None:
                if b.w[0].key == e:
                    if (SAME_SYNC is True and e != "pe") or (SAME_SYNC == "pool" and e == "pool"):
                        self._wait(e, b.w)
                else:
                    self._wait(e, b.w)
        for b in writes:
            if b.w is not None and b.w[0].key != e:
                self._wait(e, b.w)
            for sv in b.r.values():
                if sv[0].key != e:
                    self._wait(e, sv)

    def op(self, e, fn, reads=(), writes=()):
        self._deps(e, reads, writes)
        ins = fn(self.eng[e])
        s = self.st[e]
        s.count += 1
        ins.then_inc(s.sem, 1)
        self.nins += 1
        for b in reads:
            b.r[s.key] = (s, s.count)
        for b in writes:
            b.w = (s, s.count)
            b.r = {}

    def dma(self, q, stream, out_ap, in_ap, reads=(), writes=()):
        self._deps(q, reads, writes)
        ins = self.eng[q].dma_start(out=out_ap, in_=in_ap)
        stream.count += 16
        ins.then_inc(stream.sem, 16)
        self.nins += 1
        for b in reads:
            b.r[stream.key] = (stream, stream.count)
        for b in writes:
            b.w = (stream, stream.count)
            b.r = {}

    def seal(self, stream, bufs):
        for b in bufs:
            b.w = (stream, stream.count)

    def barrier(self):
        allst = list(self.st.values()) + self.dstreams
        for e in self.eng:
            for s in allst:
                if s.key != e and s.count > 0:
                    self._wait(e, (s, s.count))

    def mm(self, bank, out_ap, lhsT, rhs, start, stop, reads):
        self.op("pe", lambda pe: pe.matmul(out_ap, lhsT=lhsT, rhs=rhs, start=start, stop=stop), reads=reads, writes=[bank])

    def act(self, out_ap, in_ap, func, reads, writes, bias=None, scale=None):
        kw = {}
        if bias is not None:
            kw["bias"] = bias
        if scale is not None:
            kw["scale"] = scale
        self.op("act", lambda a: a.activation(out=out_ap, in_=in_ap, func=func, **kw), reads=reads, writes=writes)

    def tt(self, e, out_ap, in0, in1, op, reads, writes):
        self.op(e, lambda v: v.tensor_tensor(out=out_ap, in0=in0, in1=in1, op=op), reads=reads, writes=writes)

    def ts(self, e, out_ap, in0, s1, s2, op0, op1, reads, writes):
        if op1 is None:
            self.op(e, lambda v: v.tensor_scalar(out=out_ap, in0=in0, scalar1=s1, scalar2=None, op0=op0), reads=reads, writes=writes)
        else:
            self.op(e, lambda v: v.tensor_scalar(out=out_ap, in0=in0, scalar1=s1, scalar2=s2, op0=op0, op1=op1), reads=reads, writes=writes)

    def stt(self, out_ap, in0, scalar, in1, op0, op1, reads, writes):
        self.op("dve", lambda v: v.scalar_tensor_tensor(out=out_ap, in0=in0, scalar=scalar, in1=in1, op0=op0, op1=op1), reads=reads, writes=writes)

    def copy(self, e, out_ap, in_ap, reads, writes):
        if e == "act":
            self.op("act", lambda a: a.copy(out=out_ap, in_=in_ap), reads=reads, writes=writes)
        else:
            self.op(e, lambda v: v.tensor_copy(out=out_ap, in_=in_ap), reads=reads, writes=writes)


def build(nt=NT, passes="ABCD", debug=False):
    nc = bass.Bass("TRN2", target_bir_lowering=False)

    def din(name, shape):
        return nc.dram_tensor(name, list(shape), F32, kind="ExternalInput").ap()

    x_d = din("x", [S, D])
    consts_d = din("consts", [128, NCONST])
    masks_d = din("masks", [128, NMASK])
    wAin_d = din("wAin", [128, 8, 2048])
    rgwa_d = din("rgwa", [128, 8, 128])
    rgwx_d = din("rgwx", [128, 8, 128])
    wAout_d = din("wAout", [128, 8, 1024])
    wBin_d = din("wBin", [128, 8, 4096])
    wBout_d = din("wBout", [128, 8, 1024])
    wr_d = [din("wr%d" % h, [128, 8, 1024]) for h in range(2)]
    wk_d = [din("wk%d" % h, [128, 8, 1024]) for h in range(2)]
    wv_d = [din("wv%d" % h, [128, 8, 1024]) for h in range(2)]
    wg_d = [din("wg%d" % h, [128, 8, 1024]) for h in range(2)]
    wo_d = [din("wo%d" % h, [128, 8, 1024]) for h in range(2)]
    w1_d = din("w1", [128, 8, 64])
    a1_d = din("a1", [128, 8, 64])
    w2_d = [din("w2_%d" % h, [64, 1024]) for h in range(2)]
    a2_d = [din("a2_%d" % h, [64, 1024]) for h in range(2)]
    out_d = nc.dram_tensor("out", [S, D], F32, kind="ExternalOutput").ap()
    skind = "ExternalOutput" if debug else "Internal"
    h0_d = nc.dram_tensor("h0fm", [128, 8, S], F32, kind=skind).ap()
    hA_d = nc.dram_tensor("hAfm", [128, 8, S], F32, kind=skind).ap()
    h1_d = nc.dram_tensor("h1fm", [128, 8, S], F32, kind=skind).ap()
    hC_d = nc.dram_tensor("hCfm", [128, 8, S], F32, kind=skind).ap()

    es = contextlib.ExitStack()
    with es:
        k = KB(nc, es)
        k.init_psum()
        cst = k.sbuf("cst", [128, NCONST], F32)
        der = k.sbuf("der", [128, 64], F32)
        mk_f = k.sbuf("mk_f", [128, 128], F32)
        mk_b = k.sbuf("mk_b", [128, NMASK], BF16)
        mk_rw = k.sbuf("mk_rw", [128, 320], F32)
        mk_hg = k.sbuf("mk_hg", [128, 128], F32)
        zeros = k.sbuf("zeros", [128, 64], F32)
        onesf = k.sbuf("onesf", [128, 64], F32)
        cs = k.dma_stream("cstream")
        k.dma("sp", cs, cst[:, :], consts_d[:, :], writes=[cst])
        k.dma("sp", cs, mk_f[:, :], masks_d[:, M_ID:M_ID + 128], writes=[mk_f])
        k.dma("sp", cs, mk_rw[:, :], masks_d[:, M_RW:M_RW + 320], writes=[mk_rw])
        k.dma("sp", cs, mk_hg[:, :], masks_d[:, M_HG:M_HG + 128], writes=[mk_hg])
        cs2 = k.dma_stream("cstream2")
        k.dma("pool", cs2, mk_b[:, :], masks_d[:, :], writes=[mk_b])
        k.seal(cs, [cst, mk_f, mk_rw, mk_hg])
        k.op("pool", lambda g: g.memset(zeros[:, :], 0.0), writes=[zeros])
        k.op("pool", lambda g: g.memset(onesf[:, :], 1.0), writes=[onesf])
        ident_b = mk_b[:, M_ID:M_ID + 128]
        ones_b = mk_b[:, M_ONES:M_ONES + 128]
        oblk_b = mk_b[:, M_OBLK:M_OBLK + 128]

        def C(name, j=0, w=1):
            o = _off[name] + j
            return cst[:, o:o + w]

        k.act(der[:, 0:8], C("lam", 0, 8), AF.Exp, [cst], [der], scale=-1.0)
        k.act(der[:, 0:8], der[:, 0:8], AF.Ln, [der, onesf], [der], bias=onesf[:, 0:1])
        k.ts("dve", der[:, 8:16], der[:, 0:8], -16.0, None, ALU.mult, None, [der], [der])
        k.ts("dve", der[:, 0:8], der[:, 0:8], -8.0, None, ALU.mult, None, [der], [der])
        k.tt("dve", der[:, 16:24], C("lb0", 0, 8), C("lb1", 0, 8), ALU.subtract, [cst], [der])
        k.act(der[:, 16:24], der[:, 16:24], AF.Sigmoid, [der], [der])
        k.ts("dve", der[:, 24:32], der[:, 16:24], -1.0, 1.0, ALU.mult, ALU.add, [der], [der])
        k.ts("dve", der[:, 32:48], C("k_a", 0, 16), -1.0, 1.0, ALU.mult, ALU.add, [cst], [der])
        k.op("pool", lambda g: g.memset(der[:, 48:49], RMS_EPS), writes=[der])
        k.op("pool", lambda g: g.memset(der[:, 49:50], GN_EPS), writes=[der])
        eps_rms = der[:, 48:49]
        eps_gn = der[:, 49:50]

        ws = k.dma_stream("wstream")
        ldS = [k.dma_stream("ldU0"), k.dma_stream("ldU1")]
        ldR = [k.dma_stream("ldR0"), k.dma_stream("ldR1")]
        stS = [k.dma_stream("st0"), k.dma_stream("st1")]
        stS2 = [k.dma_stream("st2_0"), k.dma_stream("st2_1")]
        dr = {nm: [Buf(None, "%s_%d" % (nm, i)) for i in range(NT)] for nm in ("h0", "hA", "h1", "hC")}

        def loadw(buf, dram, nk=8, ncol=None, dcol0=0, dk0=0):
            ncol = ncol or dram.shape[2]
            for kc in range(nk):
                for c0 in range(0, ncol, 1024):
                    c1 = min(ncol, c0 + 1024)
                    k.dma("pool", ws, buf[:, kc, c0:c1], dram[:, dk0 + kc, dcol0 + c0:dcol0 + c1], writes=[buf])

        def emit_norm(pes_bufs, hU, gname, outbuf, col0, fp32_out):
            sq, sd = pes_bufs
            k.act(sq[:, :, :], hU[:, :, :], AF.Square, [hU], [sq])
            bank = k.psum()
            for c in range(8):
                k.mm(bank, bank[:, 0:T], ones_b, sq[:, c, :], c == 0, c == 7, [sq, mk_b])
            k.act(sd[:, :], bank[:, 0:T], AF.Sqrt, [bank, der], [sd], bias=eps_rms, scale=1.0 / D)
            bank.busy = False
            k.op("dve", lambda v: v.reciprocal(out=sd[:, :], in_=sd[:, :]), reads=[sd], writes=[sd])
            for c in range(8):
                k.stt(outbuf[:, c, col0:col0 + T], hU[:, c, :], C(gname, c), sd[:, :], ALU.mult, ALU.mult,
                      [hU, cst, sd], [outbuf])

        def out_proj(Wout, y, hR):
            for dc in range(8):
                bank = k.psum()
                for m in range(8):
                    k.mm(bank, bank[:, 0:T], Wout[:, m, dc * 128:(dc + 1) * 128], y[:, m, :], m == 0, m == 7, [Wout, y])
                k.tt("dve", hR[:, dc, :], hR[:, dc, :], bank[:, 0:T], ALU.add, [hR, bank], [hR])
                bank.busy = False

        def pass_A():
            with contextlib.ExitStack() as pes:
                Win = k.sbuf("A_Win", [128, 8, 2048], BF16, pes)
                Wa = k.sbuf("A_Wa", [128, 8, 128], BF16, pes)
                Wx = k.sbuf("A_Wx", [128, 8, 128], BF16, pes)
                Wout = k.sbuf("A_Wout", [128, 8, 1024], BF16, pes)
                loadw(Win, wAin_d)
                k.dma("pool", ws, Wa[:, :, :], rgwa_d[:, :, :], writes=[Wa])
                k.dma("pool", ws, Wx[:, :, :], rgwx_d[:, :, :], writes=[Wx])
                loadw(Wout, wAout_d)
                k.seal(ws, [Win, Wa, Wx, Wout])
                xts = [k.sbuf("A_xt%d" % i, [128, 2, 1024], F32, pes) for i in range(2)]
                hUs = [k.sbuf("A_hU%d" % i, [128, 8, T], F32, pes) for i in range(2)]
                sq = k.sbuf("A_sq", [128, 8, T], BF16, pes)
                sd = k.sbuf("A_sd", [128, T], F32, pes)
                u = k.sbuf("A_u", [128, 8, T], BF16, pes)
                y = k.sbuf("A_y", [128, 8, T], BF16, pes)
                xaext = [k.sbuf("A_xa%d" % c, [128, T + 3], F32, pes) for c in range(8)]
                carry = [k.sbuf("A_cy%d" % c, [128, 1], F32, pes) for c in range(8)]
                xc = [k.sbuf("A_xc%d" % i, [128, T], F32, pes) for i in range(2)]
                xcb = [k.sbuf("A_xcb%d" % i, [128, T], BF16, pes) for i in range(2)]
                sr = [k.sbuf("A_sr%d" % i, [128, T], F32, pes) for i in range(2)]
                si = [k.sbuf("A_si%d" % i, [128, T], F32, pes) for i in range(2)]
                av = [k.sbuf("A_av%d" % i, [128, T], F32, pes) for i in range(2)]
                mv = [k.sbuf("A_mv%d" % i, [128, T], F32, pes) for i in range(2)]
                uu = [k.sbuf("A_uu%d" % i, [128, T], F32, pes) for i in range(2)]
                hh = [k.sbuf("A_hh%d" % i, [128, T], F32, pes) for i in range(2)]
                sg = [k.sbuf("A_sg%d" % i, [128, T], F32, pes) for i in range(2)]
                for c in range(8):
                    k.op("pool", lambda g, c=c: g.memset(xaext[c][:, :], 0.0), writes=[xaext[c]])
                    k.op("pool", lambda g, c=c: g.memset(carry[c][:, :], 0.0), writes=[carry[c]])

                for ti in range(nt):
                    t0 = ti * T
                    xt = xts[ti % 2]
                    hU = hUs[ti % 2]
                    k.dma("sp", ldS[ti % 2], xt[:, :, :], x_d[t0:t0 + T, :].rearrange("(g p) d -> p g d", p=128), writes=[xt])
                    for cp in range(4):
                        bank = k.psum()
                        for cc in range(2):
                            c = cp * 2 + cc
                            for tg in range(2):
                                o = cc * 256 + tg * 128
                                k.op("pe", lambda pe, o=o, c=c, tg=tg, bank=bank: pe.transpose(
                                    out=bank[:, o:o + 128], in_=xt[:, tg, c * 128:(c + 1) * 128], identity=mk_f[:, :]),
                                    reads=[xt, mk_f], writes=[bank])
                        for cc in range(2):
                            c = cp * 2 + cc
                            k.copy("act" if cc == 0 else "dve", hU[:, c, :], bank[:, cc * 256:cc * 256 + 256], [bank], [hU])
                        bank.busy = False
                    k.dma("sp", stS2[ti % 2], h0_d[:, :, t0:t0 + T], hU[:, :, :], reads=[hU], writes=[dr["h0"][ti]])
                    emit_norm((sq, sd), hU, "ab_g", u, 0, False)
                    for c in range(8):
                        i2 = c % 2
                        b1 = k.psum()
                        for kc in range(8):
                            k.mm(b1, b1[:, 0:T], Win[:, kc, c * 128:(c + 1) * 128], u[:, kc, :], kc == 0, kc == 7, [Win, u])
                        for kc in range(8):
                            k.mm(b1, b1[:, T:2 * T], Win[:, kc, 1024 + c * 128:1024 + (c + 1) * 128], u[:, kc, :], kc == 0, kc == 7, [Win, u])
                        xe = xaext[c]
                        k.copy("pool", xe[:, 0:3], xe[:, T:T + 3], [xe], [xe])
                        k.copy("act", xe[:, 3:T + 3], b1[:, 0:T], [b1], [xe])
                        cw = _off["conv_w"] + c * 4
                        k.ts("dve", xc[i2][:, :], xe[:, 3:T + 3], cst[:, cw + 3:cw + 4], C("conv_b", c), ALU.mult, ALU.add, [xe, cst], [xc[i2]])
                        for j in (2, 1, 0):
                            k.stt(xc[i2][:, :], xe[:, j:j + T], cst[:, cw + j:cw + j + 1], xc[i2][:, :], ALU.mult, ALU.add, [xe, cst, xc[i2]], [xc[i2]])
                        k.copy("pool", xcb[i2][:, :], xc[i2][:, :], [xc[i2]], [xcb[i2]])
                        b2 = k.psum()
                        k.mm(b2, b2[:, 0:T], Wa[:, c, :], xcb[i2][:, :], True, True, [Wa, xcb[i2]])
                        k.mm(b2, b2[:, T:2 * T], Wx[:, c, :], xcb[i2][:, :], True, True, [Wx, xcb[i2]])
                        k.act(sr[i2][:, :], b2[:, 0:T], AF.Sigmoid, [b2, cst], [sr[i2]], bias=C("b_a", c))
                        k.act(si[i2][:, :], b2[:, T:2 * T], AF.Sigmoid, [b2, cst], [si[i2]], bias=C("b_x", c))
                        b2.busy = False
                        k.act(sg[i2][:, :], b1[:, T:2 * T], AF.Silu, [b1], [sg[i2]])
                        b1.busy = False
                        k.act(av[i2][:, :], sr[i2][:, :], AF.Exp, [sr[i2], der], [av[i2]], scale=der[:, c:c + 1])
                        k.act(mv[i2][:, :], sr[i2][:, :], AF.Exp, [sr[i2], der], [mv[i2]], scale=der[:, 8 + c:9 + c])
                        k.act(mv[i2][:, :], mv[i2][:, :], AF.Sqrt, [mv[i2], onesf], [mv[i2]], bias=onesf[:, 0:1], scale=-1.0)
                        k.tt("pool", uu[i2][:, :], si[i2][:, :], xc[i2][:, :], ALU.mult, [si[i2], xc[i2]], [uu[i2]])
                        k.tt("dve", uu[i2][:, :], uu[i2][:, :], mv[i2][:, :], ALU.mult, [uu[i2], mv[i2]], [uu[i2]])
                        k.op("dve", lambda v, i2=i2, c=c: v.tensor_tensor_scan(
                            out=hh[i2][:, :], data0=av[i2][:, :], data1=uu[i2][:, :], initial=carry[c][:, 0:1],
                            op0=ALU.mult, op1=ALU.add), reads=[av[i2], uu[i2], carry[c]], writes=[hh[i2]])
                        k.copy("pool", carry[c][:, 0:1], hh[i2][:, T - 1:T], [hh[i2]], [carry[c]])
                        k.tt("dve", y[:, c, :], hh[i2][:, :], sg[i2][:, :], ALU.mult, [hh[i2], sg[i2]], [y])
                    out_proj(Wout, y, hU)
                    k.dma("sp", stS[ti % 2], hA_d[:, :, t0:t0 + T], hU[:, :, :], reads=[hU], writes=[dr["hA"][ti]])
                k.barrier()

        def pass_B():
            with contextlib.ExitStack() as pes:
                Win = k.sbuf("B_Win", [128, 8, 4096], BF16, pes)
                Wout = k.sbuf("B_Wout", [128, 8, 1024], BF16, pes)
                loadw(Win, wBin_d)
                loadw(Wout, wBout_d)
                k.seal(ws, [Win, Wout])
                hUs = [k.sbuf("B_hU%d" % i, [128, 8, T], F32, pes) for i in range(2)]
                hRs = [k.sbuf("B_hR%d" % i, [128, 8, T], F32, pes) for i in range(2)]
                sq = k.sbuf("B_sq", [128, 8, T], BF16, pes)
                sd = k.sbuf("B_sd", [128, T], F32, pes)
                u = k.sbuf("B_u", [128, 8, T], BF16, pes)
                y = k.sbuf("B_y", [128, 8, T], BF16, pes)
                vtok = k.sbuf("B_vtok", [128, 2, 1024], BF16, pes)
                stf = [k.sbuf("B_stf%d" % h, [128, 128], F32, pes) for h in range(8)]
                stb = [k.sbuf("B_stb%d" % h, [128, 128], BF16, pes) for h in range(8)]
                NB = 2
                sig = [k.sbuf("B_sig%d" % i, [128, T], F32, pes) for i in range(NB)]
                ff = [k.sbuf("B_f%d" % i, [128, T], F32, pes) for i in range(NB)]
                kf = [k.sbuf("B_k%d" % i, [128, T], F32, pes) for i in range(NB)]
                Pc = [k.sbuf("B_P%d" % i, [128, T], F32, pes) for i in range(NB)]
                Pi = [k.sbuf("B_Pi%d" % i, [128, T], F32, pes) for i in range(NB)]
                qd = [k.sbuf("B_qd%d" % i, [128, T], BF16, pes) for i in range(NB)]
                kif = [k.sbuf("B_kif%d" % i, [128, T], F32, pes) for i in range(NB)]
                kib = [k.sbuf("B_kib%d" % i, [128, T], BF16, pes) for i in range(NB)]
                keb = [k.sbuf("B_keb%d" % i, [128, T], BF16, pes) for i in range(NB)]
                scm = [k.sbuf("B_scm%d" % i, [128, 128], BF16, pes) for i in range(NB)]
                ket = [k.sbuf("B_ket%d" % i, [128, 128], BF16, pes) for i in range(NB)]
                osq = [k.sbuf("B_osq%d" % i, [128, T], BF16, pes) for i in range(NB)]
                ors = [k.sbuf("B_ors%d" % i, [128, T], F32, pes) for i in range(NB)]
                o1 = [k.sbuf("B_o1%d" % i, [128, T], F32, pes) for i in range(NB)]
                sgb = [k.sbuf("B_sg%d" % i, [128, T], F32, pes) for i in range(NB)]
                for h in range(8):
                    k.op("pool", lambda g, h=h: g.memset(stf[h][:, :], 0.0), writes=[stf[h]])
                    k.op("pool", lambda g, h=h: g.memset(stb[h][:, :], 0.0), writes=[stb[h]])
                accb = [k.banks[0], k.banks[1]]
                for ti in range(nt):
                    t0 = ti * T
                    hU = hUs[ti % 2]
                    hR = hRs[ti % 2]
                    k.dma("sp", ldS[ti % 2], hU[:, :, :], h0_d[:, :, t0:t0 + T], reads=[dr["h0"][ti]], writes=[hU])
                    k.dma("sp", ldR[ti % 2], hR[:, :, :], hA_d[:, :, t0:t0 + T], reads=[dr["hA"][ti]], writes=[hR])
                    emit_norm((sq, sd), hU, "ab_g", u, 0, False)
                    for tg in range(2):
                        for cg in range(2):
                            bank = k.psum()
                            for kc in range(8):
                                k.mm(bank, bank[:, 0:512], u[:, kc, tg * 128:(tg + 1) * 128],
                                     Win[:, kc, 2048 + cg * 512:2048 + (cg + 1) * 512], kc == 0, kc == 7, [u, Win])
                            k.copy("act" if cg == 0 else "dve", vtok[:, tg, cg * 512:(cg + 1) * 512], bank[:, 0:512], [bank], [vtok])
                            bank.busy = False
                    for h in range(8 if BSTOP > 1 else 0):
                        i2 = h % NB
                        bq = k.psum()
                        for kc in range(8):
                            k.mm(bq, bq[:, 0:T], Win[:, kc, h * 128:(h + 1) * 128], u[:, kc, :], kc == 0, kc == 7, [Win, u])
                        for kc in range(8):
                            k.mm(bq, bq[:, T:2 * T], Win[:, kc, 1024 + h * 128:1024 + (h + 1) * 128], u[:, kc, :], kc == 0, kc == 7, [Win, u])
                        k.act(sig[i2][:, :], bq[:, T:2 * T], AF.Sigmoid, [bq], [sig[i2]])
                        k.ts("dve", ff[i2][:, :], sig[i2][:, :], der[:, 24 + h:25 + h], der[:, 16 + h:17 + h], ALU.mult, ALU.add, [sig[i2], der], [ff[i2]])
                        k.ts("pool", kf[i2][:, :], ff[i2][:, :], -1.0, 1.0, ALU.mult, ALU.add, [ff[i2]], [kf[i2]])
                        for j in range(T // 64):
                            k.op("dve", lambda v, i2=i2, j=j: v.tensor_tensor_scan(
                                out=Pc[i2][:, j * 64:(j + 1) * 64], data0=ff[i2][:, j * 64:(j + 1) * 64], data1=zeros[:, 0:64],
                                initial=1.0, op0=ALU.mult, op1=ALU.add), reads=[ff[i2], zeros], writes=[Pc[i2]])
                        k.op("dve", lambda v, i2=i2: v.reciprocal(out=Pi[i2][:, :], in_=Pc[i2][:, :]), reads=[Pc[i2]], writes=[Pi[i2]])
                        k.tt("dve", qd[i2][:, :], bq[:, 0:T], Pc[i2][:, :], ALU.mult, [bq, Pc[i2]], [qd[i2]])
                        bq.busy = False
                        k.tt("pool", kif[i2][:, :], kf[i2][:, :], Pi[i2][:, :], ALU.mult, [kf[i2], Pi[i2]], [kif[i2]])
                        k.copy("pool", kib[i2][:, :], kif[i2][:, :], [kif[i2]], [kib[i2]])
                        for j in range(T // 64):
                            k.ts("dve", keb[i2][:, j * 64:(j + 1) * 64], kif[i2][:, j * 64:(j + 1) * 64],
                                 Pc[i2][:, j * 64 + 63:j * 64 + 64], None, ALU.mult, None, [kif[i2], Pc[i2]], [keb[i2]])
                        bo = accb[h % 2]
                        for tg in range(2 if BSTOP > 2 else 0):
                            c0 = tg * 128
                            bs = k.psum()
                            if BSTOP != 2.6:
                                k.mm(bs, bs[:, 0:128], kib[i2][:, c0:c0 + 128], qd[i2][:, c0:c0 + 128], True, True, [kib[i2], qd[i2]])
                            bs2 = bs
                            if BSTOP != 2.3:
                                k.mm(bs2, bs2[:, 128:256], keb[i2][:, c0:c0 + 128], ident_b, True, True, [keb[i2], mk_b])
                            if BSTOP != 2.6:
                                k.tt("dve", scm[i2][:, :], bs[:, 0:128], mk_hg[:, :], ALU.mult, [bs, mk_hg], [scm[i2]])
                            if BSTOP != 2.3:
                                k.copy("act", ket[i2][:, :], bs2[:, 128:256], [bs2], [ket[i2]])
                            bs.busy = False
                            for jp in range(2 if BSTOP > 3 else 0):
                                cj = c0 + jp * 64
                                r0 = jp * 64
                                k.mm(bo, bo[:, cj:cj + 64], vtok[:, tg, h * 128:(h + 1) * 128], scm[i2][:, r0:r0 + 64], True, False, [vtok, scm[i2]])
                                k.mm(bo, bo[:, cj:cj + 64], stb[h][:, :], qd[i2][:, cj:cj + 64], False, True, [stb[h], qd[i2]])
                                bst = k.psum()
                                k.mm(bst, bst[:, 0:128], ket[i2][r0:r0 + 64, :], vtok[r0:r0 + 64, tg, h * 128:(h + 1) * 128], True, True, [ket[i2], vtok])
                                k.stt(stf[h][:, :], stf[h][:, :], Pc[i2][:, cj + 63:cj + 64], bst[:, 0:128], ALU.mult, ALU.add, [stf[h], Pc[i2], bst], [stf[h]])
                                bst.busy = False
                                k.copy("act", stb[h][:, :], stf[h][:, :], [stf[h]], [stb[h]])
                        bg = k.psum()
                        for kc in range(8):
                            k.mm(bg, bg[:, 0:T], Win[:, kc, 3072 + h * 128:3072 + (h + 1) * 128], u[:, kc, :], kc == 0, kc == 7, [Win, u])
                        k.act(sgb[i2][:, :], bg[:, 0:T], AF.Silu, [bg], [sgb[i2]])
                        k.act(osq[i2][:, :], bo[:, 0:T], AF.Square, [bo], [osq[i2]])
                        k.mm(bg, bg[:, T:2 * T], ones_b, osq[i2][:, :], True, True, [mk_b, osq[i2]])
                        k.act(ors[i2][:, :], bg[:, T:2 * T], AF.Sqrt, [bg, der], [ors[i2]], bias=eps_rms, scale=1.0 / 128)
                        bg.busy = False
                        k.op("dve", lambda v, i2=i2: v.reciprocal(out=ors[i2][:, :], in_=ors[i2][:, :]), reads=[ors[i2]], writes=[ors[i2]])
                        k.tt("dve", o1[i2][:, :], bo[:, 0:T], ors[i2][:, :], ALU.mult, [bo, ors[i2]], [o1[i2]])
                        k.stt(y[:, h, :], o1[i2][:, :], C("hg_g"), sgb[i2][:, :], ALU.mult, ALU.mult, [o1[i2], cst, sgb[i2]], [y])
                    out_proj(Wout, y, hR)
                    k.dma("sp", stS[ti % 2], h1_d[:, :, t0:t0 + T], hR[:, :, :], reads=[hR], writes=[dr["h1"][ti]])
                k.barrier()


        def rr(gens):
            gens = list(gens)
            while gens:
                for g_ in list(gens):
                    try:
                        next(g_)
                    except StopIteration:
                        gens.remove(g_)
                yield

        def pass_R(q, res_d, res_tr, dst_d, dst_tr, last):
            hf, qo = q // 2, (q % 2) * 512
            with contextlib.ExitStack() as pes:
                Wr = k.sbuf("R_Wr", [128, 8, 512], BF16, pes)
                Wk = k.sbuf("R_Wk", [128, 8, 512], BF16, pes)
                Wv = k.sbuf("R_Wv", [128, 8, 512], BF16, pes)
                Wg = k.sbuf("R_Wg", [128, 8, 512], BF16, pes)
                W1 = k.sbuf("R_W1", [128, 8, 64], BF16, pes)
                A1 = k.sbuf("R_A1", [128, 8, 64], BF16, pes)
                W2 = k.sbuf("R_W2", [64, 512], BF16, pes)
                A2 = k.sbuf("R_A2", [64, 512], BF16, pes)
                Wo = k.sbuf("R_Wo", [128, 4, 1024], BF16, pes)
                loadw(Wr, wr_d[hf], ncol=512, dcol0=qo)
                loadw(Wk, wk_d[hf], ncol=512, dcol0=qo)
                loadw(Wv, wv_d[hf], ncol=512, dcol0=qo)
                loadw(Wg, wg_d[hf], ncol=512, dcol0=qo)
                k.dma("pool", ws, W1[:, :, :], w1_d[:, :, :], writes=[W1])
                k.dma("pool", ws, A1[:, :, :], a1_d[:, :, :], writes=[A1])
                k.dma("pool", ws, W2[:, :], w2_d[hf][:, qo:qo + 512], writes=[W2])
                k.dma("pool", ws, A2[:, :], a2_d[hf][:, qo:qo + 512], writes=[A2])
                loadw(Wo, wo_d[hf], nk=4, dk0=(q % 2) * 4)
                k.seal(ws, [Wr, Wk, Wv, Wg, W1, A1, W2, A2, Wo])
                hUs = [k.sbuf("R_hU%d" % i, [128, 8, T], F32, pes) for i in range(2)]
                hRs = [k.sbuf("R_hR%d" % i, [128, 8, T], F32, pes) for i in range(1)] if q > 0 else hUs
                sq = k.sbuf("R_sq", [128, 8, T], BF16, pes)
                sd = k.sbuf("R_sd", [128, T], F32, pes)
                ufp = k.sbuf("R_ufp", [128, 8, T + 1], F32, pes)
                delta = k.sbuf("R_delta", [128, 8, T], F32, pes)
                xj = [k.sbuf("R_x%d" % j, [128, 8, T], BF16, pes) for j in range(6)]
                y = k.sbuf("R_y", [128, 4, T], BF16, pes)
                vtok = k.sbuf("R_vtok", [128, 2, 512], BF16, pes)
                tw = k.sbuf("R_tw", [64, T], BF16, pes)
                al = k.sbuf("R_al", [64, T], BF16, pes)
                Tf = [k.sbuf("R_Tf%d" % p, [128, 64], F32, pes) for p in range(4)]
                Tbk = [k.sbuf("R_Tbk%d" % p, [128, 128], BF16, pes) for p in range(4)]
                ot = k.sbuf("R_ot", [128, 2, 1024], F32, pes) if last else None
                f32n = ["sw", "av", "rf", "gate", "kkp", "nk", "kf", "bb", "cum", "cm", "g", "gi", "vfm", "bonus"]
                b16n = ["kk2", "rk2", "ktb", "btb", "kend", "bend", "yb16", "ysq"]
                zl = list(Tbk)
                PB = []
                for si in range(2):
                    d_ = {}
                    d_["X"] = {n_: k.sbuf("R_" + n_, [128, T], F32, pes) for n_ in f32n}
                    d_["X"]["yf"] = d_["X"]["sw"]
                    d_["X"]["mean"] = d_["X"]["cum"]
                    d_["X"]["var"] = d_["X"]["gi"]
                    d_["t16"] = {n_: k.sbuf("R_" + n_, [128, T], BF16, pes) for n_ in b16n}
                    d_["krt"] = k.sbuf("R_krt", [128, 2, T], BF16, pes)
                    d_["tokA"] = k.sbuf("R_tokA", [128, 256], BF16, pes)
                    d_["tokB"] = k.sbuf("R_tokB", [128, 256], BF16, pes)
                    d_["KEZ"] = [k.sbuf("R_kez", [128, 256], BF16, pes) for j in range(2)]
                    zl += d_["KEZ"]
                    d_["ch"] = []
                    for tg in range(2):
                        c_ = {}
                        c_["Gz"] = [k.sbuf("R_Gz", [128, 2, 320], BF16, pes) for j in range(2)]
                        c_["Qz"] = [k.sbuf("R_Qz", [128, 128], BF16, pes) for j in range(2)]
                        c_["nUz"] = [k.sbuf("R_nUz", [128, 128], BF16, pes) for j in range(2)]
                        c_["WTd"] = [k.sbuf("R_WTd", [128, 64], BF16, pes) for j in range(2)]
                        c_["Qs"] = [k.sbuf("R_Q", [128, 128], BF16, pes) for j in range(2)]
                        c_["PPs"] = [k.sbuf("R_PP", [128, 256], BF16, pes) for j in range(2)]
                        c_["IP"] = k.sbuf("R_IP", [128, 2, 64], BF16, pes)
                        c_["AKV"] = k.sbuf("R_akv", [128, 128], BF16, pes)
                        c_["UP"] = k.sbuf("R_up", [128, 128], F32, pes)
                        zl += c_["Gz"] + c_["Qz"] + c_["nUz"]
                        d_["ch"].append(c_)
                    PB.append(d_)
                for bl_ in zl:
                    k.op("pool", lambda g_, bl_=bl_: g_.memset(bl_[:, :, :] if len(bl_.t.shape) == 3 else bl_[:, :], 0.0), writes=[bl_])
                id2 = mk_b[:, M_ID2:M_ID2 + 128]
                k.op("pool", lambda g_: g_.memset(ufp[:, :, :], 0.0), writes=[ufp])
                for p in range(4):
                    k.op("pool", lambda g_, p=p: g_.memset(Tf[p][:, :], 0.0), writes=[Tf[p]])
                accb = [k.banks[0], k.banks[1]]
                xr, xw, xk, xv, xa, xg = xj

                def chain_gen(p, tg, S_):
                    c_ = S_["ch"][tg]
                    t16, krt, tokA = S_["t16"], S_["krt"], S_["tokA"]
                    Gz, Qz, WTd, Qs, PPs, IP, AKV, UP = c_["Gz"], c_["Qz"], c_["WTd"], c_["Qs"], c_["PPs"], c_["IP"], c_["AKV"], c_["UP"]
                    for hp in range(2):
                        rs_ = slice(64 * hp, 64 * hp + 64)
                        bG = k.psum()
                        for jp in range(2):
                            cs_ = slice(tg * 128 + jp * 64, tg * 128 + jp * 64 + 64)
                            ro = slice(64 * jp, 64 * jp + 64)
                            k.mm(bG, bG[ro, 0:64], t16["ktb"][rs_, cs_], krt[rs_, 0, cs_], True, True, [t16["ktb"], krt])
                            k.mm(bG, bG[ro, 64:128], t16["ktb"][rs_, cs_], krt[rs_, 1, cs_], True, True, [t16["ktb"], krt])
                            k.mm(bG, bG[ro, 128:192], t16["btb"][rs_, cs_], krt[rs_, 0, cs_], True, True, [t16["btb"], krt])
                            k.mm(bG, bG[ro, 192:256], t16["btb"][rs_, cs_], krt[rs_, 1, cs_], True, True, [t16["btb"], krt])
                            k.mm(bG, bG[ro, 256:320], krt[rs_, 0, cs_], t16["btb"][rs_, cs_], True, True, [t16["btb"], krt])
                        for jp in range(2):
                            ro = slice(64 * jp, 64 * jp + 64)
                            k.tt("dve", Gz[jp][ro, hp, :], bG[ro, 0:320], mk_rw[ro, :], ALU.mult, [bG, mk_rw], [Gz[jp]])
                        bG.busy = False
                        yield
                    Qc = Qs[0]
                    for jp in range(2):
                        ro = slice(64 * jp, 64 * jp + 64)
                        for hp in range(2):
                            k.tt("pool", Qc[ro, 64 * hp:64 * hp + 64], id2[ro, 0:64], Gz[jp][ro, hp, 128:192], ALU.subtract, [mk_b, Gz[jp]], [Qc])
                    qi = 0
                    for lvl in range(1, 6):
                        bP = k.psum()
                        for hp in range(2):
                            for jp in range(2):
                                ro = slice(64 * jp, 64 * jp + 64)
                                if lvl == 1:
                                    Pm, PTm, Pb = Gz[jp][ro, hp, 256:320], Gz[jp][ro, hp, 128:192], Gz[jp]
                                else:
                                    PPc = PPs[lvl % 2]
                                    Pm, PTm, Pb = PPc[ro, hp * 128:hp * 128 + 64], PPc[ro, hp * 128 + 64:hp * 128 + 128], PPc
                                k.mm(bP, bP[ro, hp * 128:hp * 128 + 64], PTm, Pm, True, True, [Pb])
                                if lvl < 5:
                                    k.mm(bP, bP[ro, hp * 128 + 64:hp * 128 + 128], Pm, PTm, True, True, [Pb])
                        for hp in range(2):
                            k.tt("dve", IP[:, hp, :], bP[:, hp * 128:hp * 128 + 64], id2[:, 0:64], ALU.add, [bP, mk_b], [IP])
                        if lvl < 5:
                            PPn = PPs[(lvl + 1) % 2]
                            k.copy("act", PPn[:, :], bP[:, 0:256], [bP], [PPn])
                        bP.busy = False
                        yield
                        bQ = k.psum()
                        for hp in range(2):
                            for jp in range(2):
                                ro = slice(64 * jp, 64 * jp + 64)
                                k.mm(bQ, bQ[ro, hp * 64:hp * 64 + 64], IP[ro, hp, :], Qs[qi][ro, hp * 64:hp * 64 + 64], True, True, [IP, Qs[qi]])
                        if lvl < 5:
                            Qn = Qs[1 - qi]
                            k.copy("act", Qn[:, :], bQ[:, 0:128], [bQ], [Qn])
                        else:
                            for jp in range(2):
                                ro = slice(64 * jp, 64 * jp + 64)
                                k.copy("act", Qz[jp][ro, :], bQ[ro, 0:128], [bQ], [Qz[jp]])
                        bQ.busy = False
                        qi = 1 - qi
                        yield
                    bA = k.psum()
                    for hp in range(2):
                        for jp in range(2):
                            ro = slice(64 * jp, 64 * jp + 64)
                            vc = slice(p * 128 + 64 * hp, p * 128 + 64 * hp + 64)
                            k.mm(bA, bA[ro, 64 * hp:64 * hp + 64], Gz[jp][ro, hp, 0:64], vtok[ro, tg, vc], True, True, [Gz[jp], vtok])
                    k.copy("act", AKV[:, :], bA[:, 0:128], [bA], [AKV])
                    bA.busy = False
                    bW = k.psum()
                    for jp in range(2):
                        k.mm(bW, bW[:, jp * 128:(jp + 1) * 128], tokA[:, tg * 128:(tg + 1) * 128], Qz[jp][:, :], True, True, [tokA, Qz[jp]])
                    for jp in range(2):
                        for hp in range(2):
                            rs_ = slice(64 * hp, 64 * hp + 64)
                            k.copy("dve", WTd[jp][rs_, :], bW[rs_, jp * 128 + 64 * hp:jp * 128 + 64 * hp + 64], [bW], [WTd[jp]])
                    bW.busy = False
                    yield
                    bX = k.psum()
                    for hp in range(2):
                        for jp in range(2):
                            ro = slice(64 * jp, 64 * jp + 64)
                            k.mm(bX, bX[ro, 64 * hp:64 * hp + 64], Qz[jp][ro, hp * 64:hp * 64 + 64], AKV[ro, 64 * hp:64 * hp + 64], True, True, [Qz[jp], AKV])
                    k.copy("act", UP[:, :], bX[:, 0:128], [bX], [UP])
                    bX.busy = False
                    yield

                def seq_gen(p, tg, S_, psY):
                    c_ = S_["ch"][tg]
                    X, krt, tokB, KEZ = S_["X"], S_["krt"], S_["tokB"], S_["KEZ"]
                    Gz, nUz, WTd, UP = c_["Gz"], c_["nUz"], c_["WTd"], c_["UP"]
                    for jp in range(2):
                        ro = slice(64 * jp, 64 * jp + 64)
                        c0 = tg * 128 + jp * 64
                        cs_ = slice(c0, c0 + 64)
                        bU = k.psum()
                        k.mm(bU, bU[ro, 0:128], WTd[jp][:, :], Tbk[p][:, :], True, True, [WTd[jp], Tbk[p]])
                        k.stt(nUz[jp][ro, :], bU[ro, 0:128], -1.0, UP[ro, :], ALU.mult, ALU.subtract, [bU, UP], [nUz[jp]])
                        bU.busy = False
                        k.mm(psY, psY[:, cs_], Tbk[p][:, :], krt[:, 1, cs_], True, False, [Tbk[p], krt])
                        for hp in range(2):
                            rs_ = slice(64 * hp, 64 * hp + 64)
                            vc = slice(p * 128 + 64 * hp, p * 128 + 64 * hp + 64)
                            k.mm(psY, psY[rs_, cs_], vtok[:, tg, vc], Gz[jp][:, hp, 64:128], False, False, [vtok, Gz[jp]])
                        yield
                        for hp in range(2):
                            rs_ = slice(64 * hp, 64 * hp + 64)
                            k.mm(psY, psY[rs_, cs_], nUz[jp][:, 64 * hp:64 * hp + 64], Gz[jp][:, hp, 192:256], False, True, [nUz[jp], Gz[jp]])
                        bS = k.psum()
                        for hp in range(2):
                            rs_ = slice(64 * hp, 64 * hp + 64)
                            vc = slice(p * 128 + 64 * hp, p * 128 + 64 * hp + 64)
                            hc = slice(tg * 128 + 64 * hp, tg * 128 + 64 * hp + 64)
                            k.mm(bS, bS[rs_, 0:64], KEZ[jp][:, hc], vtok[:, tg, vc], True, False, [KEZ[jp], vtok])
                            k.mm(bS, bS[rs_, 0:64], tokB[:, hc], nUz[jp][:, 64 * hp:64 * hp + 64], False, True, [tokB, nUz[jp]])
                        k.stt(Tf[p][:, :], Tf[p][:, :], X["g"][:, c0 + 63:c0 + 64], bS[:, 0:64], ALU.mult, ALU.add, [Tf[p], X["g"], bS], [Tf[p]])
                        bS.busy = False
                        for hp in range(2):
                            rs_ = slice(64 * hp, 64 * hp + 64)
                            k.copy("act", Tbk[p][rs_, 64 * hp:64 * hp + 64], Tf[p][rs_, :], [Tf[p]], [Tbk[p]])
                        yield

                def pair_gen(p, S_):
                    pg = q * 4 + p
                    cols = slice(p * 128, (p + 1) * 128)
                    X, t16, krt, tokA, tokB, KEZ = S_["X"], S_["t16"], S_["krt"], S_["tokA"], S_["tokB"], S_["KEZ"]
                    b_rk = k.psum()
                    for kc in range(8):
                        k.mm(b_rk, b_rk[:, 0:T], Wr[:, kc, cols], xr[:, kc, :], kc == 0, kc == 7, [Wr, xr])
                    for kc in range(8):
                        k.mm(b_rk, b_rk[:, T:2 * T], Wk[:, kc, cols], xk[:, kc, :], kc == 0, kc == 7, [Wk, xk])
                    k.copy("act", X["rf"][:, :], b_rk[:, 0:T], [b_rk], [X["rf"]])
                    k.ts("dve", X["kkp"][:, :], b_rk[:, T:2 * T], C("k_k", pg), None, ALU.mult, None, [b_rk, cst], [X["kkp"]])
                    k.copy("act", X["kf"][:, :], b_rk[:, T:2 * T], [b_rk], [X["kf"]])
                    b_rk.busy = False
                    yield
                    b_gw = k.psum()
                    for kc in range(8):
                        k.mm(b_gw, b_gw[:, 0:T], Wg[:, kc, cols], xg[:, kc, :], kc == 0, kc == 7, [Wg, xg])
                    k.mm(b_gw, b_gw[:, T:2 * T], W2[:, cols], tw[:, :], True, True, [W2, tw])
                    k.act(X["sw"][:, :], b_gw[:, T:2 * T], AF.Sigmoid, [b_gw, cst], [X["sw"]], bias=C("w0", pg))
                    k.act(X["gate"][:, :], b_gw[:, 0:T], AF.Silu, [b_gw], [X["gate"]])
                    b_gw.busy = False
                    yield
                    b_av = k.psum()
                    k.mm(b_av, b_av[:, 0:T], A2[:, cols], al[:, :], True, True, [A2, al])
                    for tg in range(2):
                        k.mm(b_av, b_av[:, T + tg * 128:T + (tg + 1) * 128], vtok[:, tg, cols], ident_b, True, True, [vtok, mk_b])
                    k.act(X["av"][:, :], b_av[:, 0:T], AF.Sigmoid, [b_av, cst], [X["av"]], bias=C("a0", pg))
                    k.copy("act", X["vfm"][:, :], b_av[:, T:2 * T], [b_av], [X["vfm"]])
                    b_av.busy = False
                    yield
                    k.ts("pool", X["nk"][:, :], X["av"][:, :], C("k_a", pg), der[:, 32 + pg:33 + pg], ALU.mult, ALU.add, [X["av"], cst, der], [X["nk"]])
                    k.tt("pool", X["kf"][:, :], X["kf"][:, :], X["nk"][:, :], ALU.mult, [X["kf"], X["nk"]], [X["kf"]])
                    k.act(t16["kk2"][:, :], X["kkp"][:, :], AF.Square, [X["kkp"]], [t16["kk2"]])
                    b_n = k.psum()
                    k.mm(b_n, b_n[:, 0:T], oblk_b, t16["kk2"][:, :], True, True, [mk_b, t16["kk2"]])
                    k.act(X["nk"][:, :], b_n[:, 0:T], AF.Sqrt, [b_n], [X["nk"]])
                    k.ts("dve", X["nk"][:, :], X["nk"][:, :], 1e-12, None, ALU.max, None, [X["nk"]], [X["nk"]])
                    k.op("dve", lambda v: v.reciprocal(out=X["nk"][:, :], in_=X["nk"][:, :]), reads=[X["nk"]], writes=[X["nk"]])
                    k.tt("dve", X["kkp"][:, :], X["kkp"][:, :], X["nk"][:, :], ALU.mult, [X["kkp"], X["nk"]], [X["kkp"]])
                    k.tt("pool", X["bb"][:, :], X["kkp"][:, :], X["av"][:, :], ALU.mult, [X["kkp"], X["av"]], [X["bb"]])
                    k.stt(t16["rk2"][:, :], X["rf"][:, :], C("r_k", pg), X["kf"][:, :], ALU.mult, ALU.mult, [X["rf"], cst, X["kf"]], [t16["rk2"]])
                    k.mm(b_n, b_n[:, T:2 * T], oblk_b, t16["rk2"][:, :], True, True, [mk_b, t16["rk2"]])
                    k.tt("dve", X["bonus"][:, :], b_n[:, T:2 * T], X["vfm"][:, :], ALU.mult, [b_n, X["vfm"]], [X["bonus"]])
                    b_n.busy = False
                    yield
                    for j in range(T // 64):
                        k.op("dve", lambda v, j=j: v.tensor_tensor_scan(
                            out=X["cum"][:, j * 64:(j + 1) * 64], data0=onesf[:, 0:64], data1=X["sw"][:, j * 64:(j + 1) * 64],
                            initial=0.0, op0=ALU.mult, op1=ALU.add), reads=[onesf, X["sw"]], writes=[X["cum"]])
                    k.tt("pool", X["cm"][:, :], X["cum"][:, :], X["sw"][:, :], ALU.subtract, [X["cum"], X["sw"]], [X["cm"]])
                    k.act(X["g"][:, :], X["cum"][:, :], AF.Exp, [X["cum"]], [X["g"]], scale=-DEC)
                    k.act(X["gi"][:, :], X["cum"][:, :], AF.Exp, [X["cum"]], [X["gi"]], scale=DEC)
                    k.act(X["cm"][:, :], X["cm"][:, :], AF.Exp, [X["cm"]], [X["cm"]], scale=-DEC)
                    yield
                    k.tt("pool", krt[:, 0, :], X["kkp"][:, :], X["cm"][:, :], ALU.mult, [X["kkp"], X["cm"]], [krt])
                    k.tt("dve", krt[:, 1, :], X["rf"][:, :], X["g"][:, :], ALU.mult, [X["rf"], X["g"]], [krt])
                    k.tt("dve", X["kf"][:, :], X["kf"][:, :], X["gi"][:, :], ALU.mult, [X["kf"], X["gi"]], [X["kf"]])
                    k.tt("pool", X["bb"][:, :], X["bb"][:, :], X["gi"][:, :], ALU.mult, [X["bb"], X["gi"]], [X["bb"]])
                    k.copy("act", t16["ktb"][:, :], X["kf"][:, :], [X["kf"]], [t16["ktb"]])
                    k.copy("pool", t16["btb"][:, :], X["bb"][:, :], [X["bb"]], [t16["btb"]])
                    for j in range(T // 64):
                        sl = slice(j * 64, (j + 1) * 64)
                        ge = X["g"][:, j * 64 + 63:j * 64 + 64]
                        k.ts("dve", t16["kend"][:, sl], X["kf"][:, sl], ge, None, ALU.mult, None, [X["kf"], X["g"]], [t16["kend"]])
                        k.ts("pool", t16["bend"][:, sl], X["bb"][:, sl], ge, None, ALU.mult, None, [X["bb"], X["g"]], [t16["bend"]])
                    yield
                    b_t = k.psum()
                    for tg in range(2):
                        k.mm(b_t, b_t[:, tg * 128:(tg + 1) * 128], krt[:, 0, tg * 128:(tg + 1) * 128], ident_b, True, True, [krt, mk_b])
                    for tg in range(2):
                        k.mm(b_t, b_t[:, 256 + tg * 128:256 + (tg + 1) * 128], t16["kend"][:, tg * 128:(tg + 1) * 128], ident_b, True, True, [t16["kend"], mk_b])
                    k.copy("act", tokA[:, 0:256], b_t[:, 0:256], [b_t], [tokA])
                    for jp in range(2):
                        ro = slice(64 * jp, 64 * jp + 64)
                        k.copy("act", KEZ[jp][ro, :], b_t[ro, 256:512], [b_t], [KEZ[jp]])
                    b_t.busy = False
                    b_t2 = k.psum()
                    for tg in range(2):
                        k.mm(b_t2, b_t2[:, tg * 128:(tg + 1) * 128], t16["bend"][:, tg * 128:(tg + 1) * 128], ident_b, True, True, [t16["bend"], mk_b])
                    k.copy("dve", tokB[:, :], b_t2[:, 0:256], [b_t2], [tokB])
                    b_t2.busy = False
                    yield
                    psY = accb[p % 2]
                    yield from rr([chain_gen(p, 0, S_), chain_gen(p, 1, S_)])
                    yield from seq_gen(p, 0, S_, psY)
                    yield from seq_gen(p, 1, S_, psY)
                    k.copy("act", X["yf"][:, :], psY[:, 0:T], [psY], [X["yf"]])
                    k.act(t16["ysq"][:, :], psY[:, 0:T], AF.Square, [psY], [t16["ysq"]])
                    k.copy("pool", t16["yb16"][:, :], X["yf"][:, :], [X["yf"]], [t16["yb16"]])
                    bM = k.psum()
                    k.mm(bM, bM[:, 0:T], oblk_b, t16["yb16"][:, :], True, True, [mk_b, t16["yb16"]])
                    k.mm(bM, bM[:, T:2 * T], oblk_b, t16["ysq"][:, :], True, True, [mk_b, t16["ysq"]])
                    k.ts("dve", X["mean"][:, :], bM[:, 0:T], 1.0 / 64, None, ALU.mult, None, [bM], [X["mean"]])
                    k.tt("pool", X["var"][:, :], X["mean"][:, :], X["mean"][:, :], ALU.mult, [X["mean"]], [X["var"]])
                    k.stt(X["var"][:, :], bM[:, T:2 * T], 1.0 / 64, X["var"][:, :], ALU.mult, ALU.subtract, [bM, X["var"]], [X["var"]])
                    bM.busy = False
                    yield
                    k.act(X["var"][:, :], X["var"][:, :], AF.Sqrt, [X["var"], der], [X["var"]], bias=eps_gn)
                    k.op("dve", lambda v: v.reciprocal(out=X["var"][:, :], in_=X["var"][:, :]), reads=[X["var"]], writes=[X["var"]])
                    k.tt("pool", X["yf"][:, :], X["yf"][:, :], X["mean"][:, :], ALU.subtract, [X["yf"], X["mean"]], [X["yf"]])
                    k.tt("dve", X["yf"][:, :], X["yf"][:, :], X["var"][:, :], ALU.mult, [X["yf"], X["var"]], [X["yf"]])
                    k.ts("pool", X["yf"][:, :], X["yf"][:, :], C("lnx_g", pg), C("lnx_b", pg), ALU.mult, ALU.add, [X["yf"], cst], [X["yf"]])
                    k.tt("pool", X["yf"][:, :], X["yf"][:, :], X["bonus"][:, :], ALU.add, [X["yf"], X["bonus"]], [X["yf"]])
                    k.tt("dve", y[:, p, :], X["yf"][:, :], X["gate"][:, :], ALU.mult, [X["yf"], X["gate"]], [y])
                    yield

                for ti in range(nt):
                    t0 = ti * T
                    hU = hUs[ti % 2]
                    hR = hRs[ti % len(hRs)]
                    k.dma("sp", ldS[ti % 2], hU[:, :, :], h1_d[:, :, t0:t0 + T], reads=[dr["h1"][ti]], writes=[hU])
                    if q > 0:
                        k.dma("sp", ldR[ti % 2], hR[:, :, :], res_d[:, :, t0:t0 + T], reads=[res_tr[ti]], writes=[hR])
                    k.copy("pool", ufp[:, :, 0:1], ufp[:, :, T:T + 1], [ufp], [ufp])
                    emit_norm((sq, sd), hU, "c_g", ufp, 1, True)
                    k.tt("pool", delta[:, :, :], ufp[:, :, 0:T], ufp[:, :, 1:T + 1], ALU.subtract, [ufp], [delta])
                    for j in range(6):
                        for c in range(8):
                            mo = _off["mu"] + j * 8 + c
                            k.stt(xj[j][:, c, :], delta[:, c, :], cst[:, mo:mo + 1], ufp[:, c, 1:T + 1], ALU.mult, ALU.add,
                                  [delta, cst, ufp], [xj[j]])
                    bl = k.psum()
                    for kc in range(8):
                        k.mm(bl, bl[0:64, 0:T], W1[:, kc, :], xw[:, kc, :], kc == 0, kc == 7, [W1, xw])
                    for kc in range(8):
                        k.mm(bl, bl[0:64, T:2 * T], A1[:, kc, :], xa[:, kc, :], kc == 0, kc == 7, [A1, xa])
                    k.act(tw[:, :], bl[0:64, 0:T], AF.Tanh, [bl], [tw])
                    k.copy("act", al[:, :], bl[0:64, T:2 * T], [bl], [al])
                    bl.busy = False
                    for tg in range(2):
                        bank = k.psum()
                        for kc in range(8):
                            k.mm(bank, bank[:, 0:512], xv[:, kc, tg * 128:(tg + 1) * 128], Wv[:, kc, :], kc == 0, kc == 7, [xv, Wv])
                        k.copy("act" if tg == 0 else "dve", vtok[:, tg, :], bank[:, 0:512], [bank], [vtok])
                        bank.busy = False
                    for pp in range(2):
                        for _ in rr([pair_gen(2 * pp, PB[0]), pair_gen(2 * pp + 1, PB[1])]):
                            pass
                    for dc in range(8):
                        bank = k.psum()
                        for m in range(4):
                            k.mm(bank, bank[:, 0:T], Wo[:, m, dc * 128:(dc + 1) * 128], y[:, m, :], m == 0, m == 3, [Wo, y])
                        k.tt("dve", hR[:, dc, :], hR[:, dc, :], bank[:, 0:T], ALU.add, [hR, bank], [hR])
                        bank.busy = False
                    if not last:
                        k.dma("sp", stS[ti % 2], dst_d[:, :, t0:t0 + T], hR[:, :, :], reads=[hR], writes=[dst_tr[ti]])
                    else:
                        emit_norm((sq, sd), hR, "fin_g", delta, 0, True)
                        for tg in range(2):
                            for cq in range(2):
                                bank = k.psum()
                                for c4 in range(4):
                                    c = cq * 4 + c4
                                    k.op("pe", lambda pe, bank=bank, c4=c4, c=c, tg=tg: pe.transpose(
                                        out=bank[:, c4 * 128:(c4 + 1) * 128], in_=delta[:, c, tg * 128:(tg + 1) * 128], identity=mk_f[:, :]),
                                        reads=[delta, mk_f], writes=[bank])
                                k.copy("act" if cq == 0 else "dve", ot[:, tg, cq * 512:(cq + 1) * 512], bank[:, 0:512], [bank], [ot])
                                bank.busy = False
                        k.dma("sp", stS[ti % 2], out_d[t0:t0 + T, :].rearrange("(g p) d -> p g d", p=128), ot[:, :, :], reads=[ot], writes=[])
                k.barrier()

        if "A" in passes:
            pass_A()
        if "B" in passes:
            pass_B()
        if "C" in passes:
            trs = [[Buf(None, "tr%d_%d" % (i, j)) for j in range(NT)] for i in range(3)]
            pass_R(0, None, None, hC_d, trs[0], False)
            pass_R(1, hC_d, trs[0], hA_d, trs[1], False)
            pass_R(2, hA_d, trs[1], hC_d, trs[2], False)
            pass_R(3, hC_d, trs[2], None, None, True)
        k.barrier()
        print("instructions:", k.nins)
    return nc


def _fm(w):
    n = w.shape[1]
    return np.ascontiguousarray(w.reshape(8, 128, n).transpose(1, 0, 2))


def _cv(v):
    return np.ascontiguousarray(v.reshape(-1, 128).T)


def make_masks():
    m = np.zeros((128, NMASK), np.float32)
    p = np.arange(128)[:, None]
    c = np.arange(128)[None, :]
    m[:, M_ID:M_ID + 128] = (p == c)
    m[:, M_ONES:M_ONES + 128] = 1.0
    m[:, M_OBLK:M_OBLK + 128] = (p // 64 == c // 64)
    m[:, M_HG:M_HG + 128] = (p // 64 == c // 64) & (p <= c)
    s = np.arange(128)[:, None] % 64
    t = np.arange(64)[None, :]
    strict = (s < t).astype(np.float32)
    incl = (s <= t).astype(np.float32)
    m[:, M_RW:M_RW + 64] = strict
    m[:, M_RW + 64:M_RW + 128] = incl
    m[:, M_RW + 128:M_RW + 192] = strict
    m[:, M_RW + 192:M_RW + 256] = incl
    m[:, M_RW + 256:M_RW + 320] = (t < s)
    eye = (s == t).astype(np.float32)
    m[:, M_ID2:M_ID2 + 64] = eye
    m[:, M_ID2 + 64:M_ID2 + 128] = eye
    return m


def prep_inputs(inp, b):
    f = lambda a: np.asarray(a, np.float32)
    cst = np.zeros((128, NCONST), np.float32)

    def put(name, arr):
        cst[:, _off[name]:_off[name] + arr.shape[1]] = arr

    put("ab_g", _cv(f(inp["ab_norm_g"])[0]))
    put("c_g", _cv(f(inp["c_norm_g"])[0]))
    put("fin_g", _cv(f(inp["final_g"])))
    cw = f(inp["rg_conv_w"])[0]
    put("conv_w", np.ascontiguousarray(cw.reshape(4, 8, 128).transpose(2, 1, 0).reshape(128, 32)))
    put("conv_b", _cv(f(inp["rg_conv_b"])[0]))
    put("b_a", _cv(f(inp["rg_b_a"])[0]))
    put("b_x", _cv(f(inp["rg_b_x"])[0]))
    put("lam", _cv(f(inp["rg_lambda"])[0]))
    put("lb0", _cv(f(inp["hg_lb_logits"])[0]))
    put("lb1", _cv(f(inp["hg_lb_logits"])[1]))
    put("hg_g", f(inp["hg_norm_g"])[0].reshape(128, 1))
    mu = f(inp["c_mu"])[0]
    put("mu", np.ascontiguousarray(mu.reshape(6, 8, 128).transpose(2, 0, 1).reshape(128, 48)))
    put("w0", _cv(f(inp["c_w0"])[0]))
    put("a0", _cv(f(inp["c_a0"])[0]))
    put("k_k", _cv(f(inp["c_k_k"])[0]))
    put("k_a", _cv(f(inp["c_k_a"])[0]))
    put("r_k", _cv(f(inp["c_r_k"])[0].reshape(-1)))
    put("lnx_g", _cv(f(inp["c_lnx_g"])[0]))
    put("lnx_b", _cv(f(inp["c_lnx_b"])[0]))
    win = f(inp["ab_w_in"])[0]
    wout = f(inp["ab_w_out"])[0]
    m = {
        "x": np.ascontiguousarray(f(inp["x"])[b]),
        "consts": cst,
        "masks": make_masks(),
        "wAin": _fm(win[:, 0:2048]),
        "rgwa": np.ascontiguousarray(f(inp["rg_w_a"])[0].transpose(1, 0, 2)),
        "rgwx": np.ascontiguousarray(f(inp["rg_w_x"])[0].transpose(1, 0, 2)),
        "wAout": _fm(wout[0:1024]),
        "wBin": _fm(win[:, 2048:6144]),
        "wBout": _fm(wout[1024:2048]),
        "w1": _fm(f(inp["c_w1"])[0]),
        "a1": _fm(f(inp["c_a1"])[0]),
    }
    for h in range(2):
        sl = slice(h * 1024, (h + 1) * 1024)
        m["wr%d" % h] = _fm(f(inp["c_w_r"])[0][:, sl])
        m["wk%d" % h] = _fm(f(inp["c_w_k"])[0][:, sl])
        m["wv%d" % h] = _fm(f(inp["c_w_v"])[0][:, sl])
        m["wg%d" % h] = _fm(f(inp["c_w_g"])[0][:, sl])
        m["wo%d" % h] = _fm(f(inp["c_w_o"])[0][sl, :])
        m["w2_%d" % h] = np.ascontiguousarray(f(inp["c_w2"])[0][:, sl])
        m["a2_%d" % h] = np.ascontiguousarray(f(inp["c_a2"])[0][:, sl])
    return m


def kernel(**inputs):
    nc = build()
    in_maps = [prep_inputs(inputs, i % 4) for i in range(8)]
    res = run_bass_kernel_spmd(nc, in_maps, core_ids=list(range(8)))
    out = np.stack([np.asarray(res.results[i]["out"], np.float32) for i in range(4)], axis=0)
    return out
```

```python
import contextlib
import numpy as np
import concourse.bass as bass
import concourse.mybir as mybir
from concourse.bass_utils import run_bass_kernel_spmd
from concourse.alu_op_type import AluOpType as ALU

F32 = mybir.dt.float32
BF16 = mybir.dt.bfloat16
AF = mybir.ActivationFunctionType

S = 4096
D = 1024
T = 256
NT = S // T
RMS_EPS = 1e-6
GN_EPS = 64e-5
DEC = 0.6065306597126334
SAME_SYNC = True
import os
BSTOP = float(os.environ.get('BSTOP', '99'))
RSTOP = float(os.environ.get('RSTOP', '99'))

_off = {}
_n = 0
for _name, _w in [("ab_g", 8), ("c_g", 8), ("fin_g", 8), ("conv_w", 32), ("conv_b", 8), ("b_a", 8), ("b_x", 8),
                  ("lam", 8), ("lb0", 8), ("lb1", 8), ("hg_g", 1), ("mu", 48), ("w0", 16), ("a0", 16),
                  ("k_k", 16), ("k_a", 16), ("r_k", 16), ("lnx_g", 16), ("lnx_b", 16)]:
    _off[_name] = _n
    _n += _w
NCONST = _n
M_ID = 0
M_ONES = 128
M_OBLK = 256
M_HG = 384
M_RW = 512
M_ID2 = 832
NMASK = 960


class Buf:
    __slots__ = ("t", "name", "w", "r", "busy")

    def __init__(self, t, name):
        self.t = t
        self.name = name
        self.w = None
        self.r = {}
        self.busy = False

    def __getitem__(self, idx):
        return self.t[idx]


class Stream:
    def __init__(self, sem, key):
        self.sem = sem
        self.key = key
        self.count = 0


class KB:
    def __init__(self, nc, es):
        self.nc = nc
        self.es = es
        self.eng = {"pe": nc.tensor, "act": nc.scalar, "dve": nc.vector, "pool": nc.gpsimd, "sp": nc.sync}
        self.st = {k: Stream(es.enter_context(nc.semaphore(k + "_s")), k) for k in self.eng}
        self.waited = {k: {} for k in self.eng}
        self.dstreams = []
        self.banks = []
        self.bank_i = 0
        self.nins = 0

    def dma_stream(self, name):
        s = Stream(self.es.enter_context(self.nc.semaphore(name)), name)
        self.dstreams.append(s)
        return s

    def sbuf(self, name, shape, dtype, es=None):
        es = es or self.es
        self.nins += 0
        self.uid = getattr(self, "uid", 0) + 1
        name = "%s_u%d" % (name, self.uid)
        return Buf(es.enter_context(self.nc.sbuf_tensor(name, list(shape), dtype)), name)

    def init_psum(self):
        for i in range(8):
            self.banks.append(Buf(self.es.enter_context(self.nc.psum_tensor("bank%d" % i, [128, 512], F32)), "bank%d" % i))

    def psum(self):
        b = self.banks[2 + self.bank_i % 6]
        self.bank_i += 1
        assert not b.busy, "psum bank still in use: " + b.name
        b.busy = True
        return b

    def _wait(self, e, sv):
        s, v = sv
        w = self.waited[e]
        if w.get(s.key, 0) >= v:
            return
        w[s.key] = v
        self.eng[e].wait_ge(s.sem, v)

    def _deps(self, e, reads, writes):
        for b in reads:
            if b.name.startswith("bank"):
                for sv in b.r.values():
                    if sv[0].key != e:
                        self._wait(e, sv)
            if b.w is not None:
                if b.w[0].key == e:
                    if (SAME_SYNC is True and e != "pe") or (SAME_SYNC == "pool" and e == "pool"):
                        self._wait(e, b.w)
                else:
                    self._wait(e, b.w)
        for b in writes:
            if b.w is not None and b.w[0].key != e:
                self._wait(e, b.w)
            for sv in b.r.values():
                if sv[0].key != e:
                    self._wait(e, sv)

    def op(self, e, fn, reads=(), writes=()):
        self._deps(e, reads, writes)
        ins = fn(self.eng[e])
        s = self.st[e]
        s.count += 1
        ins.then_inc(s.sem, 1)
        self.nins += 1
        for b in reads:
            b.r[s.key] = (s, s.count)
        for b in writes:
            b.w = (s, s.count)
            b.r = {}

    def dma(self, q, stream, out_ap, in_ap, reads=(), writes=()):
        self._deps(q, reads, writes)
        ins = self.eng[q].dma_start(out=out_ap, in_=in_ap)
        stream.count += 16
        ins.then_inc(stream.sem, 16)
        self.nins += 1
        for b in reads:
            b.r[stream.key] = (stream, stream.count)
        for b in writes:
            b.w = (stream, stream.count)
            b.r = {}

    def seal(self, stream, bufs):
        for b in bufs:
            b.w = (stream, stream.count)

    def barrier(self):
        allst = list(self.st.values()) + self.dstreams
        for e in self.eng:
            for s in allst:
                if s.key != e and s.count > 0:
                    self._wait(e, (s, s.count))

    def mm(self, bank, out_ap, lhsT, rhs, start, stop, reads):
        self.op("pe", lambda pe: pe.matmul(out_ap, lhsT=lhsT, rhs=rhs, start=start, stop=stop), reads=reads, writes=[bank])

    def act(self, out_ap, in_ap, func, reads, writes, bias=None, scale=None):
        kw = {}
        if bias is not None:
            kw["bias"] = bias
        if scale is not None:
            kw["scale"] = scale
        self.op("act", lambda a: a.activation(out=out_ap, in_=in_ap, func=func, **kw), reads=reads, writes=writes)

    def tt(self, e, out_ap, in0, in1, op, reads, writes):
        self.op(e, lambda v: v.tensor_tensor(out=out_ap, in0=in0, in1=in1, op=op), reads=reads, writes=writes)

    def ts(self, e, out_ap, in0, s1, s2, op0, op1, reads, writes):
        if op1 is None:
            self.op(e, lambda v: v.tensor_scalar(out=out_ap, in0=in0, scalar1=s1, scalar2=None, op0=op0), reads=reads, writes=writes)
        else:
            self.op(e, lambda v: v.tensor_scalar(out=out_ap, in0=in0, scalar1=s1, scalar2=s2, op0=op0, op1=op1), reads=reads, writes=writes)

    def stt(self, out_ap, in0, scalar, in1, op0, op1, reads, writes):
        self.op("dve", lambda v: v.scalar_tensor_tensor(out=out_ap, in0=in0, scalar=scalar, in1=in1, op0=op0, op1=op1), reads=reads, writes=writes)

    def copy(self, e, out_ap, in_ap, reads, writes):
        if e == "act":
            self.op("act", lambda a: a.copy(out=out_ap, in_=in_ap), reads=reads, writes=writes)
        else:
            self.op(e, lambda v: v.tensor_copy(out=out_ap, in_=in_ap), reads=reads, writes=writes)


def build(nt=NT, passes="ABCD", debug=False):
    nc = bass.Bass("TRN2", target_bir_lowering=False)

    def din(name, shape):
        return nc.dram_tensor(name, list(shape), F32, kind="ExternalInput").ap()

    x_d = din("x", [S, D])
    consts_d = din("consts", [128, NCONST])
    masks_d = din("masks", [128, NMASK])
    wAin_d = din("wAin", [128, 8, 2048])
    rgwa_d = din("rgwa", [128, 8, 128])
    rgwx_d = din("rgwx", [128, 8, 128])
    wAout_d = din("wAout", [128, 8, 1024])
    wBin_d = din("wBin", [128, 8, 4096])
    wBout_d = din("wBout", [128, 8, 1024])
    wr_d = [din("wr%d" % h, [128, 8, 1024]) for h in range(2)]
    wk_d = [din("wk%d" % h, [128, 8, 1024]) for h in range(2)]
    wv_d = [din("wv%d" % h, [128, 8, 1024]) for h in range(2)]
    wg_d = [din("wg%d" % h, [128, 8, 1024]) for h in range(2)]
    wo_d = [din("wo%d" % h, [128, 8, 1024]) for h in range(2)]
    w1_d = din("w1", [128, 8, 64])
    a1_d = din("a1", [128, 8, 64])
    w2_d = [din("w2_%d" % h, [64, 1024]) for h in range(2)]
    a2_d = [din("a2_%d" % h, [64, 1024]) for h in range(2)]
    out_d = nc.dram_tensor("out", [S, D], F32, kind="ExternalOutput").ap()
    skind = "ExternalOutput" if debug else "Internal"
    h0_d = nc.dram_tensor("h0fm", [128, 8, S], F32, kind=skind).ap()
    hA_d = nc.dram_tensor("hAfm", [128, 8, S], F32, kind=skind).ap()
    h1_d = nc.dram_tensor("h1fm", [128, 8, S], F32, kind=skind).ap()
    hC_d = nc.dram_tensor("hCfm", [128, 8, S], F32, kind=skind).ap()

    es = contextlib.ExitStack()
    with es:
        k = KB(nc, es)
        k.init_psum()
        cst = k.sbuf("cst", [128, NCONST], F32)
        der = k.sbuf("der", [128, 64], F32)
        mk_f = k.sbuf("mk_f", [128, 128], F32)
        mk_b = k.sbuf("mk_b", [128, NMASK], BF16)
        mk_rw = k.sbuf("mk_rw", [128, 320], F32)
        mk_hg = k.sbuf("mk_hg", [128, 128], F32)
        zeros = k.sbuf("zeros", [128, 64], F32)
        onesf = k.sbuf("onesf", [128, 64], F32)
        cs = k.dma_stream("cstream")
        k.dma("sp", cs, cst[:, :], consts_d[:, :], writes=[cst])
        k.dma("sp", cs, mk_f[:, :], masks_d[:, M_ID:M_ID + 128], writes=[mk_f])
        k.dma("sp", cs, mk_rw[:, :], masks_d[:, M_RW:M_RW + 320], writes=[mk_rw])
        k.dma("sp", cs, mk_hg[:, :], masks_d[:, M_HG:M_HG + 128], writes=[mk_hg])
        cs2 = k.dma_stream("cstream2")
        k.dma("pool", cs2, mk_b[:, :], masks_d[:, :], writes=[mk_b])
        k.seal(cs, [cst, mk_f, mk_rw, mk_hg])
        k.op("pool", lambda g: g.memset(zeros[:, :], 0.0), writes=[zeros])
        k.op("pool", lambda g: g.memset(onesf[:, :], 1.0), writes=[onesf])
        ident_b = mk_b[:, M_ID:M_ID + 128]
        ones_b = mk_b[:, M_ONES:M_ONES + 128]
        oblk_b = mk_b[:, M_OBLK:M_OBLK + 128]

        def C(name, j=0, w=1):
            o = _off[name] + j
            return cst[:, o:o + w]

        k.act(der[:, 0:8], C("lam", 0, 8), AF.Exp, [cst], [der], scale=-1.0)
        k.act(der[:, 0:8], der[:, 0:8], AF.Ln, [der, onesf], [der], bias=onesf[:, 0:1])
        k.ts("dve", der[:, 8:16], der[:, 0:8], -16.0, None, ALU.mult, None, [der], [der])
        k.ts("dve", der[:, 0:8], der[:, 0:8], -8.0, None, ALU.mult, None, [der], [der])
        k.tt("dve", der[:, 16:24], C("lb0", 0, 8), C("lb1", 0, 8), ALU.subtract, [cst], [der])
        k.act(der[:, 16:24], der[:, 16:24], AF.Sigmoid, [der], [der])
        k.ts("dve", der[:, 24:32], der[:, 16:24], -1.0, 1.0, ALU.mult, ALU.add, [der], [der])
        k.ts("dve", der[:, 32:48], C("k_a", 0, 16), -1.0, 1.0, ALU.mult, ALU.add, [cst], [der])
        k.op("pool", lambda g: g.memset(der[:, 48:49], RMS_EPS), writes=[der])
        k.op("pool", lambda g: g.memset(der[:, 49:50], GN_EPS), writes=[der])
        eps_rms = der[:, 48:49]
        eps_gn = der[:, 49:50]

        ws = k.dma_stream("wstream")
        ldS = [k.dma_stream("ldU0"), k.dma_stream("ldU1")]
        ldR = [k.dma_stream("ldR0"), k.dma_stream("ldR1")]
        stS = [k.dma_stream("st0"), k.dma_stream("st1")]
        stS2 = [k.dma_stream("st2_0"), k.dma_stream("st2_1")]
        dr = {nm: [Buf(None, "%s_%d" % (nm, i)) for i in range(NT)] for nm in ("h0", "hA", "h1", "hC")}

        def loadw(buf, dram, nk=8, ncol=None, dcol0=0, dk0=0):
            ncol = ncol or dram.shape[2]
            for kc in range(nk):
                for c0 in range(0, ncol, 1024):
                    c1 = min(ncol, c0 + 1024)
                    k.dma("pool", ws, buf[:, kc, c0:c1], dram[:, dk0 + kc, dcol0 + c0:dcol0 + c1], writes=[buf])

        def emit_norm(pes_bufs, hU, gname, outbuf, col0, fp32_out):
            sq, sd = pes_bufs
            k.act(sq[:, :, :], hU[:, :, :], AF.Square, [hU], [sq])
            bank = k.psum()
            for c in range(8):
                k.mm(bank, bank[:, 0:T], ones_b, sq[:, c, :], c == 0, c == 7, [sq, mk_b])
            k.act(sd[:, :], bank[:, 0:T], AF.Sqrt, [bank, der], [sd], bias=eps_rms, scale=1.0 / D)
            bank.busy = False
            k.op("dve", lambda v: v.reciprocal(out=sd[:, :], in_=sd[:, :]), reads=[sd], writes=[sd])
            for c in range(8):
                k.stt(outbuf[:, c, col0:col0 + T], hU[:, c, :], C(gname, c), sd[:, :], ALU.mult, ALU.mult,
                      [hU, cst, sd], [outbuf])

        def out_proj(Wout, y, hR):
            for dc in range(8):
                bank = k.psum()
                for m in range(8):
                    k.mm(bank, bank[:, 0:T], Wout[:, m, dc * 128:(dc + 1) * 128], y[:, m, :], m == 0, m == 7, [Wout, y])
                k.tt("dve", hR[:, dc, :], hR[:, dc, :], bank[:, 0:T], ALU.add, [hR, bank], [hR])
                bank.busy = False

        def rr(gens):
            gens = list(gens)
            while gens:
                for g_ in list(gens):
                    try:
                        next(g_)
                    except StopIteration:
                        gens.remove(g_)
                yield

        def pass_A():
            with contextlib.ExitStack() as pes:
                Win = k.sbuf("A_Win", [128, 8, 2048], BF16, pes)
                Wa = k.sbuf("A_Wa", [128, 8, 128], BF16, pes)
                Wx = k.sbuf("A_Wx", [128, 8, 128], BF16, pes)
                Wout = k.sbuf("A_Wout", [128, 8, 1024], BF16, pes)
                loadw(Win, wAin_d)
                k.dma("pool", ws, Wa[:, :, :], rgwa_d[:, :, :], writes=[Wa])
                k.dma("pool", ws, Wx[:, :, :], rgwx_d[:, :, :], writes=[Wx])
                loadw(Wout, wAout_d)
                k.seal(ws, [Win, Wa, Wx, Wout])
                xts = [k.sbuf("A_xt%d" % i, [128, 2, 1024], F32, pes) for i in range(2)]
                hUs = [k.sbuf("A_hU%d" % i, [128, 8, T], F32, pes) for i in range(2)]
                sq = k.sbuf("A_sq", [128, 8, T], BF16, pes)
                sd = k.sbuf("A_sd", [128, T], F32, pes)
                u = k.sbuf("A_u", [128, 8, T], BF16, pes)
                y = k.sbuf("A_y", [128, 8, T], BF16, pes)
                xaext = [k.sbuf("A_xa%d" % c, [128, T + 3], F32, pes) for c in range(8)]
                carry = [k.sbuf("A_cy%d" % c, [128, 1], F32, pes) for c in range(8)]
                xc = [k.sbuf("A_xc%d" % i, [128, T], F32, pes) for i in range(4)]
                xcb = [k.sbuf("A_xcb%d" % i, [128, T], BF16, pes) for i in range(4)]
                sr = [k.sbuf("A_sr%d" % i, [128, T], F32, pes) for i in range(4)]
                si = [k.sbuf("A_si%d" % i, [128, T], F32, pes) for i in range(4)]
                av = [k.sbuf("A_av%d" % i, [128, T], F32, pes) for i in range(4)]
                mv = [k.sbuf("A_mv%d" % i, [128, T], F32, pes) for i in range(4)]
                uu = [k.sbuf("A_uu%d" % i, [128, T], F32, pes) for i in range(4)]
                hh = [k.sbuf("A_hh%d" % i, [128, T], F32, pes) for i in range(4)]
                sg = [k.sbuf("A_sg%d" % i, [128, T], F32, pes) for i in range(4)]
                for c in range(8):
                    k.op("pool", lambda g, c=c: g.memset(xaext[c][:, :], 0.0), writes=[xaext[c]])
                    k.op("pool", lambda g, c=c: g.memset(carry[c][:, :], 0.0), writes=[carry[c]])

                for ti in range(nt):
                    t0 = ti * T
                    xt = xts[ti % 2]
                    hU = hUs[ti % 2]
                    k.dma("sp", ldS[ti % 2], xt[:, :, :], x_d[t0:t0 + T, :].rearrange("(g p) d -> p g d", p=128), writes=[xt])
                    for cp in range(4):
                        bank = k.psum()
                        for cc in range(2):
                            c = cp * 2 + cc
                            for tg in range(2):
                                o = cc * 256 + tg * 128
                                k.op("pe", lambda pe, o=o, c=c, tg=tg, bank=bank: pe.transpose(
                                    out=bank[:, o:o + 128], in_=xt[:, tg, c * 128:(c + 1) * 128], identity=mk_f[:, :]),
                                    reads=[xt, mk_f], writes=[bank])
                        for cc in range(2):
                            c = cp * 2 + cc
                            k.copy("act" if cc == 0 else "dve", hU[:, c, :], bank[:, cc * 256:cc * 256 + 256], [bank], [hU])
                        bank.busy = False
                    k.dma("sp", stS2[ti % 2], h0_d[:, :, t0:t0 + T], hU[:, :, :], reads=[hU], writes=[dr["h0"][ti]])
                    emit_norm((sq, sd), hU, "ab_g", u, 0, False)
                    def blockA(c):
                        i2 = c % 4
                        b1 = k.psum()
                        for kc in range(8):
                            k.mm(b1, b1[:, 0:T], Win[:, kc, c * 128:(c + 1) * 128], u[:, kc, :], kc == 0, kc == 7, [Win, u])
                        for kc in range(8):
                            k.mm(b1, b1[:, T:2 * T], Win[:, kc, 1024 + c * 128:1024 + (c + 1) * 128], u[:, kc, :], kc == 0, kc == 7, [Win, u])
                        xe = xaext[c]
                        k.copy("pool", xe[:, 0:3], xe[:, T:T + 3], [xe], [xe])
                        k.copy("act", xe[:, 3:T + 3], b1[:, 0:T], [b1], [xe])
                        k.act(sg[i2][:, :], b1[:, T:2 * T], AF.Silu, [b1], [sg[i2]])
                        b1.busy = False
                        yield
                        cw = _off["conv_w"] + c * 4
                        k.ts("dve", xc[i2][:, :], xe[:, 3:T + 3], cst[:, cw + 3:cw + 4], C("conv_b", c), ALU.mult, ALU.add, [xe, cst], [xc[i2]])
                        for j in (2, 1, 0):
                            k.stt(xc[i2][:, :], xe[:, j:j + T], cst[:, cw + j:cw + j + 1], xc[i2][:, :], ALU.mult, ALU.add, [xe, cst, xc[i2]], [xc[i2]])
                        k.copy("pool", xcb[i2][:, :], xc[i2][:, :], [xc[i2]], [xcb[i2]])
                        yield
                        b2 = k.psum()
                        k.mm(b2, b2[:, 0:T], Wa[:, c, :], xcb[i2][:, :], True, True, [Wa, xcb[i2]])
                        k.mm(b2, b2[:, T:2 * T], Wx[:, c, :], xcb[i2][:, :], True, True, [Wx, xcb[i2]])
                        k.act(sr[i2][:, :], b2[:, 0:T], AF.Sigmoid, [b2, cst], [sr[i2]], bias=C("b_a", c))
                        k.act(si[i2][:, :], b2[:, T:2 * T], AF.Sigmoid, [b2, cst], [si[i2]], bias=C("b_x", c))
                        b2.busy = False
                        yield
                        k.act(av[i2][:, :], sr[i2][:, :], AF.Exp, [sr[i2], der], [av[i2]], scale=der[:, c:c + 1])
                        k.act(mv[i2][:, :], sr[i2][:, :], AF.Exp, [sr[i2], der], [mv[i2]], scale=der[:, 8 + c:9 + c])
                        k.tt("pool", uu[i2][:, :], si[i2][:, :], xc[i2][:, :], ALU.mult, [si[i2], xc[i2]], [uu[i2]])
                        yield
                        k.act(mv[i2][:, :], mv[i2][:, :], AF.Sqrt, [mv[i2], onesf], [mv[i2]], bias=onesf[:, 0:1], scale=-1.0)
                        yield
                        k.tt("dve", uu[i2][:, :], uu[i2][:, :], mv[i2][:, :], ALU.mult, [uu[i2], mv[i2]], [uu[i2]])
                        k.op("dve", lambda v, i2=i2, c=c: v.tensor_tensor_scan(
                            out=hh[i2][:, :], data0=av[i2][:, :], data1=uu[i2][:, :], initial=carry[c][:, 0:1],
                            op0=ALU.mult, op1=ALU.add), reads=[av[i2], uu[i2], carry[c]], writes=[hh[i2]])
                        k.copy("pool", carry[c][:, 0:1], hh[i2][:, T - 1:T], [hh[i2]], [carry[c]])
                        k.tt("dve", y[:, c, :], hh[i2][:, :], sg[i2][:, :], ALU.mult, [hh[i2], sg[i2]], [y])
                        yield

                    for c4 in range(2):
                        for _ in rr([blockA(c4 * 4 + i) for i in range(4)]):
                            pass
                    out_proj(Wout, y, hU)
                    k.dma("sp", stS[ti % 2], hA_d[:, :, t0:t0 + T], hU[:, :, :], reads=[hU], writes=[dr["hA"][ti]])
                k.barrier()

        def pass_B():
            with contextlib.ExitStack() as pes:
                Win = k.sbuf("B_Win", [128, 8, 4096], BF16, pes)
                Wout = k.sbuf("B_Wout", [128, 8, 1024], BF16, pes)
                loadw(Win, wBin_d)
                loadw(Wout, wBout_d)
                k.seal(ws, [Win, Wout])
                hUs = [k.sbuf("B_hU%d" % i, [128, 8, T], F32, pes) for i in range(2)]
                hRs = [k.sbuf("B_hR%d" % i, [128, 8, T], F32, pes) for i in range(2)]
                sq = k.sbuf("B_sq", [128, 8, T], BF16, pes)
                sd = k.sbuf("B_sd", [128, T], F32, pes)
                u = k.sbuf("B_u", [128, 8, T], BF16, pes)
                y = k.sbuf("B_y", [128, 8, T], BF16, pes)
                vtok = k.sbuf("B_vtok", [128, 2, 1024], BF16, pes)
                stf = [k.sbuf("B_stf%d" % h, [128, 128], F32, pes) for h in range(8)]
                stb = [k.sbuf("B_stb%d" % h, [128, 128], BF16, pes) for h in range(8)]
                NB = 4
                qf = [k.sbuf("B_qf%d" % i, [128, T], F32, pes) for i in range(NB)]
                sig = [k.sbuf("B_sig%d" % i, [128, T], F32, pes) for i in range(NB)]
                ff = [k.sbuf("B_f%d" % i, [128, T], F32, pes) for i in range(NB)]
                kf = [k.sbuf("B_k%d" % i, [128, T], F32, pes) for i in range(NB)]
                Pc = [k.sbuf("B_P%d" % i, [128, T], F32, pes) for i in range(NB)]
                Pi = [k.sbuf("B_Pi%d" % i, [128, T], F32, pes) for i in range(NB)]
                qd = [k.sbuf("B_qd%d" % i, [128, T], BF16, pes) for i in range(NB)]
                kif = [k.sbuf("B_kif%d" % i, [128, T], F32, pes) for i in range(NB)]
                kib = [k.sbuf("B_kib%d" % i, [128, T], BF16, pes) for i in range(NB)]
                keb = [k.sbuf("B_keb%d" % i, [128, T], BF16, pes) for i in range(NB)]
                scm = [k.sbuf("B_scm%d" % i, [128, 128], BF16, pes) for i in range(NB)]
                ket = [k.sbuf("B_ket%d" % i, [128, 128], BF16, pes) for i in range(NB)]
                osq = [k.sbuf("B_osq%d" % i, [128, T], BF16, pes) for i in range(NB)]
                ors = [k.sbuf("B_ors%d" % i, [128, T], F32, pes) for i in range(NB)]
                o1 = [k.sbuf("B_o1%d" % i, [128, T], F32, pes) for i in range(NB)]
                sgb = [k.sbuf("B_sg%d" % i, [128, T], F32, pes) for i in range(NB)]
                for h in range(8):
                    k.op("pool", lambda g, h=h: g.memset(stf[h][:, :], 0.0), writes=[stf[h]])
                    k.op("pool", lambda g, h=h: g.memset(stb[h][:, :], 0.0), writes=[stb[h]])
                accb = [k.banks[0], k.banks[1]]
                for ti in range(nt):
                    t0 = ti * T
                    hU = hUs[ti % 2]
                    hR = hRs[ti % 2]
                    k.dma("sp", ldS[ti % 2], hU[:, :, :], h0_d[:, :, t0:t0 + T], reads=[dr["h0"][ti]], writes=[hU])
                    k.dma("sp", ldR[ti % 2], hR[:, :, :], hA_d[:, :, t0:t0 + T], reads=[dr["hA"][ti]], writes=[hR])
                    emit_norm((sq, sd), hU, "ab_g", u, 0, False)
                    for tg in range(2):
                        for cg in range(2):
                            bank = k.psum()
                            for kc in range(8):
                                k.mm(bank, bank[:, 0:512], u[:, kc, tg * 128:(tg + 1) * 128],
                                     Win[:, kc, 2048 + cg * 512:2048 + (cg + 1) * 512], kc == 0, kc == 7, [u, Win])
                            k.copy("act" if cg == 0 else "dve", vtok[:, tg, cg * 512:(cg + 1) * 512], bank[:, 0:512], [bank], [vtok])
                            bank.busy = False
                    def headB(h):
                        i2 = h % NB
                        bo = accb[(h % 4) // 2]
                        oc = (h % 2) * 256
                        bq = k.psum()
                        for kc in range(8):
                            k.mm(bq, bq[:, 0:T], Win[:, kc, h * 128:(h + 1) * 128], u[:, kc, :], kc == 0, kc == 7, [Win, u])
                        for kc in range(8):
                            k.mm(bq, bq[:, T:2 * T], Win[:, kc, 1024 + h * 128:1024 + (h + 1) * 128], u[:, kc, :], kc == 0, kc == 7, [Win, u])
                        k.act(sig[i2][:, :], bq[:, T:2 * T], AF.Sigmoid, [bq], [sig[i2]])
                        k.copy("act", qf[i2][:, :], bq[:, 0:T], [bq], [qf[i2]])
                        bq.busy = False
                        yield
                        k.ts("dve", ff[i2][:, :], sig[i2][:, :], der[:, 24 + h:25 + h], der[:, 16 + h:17 + h], ALU.mult, ALU.add, [sig[i2], der], [ff[i2]])
                        k.ts("pool", kf[i2][:, :], ff[i2][:, :], -1.0, 1.0, ALU.mult, ALU.add, [ff[i2]], [kf[i2]])
                        for j in range(T // 64):
                            k.op("dve", lambda v, i2=i2, j=j: v.tensor_tensor_scan(
                                out=Pc[i2][:, j * 64:(j + 1) * 64], data0=ff[i2][:, j * 64:(j + 1) * 64], data1=zeros[:, 0:64],
                                initial=1.0, op0=ALU.mult, op1=ALU.add), reads=[ff[i2], zeros], writes=[Pc[i2]])
                        yield
                        k.op("dve", lambda v, i2=i2: v.reciprocal(out=Pi[i2][:, :], in_=Pc[i2][:, :]), reads=[Pc[i2]], writes=[Pi[i2]])
                        k.tt("pool", qd[i2][:, :], qf[i2][:, :], Pc[i2][:, :], ALU.mult, [qf[i2], Pc[i2]], [qd[i2]])
                        yield
                        k.tt("pool", kif[i2][:, :], kf[i2][:, :], Pi[i2][:, :], ALU.mult, [kf[i2], Pi[i2]], [kif[i2]])
                        k.copy("pool", kib[i2][:, :], kif[i2][:, :], [kif[i2]], [kib[i2]])
                        for j in range(T // 64):
                            k.ts("dve", keb[i2][:, j * 64:(j + 1) * 64], kif[i2][:, j * 64:(j + 1) * 64],
                                 Pc[i2][:, j * 64 + 63:j * 64 + 64], None, ALU.mult, None, [kif[i2], Pc[i2]], [keb[i2]])
                        yield
                        for tg in range(2):
                            c0 = tg * 128
                            bs = k.psum()
                            k.mm(bs, bs[:, 0:128], kib[i2][:, c0:c0 + 128], qd[i2][:, c0:c0 + 128], True, True, [kib[i2], qd[i2]])
                            k.mm(bs, bs[:, 128:256], keb[i2][:, c0:c0 + 128], ident_b, True, True, [keb[i2], mk_b])
                            k.tt("dve", scm[i2][:, :], bs[:, 0:128], mk_hg[:, :], ALU.mult, [bs, mk_hg], [scm[i2]])
                            k.copy("act", ket[i2][:, :], bs[:, 128:256], [bs], [ket[i2]])
                            bs.busy = False
                            yield
                            for jp in range(2):
                                cj = c0 + jp * 64
                                r0 = jp * 64
                                k.mm(bo, bo[:, oc + cj:oc + cj + 64], vtok[:, tg, h * 128:(h + 1) * 128], scm[i2][:, r0:r0 + 64], True, False, [vtok, scm[i2]])
                                k.mm(bo, bo[:, oc + cj:oc + cj + 64], stb[h][:, :], qd[i2][:, cj:cj + 64], False, True, [stb[h], qd[i2]])
                                bst = k.psum()
                                k.mm(bst, bst[:, 0:128], ket[i2][r0:r0 + 64, :], vtok[r0:r0 + 64, tg, h * 128:(h + 1) * 128], True, True, [ket[i2], vtok])
                                k.stt(stf[h][:, :], stf[h][:, :], Pc[i2][:, cj + 63:cj + 64], bst[:, 0:128], ALU.mult, ALU.add, [stf[h], Pc[i2], bst], [stf[h]])
                                bst.busy = False
                                k.copy("act", stb[h][:, :], stf[h][:, :], [stf[h]], [stb[h]])
                                yield
                        bg = k.psum()
                        for kc in range(8):
                            k.mm(bg, bg[:, 0:T], Win[:, kc, 3072 + h * 128:3072 + (h + 1) * 128], u[:, kc, :], kc == 0, kc == 7, [Win, u])
                        k.act(sgb[i2][:, :], bg[:, 0:T], AF.Silu, [bg], [sgb[i2]])
                        k.act(osq[i2][:, :], bo[:, oc:oc + T], AF.Square, [bo], [osq[i2]])
                        k.copy("act", o1[i2][:, :], bo[:, oc:oc + T], [bo], [o1[i2]])
                        k.mm(bg, bg[:, T:2 * T], ones_b, osq[i2][:, :], True, True, [mk_b, osq[i2]])
                        k.act(ors[i2][:, :], bg[:, T:2 * T], AF.Sqrt, [bg, der], [ors[i2]], bias=eps_rms, scale=1.0 / 128)
                        bg.busy = False
                        yield
                        k.op("dve", lambda v, i2=i2: v.reciprocal(out=ors[i2][:, :], in_=ors[i2][:, :]), reads=[ors[i2]], writes=[ors[i2]])
                        k.tt("pool", o1[i2][:, :], o1[i2][:, :], ors[i2][:, :], ALU.mult, [o1[i2], ors[i2]], [o1[i2]])
                        k.stt(y[:, h, :], o1[i2][:, :], C("hg_g"), sgb[i2][:, :], ALU.mult, ALU.mult, [o1[i2], cst, sgb[i2]], [y])
                        yield

                    for h4 in range(2):
                        for _ in rr([headB(h4 * 4 + i) for i in range(4)]):
                            pass
                    out_proj(Wout, y, hR)
                    k.dma("sp", stS[ti % 2], h1_d[:, :, t0:t0 + T], hR[:, :, :], reads=[hR], writes=[dr["h1"][ti]])
                k.barrier()


        def pass_R(q, res_d, res_tr, dst_d, dst_tr, last):
            hf, qo = q // 2, (q % 2) * 512
            with contextlib.ExitStack() as pes:
                Wr = k.sbuf("R_Wr", [128, 8, 512], BF16, pes)
                Wk = k.sbuf("R_Wk", [128, 8, 512], BF16, pes)
                Wv = k.sbuf("R_Wv", [128, 8, 512], BF16, pes)
                Wg = k.sbuf("R_Wg", [128, 8, 512], BF16, pes)
                W1 = k.sbuf("R_W1", [128, 8, 64], BF16, pes)
                A1 = k.sbuf("R_A1", [128, 8, 64], BF16, pes)
                W2 = k.sbuf("R_W2", [64, 512], BF16, pes)
                A2 = k.sbuf("R_A2", [64, 512], BF16, pes)
                Wo = k.sbuf("R_Wo", [128, 4, 1024], BF16, pes)
                loadw(Wr, wr_d[hf], ncol=512, dcol0=qo)
                loadw(Wk, wk_d[hf], ncol=512, dcol0=qo)
                loadw(Wv, wv_d[hf], ncol=512, dcol0=qo)
                loadw(Wg, wg_d[hf], ncol=512, dcol0=qo)
                k.dma("pool", ws, W1[:, :, :], w1_d[:, :, :], writes=[W1])
                k.dma("pool", ws, A1[:, :, :], a1_d[:, :, :], writes=[A1])
                k.dma("pool", ws, W2[:, :], w2_d[hf][:, qo:qo + 512], writes=[W2])
                k.dma("pool", ws, A2[:, :], a2_d[hf][:, qo:qo + 512], writes=[A2])
                loadw(Wo, wo_d[hf], nk=4, dk0=(q % 2) * 4)
                k.seal(ws, [Wr, Wk, Wv, Wg, W1, A1, W2, A2, Wo])
                hUs = [k.sbuf("R_hU%d" % i, [128, 8, T], F32, pes) for i in range(2)]
                hRs = [k.sbuf("R_hR%d" % i, [128, 8, T], F32, pes) for i in range(1)] if q > 0 else hUs
                sq = k.sbuf("R_sq", [128, 8, T], BF16, pes)
                sd = k.sbuf("R_sd", [128, T], F32, pes)
                ufp = k.sbuf("R_ufp", [128, 8, T + 1], F32, pes)
                delta = k.sbuf("R_delta", [128, 8, T], F32, pes)
                xj = [k.sbuf("R_x%d" % j, [128, 8, T], BF16, pes) for j in range(6)]
                y = k.sbuf("R_y", [128, 4, T], BF16, pes)
                vtok = k.sbuf("R_vtok", [128, 2, 512], BF16, pes)
                tw = k.sbuf("R_tw", [64, T], BF16, pes)
                al = k.sbuf("R_al", [64, T], BF16, pes)
                Tf = [k.sbuf("R_Tf%d" % p, [128, 64], F32, pes) for p in range(4)]
                Tbk = [k.sbuf("R_Tbk%d" % p, [128, 128], BF16, pes) for p in range(4)]
                ot = k.sbuf("R_ot", [128, 2, 1024], F32, pes) if last else None
                f32n = ["sw", "av", "rf", "gate", "kkp", "nk", "kf", "bb", "cum", "cm", "g", "gi", "vfm", "bonus"]
                b16n = ["kk2", "rk2", "ktb", "btb", "kend", "bend", "yb16", "ysq"]
                zl = list(Tbk)
                PB = []
                for si in range(2):
                    d_ = {}
                    d_["X"] = {n_: k.sbuf("R_" + n_, [128, T], F32, pes) for n_ in f32n}
                    d_["X"]["yf"] = d_["X"]["sw"]
                    d_["X"]["mean"] = d_["X"]["cum"]
                    d_["X"]["var"] = d_["X"]["gi"]
                    d_["t16"] = {n_: k.sbuf("R_" + n_, [128, T], BF16, pes) for n_ in b16n}
                    d_["krt"] = k.sbuf("R_krt", [128, 2, T], BF16, pes)
                    d_["tokA"] = k.sbuf("R_tokA", [128, 256], BF16, pes)
                    d_["tokB"] = k.sbuf("R_tokB", [128, 256], BF16, pes)
                    d_["KEZ"] = [k.sbuf("R_kez", [128, 256], BF16, pes) for j in range(2)]
                    zl += d_["KEZ"]
                    d_["ch"] = []
                    for tg in range(2):
                        c_ = {}
                        c_["Gz"] = [k.sbuf("R_Gz", [128, 2, 320], BF16, pes) for j in range(2)]
                        c_["Qz"] = [k.sbuf("R_Qz", [128, 128], BF16, pes) for j in range(2)]
                        c_["nUz"] = [k.sbuf("R_nUz", [128, 128], BF16, pes) for j in range(2)]
                        c_["WTd"] = [k.sbuf("R_WTd", [128, 64], BF16, pes) for j in range(2)]
                        c_["Qs"] = [k.sbuf("R_Q", [128, 128], BF16, pes) for j in range(2)]
                        c_["PPs"] = [k.sbuf("R_PP", [128, 256], BF16, pes) for j in range(2)]
                        c_["IP"] = [k.sbuf("R_IP", [128, 2, 64], BF16, pes) for j in range(2)]
                        c_["AKV"] = k.sbuf("R_akv", [128, 128], BF16, pes)
                        c_["UP"] = k.sbuf("R_up", [128, 128], F32, pes)
                        zl += c_["Gz"] + c_["Qz"] + c_["nUz"]
                        d_["ch"].append(c_)
                    PB.append(d_)
                for bl_ in zl:
                    k.op("pool", lambda g_, bl_=bl_: g_.memset(bl_[:, :, :] if len(bl_.t.shape) == 3 else bl_[:, :], 0.0), writes=[bl_])
                id2 = mk_b[:, M_ID2:M_ID2 + 128]
                k.op("pool", lambda g_: g_.memset(ufp[:, :, :], 0.0), writes=[ufp])
                for p in range(4):
                    k.op("pool", lambda g_, p=p: g_.memset(Tf[p][:, :], 0.0), writes=[Tf[p]])
                accb = [k.banks[0], k.banks[1]]
                xr, xw, xk, xv, xa, xg = xj

                def chain_gen(p, tg, S_):
                    c_ = S_["ch"][tg]
                    t16, krt, tokA = S_["t16"], S_["krt"], S_["tokA"]
                    Gz, Qz, WTd, Qs, PPs, IPs, AKV, UP = c_["Gz"], c_["Qz"], c_["WTd"], c_["Qs"], c_["PPs"], c_["IP"], c_["AKV"], c_["UP"]
                    for hp in range(2):
                        rs_ = slice(64 * hp, 64 * hp + 64)
                        bG = k.psum()
                        for jp in range(2):
                            cs_ = slice(tg * 128 + jp * 64, tg * 128 + jp * 64 + 64)
                            ro = slice(64 * jp, 64 * jp + 64)
                            k.mm(bG, bG[ro, 0:64], t16["ktb"][rs_, cs_], krt[rs_, 0, cs_], True, True, [t16["ktb"], krt])
                            k.mm(bG, bG[ro, 64:128], t16["ktb"][rs_, cs_], krt[rs_, 1, cs_], True, True, [t16["ktb"], krt])
                            k.mm(bG, bG[ro, 128:192], t16["btb"][rs_, cs_], krt[rs_, 0, cs_], True, True, [t16["btb"], krt])
                            k.mm(bG, bG[ro, 192:256], t16["btb"][rs_, cs_], krt[rs_, 1, cs_], True, True, [t16["btb"], krt])
                            k.mm(bG, bG[ro, 256:320], krt[rs_, 0, cs_], t16["btb"][rs_, cs_], True, True, [t16["btb"], krt])
                        for jp in range(2):
                            ro = slice(64 * jp, 64 * jp + 64)
                            k.tt("dve", Gz[jp][ro, hp, :], bG[ro, 0:320], mk_rw[ro, :], ALU.mult, [bG, mk_rw], [Gz[jp]])
                        bG.busy = False
                        yield
                    Qc = Qs[0]
                    for jp in range(2):
                        ro = slice(64 * jp, 64 * jp + 64)
                        for hp in range(2):
                            k.tt("pool", Qc[ro, 64 * hp:64 * hp + 64], id2[ro, 0:64], Gz[jp][ro, hp, 128:192], ALU.subtract, [mk_b, Gz[jp]], [Qc])
                    qi = 0

                    def emit_q(lvl_, qi_):
                        IPc = IPs[lvl_ % 2]
                        bQ = k.psum()
                        for hp in range(2):
                            for jp in range(2):
                                ro = slice(64 * jp, 64 * jp + 64)
                                k.mm(bQ, bQ[ro, hp * 64:hp * 64 + 64], IPc[ro, hp, :], Qs[qi_][ro, hp * 64:hp * 64 + 64], True, True, [IPc, Qs[qi_]])
                        if lvl_ < 5:
                            Qn = Qs[1 - qi_]
                            k.copy("act", Qn[:, :], bQ[:, 0:128], [bQ], [Qn])
                        else:
                            for jp in range(2):
                                ro = slice(64 * jp, 64 * jp + 64)
                                k.copy("act", Qz[jp][ro, :], bQ[ro, 0:128], [bQ], [Qz[jp]])
                        bQ.busy = False

                    for lvl in range(1, 6):
                        if lvl > 1:
                            emit_q(lvl - 1, qi)
                            qi = 1 - qi
                        IPc = IPs[lvl % 2]
                        bP = k.psum()
                        for hp in range(2):
                            for jp in range(2):
                                ro = slice(64 * jp, 64 * jp + 64)
                                if lvl == 1:
                                    Pm, PTm, Pb = Gz[jp][ro, hp, 256:320], Gz[jp][ro, hp, 128:192], Gz[jp]
                                else:
                                    PPc = PPs[lvl % 2]
                                    Pm, PTm, Pb = PPc[ro, hp * 128:hp * 128 + 64], PPc[ro, hp * 128 + 64:hp * 128 + 128], PPc
                                k.mm(bP, bP[ro, hp * 128:hp * 128 + 64], PTm, Pm, True, True, [Pb])
                                if lvl < 5:
                                    k.mm(bP, bP[ro, hp * 128 + 64:hp * 128 + 128], Pm, PTm, True, True, [Pb])
                        for hp in range(2):
                            k.tt("dve", IPc[:, hp, :], bP[:, hp * 128:hp * 128 + 64], id2[:, 0:64], ALU.add, [bP, mk_b], [IPc])
                        if lvl < 5:
                            PPn = PPs[(lvl + 1) % 2]
                            k.copy("act", PPn[:, :], bP[:, 0:256], [bP], [PPn])
                        bP.busy = False
                        yield
                    emit_q(5, qi)
                    qi = 1 - qi
                    yield
                    bA = k.psum()
                    for hp in range(2):
                        for jp in range(2):
                            ro = slice(64 * jp, 64 * jp + 64)
                            vc = slice(p * 128 + 64 * hp, p * 128 + 64 * hp + 64)
                            k.mm(bA, bA[ro, 64 * hp:64 * hp + 64], Gz[jp][ro, hp, 0:64], vtok[ro, tg, vc], True, True, [Gz[jp], vtok])
                    k.copy("act", AKV[:, :], bA[:, 0:128], [bA], [AKV])
                    bA.busy = False
                    bW = k.psum()
                    for jp in range(2):
                        k.mm(bW, bW[:, jp * 128:(jp + 1) * 128], tokA[:, tg * 128:(tg + 1) * 128], Qz[jp][:, :], True, True, [tokA, Qz[jp]])
                    for jp in range(2):
                        for hp in range(2):
                            rs_ = slice(64 * hp, 64 * hp + 64)
                            k.copy("dve", WTd[jp][rs_, :], bW[rs_, jp * 128 + 64 * hp:jp * 128 + 64 * hp + 64], [bW], [WTd[jp]])
                    bW.busy = False
                    yield
                    bX = k.psum()
                    for hp in range(2):
                        for jp in range(2):
                            ro = slice(64 * jp, 64 * jp + 64)
                            k.mm(bX, bX[ro, 64 * hp:64 * hp + 64], Qz[jp][ro, hp * 64:hp * 64 + 64], AKV[ro, 64 * hp:64 * hp + 64], True, True, [Qz[jp], AKV])
                    k.copy("act", UP[:, :], bX[:, 0:128], [bX], [UP])
                    bX.busy = False
                    yield

                def seq_gen(p, tg, S_, psY):
                    c_ = S_["ch"][tg]
                    X, krt, tokB, KEZ = S_["X"], S_["krt"], S_["tokB"], S_["KEZ"]
                    Gz, nUz, WTd, UP = c_["Gz"], c_["nUz"], c_["WTd"], c_["UP"]
                    for jp in range(2):
                        ro = slice(64 * jp, 64 * jp + 64)
                        c0 = tg * 128 + jp * 64
                        cs_ = slice(c0, c0 + 64)
                        bU = k.psum()
                        k.mm(bU, bU[ro, 0:128], WTd[jp][:, :], Tbk[p][:, :], True, True, [WTd[jp], Tbk[p]])
                        k.stt(nUz[jp][ro, :], bU[ro, 0:128], -1.0, UP[ro, :], ALU.mult, ALU.subtract, [bU, UP], [nUz[jp]])
                        bU.busy = False
                        k.mm(psY, psY[:, cs_], Tbk[p][:, :], krt[:, 1, cs_], True, False, [Tbk[p], krt])
                        for hp in range(2):
                            rs_ = slice(64 * hp, 64 * hp + 64)
                            vc = slice(p * 128 + 64 * hp, p * 128 + 64 * hp + 64)
                            k.mm(psY, psY[rs_, cs_], vtok[:, tg, vc], Gz[jp][:, hp, 64:128], False, False, [vtok, Gz[jp]])
                        yield
                        for hp in range(2):
                            rs_ = slice(64 * hp, 64 * hp + 64)
                            k.mm(psY, psY[rs_, cs_], nUz[jp][:, 64 * hp:64 * hp + 64], Gz[jp][:, hp, 192:256], False, True, [nUz[jp], Gz[jp]])
                        bS = k.psum()
                        for hp in range(2):
                            rs_ = slice(64 * hp, 64 * hp + 64)
                            vc = slice(p * 128 + 64 * hp, p * 128 + 64 * hp + 64)
                            hc = slice(tg * 128 + 64 * hp, tg * 128 + 64 * hp + 64)
                            k.mm(bS, bS[rs_, 0:64], KEZ[jp][:, hc], vtok[:, tg, vc], True, False, [KEZ[jp], vtok])
                            k.mm(bS, bS[rs_, 0:64], tokB[:, hc], nUz[jp][:, 64 * hp:64 * hp + 64], False, True, [tokB, nUz[jp]])
                        k.stt(Tf[p][:, :], Tf[p][:, :], X["g"][:, c0 + 63:c0 + 64], bS[:, 0:64], ALU.mult, ALU.add, [Tf[p], X["g"], bS], [Tf[p]])
                        bS.busy = False
                        for hp in range(2):
                            rs_ = slice(64 * hp, 64 * hp + 64)
                            k.copy("act", Tbk[p][rs_, 64 * hp:64 * hp + 64], Tf[p][rs_, :], [Tf[p]], [Tbk[p]])
                        yield

                def pair_gen(p, S_):
                    pg = q * 4 + p
                    cols = slice(p * 128, (p + 1) * 128)
                    X, t16, krt, tokA, tokB, KEZ = S_["X"], S_["t16"], S_["krt"], S_["tokA"], S_["tokB"], S_["KEZ"]
                    b_rk = k.psum()
                    for kc in range(8):
                        k.mm(b_rk, b_rk[:, 0:T], Wr[:, kc, cols], xr[:, kc, :], kc == 0, kc == 7, [Wr, xr])
                    for kc in range(8):
                        k.mm(b_rk, b_rk[:, T:2 * T], Wk[:, kc, cols], xk[:, kc, :], kc == 0, kc == 7, [Wk, xk])
                    k.copy("act", X["rf"][:, :], b_rk[:, 0:T], [b_rk], [X["rf"]])
                    k.ts("dve", X["kkp"][:, :], b_rk[:, T:2 * T], C("k_k", pg), None, ALU.mult, None, [b_rk, cst], [X["kkp"]])
                    k.copy("act", X["kf"][:, :], b_rk[:, T:2 * T], [b_rk], [X["kf"]])
                    b_rk.busy = False
                    yield
                    b_gw = k.psum()
                    for kc in range(8):
                        k.mm(b_gw, b_gw[:, 0:T], Wg[:, kc, cols], xg[:, kc, :], kc == 0, kc == 7, [Wg, xg])
                    k.mm(b_gw, b_gw[:, T:2 * T], W2[:, cols], tw[:, :], True, True, [W2, tw])
                    k.act(X["sw"][:, :], b_gw[:, T:2 * T], AF.Sigmoid, [b_gw, cst], [X["sw"]], bias=C("w0", pg))
                    k.act(X["gate"][:, :], b_gw[:, 0:T], AF.Silu, [b_gw], [X["gate"]])
                    b_gw.busy = False
                    yield
                    b_av = k.psum()
                    k.mm(b_av, b_av[:, 0:T], A2[:, cols], al[:, :], True, True, [A2, al])
                    for tg in range(2):
                        k.mm(b_av, b_av[:, T + tg * 128:T + (tg + 1) * 128], vtok[:, tg, cols], ident_b, True, True, [vtok, mk_b])
                    k.act(X["av"][:, :], b_av[:, 0:T], AF.Sigmoid, [b_av, cst], [X["av"]], bias=C("a0", pg))
                    k.copy("act", X["vfm"][:, :], b_av[:, T:2 * T], [b_av], [X["vfm"]])
                    b_av.busy = False
                    yield
                    k.ts("pool", X["nk"][:, :], X["av"][:, :], C("k_a", pg), der[:, 32 + pg:33 + pg], ALU.mult, ALU.add, [X["av"], cst, der], [X["nk"]])
                    k.tt("pool", X["kf"][:, :], X["kf"][:, :], X["nk"][:, :], ALU.mult, [X["kf"], X["nk"]], [X["kf"]])
                    k.act(t16["kk2"][:, :], X["kkp"][:, :], AF.Square, [X["kkp"]], [t16["kk2"]])
                    b_n = k.psum()
                    k.mm(b_n, b_n[:, 0:T], oblk_b, t16["kk2"][:, :], True, True, [mk_b, t16["kk2"]])
                    k.act(X["nk"][:, :], b_n[:, 0:T], AF.Sqrt, [b_n], [X["nk"]])
                    k.ts("dve", X["nk"][:, :], X["nk"][:, :], 1e-12, None, ALU.max, None, [X["nk"]], [X["nk"]])
                    k.op("dve", lambda v: v.reciprocal(out=X["nk"][:, :], in_=X["nk"][:, :]), reads=[X["nk"]], writes=[X["nk"]])
                    k.tt("dve", X["kkp"][:, :], X["kkp"][:, :], X["nk"][:, :], ALU.mult, [X["kkp"], X["nk"]], [X["kkp"]])
                    k.tt("pool", X["bb"][:, :], X["kkp"][:, :], X["av"][:, :], ALU.mult, [X["kkp"], X["av"]], [X["bb"]])
                    k.stt(t16["rk2"][:, :], X["rf"][:, :], C("r_k", pg), X["kf"][:, :], ALU.mult, ALU.mult, [X["rf"], cst, X["kf"]], [t16["rk2"]])
                    k.mm(b_n, b_n[:, T:2 * T], oblk_b, t16["rk2"][:, :], True, True, [mk_b, t16["rk2"]])
                    k.tt("dve", X["bonus"][:, :], b_n[:, T:2 * T], X["vfm"][:, :], ALU.mult, [b_n, X["vfm"]], [X["bonus"]])
                    b_n.busy = False
                    yield
                    for j in range(T // 64):
                        k.op("dve", lambda v, j=j: v.tensor_tensor_scan(
                            out=X["cum"][:, j * 64:(j + 1) * 64], data0=onesf[:, 0:64], data1=X["sw"][:, j * 64:(j + 1) * 64],
                            initial=0.0, op0=ALU.mult, op1=ALU.add), reads=[onesf, X["sw"]], writes=[X["cum"]])
                    k.tt("pool", X["cm"][:, :], X["cum"][:, :], X["sw"][:, :], ALU.subtract, [X["cum"], X["sw"]], [X["cm"]])
                    k.act(X["g"][:, :], X["cum"][:, :], AF.Exp, [X["cum"]], [X["g"]], scale=-DEC)
                    k.act(X["gi"][:, :], X["cum"][:, :], AF.Exp, [X["cum"]], [X["gi"]], scale=DEC)
                    k.act(X["cm"][:, :], X["cm"][:, :], AF.Exp, [X["cm"]], [X["cm"]], scale=-DEC)
                    yield
                    k.tt("pool", krt[:, 0, :], X["kkp"][:, :], X["cm"][:, :], ALU.mult, [X["kkp"], X["cm"]], [krt])
                    k.tt("dve", krt[:, 1, :], X["rf"][:, :], X["g"][:, :], ALU.mult, [X["rf"], X["g"]], [krt])
                    k.tt("dve", X["kf"][:, :], X["kf"][:, :], X["gi"][:, :], ALU.mult, [X["kf"], X["gi"]], [X["kf"]])
                    k.tt("pool", X["bb"][:, :], X["bb"][:, :], X["gi"][:, :], ALU.mult, [X["bb"], X["gi"]], [X["bb"]])
                    k.copy("act", t16["ktb"][:, :], X["kf"][:, :], [X["kf"]], [t16["ktb"]])
                    k.copy("pool", t16["btb"][:, :], X["bb"][:, :], [X["bb"]], [t16["btb"]])
                    for j in range(T // 64):
                        sl = slice(j * 64, (j + 1) * 64)
                        ge = X["g"][:, j * 64 + 63:j * 64 + 64]
                        k.ts("dve", t16["kend"][:, sl], X["kf"][:, sl], ge, None, ALU.mult, None, [X["kf"], X["g"]], [t16["kend"]])
                        k.ts("pool", t16["bend"][:, sl], X["bb"][:, sl], ge, None, ALU.mult, None, [X["bb"], X["g"]], [t16["bend"]])
                    yield
                    b_t = k.psum()
                    for tg in range(2):
                        k.mm(b_t, b_t[:, tg * 128:(tg + 1) * 128], krt[:, 0, tg * 128:(tg + 1) * 128], ident_b, True, True, [krt, mk_b])
                    for tg in range(2):
                        k.mm(b_t, b_t[:, 256 + tg * 128:256 + (tg + 1) * 128], t16["kend"][:, tg * 128:(tg + 1) * 128], ident_b, True, True, [t16["kend"], mk_b])
                    k.copy("act", tokA[:, 0:256], b_t[:, 0:256], [b_t], [tokA])
                    for jp in range(2):
                        ro = slice(64 * jp, 64 * jp + 64)
                        k.copy("act", KEZ[jp][ro, :], b_t[ro, 256:512], [b_t], [KEZ[jp]])
                    b_t.busy = False
                    b_t2 = k.psum()
                    for tg in range(2):
                        k.mm(b_t2, b_t2[:, tg * 128:(tg + 1) * 128], t16["bend"][:, tg * 128:(tg + 1) * 128], ident_b, True, True, [t16["bend"], mk_b])
                    k.copy("dve", tokB[:, :], b_t2[:, 0:256], [b_t2], [tokB])
                    b_t2.busy = False
                    yield
                    psY = accb[p % 2]
                    yield from rr([chain_gen(p, 0, S_), chain_gen(p, 1, S_)])
                    yield from seq_gen(p, 0, S_, psY)
                    yield from seq_gen(p, 1, S_, psY)
                    k.copy("act", X["yf"][:, :], psY[:, 0:T], [psY], [X["yf"]])
                    k.act(t16["ysq"][:, :], psY[:, 0:T], AF.Square, [psY], [t16["ysq"]])
                    k.copy("pool", t16["yb16"][:, :], X["yf"][:, :], [X["yf"]], [t16["yb16"]])
                    bM = k.psum()
                    k.mm(bM, bM[:, 0:T], oblk_b, t16["yb16"][:, :], True, True, [mk_b, t16["yb16"]])
                    k.mm(bM, bM[:, T:2 * T], oblk_b, t16["ysq"][:, :], True, True, [mk_b, t16["ysq"]])
                    k.ts("dve", X["mean"][:, :], bM[:, 0:T], 1.0 / 64, None, ALU.mult, None, [bM], [X["mean"]])
                    k.tt("pool", X["var"][:, :], X["mean"][:, :], X["mean"][:, :], ALU.mult, [X["mean"]], [X["var"]])
                    k.stt(X["var"][:, :], bM[:, T:2 * T], 1.0 / 64, X["var"][:, :], ALU.mult, ALU.subtract, [bM, X["var"]], [X["var"]])
                    bM.busy = False
                    yield
                    k.act(X["var"][:, :], X["var"][:, :], AF.Sqrt, [X["var"], der], [X["var"]], bias=eps_gn)
                    k.op("dve", lambda v: v.reciprocal(out=X["var"][:, :], in_=X["var"][:, :]), reads=[X["var"]], writes=[X["var"]])
                    k.tt("pool", X["yf"][:, :], X["yf"][:, :], X["mean"][:, :], ALU.subtract, [X["yf"], X["mean"]], [X["yf"]])
                    k.tt("dve", X["yf"][:, :], X["yf"][:, :], X["var"][:, :], ALU.mult, [X["yf"], X["var"]], [X["yf"]])
                    k.ts("pool", X["yf"][:, :], X["yf"][:, :], C("lnx_g", pg), C("lnx_b", pg), ALU.mult, ALU.add, [X["yf"], cst], [X["yf"]])
                    k.tt("pool", X["yf"][:, :], X["yf"][:, :], X["bonus"][:, :], ALU.add, [X["yf"], X["bonus"]], [X["yf"]])
                    k.tt("dve", y[:, p, :], X["yf"][:, :], X["gate"][:, :], ALU.mult, [X["yf"], X["gate"]], [y])
                    yield

                for ti in range(nt):
                    t0 = ti * T
                    hU = hUs[ti % 2]
                    hR = hRs[ti % len(hRs)]
                    k.dma("sp", ldS[ti % 2], hU[:, :, :], h1_d[:, :, t0:t0 + T], reads=[dr["h1"][ti]], writes=[hU])
                    if q > 0:
                        k.dma("sp", ldR[ti % 2], hR[:, :, :], res_d[:, :, t0:t0 + T], reads=[res_tr[ti]], writes=[hR])
                    k.copy("pool", ufp[:, :, 0:1], ufp[:, :, T:T + 1], [ufp], [ufp])
                    emit_norm((sq, sd), hU, "c_g", ufp, 1, True)
                    k.tt("pool", delta[:, :, :], ufp[:, :, 0:T], ufp[:, :, 1:T + 1], ALU.subtract, [ufp], [delta])
                    for j in range(6):
                        for c in range(8):
                            mo = _off["mu"] + j * 8 + c
                            k.stt(xj[j][:, c, :], delta[:, c, :], cst[:, mo:mo + 1], ufp[:, c, 1:T + 1], ALU.mult, ALU.add,
                                  [delta, cst, ufp], [xj[j]])
                    bl = k.psum()
                    for kc in range(8):
                        k.mm(bl, bl[0:64, 0:T], W1[:, kc, :], xw[:, kc, :], kc == 0, kc == 7, [W1, xw])
                    for kc in range(8):
                        k.mm(bl, bl[0:64, T:2 * T], A1[:, kc, :], xa[:, kc, :], kc == 0, kc == 7, [A1, xa])
                    k.act(tw[:, :], bl[0:64, 0:T], AF.Tanh, [bl], [tw])
                    k.copy("act", al[:, :], bl[0:64, T:2 * T], [bl], [al])
                    bl.busy = False
                    for tg in range(2):
                        bank = k.psum()
                        for kc in range(8):
                            k.mm(bank, bank[:, 0:512], xv[:, kc, tg * 128:(tg + 1) * 128], Wv[:, kc, :], kc == 0, kc == 7, [xv, Wv])
                        k.copy("act" if tg == 0 else "dve", vtok[:, tg, :], bank[:, 0:512], [bank], [vtok])
                        bank.busy = False
                    for pp in range(2):
                        for _ in rr([pair_gen(2 * pp, PB[0]), pair_gen(2 * pp + 1, PB[1])]):
                            pass
                    for dc in range(8):
                        bank = k.psum()
                        for m in range(4):
                            k.mm(bank, bank[:, 0:T], Wo[:, m, dc * 128:(dc + 1) * 128], y[:, m, :], m == 0, m == 3, [Wo, y])
                        k.tt("dve", hR[:, dc, :], hR[:, dc, :], bank[:, 0:T], ALU.add, [hR, bank], [hR])
                        bank.busy = False
                    if not last:
                        k.dma("sp", stS[ti % 2], dst_d[:, :, t0:t0 + T], hR[:, :, :], reads=[hR], writes=[dst_tr[ti]])
                    else:
                        emit_norm((sq, sd), hR, "fin_g", delta, 0, True)
                        for tg in range(2):
                            for cq in range(2):
                                bank = k.psum()
                                for c4 in range(4):
                                    c = cq * 4 + c4
                                    k.op("pe", lambda pe, bank=bank, c4=c4, c=c, tg=tg: pe.transpose(
                                        out=bank[:, c4 * 128:(c4 + 1) * 128], in_=delta[:, c, tg * 128:(tg + 1) * 128], identity=mk_f[:, :]),
                                        reads=[delta, mk_f], writes=[bank])
                                k.copy("act" if cq == 0 else "dve", ot[:, tg, cq * 512:(cq + 1) * 512], bank[:, 0:512], [bank], [ot])
                                bank.busy = False
                        k.dma("sp", stS[ti % 2], out_d[t0:t0 + T, :].rearrange("(g p) d -> p g d", p=128), ot[:, :, :], reads=[ot], writes=[])
                k.barrier()

        if "A" in passes:
            pass_A()
        if "B" in passes:
            pass_B()
        if "C" in passes:
            trs = [[Buf(None, "tr%d_%d" % (i, j)) for j in range(NT)] for i in range(3)]
            pass_R(0, None, None, hC_d, trs[0], False)
            pass_R(1, hC_d, trs[0], hA_d, trs[1], False)
            pass_R(2, hA_d, trs[1], hC_d, trs[2], False)
            pass_R(3, hC_d, trs[2], None, None, True)
        k.barrier()
        print("instructions:", k.nins)
    return nc


def _fm(w):
    n = w.shape[1]
    return np.ascontiguousarray(w.reshape(8, 128, n).transpose(1, 0, 2))


def _cv(v):
    return np.ascontiguousarray(v.reshape(-1, 128).T)


def make_masks():
    m = np.zeros((128, NMASK), np.float32)
    p = np.arange(128)[:, None]
    c = np.arange(128)[None, :]
    m[:, M_ID:M_ID + 128] = (p == c)
    m[:, M_ONES:M_ONES + 128] = 1.0
    m[:, M_OBLK:M_OBLK + 128] = (p // 64 == c // 64)
    m[:, M_HG:M_HG + 128] = (p // 64 == c // 64) & (p <= c)
    s = np.arange(128)[:, None] % 64
    t = np.arange(64)[None, :]
    strict = (s < t).astype(np.float32)
    incl = (s <= t).astype(np.float32)
    m[:, M_RW:M_RW + 64] = strict
    m[:, M_RW + 64:M_RW + 128] = incl
    m[:, M_RW + 128:M_RW + 192] = strict
    m[:, M_RW + 192:M_RW + 256] = incl
    m[:, M_RW + 256:M_RW + 320] = (t < s)
    eye = (s == t).astype(np.float32)
    m[:, M_ID2:M_ID2 + 64] = eye
    m[:, M_ID2 + 64:M_ID2 + 128] = eye
    return m


def prep_inputs(inp, b):
    f = lambda a: np.asarray(a, np.float32)
    cst = np.zeros((128, NCONST), np.float32)

    def put(name, arr):
        cst[:, _off[name]:_off[name] + arr.shape[1]] = arr

    put("ab_g", _cv(f(inp["ab_norm_g"])[0]))
    put("c_g", _cv(f(inp["c_norm_g"])[0]))
    put("fin_g", _cv(f(inp["final_g"])))
    cw = f(inp["rg_conv_w"])[0]
    put("conv_w", np.ascontiguousarray(cw.reshape(4, 8, 128).transpose(2, 1, 0).reshape(128, 32)))
    put("conv_b", _cv(f(inp["rg_conv_b"])[0]))
    put("b_a", _cv(f(inp["rg_b_a"])[0]))
    put("b_x", _cv(f(inp["rg_b_x"])[0]))
    put("lam", _cv(f(inp["rg_lambda"])[0]))
    put("lb0", _cv(f(inp["hg_lb_logits"])[0]))
    put("lb1", _cv(f(inp["hg_lb_logits"])[1]))
    put("hg_g", f(inp["hg_norm_g"])[0].reshape(128, 1))
    mu = f(inp["c_mu"])[0]
    put("mu", np.ascontiguousarray(mu.reshape(6, 8, 128).transpose(2, 0, 1).reshape(128, 48)))
    put("w0", _cv(f(inp["c_w0"])[0]))
    put("a0", _cv(f(inp["c_a0"])[0]))
    put("k_k", _cv(f(inp["c_k_k"])[0]))
    put("k_a", _cv(f(inp["c_k_a"])[0]))
    put("r_k", _cv(f(inp["c_r_k"])[0].reshape(-1)))
    put("lnx_g", _cv(f(inp["c_lnx_g"])[0]))
    put("lnx_b", _cv(f(inp["c_lnx_b"])[0]))
    win = f(inp["ab_w_in"])[0]
    wout = f(inp["ab_w_out"])[0]
    m = {
        "x": np.ascontiguousarray(f(inp["x"])[b]),
        "consts": cst,
        "masks": make_masks(),
        "wAin": _fm(win[:, 0:2048]),
        "rgwa": np.ascontiguousarray(f(inp["rg_w_a"])[0].transpose(1, 0, 2)),
        "rgwx": np.ascontiguousarray(f(inp["rg_w_x"])[0].transpose(1, 0, 2)),
        "wAout": _fm(wout[0:1024]),
        "wBin": _fm(win[:, 2048:6144]),
        "wBout": _fm(wout[1024:2048]),
        "w1": _fm(f(inp["c_w1"])[0]),
        "a1": _fm(f(inp["c_a1"])[0]),
    }
    for h in range(2):
        sl = slice(h * 1024, (h + 1) * 1024)
        m["wr%d" % h] = _fm(f(inp["c_w_r"])[0][:, sl])
        m["wk%d" % h] = _fm(f(inp["c_w_k"])[0][:, sl])
        m["wv%d" % h] = _fm(f(inp["c_w_v"])[0][:, sl])
        m["wg%d" % h] = _fm(f(inp["c_w_g"])[0][:, sl])
        m["wo%d" % h] = _fm(f(inp["c_w_o"])[0][sl, :])
        m["w2_%d" % h] = np.ascontiguousarray(f(inp["c_w2"])[0][:, sl])
        m["a2_%d" % h] = np.ascontiguousarray(f(inp["c_a2"])[0][:, sl])
    return m


def kernel(**inputs):
    nc = build()
    in_maps = [prep_inputs(inputs, i % 4) for i in range(8)]
    res = run_bass_kernel_spmd(nc, in_maps, core_ids=list(range(8)))
    out = np.stack([np.asarray(res.results[i]["out"], np.float32) for i in range(4)], axis=0)
    return out
```

```python
import contextlib
import numpy as np
import concourse.bass as bass
import concourse.mybir as mybir
from concourse.bass_utils import run_bass_kernel_spmd
from concourse.alu_op_type import AluOpType as ALU

F32 = mybir.dt.float32
BF16 = mybir.dt.bfloat16
AF = mybir.ActivationFunctionType

S = 4096
D = 1024
T = 256
NT = S // T
RMS_EPS = 1e-6
GN_EPS = 64e-5
DEC = 0.6065306597126334
SAME_SYNC = True
import os
BSTOP = float(os.environ.get('BSTOP', '99'))
RSTOP = float(os.environ.get('RSTOP', '99'))

_off = {}
_n = 0
for _name, _w in [("ab_g", 8), ("c_g", 8), ("fin_g", 8), ("conv_w", 32), ("conv_b", 8), ("b_a", 8), ("b_x", 8),
                  ("lam", 8), ("lb0", 8), ("lb1", 8), ("hg_g", 1), ("mu", 48), ("w0", 16), ("a0", 16),
                  ("k_k", 16), ("k_a", 16), ("r_k", 16), ("lnx_g", 16), ("lnx_b", 16)]:
    _off[_name] = _n
    _n += _w
NCONST = _n
M_ID = 0
M_ONES = 128
M_OBLK = 256
M_HG = 384
M_RW = 512
M_ID2 = 832
NMASK = 960


class Buf:
    __slots__ = ("t", "name", "w", "r", "busy")

    def __init__(self, t, name):
        self.t = t
        self.name = name
        self.w = None
        self.r = {}
        self.busy = False

    def __getitem__(self, idx):
        return self.t[idx]


class Stream:
    def __init__(self, sem, key):
        self.sem = sem
        self.key = key
        self.count = 0


class KB:
    def __init__(self, nc, es):
        self.nc = nc
        self.es = es
        self.eng = {"pe": nc.tensor, "act": nc.scalar, "dve": nc.vector, "pool": nc.gpsimd, "sp": nc.sync}
        self.st = {k: Stream(es.enter_context(nc.semaphore(k + "_s")), k) for k in self.eng}
        self.waited = {k: {} for k in self.eng}
        self.dstreams = []
        self.banks = []
        self.bank_i = 0
        self.nins = 0

    def dma_stream(self, name):
        s = Stream(self.es.enter_context(self.nc.semaphore(name)), name)
        self.dstreams.append(s)
        return s

    def sbuf(self, name, shape, dtype, es=None):
        es = es or self.es
        self.nins += 0
        self.uid = getattr(self, "uid", 0) + 1
        name = "%s_u%d" % (name, self.uid)
        return Buf(es.enter_context(self.nc.sbuf_tensor(name, list(shape), dtype)), name)

    def init_psum(self):
        for i in range(8):
            self.banks.append(Buf(self.es.enter_context(self.nc.psum_tensor("bank%d" % i, [128, 512], F32)), "bank%d" % i))

    def psum(self):
        b = self.banks[2 + self.bank_i % 6]
        self.bank_i += 1
        assert not b.busy, "psum bank still in use: " + b.name
        b.busy = True
        return b

    def _wait(self, e, sv):
        s, v = sv
        w = self.waited[e]
        if w.get(s.key, 0) >= v:
            return
        w[s.key] = v
        self.eng[e].wait_ge(s.sem, v)

    def _deps(self, e, reads, writes):
        for b in reads:
            if b.name.startswith("bank"):
                for sv in b.r.values():
                    if sv[0].key != e:
                        self._wait(e, sv)
            if b.w is not None:
                if b.w[0].key == e:
                    if (SAME_SYNC is True and e != "pe") or (SAME_SYNC == "pool" and e == "pool"):
                        self._wait(e, b.w)
                else:
                    self._wait(e, b.w)
        for b in writes:
            if b.w is not None and b.w[0].key != e:
                self._wait(e, b.w)
            for sv in b.r.values():
                if sv[0].key != e:
                    self._wait(e, sv)

    def op(self, e, fn, reads=(), writes=()):
        self._deps(e, reads, writes)
        ins = fn(self.eng[e])
        s = self.st[e]
        s.count += 1
        ins.then_inc(s.sem, 1)
        self.nins += 1
        for b in reads:
            b.r[s.key] = (s, s.count)
        for b in writes:
            b.w = (s, s.count)
            b.r = {}

    def dma(self, q, stream, out_ap, in_ap, reads=(), writes=()):
        self._deps(q, reads, writes)
        ins = self.eng[q].dma_start(out=out_ap, in_=in_ap)
        stream.count += 16
        ins.then_inc(stream.sem, 16)
        self.nins += 1
        for b in reads:
            b.r[stream.key] = (stream, stream.count)
        for b in writes:
            b.w = (stream, stream.count)
            b.r = {}

    def seal(self, stream, bufs):
        for b in bufs:
            b.w = (stream, stream.count)

    def barrier(self):
        allst = list(self.st.values()) + self.dstreams
        for e in self.eng:
            for s in allst:
                if s.key != e and s.count > 0:
                    self._wait(e, (s, s.count))

    def mm(self, bank, out_ap, lhsT, rhs, start, stop, reads):
        self.op("pe", lambda pe: pe.matmul(out_ap, lhsT=lhsT, rhs=rhs, start=start, stop=stop), reads=reads, writes=[bank])

    def act(self, out_ap, in_ap, func, reads, writes, bias=None, scale=None):
        kw = {}
        if bias is not None:
            kw["bias"] = bias
        if scale is not None:
            kw["scale"] = scale
        self.op("act", lambda a: a.activation(out=out_ap, in_=in_ap, func=func, **kw), reads=reads, writes=writes)

    def tt(self, e, out_ap, in0, in1, op, reads, writes):
        self.op(e, lambda v: v.tensor_tensor(out=out_ap, in0=in0, in1=in1, op=op), reads=reads, writes=writes)

    def ts(self, e, out_ap, in0, s1, s2, op0, op1, reads, writes):
        if op1 is None:
            self.op(e, lambda v: v.tensor_scalar(out=out_ap, in0=in0, scalar1=s1, scalar2=None, op0=op0), reads=reads, writes=writes)
        else:
            self.op(e, lambda v: v.tensor_scalar(out=out_ap, in0=in0, scalar1=s1, scalar2=s2, op0=op0, op1=op1), reads=reads, writes=writes)

    def stt(self, out_ap, in0, scalar, in1, op0, op1, reads, writes):
        self.op("dve", lambda v: v.scalar_tensor_tensor(out=out_ap, in0=in0, scalar=scalar, in1=in1, op0=op0, op1=op1), reads=reads, writes=writes)

    def copy(self, e, out_ap, in_ap, reads, writes):
        if e == "act":
            self.op("act", lambda a: a.copy(out=out_ap, in_=in_ap), reads=reads, writes=writes)
        else:
            self.op(e, lambda v: v.tensor_copy(out=out_ap, in_=in_ap), reads=reads, writes=writes)


def build(nt=NT, passes="ABCD", debug=False):
    nc = bass.Bass("TRN2", target_bir_lowering=False)

    def din(name, shape):
        return nc.dram_tensor(name, list(shape), F32, kind="ExternalInput").ap()

    x_d = din("x", [S, D])
    consts_d = din("consts", [128, NCONST])
    masks_d = din("masks", [128, NMASK])
    wAin_d = din("wAin", [128, 8, 2048])
    rgwa_d = din("rgwa", [128, 8, 128])
    rgwx_d = din("rgwx", [128, 8, 128])
    wAout_d = din("wAout", [128, 8, 1024])
    wBin_d = din("wBin", [128, 8, 4096])
    wBout_d = din("wBout", [128, 8, 1024])
    wr_d = [din("wr%d" % h, [128, 8, 1024]) for h in range(2)]
    wk_d = [din("wk%d" % h, [128, 8, 1024]) for h in range(2)]
    wv_d = [din("wv%d" % h, [128, 8, 1024]) for h in range(2)]
    wg_d = [din("wg%d" % h, [128, 8, 1024]) for h in range(2)]
    wo_d = [din("wo%d" % h, [128, 8, 1024]) for h in range(2)]
    w1_d = din("w1", [128, 8, 64])
    a1_d = din("a1", [128, 8, 64])
    w2_d = [din("w2_%d" % h, [64, 1024]) for h in range(2)]
    a2_d = [din("a2_%d" % h, [64, 1024]) for h in range(2)]
    out_d = nc.dram_tensor("out", [S, D], F32, kind="ExternalOutput").ap()
    skind = "ExternalOutput" if debug else "Internal"
    h0_d = nc.dram_tensor("h0fm", [128, 8, S], F32, kind=skind).ap()
    hA_d = nc.dram_tensor("hAfm", [128, 8, S], F32, kind=skind).ap()
    h1_d = nc.dram_tensor("h1fm", [128, 8, S], F32, kind=skind).ap()
    hC_d = nc.dram_tensor("hCfm", [128, 8, S], F32, kind=skind).ap()

    es = contextlib.ExitStack()
    with es:
        k = KB(nc, es)
        k.init_psum()
        cst = k.sbuf("cst", [128, NCONST], F32)
        der = k.sbuf("der", [128, 64], F32)
        mk_f = k.sbuf("mk_f", [128, 128], F32)
        mk_b = k.sbuf("mk_b", [128, NMASK], BF16)
        mk_rw = k.sbuf("mk_rw", [128, 320], F32)
        mk_hg = k.sbuf("mk_hg", [128, 128], F32)
        zeros = k.sbuf("zeros", [128, 64], F32)
        onesf = k.sbuf("onesf", [128, 64], F32)
        cs = k.dma_stream("cstream")
        k.dma("sp", cs, cst[:, :], consts_d[:, :], writes=[cst])
        k.dma("sp", cs, mk_f[:, :], masks_d[:, M_ID:M_ID + 128], writes=[mk_f])
        k.dma("sp", cs, mk_rw[:, :], masks_d[:, M_RW:M_RW + 320], writes=[mk_rw])
        k.dma("sp", cs, mk_hg[:, :], masks_d[:, M_HG:M_HG + 128], writes=[mk_hg])
        cs2 = k.dma_stream("cstream2")
        k.dma("pool", cs2, mk_b[:, :], masks_d[:, :], writes=[mk_b])
        k.seal(cs, [cst, mk_f, mk_rw, mk_hg])
        k.op("pool", lambda g: g.memset(zeros[:, :], 0.0), writes=[zeros])
        k.op("pool", lambda g: g.memset(onesf[:, :], 1.0), writes=[onesf])
        ident_b = mk_b[:, M_ID:M_ID + 128]
        ones_b = mk_b[:, M_ONES:M_ONES + 128]
        oblk_b = mk_b[:, M_OBLK:M_OBLK + 128]

        def C(name, j=0, w=1):
            o = _off[name] + j
            return cst[:, o:o + w]

        k.act(der[:, 0:8], C("lam", 0, 8), AF.Exp, [cst], [der], scale=-1.0)
        k.act(der[:, 0:8], der[:, 0:8], AF.Ln, [der, onesf], [der], bias=onesf[:, 0:1])
        k.ts("dve", der[:, 8:16], der[:, 0:8], -16.0, None, ALU.mult, None, [der], [der])
        k.ts("dve", der[:, 0:8], der[:, 0:8], -8.0, None, ALU.mult, None, [der], [der])
        k.tt("dve", der[:, 16:24], C("lb0", 0, 8), C("lb1", 0, 8), ALU.subtract, [cst], [der])
        k.act(der[:, 16:24], der[:, 16:24], AF.Sigmoid, [der], [der])
        k.ts("dve", der[:, 24:32], der[:, 16:24], -1.0, 1.0, ALU.mult, ALU.add, [der], [der])
        k.ts("dve", der[:, 32:48], C("k_a", 0, 16), -1.0, 1.0, ALU.mult, ALU.add, [cst], [der])
        k.op("pool", lambda g: g.memset(der[:, 48:49], RMS_EPS), writes=[der])
        k.op("pool", lambda g: g.memset(der[:, 49:50], GN_EPS), writes=[der])
        eps_rms = der[:, 48:49]
        eps_gn = der[:, 49:50]

        ws = k.dma_stream("wstream")
        ldS = [k.dma_stream("ldU0"), k.dma_stream("ldU1")]
        ldR = [k.dma_stream("ldR0"), k.dma_stream("ldR1")]
        stS = [k.dma_stream("st0"), k.dma_stream("st1")]
        stS2 = [k.dma_stream("st2_0"), k.dma_stream("st2_1")]
        dr = {nm: [Buf(None, "%s_%d" % (nm, i)) for i in range(NT)] for nm in ("h0", "hA", "h1", "hC")}

        def loadw(buf, dram, nk=8, ncol=None, dcol0=0, dk0=0):
            ncol = ncol or dram.shape[2]
            for kc in range(nk):
                for c0 in range(0, ncol, 1024):
                    c1 = min(ncol, c0 + 1024)
                    k.dma("pool", ws, buf[:, kc, c0:c1], dram[:, dk0 + kc, dcol0 + c0:dcol0 + c1], writes=[buf])

        def emit_norm(pes_bufs, hU, gname, outbuf, col0, fp32_out):
            sq, sd = pes_bufs
            k.act(sq[:, :, :], hU[:, :, :], AF.Square, [hU], [sq])
            bank = k.psum()
            for c in range(8):
                k.mm(bank, bank[:, 0:T], ones_b, sq[:, c, :], c == 0, c == 7, [sq, mk_b])
            k.act(sd[:, :], bank[:, 0:T], AF.Sqrt, [bank, der], [sd], bias=eps_rms, scale=1.0 / D)
            bank.busy = False
            k.op("dve", lambda v: v.reciprocal(out=sd[:, :], in_=sd[:, :]), reads=[sd], writes=[sd])
            for c in range(8):
                k.stt(outbuf[:, c, col0:col0 + T], hU[:, c, :], C(gname, c), sd[:, :], ALU.mult, ALU.mult,
                      [hU, cst, sd], [outbuf])

        def out_proj(Wout, y, hR):
            for dc in range(8):
                bank = k.psum()
                for m in range(8):
                    k.mm(bank, bank[:, 0:T], Wout[:, m, dc * 128:(dc + 1) * 128], y[:, m, :], m == 0, m == 7, [Wout, y])
                k.tt("dve", hR[:, dc, :], hR[:, dc, :], bank[:, 0:T], ALU.add, [hR, bank], [hR])
                bank.busy = False

        def rr(gens):
            gens = list(gens)
            while gens:
                for g_ in list(gens):
                    try:
                        next(g_)
                    except StopIteration:
                        gens.remove(g_)
                yield

        def pass_A():
            with contextlib.ExitStack() as pes:
                Win = k.sbuf("A_Win", [128, 8, 2048], BF16, pes)
                Wa = k.sbuf("A_Wa", [128, 8, 128], BF16, pes)
                Wx = k.sbuf("A_Wx", [128, 8, 128], BF16, pes)
                Wout = k.sbuf("A_Wout", [128, 8, 1024], BF16, pes)
                loadw(Win, wAin_d)
                k.dma("pool", ws, Wa[:, :, :], rgwa_d[:, :, :], writes=[Wa])
                k.dma("pool", ws, Wx[:, :, :], rgwx_d[:, :, :], writes=[Wx])
                loadw(Wout, wAout_d)
                k.seal(ws, [Win, Wa, Wx, Wout])
                xts = [k.sbuf("A_xt%d" % i, [128, 2, 1024], F32, pes) for i in range(2)]
                hUs = [k.sbuf("A_hU%d" % i, [128, 8, T], F32, pes) for i in range(2)]
                sq = k.sbuf("A_sq", [128, 8, T], BF16, pes)
                sd = k.sbuf("A_sd", [128, T], F32, pes)
                u = k.sbuf("A_u", [128, 8, T], BF16, pes)
                y = k.sbuf("A_y", [128, 8, T], BF16, pes)
                xaext = [k.sbuf("A_xa%d" % c, [128, T + 3], F32, pes) for c in range(8)]
                carry = [k.sbuf("A_cy%d" % c, [128, 1], F32, pes) for c in range(8)]
                xc = [k.sbuf("A_xc%d" % i, [128, T], F32, pes) for i in range(4)]
                xcb = [k.sbuf("A_xcb%d" % i, [128, T], BF16, pes) for i in range(4)]
                sr = [k.sbuf("A_sr%d" % i, [128, T], F32, pes) for i in range(4)]
                si = [k.sbuf("A_si%d" % i, [128, T], F32, pes) for i in range(4)]
                av = [k.sbuf("A_av%d" % i, [128, T], F32, pes) for i in range(4)]
                mv = [k.sbuf("A_mv%d" % i, [128, T], F32, pes) for i in range(4)]
                uu = [k.sbuf("A_uu%d" % i, [128, T], F32, pes) for i in range(4)]
                hh = [k.sbuf("A_hh%d" % i, [128, T], F32, pes) for i in range(4)]
                sg = [k.sbuf("A_sg%d" % i, [128, T], F32, pes) for i in range(4)]
                for c in range(8):
                    k.op("pool", lambda g, c=c: g.memset(xaext[c][:, :], 0.0), writes=[xaext[c]])
                    k.op("pool", lambda g, c=c: g.memset(carry[c][:, :], 0.0), writes=[carry[c]])

                for ti in range(nt):
                    t0 = ti * T
                    xt = xts[ti % 2]
                    hU = hUs[ti % 2]
                    k.dma("sp", ldS[ti % 2], xt[:, :, :], x_d[t0:t0 + T, :].rearrange("(g p) d -> p g d", p=128), writes=[xt])
                    for cp in range(4):
                        bank = k.psum()
                        for cc in range(2):
                            c = cp * 2 + cc
                            for tg in range(2):
                                o = cc * 256 + tg * 128
                                k.op("pe", lambda pe, o=o, c=c, tg=tg, bank=bank: pe.transpose(
                                    out=bank[:, o:o + 128], in_=xt[:, tg, c * 128:(c + 1) * 128], identity=mk_f[:, :]),
                                    reads=[xt, mk_f], writes=[bank])
                        for cc in range(2):
                            c = cp * 2 + cc
                            k.copy("act" if cc == 0 else "dve", hU[:, c, :], bank[:, cc * 256:cc * 256 + 256], [bank], [hU])
                        bank.busy = False
                    k.dma("sp", stS2[ti % 2], h0_d[:, :, t0:t0 + T], hU[:, :, :], reads=[hU], writes=[dr["h0"][ti]])
                    emit_norm((sq, sd), hU, "ab_g", u, 0, False)
                    def blockA(c):
                        i2 = c % 4
                        b1 = k.psum()
                        for kc in range(8):
                            k.mm(b1, b1[:, 0:T], Win[:, kc, c * 128:(c + 1) * 128], u[:, kc, :], kc == 0, kc == 7, [Win, u])
                        for kc in range(8):
                            k.mm(b1, b1[:, T:2 * T], Win[:, kc, 1024 + c * 128:1024 + (c + 1) * 128], u[:, kc, :], kc == 0, kc == 7, [Win, u])
                        xe = xaext[c]
                        k.copy("pool", xe[:, 0:3], xe[:, T:T + 3], [xe], [xe])
                        k.copy("act", xe[:, 3:T + 3], b1[:, 0:T], [b1], [xe])
                        k.act(sg[i2][:, :], b1[:, T:2 * T], AF.Silu, [b1], [sg[i2]])
                        b1.busy = False
                        yield
                        cw = _off["conv_w"] + c * 4
                        k.ts("dve", xc[i2][:, :], xe[:, 3:T + 3], cst[:, cw + 3:cw + 4], C("conv_b", c), ALU.mult, ALU.add, [xe, cst], [xc[i2]])
                        for j in (2, 1, 0):
                            k.stt(xc[i2][:, :], xe[:, j:j + T], cst[:, cw + j:cw + j + 1], xc[i2][:, :], ALU.mult, ALU.add, [xe, cst, xc[i2]], [xc[i2]])
                        k.copy("pool", xcb[i2][:, :], xc[i2][:, :], [xc[i2]], [xcb[i2]])
                        yield
                        b2 = k.psum()
                        k.mm(b2, b2[:, 0:T], Wa[:, c, :], xcb[i2][:, :], True, True, [Wa, xcb[i2]])
                        k.mm(b2, b2[:, T:2 * T], Wx[:, c, :], xcb[i2][:, :], True, True, [Wx, xcb[i2]])
                        k.act(sr[i2][:, :], b2[:, 0:T], AF.Sigmoid, [b2, cst], [sr[i2]], bias=C("b_a", c))
                        k.act(si[i2][:, :], b2[:, T:2 * T], AF.Sigmoid, [b2, cst], [si[i2]], bias=C("b_x", c))
                        b2.busy = False
                        yield
                        k.act(av[i2][:, :], sr[i2][:, :], AF.Exp, [sr[i2], der], [av[i2]], scale=der[:, c:c + 1])
                        k.act(mv[i2][:, :], sr[i2][:, :], AF.Exp, [sr[i2], der], [mv[i2]], scale=der[:, 8 + c:9 + c])
                        k.tt("pool", uu[i2][:, :], si[i2][:, :], xc[i2][:, :], ALU.mult, [si[i2], xc[i2]], [uu[i2]])
                        yield
                        k.act(mv[i2][:, :], mv[i2][:, :], AF.Sqrt, [mv[i2], onesf], [mv[i2]], bias=onesf[:, 0:1], scale=-1.0)
                        yield
                        k.tt("dve", uu[i2][:, :], uu[i2][:, :], mv[i2][:, :], ALU.mult, [uu[i2], mv[i2]], [uu[i2]])
                        k.op("dve", lambda v, i2=i2, c=c: v.tensor_tensor_scan(
                            out=hh[i2][:, :], data0=av[i2][:, :], data1=uu[i2][:, :], initial=carry[c][:, 0:1],
                            op0=ALU.mult, op1=ALU.add), reads=[av[i2], uu[i2], carry[c]], writes=[hh[i2]])
                        k.copy("pool", carry[c][:, 0:1], hh[i2][:, T - 1:T], [hh[i2]], [carry[c]])
                        k.tt("dve", y[:, c, :], hh[i2][:, :], sg[i2][:, :], ALU.mult, [hh[i2], sg[i2]], [y])
                        yield

                    for c4 in range(2):
                        for _ in rr([blockA(c4 * 4 + i) for i in range(4)]):
                            pass
                    out_proj(Wout, y, hU)
                    k.dma("sp", stS[ti % 2], hA_d[:, :, t0:t0 + T], hU[:, :, :], reads=[hU], writes=[dr["hA"][ti]])
                k.barrier()

        def pass_B():
            with contextlib.ExitStack() as pes:
                Win = k.sbuf("B_Win", [128, 8, 4096], BF16, pes)
                Wout = k.sbuf("B_Wout", [128, 8, 1024], BF16, pes)
                loadw(Win, wBin_d)
                loadw(Wout, wBout_d)
                k.seal(ws, [Win, Wout])
                hUs = [k.sbuf("B_hU%d" % i, [128, 8, T], F32, pes) for i in range(2)]
                hRs = [k.sbuf("B_hR%d" % i, [128, 8, T], F32, pes) for i in range(2)]
                sq = k.sbuf("B_sq", [128, 8, T], BF16, pes)
                sd = k.sbuf("B_sd", [128, T], F32, pes)
                u = k.sbuf("B_u", [128, 8, T], BF16, pes)
                y = k.sbuf("B_y", [128, 8, T], BF16, pes)
                vtok = k.sbuf("B_vtok", [128, 2, 1024], BF16, pes)
                stf = [k.sbuf("B_stf%d" % h, [128, 128], F32, pes) for h in range(8)]
                stb = [k.sbuf("B_stb%d" % h, [128, 128], BF16, pes) for h in range(8)]
                NB = 4
                qf = [k.sbuf("B_qf%d" % i, [128, T], F32, pes) for i in range(NB)]
                sig = [k.sbuf("B_sig%d" % i, [128, T], F32, pes) for i in range(NB)]
                ff = [k.sbuf("B_f%d" % i, [128, T], F32, pes) for i in range(NB)]
                kf = [k.sbuf("B_k%d" % i, [128, T], F32, pes) for i in range(NB)]
                Pc = [k.sbuf("B_P%d" % i, [128, T], F32, pes) for i in range(NB)]
                Pi = [k.sbuf("B_Pi%d" % i, [128, T], F32, pes) for i in range(NB)]
                qd = [k.sbuf("B_qd%d" % i, [128, T], BF16, pes) for i in range(NB)]
                kif = [k.sbuf("B_kif%d" % i, [128, T], F32, pes) for i in range(NB)]
                kib = [k.sbuf("B_kib%d" % i, [128, T], BF16, pes) for i in range(NB)]
                keb = [k.sbuf("B_keb%d" % i, [128, T], BF16, pes) for i in range(NB)]
                scm = [k.sbuf("B_scm%d" % i, [128, 128], BF16, pes) for i in range(NB)]
                ket = [k.sbuf("B_ket%d" % i, [128, 128], BF16, pes) for i in range(NB)]
                osq = [k.sbuf("B_osq%d" % i, [128, T], BF16, pes) for i in range(NB)]
                ors = [k.sbuf("B_ors%d" % i, [128, T], F32, pes) for i in range(NB)]
                o1 = [k.sbuf("B_o1%d" % i, [128, T], F32, pes) for i in range(NB)]
                sgb = [k.sbuf("B_sg%d" % i, [128, T], F32, pes) for i in range(NB)]
                for h in range(8):
                    k.op("pool", lambda g, h=h: g.memset(stf[h][:, :], 0.0), writes=[stf[h]])
                    k.op("pool", lambda g, h=h: g.memset(stb[h][:, :], 0.0), writes=[stb[h]])
                accb = [k.banks[0], k.banks[1]]
                for ti in range(nt):
                    t0 = ti * T
                    hU = hUs[ti % 2]
                    hR = hRs[ti % 2]
                    k.dma("sp", ldS[ti % 2], hU[:, :, :], h0_d[:, :, t0:t0 + T], reads=[dr["h0"][ti]], writes=[hU])
                    k.dma("sp", ldR[ti % 2], hR[:, :, :], hA_d[:, :, t0:t0 + T], reads=[dr["hA"][ti]], writes=[hR])
                    emit_norm((sq, sd), hU, "ab_g", u, 0, False)
                    for tg in range(2):
                        for cg in range(2):
                            bank = k.psum()
                            for kc in range(8):
                                k.mm(bank, bank[:, 0:512], u[:, kc, tg * 128:(tg + 1) * 128],
                                     Win[:, kc, 2048 + cg * 512:2048 + (cg + 1) * 512], kc == 0, kc == 7, [u, Win])
                            k.copy("act" if cg == 0 else "dve", vtok[:, tg, cg * 512:(cg + 1) * 512], bank[:, 0:512], [bank], [vtok])
                            bank.busy = False
                    def headB(h):
                        i2 = h % NB
                        bo = accb[(h % 4) // 2]
                        oc = (h % 2) * 256
                        bq = k.psum()
                        for kc in range(8):
                            k.mm(bq, bq[:, 0:T], Win[:, kc, h * 128:(h + 1) * 128], u[:, kc, :], kc == 0, kc == 7, [Win, u])
                        for kc in range(8):
                            k.mm(bq, bq[:, T:2 * T], Win[:, kc, 1024 + h * 128:1024 + (h + 1) * 128], u[:, kc, :], kc == 0, kc == 7, [Win, u])
                        k.act(sig[i2][:, :], bq[:, T:2 * T], AF.Sigmoid, [bq], [sig[i2]])
                        k.copy("act", qf[i2][:, :], bq[:, 0:T], [bq], [qf[i2]])
                        bq.busy = False
                        yield
                        k.ts("dve", ff[i2][:, :], sig[i2][:, :], der[:, 24 + h:25 + h], der[:, 16 + h:17 + h], ALU.mult, ALU.add, [sig[i2], der], [ff[i2]])
                        k.ts("pool", kf[i2][:, :], ff[i2][:, :], -1.0, 1.0, ALU.mult, ALU.add, [ff[i2]], [kf[i2]])
                        for j in range(T // 64):
                            k.op("dve", lambda v, i2=i2, j=j: v.tensor_tensor_scan(
                                out=Pc[i2][:, j * 64:(j + 1) * 64], data0=ff[i2][:, j * 64:(j + 1) * 64], data1=zeros[:, 0:64],
                                initial=1.0, op0=ALU.mult, op1=ALU.add), reads=[ff[i2], zeros], writes=[Pc[i2]])
                        yield
                        k.op("dve", lambda v, i2=i2: v.reciprocal(out=Pi[i2][:, :], in_=Pc[i2][:, :]), reads=[Pc[i2]], writes=[Pi[i2]])
                        k.tt("pool", qd[i2][:, :], qf[i2][:, :], Pc[i2][:, :], ALU.mult, [qf[i2], Pc[i2]], [qd[i2]])
                        yield
                        k.tt("pool", kif[i2][:, :], kf[i2][:, :], Pi[i2][:, :], ALU.mult, [kf[i2], Pi[i2]], [kif[i2]])
                        k.copy("pool", kib[i2][:, :], kif[i2][:, :], [kif[i2]], [kib[i2]])
                        for j in range(T // 64):
                            k.ts("dve", keb[i2][:, j * 64:(j + 1) * 64], kif[i2][:, j * 64:(j + 1) * 64],
                                 Pc[i2][:, j * 64 + 63:j * 64 + 64], None, ALU.mult, None, [kif[i2], Pc[i2]], [keb[i2]])
                        yield
                        for tg in range(2):
                            c0 = tg * 128
                            bs = k.psum()
                            k.mm(bs, bs[:, 0:128], kib[i2][:, c0:c0 + 128], qd[i2][:, c0:c0 + 128], True, True, [kib[i2], qd[i2]])
                            k.mm(bs, bs[:, 128:256], keb[i2][:, c0:c0 + 128], ident_b, True, True, [keb[i2], mk_b])
                            k.tt("dve", scm[i2][:, :], bs[:, 0:128], mk_hg[:, :], ALU.mult, [bs, mk_hg], [scm[i2]])
                            k.copy("act", ket[i2][:, :], bs[:, 128:256], [bs], [ket[i2]])
                            bs.busy = False
                            yield
                            for jp in range(2):
                                cj = c0 + jp * 64
                                r0 = jp * 64
                                k.mm(bo, bo[:, oc + cj:oc + cj + 64], vtok[:, tg, h * 128:(h + 1) * 128], scm[i2][:, r0:r0 + 64], True, False, [vtok, scm[i2]])
                                k.mm(bo, bo[:, oc + cj:oc + cj + 64], stb[h][:, :], qd[i2][:, cj:cj + 64], False, True, [stb[h], qd[i2]])
                                bst = k.psum()
                                k.mm(bst, bst[:, 0:128], ket[i2][r0:r0 + 64, :], vtok[r0:r0 + 64, tg, h * 128:(h + 1) * 128], True, True, [ket[i2], vtok])
                                k.stt(stf[h][:, :], stf[h][:, :], Pc[i2][:, cj + 63:cj + 64], bst[:, 0:128], ALU.mult, ALU.add, [stf[h], Pc[i2], bst], [stf[h]])
                                bst.busy = False
                                k.copy("act", stb[h][:, :], stf[h][:, :], [stf[h]], [stb[h]])
                                yield
                        bg = k.psum()
                        for kc in range(8):
                            k.mm(bg, bg[:, 0:T], Win[:, kc, 3072 + h * 128:3072 + (h + 1) * 128], u[:, kc, :], kc == 0, kc == 7, [Win, u])
                        k.act(sgb[i2][:, :], bg[:, 0:T], AF.Silu, [bg], [sgb[i2]])
                        k.act(osq[i2][:, :], bo[:, oc:oc + T], AF.Square, [bo], [osq[i2]])
                        k.copy("act", o1[i2][:, :], bo[:, oc:oc + T], [bo], [o1[i2]])
                        k.mm(bg, bg[:, T:2 * T], ones_b, osq[i2][:, :], True, True, [mk_b, osq[i2]])
                        k.act(ors[i2][:, :], bg[:, T:2 * T], AF.Sqrt, [bg, der], [ors[i2]], bias=eps_rms, scale=1.0 / 128)
                        bg.busy = False
                        yield
                        k.op("dve", lambda v, i2=i2: v.reciprocal(out=ors[i2][:, :], in_=ors[i2][:, :]), reads=[ors[i2]], writes=[ors[i2]])
                        k.tt("pool", o1[i2][:, :], o1[i2][:, :], ors[i2][:, :], ALU.mult, [o1[i2], ors[i2]], [o1[i2]])
                        k.stt(y[:, h, :], o1[i2][:, :], C("hg_g"), sgb[i2][:, :], ALU.mult, ALU.mult, [o1[i2], cst, sgb[i2]], [y])
                        yield

                    for h4 in range(2):
                        for _ in rr([headB(h4 * 4 + i) for i in range(4)]):
                            pass
                    out_proj(Wout, y, hR)
                    k.dma("sp", stS[ti % 2], h1_d[:, :, t0:t0 + T], hR[:, :, :], reads=[hR], writes=[dr["h1"][ti]])
                k.barrier()


        def pass_R(q, res_d, res_tr, dst_d, dst_tr, last):
            hf, qo = q // 2, (q % 2) * 512
            with contextlib.ExitStack() as pes:
                Wr = k.sbuf("R_Wr", [128, 8, 512], BF16, pes)
                Wk = k.sbuf("R_Wk", [128, 8, 512], BF16, pes)
                Wv = k.sbuf("R_Wv", [128, 8, 512], BF16, pes)
                Wg = k.sbuf("R_Wg", [128, 8, 512], BF16, pes)
                W1 = k.sbuf("R_W1", [128, 8, 64], BF16, pes)
                A1 = k.sbuf("R_A1", [128, 8, 64], BF16, pes)
                W2 = k.sbuf("R_W2", [64, 512], BF16, pes)
                A2 = k.sbuf("R_A2", [64, 512], BF16, pes)
                Wo = k.sbuf("R_Wo", [128, 4, 1024], BF16, pes)
                loadw(Wr, wr_d[hf], ncol=512, dcol0=qo)
                loadw(Wk, wk_d[hf], ncol=512, dcol0=qo)
                loadw(Wv, wv_d[hf], ncol=512, dcol0=qo)
                loadw(Wg, wg_d[hf], ncol=512, dcol0=qo)
                k.dma("pool", ws, W1[:, :, :], w1_d[:, :, :], writes=[W1])
                k.dma("pool", ws, A1[:, :, :], a1_d[:, :, :], writes=[A1])
                k.dma("pool", ws, W2[:, :], w2_d[hf][:, qo:qo + 512], writes=[W2])
                k.dma("pool", ws, A2[:, :], a2_d[hf][:, qo:qo + 512], writes=[A2])
                loadw(Wo, wo_d[hf], nk=4, dk0=(q % 2) * 4)
                k.seal(ws, [Wr, Wk, Wv, Wg, W1, A1, W2, A2, Wo])
                hUs = [k.sbuf("R_hU%d" % i, [128, 8, T], F32, pes) for i in range(2)]
                hRs = [k.sbuf("R_hR%d" % i, [128, 8, T], F32, pes) for i in range(1)] if q > 0 else hUs
                sq = k.sbuf("R_sq", [128, 8, T], BF16, pes)
                sd = k.sbuf("R_sd", [128, T], F32, pes)
                ufp = k.sbuf("R_ufp", [128, 8, T + 1], F32, pes)
                delta = k.sbuf("R_delta", [128, 8, T], F32, pes)
                xj = [k.sbuf("R_x%d" % j, [128, 8, T], BF16, pes) for j in range(6)]
                y = k.sbuf("R_y", [128, 4, T], BF16, pes)
                xtmp = [k.sbuf("R_xtmp", [128, T], F32, pes) for _ in range(2)]
                vtok = k.sbuf("R_vtok", [128, 2, 512], BF16, pes)
                tw = k.sbuf("R_tw", [64, T], BF16, pes)
                al = k.sbuf("R_al", [64, T], BF16, pes)
                Tf = [k.sbuf("R_Tf%d" % p, [128, 64], F32, pes) for p in range(4)]
                Tbk = [k.sbuf("R_Tbk%d" % p, [128, 128], BF16, pes) for p in range(4)]
                ot = k.sbuf("R_ot", [128, 2, 1024], F32, pes) if last else None
                f32n = ["sw", "av", "rf", "gate", "kkp", "nk", "kf", "bb", "cum", "cm", "g", "gi", "vfm", "bonus"]
                b16n = ["kk2", "rk2", "ktb", "btb", "kend", "bend", "yb16", "ysq"]
                zl = list(Tbk)
                PB = []
                for si in range(2):
                    d_ = {}
                    d_["X"] = {n_: k.sbuf("R_" + n_, [128, T], F32, pes) for n_ in f32n}
                    d_["X"]["yf"] = d_["X"]["sw"]
                    d_["X"]["mean"] = d_["X"]["cum"]
                    d_["X"]["var"] = d_["X"]["gi"]
                    d_["t16"] = {n_: k.sbuf("R_" + n_, [128, T], BF16, pes) for n_ in b16n}
                    d_["krt"] = k.sbuf("R_krt", [128, 2, T], BF16, pes)
                    d_["tokA"] = k.sbuf("R_tokA", [128, 256], BF16, pes)
                    d_["tokB"] = k.sbuf("R_tokB", [128, 256], BF16, pes)
                    d_["KEZ"] = [k.sbuf("R_kez", [128, 256], BF16, pes) for j in range(2)]
                    zl += d_["KEZ"]
                    d_["ch"] = []
                    for tg in range(2):
                        c_ = {}
                        c_["Gz"] = [k.sbuf("R_Gz", [128, 2, 320], BF16, pes) for j in range(2)]
                        c_["Qz"] = [k.sbuf("R_Qz", [128, 128], BF16, pes) for j in range(2)]
                        c_["nUz"] = [k.sbuf("R_nUz", [128, 128], BF16, pes) for j in range(2)]
                        c_["WTd"] = [k.sbuf("R_WTd", [128, 64], BF16, pes) for j in range(2)]
                        c_["Qs"] = [k.sbuf("R_Q", [128, 128], BF16, pes) for j in range(2)]
                        c_["PPs"] = [k.sbuf("R_PP", [128, 256], BF16, pes) for j in range(2)]
                        c_["IP"] = [k.sbuf("R_IP", [128, 2, 64], BF16, pes) for j in range(2)]
                        c_["AKV"] = k.sbuf("R_akv", [128, 128], BF16, pes)
                        c_["UP"] = k.sbuf("R_up", [128, 128], F32, pes)
                        zl += c_["Gz"] + c_["Qz"] + c_["nUz"]
                        d_["ch"].append(c_)
                    PB.append(d_)
                for bl_ in zl:
                    k.op("pool", lambda g_, bl_=bl_: g_.memset(bl_[:, :, :] if len(bl_.t.shape) == 3 else bl_[:, :], 0.0), writes=[bl_])
                id2 = mk_b[:, M_ID2:M_ID2 + 128]
                k.op("pool", lambda g_: g_.memset(ufp[:, :, :], 0.0), writes=[ufp])
                for p in range(4):
                    k.op("pool", lambda g_, p=p: g_.memset(Tf[p][:, :], 0.0), writes=[Tf[p]])
                accb = [k.banks[0], k.banks[1]]
                xr, xw, xk, xv, xa, xg = xj

                def chain_gen(p, tg, S_):
                    c_ = S_["ch"][tg]
                    t16, krt, tokA = S_["t16"], S_["krt"], S_["tokA"]
                    Gz, Qz, WTd, Qs, PPs, IPs, AKV, UP = c_["Gz"], c_["Qz"], c_["WTd"], c_["Qs"], c_["PPs"], c_["IP"], c_["AKV"], c_["UP"]
                    for hp in range(2):
                        rs_ = slice(64 * hp, 64 * hp + 64)
                        bG = k.psum()
                        for jp in range(2):
                            cs_ = slice(tg * 128 + jp * 64, tg * 128 + jp * 64 + 64)
                            ro = slice(64 * jp, 64 * jp + 64)
                            k.mm(bG, bG[ro, 0:64], t16["ktb"][rs_, cs_], krt[rs_, 0, cs_], True, True, [t16["ktb"], krt])
                            k.mm(bG, bG[ro, 64:128], t16["ktb"][rs_, cs_], krt[rs_, 1, cs_], True, True, [t16["ktb"], krt])
                            k.mm(bG, bG[ro, 128:192], t16["btb"][rs_, cs_], krt[rs_, 0, cs_], True, True, [t16["btb"], krt])
                            k.mm(bG, bG[ro, 192:256], t16["btb"][rs_, cs_], krt[rs_, 1, cs_], True, True, [t16["btb"], krt])
                            k.mm(bG, bG[ro, 256:320], krt[rs_, 0, cs_], t16["btb"][rs_, cs_], True, True, [t16["btb"], krt])
                        for jp in range(2):
                            ro = slice(64 * jp, 64 * jp + 64)
                            k.tt("dve", Gz[jp][ro, hp, :], bG[ro, 0:320], mk_rw[ro, :], ALU.mult, [bG, mk_rw], [Gz[jp]])
                        bG.busy = False
                        yield
                    Qc = Qs[0]
                    for jp in range(2):
                        ro = slice(64 * jp, 64 * jp + 64)
                        for hp in range(2):
                            k.tt("pool", Qc[ro, 64 * hp:64 * hp + 64], id2[ro, 0:64], Gz[jp][ro, hp, 128:192], ALU.subtract, [mk_b, Gz[jp]], [Qc])
                    qi = 0

                    def emit_q(lvl_, qi_):
                        IPc = IPs[lvl_ % 2]
                        bQ = k.psum()
                        for hp in range(2):
                            for jp in range(2):
                                ro = slice(64 * jp, 64 * jp + 64)
                                k.mm(bQ, bQ[ro, hp * 64:hp * 64 + 64], IPc[ro, hp, :], Qs[qi_][ro, hp * 64:hp * 64 + 64], True, True, [IPc, Qs[qi_]])
                        if lvl_ < 5:
                            Qn = Qs[1 - qi_]
                            k.copy("act", Qn[:, :], bQ[:, 0:128], [bQ], [Qn])
                        else:
                            for jp in range(2):
                                ro = slice(64 * jp, 64 * jp + 64)
                                k.copy("act", Qz[jp][ro, :], bQ[ro, 0:128], [bQ], [Qz[jp]])
                        bQ.busy = False

                    for lvl in range(1, 6):
                        if lvl > 1:
                            emit_q(lvl - 1, qi)
                            qi = 1 - qi
                        IPc = IPs[lvl % 2]
                        bP = k.psum()
                        for hp in range(2):
                            for jp in range(2):
                                ro = slice(64 * jp, 64 * jp + 64)
                                if lvl == 1:
                                    Pm, PTm, Pb = Gz[jp][ro, hp, 256:320], Gz[jp][ro, hp, 128:192], Gz[jp]
                                else:
                                    PPc = PPs[lvl % 2]
                                    Pm, PTm, Pb = PPc[ro, hp * 128:hp * 128 + 64], PPc[ro, hp * 128 + 64:hp * 128 + 128], PPc
                                k.mm(bP, bP[ro, hp * 128:hp * 128 + 64], PTm, Pm, True, True, [Pb])
                                if lvl < 5:
                                    k.mm(bP, bP[ro, hp * 128 + 64:hp * 128 + 128], Pm, PTm, True, True, [Pb])
                        for hp in range(2):
                            k.tt("dve", IPc[:, hp, :], bP[:, hp * 128:hp * 128 + 64], id2[:, 0:64], ALU.add, [bP, mk_b], [IPc])
                        if lvl < 5:
                            PPn = PPs[(lvl + 1) % 2]
                            k.copy("act", PPn[:, :], bP[:, 0:256], [bP], [PPn])
                        bP.busy = False
                        yield
                    emit_q(5, qi)
                    qi = 1 - qi
                    yield
                    bA = k.psum()
                    for hp in range(2):
                        for jp in range(2):
                            ro = slice(64 * jp, 64 * jp + 64)
                            vc = slice(p * 128 + 64 * hp, p * 128 + 64 * hp + 64)
                            k.mm(bA, bA[ro, 64 * hp:64 * hp + 64], Gz[jp][ro, hp, 0:64], vtok[ro, tg, vc], True, True, [Gz[jp], vtok])
                    k.copy("act", AKV[:, :], bA[:, 0:128], [bA], [AKV])
                    bA.busy = False
                    bW = k.psum()
                    for jp in range(2):
                        k.mm(bW, bW[:, jp * 128:(jp + 1) * 128], tokA[:, tg * 128:(tg + 1) * 128], Qz[jp][:, :], True, True, [tokA, Qz[jp]])
                    for jp in range(2):
                        for hp in range(2):
                            rs_ = slice(64 * hp, 64 * hp + 64)
                            k.copy("dve", WTd[jp][rs_, :], bW[rs_, jp * 128 + 64 * hp:jp * 128 + 64 * hp + 64], [bW], [WTd[jp]])
                    bW.busy = False
                    yield
                    bX = k.psum()
                    for hp in range(2):
                        for jp in range(2):
                            ro = slice(64 * jp, 64 * jp + 64)
                            k.mm(bX, bX[ro, 64 * hp:64 * hp + 64], Qz[jp][ro, hp * 64:hp * 64 + 64], AKV[ro, 64 * hp:64 * hp + 64], True, True, [Qz[jp], AKV])
                    k.copy("act", UP[:, :], bX[:, 0:128], [bX], [UP])
                    bX.busy = False
                    yield

                def seq_gen(p, tg, S_, psY):
                    c_ = S_["ch"][tg]
                    X, krt, tokB, KEZ = S_["X"], S_["krt"], S_["tokB"], S_["KEZ"]
                    Gz, nUz, WTd, UP = c_["Gz"], c_["nUz"], c_["WTd"], c_["UP"]
                    for jp in range(2):
                        ro = slice(64 * jp, 64 * jp + 64)
                        c0 = tg * 128 + jp * 64
                        cs_ = slice(c0, c0 + 64)
                        bU = k.psum()
                        k.mm(bU, bU[ro, 0:128], WTd[jp][:, :], Tbk[p][:, :], True, True, [WTd[jp], Tbk[p]])
                        k.stt(nUz[jp][ro, :], bU[ro, 0:128], -1.0, UP[ro, :], ALU.mult, ALU.subtract, [bU, UP], [nUz[jp]])
                        bU.busy = False
                        k.mm(psY, psY[:, cs_], Tbk[p][:, :], krt[:, 1, cs_], True, False, [Tbk[p], krt])
                        for hp in range(2):
                            rs_ = slice(64 * hp, 64 * hp + 64)
                            vc = slice(p * 128 + 64 * hp, p * 128 + 64 * hp + 64)
                            k.mm(psY, psY[rs_, cs_], vtok[:, tg, vc], Gz[jp][:, hp, 64:128], False, False, [vtok, Gz[jp]])
                        yield
                        for hp in range(2):
                            rs_ = slice(64 * hp, 64 * hp + 64)
                            k.mm(psY, psY[rs_, cs_], nUz[jp][:, 64 * hp:64 * hp + 64], Gz[jp][:, hp, 192:256], False, True, [nUz[jp], Gz[jp]])
                        bS = k.psum()
                        for hp in range(2):
                            rs_ = slice(64 * hp, 64 * hp + 64)
                            vc = slice(p * 128 + 64 * hp, p * 128 + 64 * hp + 64)
                            hc = slice(tg * 128 + 64 * hp, tg * 128 + 64 * hp + 64)
                            k.mm(bS, bS[rs_, 0:64], KEZ[jp][:, hc], vtok[:, tg, vc], True, False, [KEZ[jp], vtok])
                            k.mm(bS, bS[rs_, 0:64], tokB[:, hc], nUz[jp][:, 64 * hp:64 * hp + 64], False, True, [tokB, nUz[jp]])
                        k.stt(Tf[p][:, :], Tf[p][:, :], X["g"][:, c0 + 63:c0 + 64], bS[:, 0:64], ALU.mult, ALU.add, [Tf[p], X["g"], bS], [Tf[p]])
                        bS.busy = False
                        for hp in range(2):
                            rs_ = slice(64 * hp, 64 * hp + 64)
                            k.copy("dve", Tbk[p][rs_, 64 * hp:64 * hp + 64], Tf[p][rs_, :], [Tf[p]], [Tbk[p]])
                        yield

                def pair_gen(p, S_):
                    pg = q * 4 + p
                    cols = slice(p * 128, (p + 1) * 128)
                    X, t16, krt, tokA, tokB, KEZ = S_["X"], S_["t16"], S_["krt"], S_["tokA"], S_["tokB"], S_["KEZ"]
                    b_rk = k.psum()
                    for kc in range(8):
                        k.mm(b_rk, b_rk[:, 0:T], Wr[:, kc, cols], xr[:, kc, :], kc == 0, kc == 7, [Wr, xr])
                    for kc in range(8):
                        k.mm(b_rk, b_rk[:, T:2 * T], Wk[:, kc, cols], xk[:, kc, :], kc == 0, kc == 7, [Wk, xk])
                    k.copy("act", X["rf"][:, :], b_rk[:, 0:T], [b_rk], [X["rf"]])
                    k.ts("dve", X["kkp"][:, :], b_rk[:, T:2 * T], C("k_k", pg), None, ALU.mult, None, [b_rk, cst], [X["kkp"]])
                    k.copy("act", X["kf"][:, :], b_rk[:, T:2 * T], [b_rk], [X["kf"]])
                    b_rk.busy = False
                    yield
                    b_gw = k.psum()
                    for kc in range(8):
                        k.mm(b_gw, b_gw[:, 0:T], Wg[:, kc, cols], xg[:, kc, :], kc == 0, kc == 7, [Wg, xg])
                    k.mm(b_gw, b_gw[:, T:2 * T], W2[:, cols], tw[:, :], True, True, [W2, tw])
                    k.act(X["sw"][:, :], b_gw[:, T:2 * T], AF.Sigmoid, [b_gw, cst], [X["sw"]], bias=C("w0", pg))
                    k.act(X["gate"][:, :], b_gw[:, 0:T], AF.Silu, [b_gw], [X["gate"]])
                    b_gw.busy = False
                    yield
                    b_av = k.psum()
                    k.mm(b_av, b_av[:, 0:T], A2[:, cols], al[:, :], True, True, [A2, al])
                    for tg in range(2):
                        k.mm(b_av, b_av[:, T + tg * 128:T + (tg + 1) * 128], vtok[:, tg, cols], ident_b, True, True, [vtok, mk_b])
                    k.act(X["av"][:, :], b_av[:, 0:T], AF.Sigmoid, [b_av, cst], [X["av"]], bias=C("a0", pg))
                    k.copy("act", X["vfm"][:, :], b_av[:, T:2 * T], [b_av], [X["vfm"]])
                    b_av.busy = False
                    yield
                    k.ts("pool", X["nk"][:, :], X["av"][:, :], C("k_a", pg), der[:, 32 + pg:33 + pg], ALU.mult, ALU.add, [X["av"], cst, der], [X["nk"]])
                    k.tt("pool", X["kf"][:, :], X["kf"][:, :], X["nk"][:, :], ALU.mult, [X["kf"], X["nk"]], [X["kf"]])
                    k.act(t16["kk2"][:, :], X["kkp"][:, :], AF.Square, [X["kkp"]], [t16["kk2"]])
                    b_n = k.psum()
                    k.mm(b_n, b_n[:, 0:T], oblk_b, t16["kk2"][:, :], True, True, [mk_b, t16["kk2"]])
                    k.act(X["nk"][:, :], b_n[:, 0:T], AF.Sqrt, [b_n], [X["nk"]])
                    k.ts("dve", X["nk"][:, :], X["nk"][:, :], 1e-12, None, ALU.max, None, [X["nk"]], [X["nk"]])
                    k.op("dve", lambda v: v.reciprocal(out=X["nk"][:, :], in_=X["nk"][:, :]), reads=[X["nk"]], writes=[X["nk"]])
                    k.tt("dve", X["kkp"][:, :], X["kkp"][:, :], X["nk"][:, :], ALU.mult, [X["kkp"], X["nk"]], [X["kkp"]])
                    k.tt("pool", X["bb"][:, :], X["kkp"][:, :], X["av"][:, :], ALU.mult, [X["kkp"], X["av"]], [X["bb"]])
                    k.stt(t16["rk2"][:, :], X["rf"][:, :], C("r_k", pg), X["kf"][:, :], ALU.mult, ALU.mult, [X["rf"], cst, X["kf"]], [t16["rk2"]])
                    k.mm(b_n, b_n[:, T:2 * T], oblk_b, t16["rk2"][:, :], True, True, [mk_b, t16["rk2"]])
                    k.tt("dve", X["bonus"][:, :], b_n[:, T:2 * T], X["vfm"][:, :], ALU.mult, [b_n, X["vfm"]], [X["bonus"]])
                    b_n.busy = False
                    yield
                    for j in range(T // 64):
                        k.op("dve", lambda v, j=j: v.tensor_tensor_scan(
                            out=X["cum"][:, j * 64:(j + 1) * 64], data0=onesf[:, 0:64], data1=X["sw"][:, j * 64:(j + 1) * 64],
                            initial=0.0, op0=ALU.mult, op1=ALU.add), reads=[onesf, X["sw"]], writes=[X["cum"]])
                    k.tt("pool", X["cm"][:, :], X["cum"][:, :], X["sw"][:, :], ALU.subtract, [X["cum"], X["sw"]], [X["cm"]])
                    k.act(X["g"][:, :], X["cum"][:, :], AF.Exp, [X["cum"]], [X["g"]], scale=-DEC)
                    k.act(X["gi"][:, :], X["cum"][:, :], AF.Exp, [X["cum"]], [X["gi"]], scale=DEC)
                    k.act(X["cm"][:, :], X["cm"][:, :], AF.Exp, [X["cm"]], [X["cm"]], scale=-DEC)
                    yield
                    k.tt("pool", krt[:, 0, :], X["kkp"][:, :], X["cm"][:, :], ALU.mult, [X["kkp"], X["cm"]], [krt])
                    k.tt("dve", krt[:, 1, :], X["rf"][:, :], X["g"][:, :], ALU.mult, [X["rf"], X["g"]], [krt])
                    k.tt("dve", X["kf"][:, :], X["kf"][:, :], X["gi"][:, :], ALU.mult, [X["kf"], X["gi"]], [X["kf"]])
                    k.tt("pool", X["bb"][:, :], X["bb"][:, :], X["gi"][:, :], ALU.mult, [X["bb"], X["gi"]], [X["bb"]])
                    k.copy("act", t16["ktb"][:, :], X["kf"][:, :], [X["kf"]], [t16["ktb"]])
                    k.copy("pool", t16["btb"][:, :], X["bb"][:, :], [X["bb"]], [t16["btb"]])
                    for j in range(T // 64):
                        sl = slice(j * 64, (j + 1) * 64)
                        ge = X["g"][:, j * 64 + 63:j * 64 + 64]
                        k.ts("dve", t16["kend"][:, sl], X["kf"][:, sl], ge, None, ALU.mult, None, [X["kf"], X["g"]], [t16["kend"]])
                        k.ts("pool", t16["bend"][:, sl], X["bb"][:, sl], ge, None, ALU.mult, None, [X["bb"], X["g"]], [t16["bend"]])
                    yield
                    b_t = k.psum()
                    for tg in range(2):
                        k.mm(b_t, b_t[:, tg * 128:(tg + 1) * 128], krt[:, 0, tg * 128:(tg + 1) * 128], ident_b, True, True, [krt, mk_b])
                    for tg in range(2):
                        k.mm(b_t, b_t[:, 256 + tg * 128:256 + (tg + 1) * 128], t16["kend"][:, tg * 128:(tg + 1) * 128], ident_b, True, True, [t16["kend"], mk_b])
                    k.copy("act", tokA[:, 0:256], b_t[:, 0:256], [b_t], [tokA])
                    for jp in range(2):
                        ro = slice(64 * jp, 64 * jp + 64)
                        k.copy("act", KEZ[jp][ro, :], b_t[ro, 256:512], [b_t], [KEZ[jp]])
                    b_t.busy = False
                    b_t2 = k.psum()
                    for tg in range(2):
                        k.mm(b_t2, b_t2[:, tg * 128:(tg + 1) * 128], t16["bend"][:, tg * 128:(tg + 1) * 128], ident_b, True, True, [t16["bend"], mk_b])
                    k.copy("dve", tokB[:, :], b_t2[:, 0:256], [b_t2], [tokB])
                    b_t2.busy = False
                    yield
                    psY = accb[p % 2]
                    yield from rr([chain_gen(p, 0, S_), chain_gen(p, 1, S_)])
                    yield from seq_gen(p, 0, S_, psY)
                    yield from seq_gen(p, 1, S_, psY)
                    k.copy("act", X["yf"][:, :], psY[:, 0:T], [psY], [X["yf"]])
                    k.act(t16["ysq"][:, :], psY[:, 0:T], AF.Square, [psY], [t16["ysq"]])
                    k.copy("pool", t16["yb16"][:, :], X["yf"][:, :], [X["yf"]], [t16["yb16"]])
                    bM = k.psum()
                    k.mm(bM, bM[:, 0:T], oblk_b, t16["yb16"][:, :], True, True, [mk_b, t16["yb16"]])
                    k.mm(bM, bM[:, T:2 * T], oblk_b, t16["ysq"][:, :], True, True, [mk_b, t16["ysq"]])
                    k.ts("dve", X["mean"][:, :], bM[:, 0:T], 1.0 / 64, None, ALU.mult, None, [bM], [X["mean"]])
                    k.tt("pool", X["var"][:, :], X["mean"][:, :], X["mean"][:, :], ALU.mult, [X["mean"]], [X["var"]])
                    k.stt(X["var"][:, :], bM[:, T:2 * T], 1.0 / 64, X["var"][:, :], ALU.mult, ALU.subtract, [bM, X["var"]], [X["var"]])
                    bM.busy = False
                    yield
                    k.act(X["var"][:, :], X["var"][:, :], AF.Sqrt, [X["var"], der], [X["var"]], bias=eps_gn)
                    k.op("dve", lambda v: v.reciprocal(out=X["var"][:, :], in_=X["var"][:, :]), reads=[X["var"]], writes=[X["var"]])
                    k.tt("pool", X["yf"][:, :], X["yf"][:, :], X["mean"][:, :], ALU.subtract, [X["yf"], X["mean"]], [X["yf"]])
                    k.tt("dve", X["yf"][:, :], X["yf"][:, :], X["var"][:, :], ALU.mult, [X["yf"], X["var"]], [X["yf"]])
                    k.ts("pool", X["yf"][:, :], X["yf"][:, :], C("lnx_g", pg), C("lnx_b", pg), ALU.mult, ALU.add, [X["yf"], cst], [X["yf"]])
                    k.tt("pool", X["yf"][:, :], X["yf"][:, :], X["bonus"][:, :], ALU.add, [X["yf"], X["bonus"]], [X["yf"]])
                    k.tt("dve", y[:, p, :], X["yf"][:, :], X["gate"][:, :], ALU.mult, [X["yf"], X["gate"]], [y])
                    yield

                for ti in range(nt):
                    t0 = ti * T
                    hU = hUs[ti % 2]
                    hR = hRs[ti % len(hRs)]
                    k.dma("sp", ldS[ti % 2], hU[:, :, :], h1_d[:, :, t0:t0 + T], reads=[dr["h1"][ti]], writes=[hU])
                    if q > 0:
                        k.dma("sp", ldR[ti % 2], hR[:, :, :], res_d[:, :, t0:t0 + T], reads=[res_tr[ti]], writes=[hR])
                    k.copy("pool", ufp[:, :, 0:1], ufp[:, :, T:T + 1], [ufp], [ufp])
                    emit_norm((sq, sd), hU, "c_g", ufp, 1, True)
                    k.tt("pool", delta[:, :, :], ufp[:, :, 0:T], ufp[:, :, 1:T + 1], ALU.subtract, [ufp], [delta])
                    for j in range(6):
                        for c in range(8):
                            mo = _off["mu"] + j * 8 + c
                            if j in (1, 4) or (j == 3 and c < 4):
                                tb_ = xtmp[(j * 8 + c) % 2]
                                k.act(tb_[:, :], delta[:, c, :], AF.Copy, [delta, cst], [tb_], scale=cst[:, mo:mo + 1])
                                k.tt("pool", xj[j][:, c, :], tb_[:, :], ufp[:, c, 1:T + 1], ALU.add, [tb_, ufp], [xj[j]])
                            else:
                                k.stt(xj[j][:, c, :], delta[:, c, :], cst[:, mo:mo + 1], ufp[:, c, 1:T + 1], ALU.mult, ALU.add,
                                      [delta, cst, ufp], [xj[j]])
                    bl = k.psum()
                    for kc in range(8):
                        k.mm(bl, bl[0:64, 0:T], W1[:, kc, :], xw[:, kc, :], kc == 0, kc == 7, [W1, xw])
                    for kc in range(8):
                        k.mm(bl, bl[0:64, T:2 * T], A1[:, kc, :], xa[:, kc, :], kc == 0, kc == 7, [A1, xa])
                    k.act(tw[:, :], bl[0:64, 0:T], AF.Tanh, [bl], [tw])
                    k.copy("act", al[:, :], bl[0:64, T:2 * T], [bl], [al])
                    bl.busy = False
                    for tg in range(2):
                        bank = k.psum()
                        for kc in range(8):
                            k.mm(bank, bank[:, 0:512], xv[:, kc, tg * 128:(tg + 1) * 128], Wv[:, kc, :], kc == 0, kc == 7, [xv, Wv])
                        k.copy("act" if tg == 0 else "dve", vtok[:, tg, :], bank[:, 0:512], [bank], [vtok])
                        bank.busy = False
                    for pp in range(2):
                        for _ in rr([pair_gen(2 * pp, PB[0]), pair_gen(2 * pp + 1, PB[1])]):
                            pass
                    for dc in range(8):
                        bank = k.psum()
                        for m in range(4):
                            k.mm(bank, bank[:, 0:T], Wo[:, m, dc * 128:(dc + 1) * 128], y[:, m, :], m == 0, m == 3, [Wo, y])
                        k.tt("dve", hR[:, dc, :], hR[:, dc, :], bank[:, 0:T], ALU.add, [hR, bank], [hR])
                        bank.busy = False
                    if not last:
                        k.dma("sp", stS[ti % 2], dst_d[:, :, t0:t0 + T], hR[:, :, :], reads=[hR], writes=[dst_tr[ti]])
                    else:
                        emit_norm((sq, sd), hR, "fin_g", delta, 0, True)
                        for tg in range(2):
                            for cq in range(2):
                                bank = k.psum()
                                for c4 in range(4):
                                    c = cq * 4 + c4
                                    k.op("pe", lambda pe, bank=bank, c4=c4, c=c, tg=tg: pe.transpose(
                                        out=bank[:, c4 * 128:(c4 + 1) * 128], in_=delta[:, c, tg * 128:(tg + 1) * 128], identity=mk_f[:, :]),
                                        reads=[delta, mk_f], writes=[bank])
                                k.copy("act" if cq == 0 else "dve", ot[:, tg, cq * 512:(cq + 1) * 512], bank[:, 0:512], [bank], [ot])
                                bank.busy = False
                        k.dma("sp", stS[ti % 2], out_d[t0:t0 + T, :].rearrange("(g p) d -> p g d", p=128), ot[:, :, :], reads=[ot], writes=[])
                k.barrier()

        if "A" in passes:
            pass_A()
        if "B" in passes:
            pass_B()
        if "C" in passes:
            trs = [[Buf(None, "tr%d_%d" % (i, j)) for j in range(NT)] for i in range(3)]
            pass_R(0, None, None, hC_d, trs[0], False)
            pass_R(1, hC_d, trs[0], hA_d, trs[1], False)
            pass_R(2, hA_d, trs[1], hC_d, trs[2], False)
            pass_R(3, hC_d, trs[2], None, None, True)
        k.barrier()
        print("instructions:", k.nins)
    return nc


def _fm(w):
    n = w.shape[1]
    return np.ascontiguousarray(w.reshape(8, 128, n).transpose(1, 0, 2))


def _cv(v):
    return np.ascontiguousarray(v.reshape(-1, 128).T)


def make_masks():
    m = np.zeros((128, NMASK), np.float32)
    p = np.arange(128)[:, None]
    c = np.arange(128)[None, :]
    m[:, M_ID:M_ID + 128] = (p == c)
    m[:, M_ONES:M_ONES + 128] = 1.0
    m[:, M_OBLK:M_OBLK + 128] = (p // 64 == c // 64)
    m[:, M_HG:M_HG + 128] = (p // 64 == c // 64) & (p <= c)
    s = np.arange(128)[:, None] % 64
    t = np.arange(64)[None, :]
    strict = (s < t).astype(np.float32)
    incl = (s <= t).astype(np.float32)
    m[:, M_RW:M_RW + 64] = strict
    m[:, M_RW + 64:M_RW + 128] = incl
    m[:, M_RW + 128:M_RW + 192] = strict
    m[:, M_RW + 192:M_RW + 256] = incl
    m[:, M_RW + 256:M_RW + 320] = (t < s)
    eye = (s == t).astype(np.float32)
    m[:, M_ID2:M_ID2 + 64] = eye
    m[:, M_ID2 + 64:M_ID2 + 128] = eye
    return m


def prep_inputs(inp, b):
    f = lambda a: np.asarray(a, np.float32)
    cst = np.zeros((128, NCONST), np.float32)

    def put(name, arr):
        cst[:, _off[name]:_off[name] + arr.shape[1]] = arr

    put("ab_g", _cv(f(inp["ab_norm_g"])[0]))
    put("c_g", _cv(f(inp["c_norm_g"])[0]))
    put("fin_g", _cv(f(inp["final_g"])))
    cw = f(inp["rg_conv_w"])[0]
    put("conv_w", np.ascontiguousarray(cw.reshape(4, 8, 128).transpose(2, 1, 0).reshape(128, 32)))
    put("conv_b", _cv(f(inp["rg_conv_b"])[0]))
    put("b_a", _cv(f(inp["rg_b_a"])[0]))
    put("b_x", _cv(f(inp["rg_b_x"])[0]))
    put("lam", _cv(f(inp["rg_lambda"])[0]))
    put("lb0", _cv(f(inp["hg_lb_logits"])[0]))
    put("lb1", _cv(f(inp["hg_lb_logits"])[1]))
    put("hg_g", f(inp["hg_norm_g"])[0].reshape(128, 1))
    mu = f(inp["c_mu"])[0]
    put("mu", np.ascontiguousarray(mu.reshape(6, 8, 128).transpose(2, 0, 1).reshape(128, 48)))
    put("w0", _cv(f(inp["c_w0"])[0]))
    put("a0", _cv(f(inp["c_a0"])[0]))
    put("k_k", _cv(f(inp["c_k_k"])[0]))
    put("k_a", _cv(f(inp["c_k_a"])[0]))
    put("r_k", _cv(f(inp["c_r_k"])[0].reshape(-1)))
    put("lnx_g", _cv(f(inp["c_lnx_g"])[0]))
    put("lnx_b", _cv(f(inp["c_lnx_b"])[0]))
    win = f(inp["ab_w_in"])[0]
    wout = f(inp["ab_w_out"])[0]
    m = {
        "x": np.ascontiguousarray(f(inp["x"])[b]),
        "consts": cst,
        "masks": make_masks(),
        "wAin": _fm(win[:, 0:2048]),
        "rgwa": np.ascontiguousarray(f(inp["rg_w_a"])[0].transpose(1, 0, 2)),
        "rgwx": np.ascontiguousarray(f(inp["rg_w_x"])[0].transpose(1, 0, 2)),
        "wAout": _fm(wout[0:1024]),
        "wBin": _fm(win[:, 2048:6144]),
        "wBout": _fm(wout[1024:2048]),
        "w1": _fm(f(inp["c_w1"])[0]),
        "a1": _fm(f(inp["c_a1"])[0]),
    }
    for h in range(2):
        sl = slice(h * 1024, (h + 1) * 1024)
        m["wr%d" % h] = _fm(f(inp["c_w_r"])[0][:, sl])
        m["wk%d" % h] = _fm(f(inp["c_w_k"])[0][:, sl])
        m["wv%d" % h] = _fm(f(inp["c_w_v"])[0][:, sl])
        m["wg%d" % h] = _fm(f(inp["c_w_g"])[0][:, sl])
        m["wo%d" % h] = _fm(f(inp["c_w_o"])[0][sl, :])
        m["w2_%d" % h] = np.ascontiguousarray(f(inp["c_w2"])[0][:, sl])
        m["a2_%d" % h] = np.ascontiguousarray(f(inp["c_a2"])[0][:, sl])
    return m


def kernel(**inputs):
    nc = build()
    in_maps = [prep_inputs(inputs, i % 4) for i in range(8)]
    res = run_bass_kernel_spmd(nc, in_maps, core_ids=list(range(8)))
    out = np.stack([np.asarray(res.results[i]["out"], np.float32) for i in range(4)], axis=0)
    return out
```

```python
import contextlib
import numpy as np
import concourse.bass as bass
import concourse.mybir as mybir
from concourse.bass_utils import run_bass_kernel_spmd
from concourse.alu_op_type import AluOpType as ALU

F32 = mybir.dt.float32
BF16 = mybir.dt.bfloat16
AF = mybir.ActivationFunctionType

S = 4096
D = 1024
T = 256
NT = S // T
RMS_EPS = 1e-6
GN_EPS = 64e-5
DEC = 0.6065306597126334
SAME_SYNC = True
import os
BSTOP = float(os.environ.get('BSTOP', '99'))
RSTOP = float(os.environ.get('RSTOP', '99'))

_off = {}
_n = 0
for _name, _w in [("ab_g", 8), ("c_g", 8), ("fin_g", 8), ("conv_w", 32), ("conv_b", 8), ("b_a", 8), ("b_x", 8),
                  ("lam", 8), ("lb0", 8), ("lb1", 8), ("hg_g", 1), ("mu", 48), ("w0", 16), ("a0", 16),
                  ("k_k", 16), ("k_a", 16), ("r_k", 16), ("lnx_g", 16), ("lnx_b", 16)]:
    _off[_name] = _n
    _n += _w
NCONST = _n
M_ID = 0
M_ONES = 128
M_OBLK = 256
M_HG = 384
M_RW = 512
M_ID2 = 832
NMASK = 960


class Buf:
    __slots__ = ("t", "name", "w", "r", "busy")

    def __init__(self, t, name):
        self.t = t
        self.name = name
        self.w = None
        self.r = {}
        self.busy = False

    def __getitem__(self, idx):
        return self.t[idx]


class Stream:
    def __init__(self, sem, key):
        self.sem = sem
        self.key = key
        self.count = 0


class KB:
    def __init__(self, nc, es):
        self.nc = nc
        self.es = es
        self.eng = {"pe": nc.tensor, "act": nc.scalar, "dve": nc.vector, "pool": nc.gpsimd, "sp": nc.sync}
        self.st = {k: Stream(es.enter_context(nc.semaphore(k + "_s")), k) for k in self.eng}
        self.waited = {k: {} for k in self.eng}
        self.dstreams = []
        self.banks = []
        self.bank_i = 0
        self.nins = 0

    def dma_stream(self, name):
        s = Stream(self.es.enter_context(self.nc.semaphore(name)), name)
        self.dstreams.append(s)
        return s

    def sbuf(self, name, shape, dtype, es=None):
        es = es or self.es
        self.nins += 0
        self.uid = getattr(self, "uid", 0) + 1
        name = "%s_u%d" % (name, self.uid)
        return Buf(es.enter_context(self.nc.sbuf_tensor(name, list(shape), dtype)), name)

    def init_psum(self):
        for i in range(8):
            self.banks.append(Buf(self.es.enter_context(self.nc.psum_tensor("bank%d" % i, [128, 512], F32)), "bank%d" % i))

    def psum(self):
        b = self.banks[2 + self.bank_i % 6]
        self.bank_i += 1
        assert not b.busy, "psum bank still in use: " + b.name
        b.busy = True
        return b

    def _wait(self, e, sv):
        s, v = sv
        w = self.waited[e]
        if w.get(s.key, 0) >= v:
            return
        w[s.key] = v
        self.eng[e].wait_ge(s.sem, v)

    def _deps(self, e, reads, writes):
        for b in reads:
            if b.name.startswith("bank"):
                for sv in b.r.values():
                    if sv[0].key != e:
                        self._wait(e, sv)
            if b.w is not None:
                if b.w[0].key == e:
                    if (SAME_SYNC is True and e != "pe") or (SAME_SYNC == "pool" and e == "pool"):
                        self._wait(e, b.w)
                else:
                    self._wait(e, b.w)
        for b in writes:
            if b.w is not None and b.w[0].key != e:
                self._wait(e, b.w)
            for sv in b.r.values():
                if sv[0].key != e:
                    self._wait(e, sv)

    def op(self, e, fn, reads=(), writes=()):
        self._deps(e, reads, writes)
        ins = fn(self.eng[e])
        s = self.st[e]
        s.count += 1
        ins.then_inc(s.sem, 1)
        self.nins += 1
        for b in reads:
            b.r[s.key] = (s, s.count)
        for b in writes:
            b.w = (s, s.count)
            b.r = {}

    def dma(self, q, stream, out_ap, in_ap, reads=(), writes=()):
        self._deps(q, reads, writes)
        ins = self.eng[q].dma_start(out=out_ap, in_=in_ap)
        stream.count += 16
        ins.then_inc(stream.sem, 16)
        self.nins += 1
        for b in reads:
            b.r[stream.key] = (stream, stream.count)
        for b in writes:
            b.w = (stream, stream.count)
            b.r = {}

    def seal(self, stream, bufs):
        for b in bufs:
            b.w = (stream, stream.count)

    def barrier(self):
        allst = list(self.st.values()) + self.dstreams
        for e in self.eng:
            for s in allst:
                if s.key != e and s.count > 0:
                    self._wait(e, (s, s.count))

    def mm(self, bank, out_ap, lhsT, rhs, start, stop, reads):
        self.op("pe", lambda pe: pe.matmul(out_ap, lhsT=lhsT, rhs=rhs, start=start, stop=stop), reads=reads, writes=[bank])

    def act(self, out_ap, in_ap, func, reads, writes, bias=None, scale=None):
        kw = {}
        if bias is not None:
            kw["bias"] = bias
        if scale is not None:
            kw["scale"] = scale
        self.op("act", lambda a: a.activation(out=out_ap, in_=in_ap, func=func, **kw), reads=reads, writes=writes)

    def tt(self, e, out_ap, in0, in1, op, reads, writes):
        self.op(e, lambda v: v.tensor_tensor(out=out_ap, in0=in0, in1=in1, op=op), reads=reads, writes=writes)

    def ts(self, e, out_ap, in0, s1, s2, op0, op1, reads, writes):
        if op1 is None:
            self.op(e, lambda v: v.tensor_scalar(out=out_ap, in0=in0, scalar1=s1, scalar2=None, op0=op0), reads=reads, writes=writes)
        else:
            self.op(e, lambda v: v.tensor_scalar(out=out_ap, in0=in0, scalar1=s1, scalar2=s2, op0=op0, op1=op1), reads=reads, writes=writes)

    def stt(self, out_ap, in0, scalar, in1, op0, op1, reads, writes):
        self.op("dve", lambda v: v.scalar_tensor_tensor(out=out_ap, in0=in0, scalar=scalar, in1=in1, op0=op0, op1=op1), reads=reads, writes=writes)

    def copy(self, e, out_ap, in_ap, reads, writes):
        if e == "act":
            self.op("act", lambda a: a.copy(out=out_ap, in_=in_ap), reads=reads, writes=writes)
        else:
            self.op(e, lambda v: v.tensor_copy(out=out_ap, in_=in_ap), reads=reads, writes=writes)


def build(nt=NT, passes="ABCD", debug=False):
    nc = bass.Bass("TRN2", target_bir_lowering=False)

    def din(name, shape):
        return nc.dram_tensor(name, list(shape), F32, kind="ExternalInput").ap()

    x_d = din("x", [S, D])
    consts_d = din("consts", [128, NCONST])
    masks_d = din("masks", [128, NMASK])
    wAin_d = din("wAin", [128, 8, 2048])
    rgwa_d = din("rgwa", [128, 8, 128])
    rgwx_d = din("rgwx", [128, 8, 128])
    wAout_d = din("wAout", [128, 8, 1024])
    wBin_d = din("wBin", [128, 8, 4096])
    wBout_d = din("wBout", [128, 8, 1024])
    wr_d = [din("wr%d" % h, [128, 8, 1024]) for h in range(2)]
    wk_d = [din("wk%d" % h, [128, 8, 1024]) for h in range(2)]
    wv_d = [din("wv%d" % h, [128, 8, 1024]) for h in range(2)]
    wg_d = [din("wg%d" % h, [128, 8, 1024]) for h in range(2)]
    wo_d = [din("wo%d" % h, [128, 8, 1024]) for h in range(2)]
    w1_d = din("w1", [128, 8, 64])
    a1_d = din("a1", [128, 8, 64])
    w2_d = [din("w2_%d" % h, [64, 1024]) for h in range(2)]
    a2_d = [din("a2_%d" % h, [64, 1024]) for h in range(2)]
    out_d = nc.dram_tensor("out", [S, D], F32, kind="ExternalOutput").ap()
    skind = "ExternalOutput" if debug else "Internal"
    h0_d = nc.dram_tensor("h0fm", [128, 8, S], F32, kind=skind).ap()
    hA_d = nc.dram_tensor("hAfm", [128, 8, S], F32, kind=skind).ap()
    h1_d = nc.dram_tensor("h1fm", [128, 8, S], F32, kind=skind).ap()
    hC_d = nc.dram_tensor("hCfm", [128, 8, S], F32, kind=skind).ap()
    xj_d = [nc.dram_tensor("xjfm%d" % j, [128, 8, S], BF16, kind="Internal").ap() for j in range(6)]
    lo_d = [nc.dram_tensor("lofm%d" % j, [64, S], BF16, kind="Internal").ap() for j in range(2)]

    es = contextlib.ExitStack()
    with es:
        k = KB(nc, es)
        k.init_psum()
        cst = k.sbuf("cst", [128, NCONST], F32)
        der = k.sbuf("der", [128, 64], F32)
        mk_f = k.sbuf("mk_f", [128, 128], F32)
        mk_b = k.sbuf("mk_b", [128, NMASK], BF16)
        mk_rw = k.sbuf("mk_rw", [128, 320], F32)
        mk_hg = k.sbuf("mk_hg", [128, 128], F32)
        zeros = k.sbuf("zeros", [128, 64], F32)
        onesf = k.sbuf("onesf", [128, 64], F32)
        cs = k.dma_stream("cstream")
        k.dma("sp", cs, cst[:, :], consts_d[:, :], writes=[cst])
        k.dma("sp", cs, mk_f[:, :], masks_d[:, M_ID:M_ID + 128], writes=[mk_f])
        k.dma("sp", cs, mk_rw[:, :], masks_d[:, M_RW:M_RW + 320], writes=[mk_rw])
        k.dma("sp", cs, mk_hg[:, :], masks_d[:, M_HG:M_HG + 128], writes=[mk_hg])
        cs2 = k.dma_stream("cstream2")
        k.dma("pool", cs2, mk_b[:, :], masks_d[:, :], writes=[mk_b])
        k.seal(cs, [cst, mk_f, mk_rw, mk_hg])
        k.op("pool", lambda g: g.memset(zeros[:, :], 0.0), writes=[zeros])
        k.op("pool", lambda g: g.memset(onesf[:, :], 1.0), writes=[onesf])
        ident_b = mk_b[:, M_ID:M_ID + 128]
        ones_b = mk_b[:, M_ONES:M_ONES + 128]
        oblk_b = mk_b[:, M_OBLK:M_OBLK + 128]

        def C(name, j=0, w=1):
            o = _off[name] + j
            return cst[:, o:o + w]

        k.act(der[:, 0:8], C("lam", 0, 8), AF.Exp, [cst], [der], scale=-1.0)
        k.act(der[:, 0:8], der[:, 0:8], AF.Ln, [der, onesf], [der], bias=onesf[:, 0:1])
        k.ts("dve", der[:, 8:16], der[:, 0:8], -16.0, None, ALU.mult, None, [der], [der])
        k.ts("dve", der[:, 0:8], der[:, 0:8], -8.0, None, ALU.mult, None, [der], [der])
        k.tt("dve", der[:, 16:24], C("lb0", 0, 8), C("lb1", 0, 8), ALU.subtract, [cst], [der])
        k.act(der[:, 16:24], der[:, 16:24], AF.Sigmoid, [der], [der])
        k.ts("dve", der[:, 24:32], der[:, 16:24], -1.0, 1.0, ALU.mult, ALU.add, [der], [der])
        k.ts("dve", der[:, 32:48], C("k_a", 0, 16), -1.0, 1.0, ALU.mult, ALU.add, [cst], [der])
        k.op("pool", lambda g: g.memset(der[:, 48:49], RMS_EPS), writes=[der])
        k.op("pool", lambda g: g.memset(der[:, 49:50], GN_EPS), writes=[der])
        eps_rms = der[:, 48:49]
        eps_gn = der[:, 49:50]

        ws = k.dma_stream("wstream")
        ldS = [k.dma_stream("ldU0"), k.dma_stream("ldU1")]
        ldR = [k.dma_stream("ldR0"), k.dma_stream("ldR1")]
        stS = [k.dma_stream("st0"), k.dma_stream("st1")]
        stS2 = [k.dma_stream("st2_0"), k.dma_stream("st2_1")]
        ldX = k.dma_stream("ldX")
        stX = k.dma_stream("stX")
        xtr = [Buf(None, "xtr%d" % i) for i in range(NT)]
        dr = {nm: [Buf(None, "%s_%d" % (nm, i)) for i in range(NT)] for nm in ("h0", "hA", "h1", "hC")}

        def loadw(buf, dram, nk=8, ncol=None, dcol0=0, dk0=0):
            ncol = ncol or dram.shape[2]
            for kc in range(nk):
                for c0 in range(0, ncol, 1024):
                    c1 = min(ncol, c0 + 1024)
                    k.dma("pool", ws, buf[:, kc, c0:c1], dram[:, dk0 + kc, dcol0 + c0:dcol0 + c1], writes=[buf])

        def emit_norm(pes_bufs, hU, gname, outbuf, col0, fp32_out):
            sq, sd = pes_bufs
            k.act(sq[:, :, :], hU[:, :, :], AF.Square, [hU], [sq])
            bank = k.psum()
            for c in range(8):
                k.mm(bank, bank[:, 0:T], ones_b, sq[:, c, :], c == 0, c == 7, [sq, mk_b])
            k.act(sd[:, :], bank[:, 0:T], AF.Sqrt, [bank, der], [sd], bias=eps_rms, scale=1.0 / D)
            bank.busy = False
            k.op("dve", lambda v: v.reciprocal(out=sd[:, :], in_=sd[:, :]), reads=[sd], writes=[sd])
            for c in range(8):
                k.stt(outbuf[:, c, col0:col0 + T], hU[:, c, :], C(gname, c), sd[:, :], ALU.mult, ALU.mult,
                      [hU, cst, sd], [outbuf])

        def out_proj(Wout, y, hR):
            for dc in range(8):
                bank = k.psum()
                for m in range(8):
                    k.mm(bank, bank[:, 0:T], Wout[:, m, dc * 128:(dc + 1) * 128], y[:, m, :], m == 0, m == 7, [Wout, y])
                k.tt("dve", hR[:, dc, :], hR[:, dc, :], bank[:, 0:T], ALU.add, [hR, bank], [hR])
                bank.busy = False

        def rr(gens):
            gens = list(gens)
            while gens:
                for g_ in list(gens):
                    try:
                        next(g_)
                    except StopIteration:
                        gens.remove(g_)
                yield

        def pass_A():
            with contextlib.ExitStack() as pes:
                Win = k.sbuf("A_Win", [128, 8, 2048], BF16, pes)
                Wa = k.sbuf("A_Wa", [128, 8, 128], BF16, pes)
                Wx = k.sbuf("A_Wx", [128, 8, 128], BF16, pes)
                Wout = k.sbuf("A_Wout", [128, 8, 1024], BF16, pes)
                loadw(Win, wAin_d)
                k.dma("pool", ws, Wa[:, :, :], rgwa_d[:, :, :], writes=[Wa])
                k.dma("pool", ws, Wx[:, :, :], rgwx_d[:, :, :], writes=[Wx])
                loadw(Wout, wAout_d)
                k.seal(ws, [Win, Wa, Wx, Wout])
                xts = [k.sbuf("A_xt%d" % i, [128, 2, 1024], F32, pes) for i in range(2)]
                hUs = [k.sbuf("A_hU%d" % i, [128, 8, T], F32, pes) for i in range(2)]
                sq = k.sbuf("A_sq", [128, 8, T], BF16, pes)
                sd = k.sbuf("A_sd", [128, T], F32, pes)
                u = k.sbuf("A_u", [128, 8, T], BF16, pes)
                y = k.sbuf("A_y", [128, 8, T], BF16, pes)
                xaext = [k.sbuf("A_xa%d" % c, [128, T + 3], F32, pes) for c in range(8)]
                carry = [k.sbuf("A_cy%d" % c, [128, 1], F32, pes) for c in range(8)]
                xc = [k.sbuf("A_xc%d" % i, [128, T], F32, pes) for i in range(4)]
                xcb = [k.sbuf("A_xcb%d" % i, [128, T], BF16, pes) for i in range(4)]
                sr = [k.sbuf("A_sr%d" % i, [128, T], F32, pes) for i in range(4)]
                si = [k.sbuf("A_si%d" % i, [128, T], F32, pes) for i in range(4)]
                av = [k.sbuf("A_av%d" % i, [128, T], F32, pes) for i in range(4)]
                mv = [k.sbuf("A_mv%d" % i, [128, T], F32, pes) for i in range(4)]
                uu = [k.sbuf("A_uu%d" % i, [128, T], F32, pes) for i in range(4)]
                hh = [k.sbuf("A_hh%d" % i, [128, T], F32, pes) for i in range(4)]
                sg = [k.sbuf("A_sg%d" % i, [128, T], F32, pes) for i in range(4)]
                for c in range(8):
                    k.op("pool", lambda g, c=c: g.memset(xaext[c][:, :], 0.0), writes=[xaext[c]])
                    k.op("pool", lambda g, c=c: g.memset(carry[c][:, :], 0.0), writes=[carry[c]])

                for ti in range(nt):
                    t0 = ti * T
                    xt = xts[ti % 2]
                    hU = hUs[ti % 2]
                    k.dma("sp", ldS[ti % 2], xt[:, :, :], x_d[t0:t0 + T, :].rearrange("(g p) d -> p g d", p=128), writes=[xt])
                    for cp in range(4):
                        bank = k.psum()
                        for cc in range(2):
                            c = cp * 2 + cc
                            for tg in range(2):
                                o = cc * 256 + tg * 128
                                k.op("pe", lambda pe, o=o, c=c, tg=tg, bank=bank: pe.transpose(
                                    out=bank[:, o:o + 128], in_=xt[:, tg, c * 128:(c + 1) * 128], identity=mk_f[:, :]),
                                    reads=[xt, mk_f], writes=[bank])
                        for cc in range(2):
                            c = cp * 2 + cc
                            k.copy("act" if cc == 0 else "dve", hU[:, c, :], bank[:, cc * 256:cc * 256 + 256], [bank], [hU])
                        bank.busy = False
                    k.dma("sp", stS2[ti % 2], h0_d[:, :, t0:t0 + T], hU[:, :, :], reads=[hU], writes=[dr["h0"][ti]])
                    emit_norm((sq, sd), hU, "ab_g", u, 0, False)
                    def blockA(c):
                        i2 = c % 4
                        b1 = k.psum()
                        for kc in range(8):
                            k.mm(b1, b1[:, 0:T], Win[:, kc, c * 128:(c + 1) * 128], u[:, kc, :], kc == 0, kc == 7, [Win, u])
                        for kc in range(8):
                            k.mm(b1, b1[:, T:2 * T], Win[:, kc, 1024 + c * 128:1024 + (c + 1) * 128], u[:, kc, :], kc == 0, kc == 7, [Win, u])
                        xe = xaext[c]
                        k.copy("pool", xe[:, 0:3], xe[:, T:T + 3], [xe], [xe])
                        k.copy("act", xe[:, 3:T + 3], b1[:, 0:T], [b1], [xe])
                        k.act(sg[i2][:, :], b1[:, T:2 * T], AF.Silu, [b1], [sg[i2]])
                        b1.busy = False
                        yield
                        cw = _off["conv_w"] + c * 4
                        k.ts("dve", xc[i2][:, :], xe[:, 3:T + 3], cst[:, cw + 3:cw + 4], C("conv_b", c), ALU.mult, ALU.add, [xe, cst], [xc[i2]])
                        for j in (2, 1, 0):
                            k.stt(xc[i2][:, :], xe[:, j:j + T], cst[:, cw + j:cw + j + 1], xc[i2][:, :], ALU.mult, ALU.add, [xe, cst, xc[i2]], [xc[i2]])
                        k.copy("pool", xcb[i2][:, :], xc[i2][:, :], [xc[i2]], [xcb[i2]])
                        yield
                        b2 = k.psum()
                        k.mm(b2, b2[:, 0:T], Wa[:, c, :], xcb[i2][:, :], True, True, [Wa, xcb[i2]])
                        k.mm(b2, b2[:, T:2 * T], Wx[:, c, :], xcb[i2][:, :], True, True, [Wx, xcb[i2]])
                        k.act(sr[i2][:, :], b2[:, 0:T], AF.Sigmoid, [b2, cst], [sr[i2]], bias=C("b_a", c))
                        k.act(si[i2][:, :], b2[:, T:2 * T], AF.Sigmoid, [b2, cst], [si[i2]], bias=C("b_x", c))
                        b2.busy = False
                        yield
                        k.act(av[i2][:, :], sr[i2][:, :], AF.Exp, [sr[i2], der], [av[i2]], scale=der[:, c:c + 1])
                        k.act(mv[i2][:, :], sr[i2][:, :], AF.Exp, [sr[i2], der], [mv[i2]], scale=der[:, 8 + c:9 + c])
                        k.tt("pool", uu[i2][:, :], si[i2][:, :], xc[i2][:, :], ALU.mult, [si[i2], xc[i2]], [uu[i2]])
                        yield
                        k.act(mv[i2][:, :], mv[i2][:, :], AF.Sqrt, [mv[i2], onesf], [mv[i2]], bias=onesf[:, 0:1], scale=-1.0)
                        yield
                        k.tt("dve", uu[i2][:, :], uu[i2][:, :], mv[i2][:, :], ALU.mult, [uu[i2], mv[i2]], [uu[i2]])
                        k.op("dve", lambda v, i2=i2, c=c: v.tensor_tensor_scan(
                            out=hh[i2][:, :], data0=av[i2][:, :], data1=uu[i2][:, :], initial=carry[c][:, 0:1],
                            op0=ALU.mult, op1=ALU.add), reads=[av[i2], uu[i2], carry[c]], writes=[hh[i2]])
                        k.copy("pool", carry[c][:, 0:1], hh[i2][:, T - 1:T], [hh[i2]], [carry[c]])
                        k.tt("dve", y[:, c, :], hh[i2][:, :], sg[i2][:, :], ALU.mult, [hh[i2], sg[i2]], [y])
                        yield

                    for c4 in range(2):
                        for _ in rr([blockA(c4 * 4 + i) for i in range(4)]):
                            pass
                    out_proj(Wout, y, hU)
                    k.dma("sp", stS[ti % 2], hA_d[:, :, t0:t0 + T], hU[:, :, :], reads=[hU], writes=[dr["hA"][ti]])
                k.barrier()

        def pass_B():
            with contextlib.ExitStack() as pes:
                Win = k.sbuf("B_Win", [128, 8, 4096], BF16, pes)
                Wout = k.sbuf("B_Wout", [128, 8, 1024], BF16, pes)
                loadw(Win, wBin_d)
                loadw(Wout, wBout_d)
                k.seal(ws, [Win, Wout])
                hUs = [k.sbuf("B_hU%d" % i, [128, 8, T], F32, pes) for i in range(2)]
                hRs = [k.sbuf("B_hR%d" % i, [128, 8, T], F32, pes) for i in range(2)]
                sq = k.sbuf("B_sq", [128, 8, T], BF16, pes)
                sd = k.sbuf("B_sd", [128, T], F32, pes)
                u = k.sbuf("B_u", [128, 8, T], BF16, pes)
                y = k.sbuf("B_y", [128, 8, T], BF16, pes)
                vtok = k.sbuf("B_vtok", [128, 2, 1024], BF16, pes)
                stf = [k.sbuf("B_stf%d" % h, [128, 128], F32, pes) for h in range(8)]
                stb = [k.sbuf("B_stb%d" % h, [128, 128], BF16, pes) for h in range(8)]
                NB = 4
                qf = [k.sbuf("B_qf%d" % i, [128, T], F32, pes) for i in range(NB)]
                sig = [k.sbuf("B_sig%d" % i, [128, T], F32, pes) for i in range(NB)]
                ff = [k.sbuf("B_f%d" % i, [128, T], F32, pes) for i in range(NB)]
                kf = [k.sbuf("B_k%d" % i, [128, T], F32, pes) for i in range(NB)]
                Pc = [k.sbuf("B_P%d" % i, [128, T], F32, pes) for i in range(NB)]
                Pi = [k.sbuf("B_Pi%d" % i, [128, T], F32, pes) for i in range(NB)]
                qd = [k.sbuf("B_qd%d" % i, [128, T], BF16, pes) for i in range(NB)]
                kif = [k.sbuf("B_kif%d" % i, [128, T], F32, pes) for i in range(NB)]
                kib = [k.sbuf("B_kib%d" % i, [128, T], BF16, pes) for i in range(NB)]
                keb = [k.sbuf("B_keb%d" % i, [128, T], BF16, pes) for i in range(NB)]
                scm = [k.sbuf("B_scm%d" % i, [128, 128], BF16, pes) for i in range(NB)]
                ket = [k.sbuf("B_ket%d" % i, [128, 128], BF16, pes) for i in range(NB)]
                osq = [k.sbuf("B_osq%d" % i, [128, T], BF16, pes) for i in range(NB)]
                ors = [k.sbuf("B_ors%d" % i, [128, T], F32, pes) for i in range(NB)]
                o1 = [k.sbuf("B_o1%d" % i, [128, T], F32, pes) for i in range(NB)]
                sgb = [k.sbuf("B_sg%d" % i, [128, T], F32, pes) for i in range(NB)]
                for h in range(8):
                    k.op("pool", lambda g, h=h: g.memset(stf[h][:, :], 0.0), writes=[stf[h]])
                    k.op("pool", lambda g, h=h: g.memset(stb[h][:, :], 0.0), writes=[stb[h]])
                accb = [k.banks[0], k.banks[1]]
                for ti in range(nt):
                    t0 = ti * T
                    hU = hUs[ti % 2]
                    hR = hRs[ti % 2]
                    k.dma("sp", ldS[ti % 2], hU[:, :, :], h0_d[:, :, t0:t0 + T], reads=[dr["h0"][ti]], writes=[hU])
                    k.dma("sp", ldR[ti % 2], hR[:, :, :], hA_d[:, :, t0:t0 + T], reads=[dr["hA"][ti]], writes=[hR])
                    emit_norm((sq, sd), hU, "ab_g", u, 0, False)
                    for tg in range(2):
                        for cg in range(2):
                            bank = k.psum()
                            for kc in range(8):
                                k.mm(bank, bank[:, 0:512], u[:, kc, tg * 128:(tg + 1) * 128],
                                     Win[:, kc, 2048 + cg * 512:2048 + (cg + 1) * 512], kc == 0, kc == 7, [u, Win])
                            k.copy("act" if cg == 0 else "dve", vtok[:, tg, cg * 512:(cg + 1) * 512], bank[:, 0:512], [bank], [vtok])
                            bank.busy = False
                    def headB(h):
                        i2 = h % NB
                        bo = accb[(h % 4) // 2]
                        oc = (h % 2) * 256
                        bq = k.psum()
                        for kc in range(8):
                            k.mm(bq, bq[:, 0:T], Win[:, kc, h * 128:(h + 1) * 128], u[:, kc, :], kc == 0, kc == 7, [Win, u])
                        for kc in range(8):
                            k.mm(bq, bq[:, T:2 * T], Win[:, kc, 1024 + h * 128:1024 + (h + 1) * 128], u[:, kc, :], kc == 0, kc == 7, [Win, u])
                        k.act(sig[i2][:, :], bq[:, T:2 * T], AF.Sigmoid, [bq], [sig[i2]])
                        k.copy("act", qf[i2][:, :], bq[:, 0:T], [bq], [qf[i2]])
                        bq.busy = False
                        yield
                        k.ts("dve", ff[i2][:, :], sig[i2][:, :], der[:, 24 + h:25 + h], der[:, 16 + h:17 + h], ALU.mult, ALU.add, [sig[i2], der], [ff[i2]])
                        k.ts("pool", kf[i2][:, :], ff[i2][:, :], -1.0, 1.0, ALU.mult, ALU.add, [ff[i2]], [kf[i2]])
                        for j in range(T // 64):
                            k.op("dve", lambda v, i2=i2, j=j: v.tensor_tensor_scan(
                                out=Pc[i2][:, j * 64:(j + 1) * 64], data0=ff[i2][:, j * 64:(j + 1) * 64], data1=zeros[:, 0:64],
                                initial=1.0, op0=ALU.mult, op1=ALU.add), reads=[ff[i2], zeros], writes=[Pc[i2]])
                        yield
                        k.op("dve", lambda v, i2=i2: v.reciprocal(out=Pi[i2][:, :], in_=Pc[i2][:, :]), reads=[Pc[i2]], writes=[Pi[i2]])
                        k.tt("pool", qd[i2][:, :], qf[i2][:, :], Pc[i2][:, :], ALU.mult, [qf[i2], Pc[i2]], [qd[i2]])
                        yield
                        k.tt("pool", kif[i2][:, :], kf[i2][:, :], Pi[i2][:, :], ALU.mult, [kf[i2], Pi[i2]], [kif[i2]])
                        k.copy("pool", kib[i2][:, :], kif[i2][:, :], [kif[i2]], [kib[i2]])
                        for j in range(T // 64):
                            k.ts("dve", keb[i2][:, j * 64:(j + 1) * 64], kif[i2][:, j * 64:(j + 1) * 64],
                                 Pc[i2][:, j * 64 + 63:j * 64 + 64], None, ALU.mult, None, [kif[i2], Pc[i2]], [keb[i2]])
                        yield
                        for tg in range(2):
                            c0 = tg * 128
                            bs = k.psum()
                            k.mm(bs, bs[:, 0:128], kib[i2][:, c0:c0 + 128], qd[i2][:, c0:c0 + 128], True, True, [kib[i2], qd[i2]])
                            k.mm(bs, bs[:, 128:256], keb[i2][:, c0:c0 + 128], ident_b, True, True, [keb[i2], mk_b])
                            k.tt("dve", scm[i2][:, :], bs[:, 0:128], mk_hg[:, :], ALU.mult, [bs, mk_hg], [scm[i2]])
                            k.copy("act", ket[i2][:, :], bs[:, 128:256], [bs], [ket[i2]])
                            bs.busy = False
                            yield
                            for jp in range(2):
                                cj = c0 + jp * 64
                                r0 = jp * 64
                                k.mm(bo, bo[:, oc + cj:oc + cj + 64], vtok[:, tg, h * 128:(h + 1) * 128], scm[i2][:, r0:r0 + 64], True, False, [vtok, scm[i2]])
                                k.mm(bo, bo[:, oc + cj:oc + cj + 64], stb[h][:, :], qd[i2][:, cj:cj + 64], False, True, [stb[h], qd[i2]])
                                bst = k.psum()
                                k.mm(bst, bst[:, 0:128], ket[i2][r0:r0 + 64, :], vtok[r0:r0 + 64, tg, h * 128:(h + 1) * 128], True, True, [ket[i2], vtok])
                                k.stt(stf[h][:, :], stf[h][:, :], Pc[i2][:, cj + 63:cj + 64], bst[:, 0:128], ALU.mult, ALU.add, [stf[h], Pc[i2], bst], [stf[h]])
                                bst.busy = False
                                k.copy("act", stb[h][:, :], stf[h][:, :], [stf[h]], [stb[h]])
                                yield
                        bg = k.psum()
                        for kc in range(8):
                            k.mm(bg, bg[:, 0:T], Win[:, kc, 3072 + h * 128:3072 + (h + 1) * 128], u[:, kc, :], kc == 0, kc == 7, [Win, u])
                        k.act(sgb[i2][:, :], bg[:, 0:T], AF.Silu, [bg], [sgb[i2]])
                        k.act(osq[i2][:, :], bo[:, oc:oc + T], AF.Square, [bo], [osq[i2]])
                        k.copy("act", o1[i2][:, :], bo[:, oc:oc + T], [bo], [o1[i2]])
                        k.mm(bg, bg[:, T:2 * T], ones_b, osq[i2][:, :], True, True, [mk_b, osq[i2]])
                        k.act(ors[i2][:, :], bg[:, T:2 * T], AF.Sqrt, [bg, der], [ors[i2]], bias=eps_rms, scale=1.0 / 128)
                        bg.busy = False
                        yield
                        k.op("dve", lambda v, i2=i2: v.reciprocal(out=ors[i2][:, :], in_=ors[i2][:, :]), reads=[ors[i2]], writes=[ors[i2]])
                        k.tt("pool", o1[i2][:, :], o1[i2][:, :], ors[i2][:, :], ALU.mult, [o1[i2], ors[i2]], [o1[i2]])
                        k.stt(y[:, h, :], o1[i2][:, :], C("hg_g"), sgb[i2][:, :], ALU.mult, ALU.mult, [o1[i2], cst, sgb[i2]], [y])
                        yield

                    for h4 in range(2):
                        for _ in rr([headB(h4 * 4 + i) for i in range(4)]):
                            pass
                    out_proj(Wout, y, hR)
                    k.dma("sp", stS[ti % 2], h1_d[:, :, t0:t0 + T], hR[:, :, :], reads=[hR], writes=[dr["h1"][ti]])
                k.barrier()


        def pass_R(q, res_d, res_tr, dst_d, dst_tr, last):
            hf, qo = q // 2, (q % 2) * 512
            with contextlib.ExitStack() as pes:
                Wr = k.sbuf("R_Wr", [128, 8, 512], BF16, pes)
                Wk = k.sbuf("R_Wk", [128, 8, 512], BF16, pes)
                Wv = k.sbuf("R_Wv", [128, 8, 512], BF16, pes)
                Wg = k.sbuf("R_Wg", [128, 8, 512], BF16, pes)
                W1 = k.sbuf("R_W1", [128, 8, 64], BF16, pes)
                A1 = k.sbuf("R_A1", [128, 8, 64], BF16, pes)
                W2 = k.sbuf("R_W2", [64, 512], BF16, pes)
                A2 = k.sbuf("R_A2", [64, 512], BF16, pes)
                Wo = k.sbuf("R_Wo", [128, 4, 1024], BF16, pes)
                loadw(Wr, wr_d[hf], ncol=512, dcol0=qo)
                loadw(Wk, wk_d[hf], ncol=512, dcol0=qo)
                loadw(Wv, wv_d[hf], ncol=512, dcol0=qo)
                loadw(Wg, wg_d[hf], ncol=512, dcol0=qo)
                k.dma("pool", ws, W1[:, :, :], w1_d[:, :, :], writes=[W1])
                k.dma("pool", ws, A1[:, :, :], a1_d[:, :, :], writes=[A1])
                k.dma("pool", ws, W2[:, :], w2_d[hf][:, qo:qo + 512], writes=[W2])
                k.dma("pool", ws, A2[:, :], a2_d[hf][:, qo:qo + 512], writes=[A2])
                loadw(Wo, wo_d[hf], nk=4, dk0=(q % 2) * 4)
                k.seal(ws, [Wr, Wk, Wv, Wg, W1, A1, W2, A2, Wo])
                hUs = [k.sbuf("R_hU%d" % i, [128, 8, T], F32, pes) for i in range(2)] if q == 0 else None
                hRs = [k.sbuf("R_hR%d" % i, [128, 8, T], F32, pes) for i in range(2)] if q > 0 else hUs
                sq = k.sbuf("R_sq", [128, 8, T], BF16, pes)
                sd = k.sbuf("R_sd", [128, T], F32, pes)
                ufp = k.sbuf("R_ufp", [128, 8, T + 1], F32, pes)
                delta = k.sbuf("R_delta", [128, 8, T], F32, pes)
                xj = [k.sbuf("R_x%d" % j, [128, 8, T], BF16, pes) for j in range(6)]
                y = k.sbuf("R_y", [128, 4, T], BF16, pes)
                xtmp = [k.sbuf("R_xtmp", [128, T], F32, pes) for _ in range(2)]
                vtok = k.sbuf("R_vtok", [128, 2, 512], BF16, pes)
                tw = k.sbuf("R_tw", [64, T], BF16, pes)
                al = k.sbuf("R_al", [64, T], BF16, pes)
                Tf = [k.sbuf("R_Tf%d" % p, [128, 64], F32, pes) for p in range(4)]
                Tbk = [k.sbuf("R_Tbk%d" % p, [128, 128], BF16, pes) for p in range(4)]
                ot = k.sbuf("R_ot", [128, 2, 1024], F32, pes) if last else None
                f32n = ["sw", "av", "rf", "gate", "kkp", "nk", "kf", "bb", "cum", "cm", "g", "gi", "vfm", "bonus"]
                b16n = ["kk2", "rk2", "ktb", "btb", "kend", "bend", "yb16", "ysq"]
                zl = list(Tbk)
                PB = []
                for si in range(2):
                    d_ = {}
                    d_["X"] = {n_: k.sbuf("R_" + n_, [128, T], F32, pes) for n_ in f32n}
                    d_["X"]["yf"] = d_["X"]["sw"]
                    d_["X"]["mean"] = d_["X"]["cum"]
                    d_["X"]["var"] = d_["X"]["gi"]
                    d_["t16"] = {n_: k.sbuf("R_" + n_, [128, T], BF16, pes) for n_ in b16n}
                    d_["krt"] = k.sbuf("R_krt", [128, 2, T], BF16, pes)
                    d_["tokA"] = k.sbuf("R_tokA", [128, 256], BF16, pes)
                    d_["tokB"] = k.sbuf("R_tokB", [128, 256], BF16, pes)
                    d_["KEZ"] = [k.sbuf("R_kez", [128, 256], BF16, pes) for j in range(2)]
                    zl += d_["KEZ"]
                    d_["ch"] = []
                    for tg in range(2):
                        c_ = {}
                        c_["Gz"] = [k.sbuf("R_Gz", [128, 2, 320], BF16, pes) for j in range(2)]
                        c_["Qz"] = [k.sbuf("R_Qz", [128, 128], BF16, pes) for j in range(2)]
                        c_["nUz"] = [k.sbuf("R_nUz", [128, 128], BF16, pes) for j in range(2)]
                        c_["WTd"] = [k.sbuf("R_WTd", [128, 64], BF16, pes) for j in range(2)]
                        c_["Qs"] = [k.sbuf("R_Q", [128, 128], BF16, pes) for j in range(2)]
                        c_["PPs"] = [k.sbuf("R_PP", [128, 256], BF16, pes) for j in range(2)]
                        c_["IP"] = [k.sbuf("R_IP", [128, 2, 64], BF16, pes) for j in range(2)]
                        c_["AKV"] = k.sbuf("R_akv", [128, 128], BF16, pes)
                        c_["UP"] = k.sbuf("R_up", [128, 128], F32, pes)
                        zl += c_["Gz"] + c_["Qz"] + c_["nUz"]
                        d_["ch"].append(c_)
                    PB.append(d_)
                for bl_ in zl:
                    k.op("pool", lambda g_, bl_=bl_: g_.memset(bl_[:, :, :] if len(bl_.t.shape) == 3 else bl_[:, :], 0.0), writes=[bl_])
                id2 = mk_b[:, M_ID2:M_ID2 + 128]
                k.op("pool", lambda g_: g_.memset(ufp[:, :, :], 0.0), writes=[ufp])
                for p in range(4):
                    k.op("pool", lambda g_, p=p: g_.memset(Tf[p][:, :], 0.0), writes=[Tf[p]])
                accb = [k.banks[0], k.banks[1]]
                xr, xw, xk, xv, xa, xg = xj

                def chain_gen(p, tg, S_):
                    c_ = S_["ch"][tg]
                    t16, krt, tokA = S_["t16"], S_["krt"], S_["tokA"]
                    Gz, Qz, WTd, Qs, PPs, IPs, AKV, UP = c_["Gz"], c_["Qz"], c_["WTd"], c_["Qs"], c_["PPs"], c_["IP"], c_["AKV"], c_["UP"]
                    for hp in range(2):
                        rs_ = slice(64 * hp, 64 * hp + 64)
                        bG = k.psum()
                        for jp in range(2):
                            cs_ = slice(tg * 128 + jp * 64, tg * 128 + jp * 64 + 64)
                            ro = slice(64 * jp, 64 * jp + 64)
                            k.mm(bG, bG[ro, 0:64], t16["ktb"][rs_, cs_], krt[rs_, 0, cs_], True, True, [t16["ktb"], krt])
                            k.mm(bG, bG[ro, 64:128], t16["ktb"][rs_, cs_], krt[rs_, 1, cs_], True, True, [t16["ktb"], krt])
                            k.mm(bG, bG[ro, 128:192], t16["btb"][rs_, cs_], krt[rs_, 0, cs_], True, True, [t16["btb"], krt])
                            k.mm(bG, bG[ro, 192:256], t16["btb"][rs_, cs_], krt[rs_, 1, cs_], True, True, [t16["btb"], krt])
                            k.mm(bG, bG[ro, 256:320], krt[rs_, 0, cs_], t16["btb"][rs_, cs_], True, True, [t16["btb"], krt])
                        for jp in range(2):
                            ro = slice(64 * jp, 64 * jp + 64)
                            k.tt("dve", Gz[jp][ro, hp, :], bG[ro, 0:320], mk_rw[ro, :], ALU.mult, [bG, mk_rw], [Gz[jp]])
                        bG.busy = False
                        yield
                    Qc = Qs[0]
                    for jp in range(2):
                        ro = slice(64 * jp, 64 * jp + 64)
                        for hp in range(2):
                            k.tt("pool", Qc[ro, 64 * hp:64 * hp + 64], id2[ro, 0:64], Gz[jp][ro, hp, 128:192], ALU.subtract, [mk_b, Gz[jp]], [Qc])
                    qi = 0

                    def emit_q(lvl_, qi_):
                        IPc = IPs[lvl_ % 2]
                        bQ = k.psum()
                        for hp in range(2):
                            for jp in range(2):
                                ro = slice(64 * jp, 64 * jp + 64)
                                k.mm(bQ, bQ[ro, hp * 64:hp * 64 + 64], IPc[ro, hp, :], Qs[qi_][ro, hp * 64:hp * 64 + 64], True, True, [IPc, Qs[qi_]])
                        if lvl_ < 5:
                            Qn = Qs[1 - qi_]
                            k.copy("act", Qn[:, :], bQ[:, 0:128], [bQ], [Qn])
                        else:
                            for jp in range(2):
                                ro = slice(64 * jp, 64 * jp + 64)
                                k.copy("act", Qz[jp][ro, :], bQ[ro, 0:128], [bQ], [Qz[jp]])
                        bQ.busy = False

                    for lvl in range(1, 6):
                        if lvl > 1:
                            emit_q(lvl - 1, qi)
                            qi = 1 - qi
                        IPc = IPs[lvl % 2]
                        bP = k.psum()
                        for hp in range(2):
                            for jp in range(2):
                                ro = slice(64 * jp, 64 * jp + 64)
                                if lvl == 1:
                                    Pm, PTm, Pb = Gz[jp][ro, hp, 256:320], Gz[jp][ro, hp, 128:192], Gz[jp]
                                else:
                                    PPc = PPs[lvl % 2]
                                    Pm, PTm, Pb = PPc[ro, hp * 128:hp * 128 + 64], PPc[ro, hp * 128 + 64:hp * 128 + 128], PPc
                                k.mm(bP, bP[ro, hp * 128:hp * 128 + 64], PTm, Pm, True, True, [Pb])
                                if lvl < 5:
                                    k.mm(bP, bP[ro, hp * 128 + 64:hp * 128 + 128], Pm, PTm, True, True, [Pb])
                        for hp in range(2):
                            k.tt("dve", IPc[:, hp, :], bP[:, hp * 128:hp * 128 + 64], id2[:, 0:64], ALU.add, [bP, mk_b], [IPc])
                        if lvl < 5:
                            PPn = PPs[(lvl + 1) % 2]
                            k.copy("act", PPn[:, :], bP[:, 0:256], [bP], [PPn])
                        bP.busy = False
                        yield
                    emit_q(5, qi)
                    qi = 1 - qi
                    yield
                    bA = k.psum()
                    for hp in range(2):
                        for jp in range(2):
                            ro = slice(64 * jp, 64 * jp + 64)
                            vc = slice(p * 128 + 64 * hp, p * 128 + 64 * hp + 64)
                            k.mm(bA, bA[ro, 64 * hp:64 * hp + 64], Gz[jp][ro, hp, 0:64], vtok[ro, tg, vc], True, True, [Gz[jp], vtok])
                    k.copy("act", AKV[:, :], bA[:, 0:128], [bA], [AKV])
                    bA.busy = False
                    bW = k.psum()
                    for jp in range(2):
                        k.mm(bW, bW[:, jp * 128:(jp + 1) * 128], tokA[:, tg * 128:(tg + 1) * 128], Qz[jp][:, :], True, True, [tokA, Qz[jp]])
                    for jp in range(2):
                        for hp in range(2):
                            rs_ = slice(64 * hp, 64 * hp + 64)
                            k.copy("dve", WTd[jp][rs_, :], bW[rs_, jp * 128 + 64 * hp:jp * 128 + 64 * hp + 64], [bW], [WTd[jp]])
                    bW.busy = False
                    yield
                    bX = k.psum()
                    for hp in range(2):
                        for jp in range(2):
                            ro = slice(64 * jp, 64 * jp + 64)
                            k.mm(bX, bX[ro, 64 * hp:64 * hp + 64], Qz[jp][ro, hp * 64:hp * 64 + 64], AKV[ro, 64 * hp:64 * hp + 64], True, True, [Qz[jp], AKV])
                    k.copy("act", UP[:, :], bX[:, 0:128], [bX], [UP])
                    bX.busy = False
                    yield

                def seq_gen(p, tg, S_, psY):
                    c_ = S_["ch"][tg]
                    X, krt, tokB, KEZ = S_["X"], S_["krt"], S_["tokB"], S_["KEZ"]
                    Gz, nUz, WTd, UP = c_["Gz"], c_["nUz"], c_["WTd"], c_["UP"]
                    for jp in range(2):
                        ro = slice(64 * jp, 64 * jp + 64)
                        c0 = tg * 128 + jp * 64
                        cs_ = slice(c0, c0 + 64)
                        bU = k.psum()
                        k.mm(bU, bU[ro, 0:128], WTd[jp][:, :], Tbk[p][:, :], True, True, [WTd[jp], Tbk[p]])
                        k.stt(nUz[jp][ro, :], bU[ro, 0:128], -1.0, UP[ro, :], ALU.mult, ALU.subtract, [bU, UP], [nUz[jp]])
                        bU.busy = False
                        k.mm(psY, psY[:, cs_], Tbk[p][:, :], krt[:, 1, cs_], True, False, [Tbk[p], krt])
                        for hp in range(2):
                            rs_ = slice(64 * hp, 64 * hp + 64)
                            vc = slice(p * 128 + 64 * hp, p * 128 + 64 * hp + 64)
                            k.mm(psY, psY[rs_, cs_], vtok[:, tg, vc], Gz[jp][:, hp, 64:128], False, False, [vtok, Gz[jp]])
                        yield
                        for hp in range(2):
                            rs_ = slice(64 * hp, 64 * hp + 64)
                            k.mm(psY, psY[rs_, cs_], nUz[jp][:, 64 * hp:64 * hp + 64], Gz[jp][:, hp, 192:256], False, True, [nUz[jp], Gz[jp]])
                        bS = k.psum()
                        for hp in range(2):
                            rs_ = slice(64 * hp, 64 * hp + 64)
                            vc = slice(p * 128 + 64 * hp, p * 128 + 64 * hp + 64)
                            hc = slice(tg * 128 + 64 * hp, tg * 128 + 64 * hp + 64)
                            k.mm(bS, bS[rs_, 0:64], KEZ[jp][:, hc], vtok[:, tg, vc], True, False, [KEZ[jp], vtok])
                            k.mm(bS, bS[rs_, 0:64], tokB[:, hc], nUz[jp][:, 64 * hp:64 * hp + 64], False, True, [tokB, nUz[jp]])
                        k.stt(Tf[p][:, :], Tf[p][:, :], X["g"][:, c0 + 63:c0 + 64], bS[:, 0:64], ALU.mult, ALU.add, [Tf[p], X["g"], bS], [Tf[p]])
                        bS.busy = False
                        for hp in range(2):
                            rs_ = slice(64 * hp, 64 * hp + 64)
                            k.copy("dve", Tbk[p][rs_, 64 * hp:64 * hp + 64], Tf[p][rs_, :], [Tf[p]], [Tbk[p]])
                        yield

                def pair_gen(p, S_):
                    pg = q * 4 + p
                    cols = slice(p * 128, (p + 1) * 128)
                    X, t16, krt, tokA, tokB, KEZ = S_["X"], S_["t16"], S_["krt"], S_["tokA"], S_["tokB"], S_["KEZ"]
                    b_rk = k.psum()
                    for kc in range(8):
                        k.mm(b_rk, b_rk[:, 0:T], Wr[:, kc, cols], xr[:, kc, :], kc == 0, kc == 7, [Wr, xr])
                    for kc in range(8):
                        k.mm(b_rk, b_rk[:, T:2 * T], Wk[:, kc, cols], xk[:, kc, :], kc == 0, kc == 7, [Wk, xk])
                    k.copy("act", X["rf"][:, :], b_rk[:, 0:T], [b_rk], [X["rf"]])
                    k.ts("dve", X["kkp"][:, :], b_rk[:, T:2 * T], C("k_k", pg), None, ALU.mult, None, [b_rk, cst], [X["kkp"]])
                    k.copy("act", X["kf"][:, :], b_rk[:, T:2 * T], [b_rk], [X["kf"]])
                    b_rk.busy = False
                    yield
                    b_gw = k.psum()
                    for kc in range(8):
                        k.mm(b_gw, b_gw[:, 0:T], Wg[:, kc, cols], xg[:, kc, :], kc == 0, kc == 7, [Wg, xg])
                    k.mm(b_gw, b_gw[:, T:2 * T], W2[:, cols], tw[:, :], True, True, [W2, tw])
                    k.act(X["sw"][:, :], b_gw[:, T:2 * T], AF.Sigmoid, [b_gw, cst], [X["sw"]], bias=C("w0", pg))
                    k.act(X["gate"][:, :], b_gw[:, 0:T], AF.Silu, [b_gw], [X["gate"]])
                    b_gw.busy = False
                    yield
                    b_av = k.psum()
                    k.mm(b_av, b_av[:, 0:T], A2[:, cols], al[:, :], True, True, [A2, al])
                    for tg in range(2):
                        k.mm(b_av, b_av[:, T + tg * 128:T + (tg + 1) * 128], vtok[:, tg, cols], ident_b, True, True, [vtok, mk_b])
                    k.act(X["av"][:, :], b_av[:, 0:T], AF.Sigmoid, [b_av, cst], [X["av"]], bias=C("a0", pg))
                    k.copy("act", X["vfm"][:, :], b_av[:, T:2 * T], [b_av], [X["vfm"]])
                    b_av.busy = False
                    yield
                    k.ts("pool", X["nk"][:, :], X["av"][:, :], C("k_a", pg), der[:, 32 + pg:33 + pg], ALU.mult, ALU.add, [X["av"], cst, der], [X["nk"]])
                    k.tt("pool", X["kf"][:, :], X["kf"][:, :], X["nk"][:, :], ALU.mult, [X["kf"], X["nk"]], [X["kf"]])
                    k.act(t16["kk2"][:, :], X["kkp"][:, :], AF.Square, [X["kkp"]], [t16["kk2"]])
                    b_n = k.psum()
                    k.mm(b_n, b_n[:, 0:T], oblk_b, t16["kk2"][:, :], True, True, [mk_b, t16["kk2"]])
                    k.act(X["nk"][:, :], b_n[:, 0:T], AF.Sqrt, [b_n], [X["nk"]])
                    k.ts("dve", X["nk"][:, :], X["nk"][:, :], 1e-12, None, ALU.max, None, [X["nk"]], [X["nk"]])
                    k.op("dve", lambda v: v.reciprocal(out=X["nk"][:, :], in_=X["nk"][:, :]), reads=[X["nk"]], writes=[X["nk"]])
                    k.tt("dve", X["kkp"][:, :], X["kkp"][:, :], X["nk"][:, :], ALU.mult, [X["kkp"], X["nk"]], [X["kkp"]])
                    k.tt("pool", X["bb"][:, :], X["kkp"][:, :], X["av"][:, :], ALU.mult, [X["kkp"], X["av"]], [X["bb"]])
                    k.stt(t16["rk2"][:, :], X["rf"][:, :], C("r_k", pg), X["kf"][:, :], ALU.mult, ALU.mult, [X["rf"], cst, X["kf"]], [t16["rk2"]])
                    k.mm(b_n, b_n[:, T:2 * T], oblk_b, t16["rk2"][:, :], True, True, [mk_b, t16["rk2"]])
                    k.tt("dve", X["bonus"][:, :], b_n[:, T:2 * T], X["vfm"][:, :], ALU.mult, [b_n, X["vfm"]], [X["bonus"]])
                    b_n.busy = False
                    yield
                    for j in range(T // 64):
                        k.op("dve", lambda v, j=j: v.tensor_tensor_scan(
                            out=X["cum"][:, j * 64:(j + 1) * 64], data0=onesf[:, 0:64], data1=X["sw"][:, j * 64:(j + 1) * 64],
                            initial=0.0, op0=ALU.mult, op1=ALU.add), reads=[onesf, X["sw"]], writes=[X["cum"]])
                    k.tt("pool", X["cm"][:, :], X["cum"][:, :], X["sw"][:, :], ALU.subtract, [X["cum"], X["sw"]], [X["cm"]])
                    k.act(X["g"][:, :], X["cum"][:, :], AF.Exp, [X["cum"]], [X["g"]], scale=-DEC)
                    k.act(X["gi"][:, :], X["cum"][:, :], AF.Exp, [X["cum"]], [X["gi"]], scale=DEC)
                    k.act(X["cm"][:, :], X["cm"][:, :], AF.Exp, [X["cm"]], [X["cm"]], scale=-DEC)
                    yield
                    k.tt("pool", krt[:, 0, :], X["kkp"][:, :], X["cm"][:, :], ALU.mult, [X["kkp"], X["cm"]], [krt])
                    k.tt("dve", krt[:, 1, :], X["rf"][:, :], X["g"][:, :], ALU.mult, [X["rf"], X["g"]], [krt])
                    k.tt("dve", X["kf"][:, :], X["kf"][:, :], X["gi"][:, :], ALU.mult, [X["kf"], X["gi"]], [X["kf"]])
                    k.tt("pool", X["bb"][:, :], X["bb"][:, :], X["gi"][:, :], ALU.mult, [X["bb"], X["gi"]], [X["bb"]])
                    k.copy("act", t16["ktb"][:, :], X["kf"][:, :], [X["kf"]], [t16["ktb"]])
                    k.copy("pool", t16["btb"][:, :], X["bb"][:, :], [X["bb"]], [t16["btb"]])
                    for j in range(T // 64):
                        sl = slice(j * 64, (j + 1) * 64)
                        ge = X["g"][:, j * 64 + 63:j * 64 + 64]
                        k.ts("dve", t16["kend"][:, sl], X["kf"][:, sl], ge, None, ALU.mult, None, [X["kf"], X["g"]], [t16["kend"]])
                        k.ts("pool", t16["bend"][:, sl], X["bb"][:, sl], ge, None, ALU.mult, None, [X["bb"], X["g"]], [t16["bend"]])
                    yield
                    b_t = k.psum()
                    for tg in range(2):
                        k.mm(b_t, b_t[:, tg * 128:(tg + 1) * 128], krt[:, 0, tg * 128:(tg + 1) * 128], ident_b, True, True, [krt, mk_b])
                    for tg in range(2):
                        k.mm(b_t, b_t[:, 256 + tg * 128:256 + (tg + 1) * 128], t16["kend"][:, tg * 128:(tg + 1) * 128], ident_b, True, True, [t16["kend"], mk_b])
                    k.copy("act", tokA[:, 0:256], b_t[:, 0:256], [b_t], [tokA])
                    for jp in range(2):
                        ro = slice(64 * jp, 64 * jp + 64)
                        k.copy("act", KEZ[jp][ro, :], b_t[ro, 256:512], [b_t], [KEZ[jp]])
                    b_t.busy = False
                    b_t2 = k.psum()
                    for tg in range(2):
                        k.mm(b_t2, b_t2[:, tg * 128:(tg + 1) * 128], t16["bend"][:, tg * 128:(tg + 1) * 128], ident_b, True, True, [t16["bend"], mk_b])
                    k.copy("dve", tokB[:, :], b_t2[:, 0:256], [b_t2], [tokB])
                    b_t2.busy = False
                    yield
                    psY = accb[p % 2]
                    yield from rr([chain_gen(p, 0, S_), chain_gen(p, 1, S_)])
                    yield from seq_gen(p, 0, S_, psY)
                    yield from seq_gen(p, 1, S_, psY)
                    k.copy("act", X["yf"][:, :], psY[:, 0:T], [psY], [X["yf"]])
                    k.act(t16["ysq"][:, :], psY[:, 0:T], AF.Square, [psY], [t16["ysq"]])
                    k.copy("pool", t16["yb16"][:, :], X["yf"][:, :], [X["yf"]], [t16["yb16"]])
                    bM = k.psum()
                    k.mm(bM, bM[:, 0:T], oblk_b, t16["yb16"][:, :], True, True, [mk_b, t16["yb16"]])
                    k.mm(bM, bM[:, T:2 * T], oblk_b, t16["ysq"][:, :], True, True, [mk_b, t16["ysq"]])
                    k.ts("dve", X["mean"][:, :], bM[:, 0:T], 1.0 / 64, None, ALU.mult, None, [bM], [X["mean"]])
                    k.tt("pool", X["var"][:, :], X["mean"][:, :], X["mean"][:, :], ALU.mult, [X["mean"]], [X["var"]])
                    k.stt(X["var"][:, :], bM[:, T:2 * T], 1.0 / 64, X["var"][:, :], ALU.mult, ALU.subtract, [bM, X["var"]], [X["var"]])
                    bM.busy = False
                    yield
                    k.act(X["var"][:, :], X["var"][:, :], AF.Sqrt, [X["var"], der], [X["var"]], bias=eps_gn)
                    k.op("dve", lambda v: v.reciprocal(out=X["var"][:, :], in_=X["var"][:, :]), reads=[X["var"]], writes=[X["var"]])
                    k.tt("pool", X["yf"][:, :], X["yf"][:, :], X["mean"][:, :], ALU.subtract, [X["yf"], X["mean"]], [X["yf"]])
                    k.tt("dve", X["yf"][:, :], X["yf"][:, :], X["var"][:, :], ALU.mult, [X["yf"], X["var"]], [X["yf"]])
                    k.ts("pool", X["yf"][:, :], X["yf"][:, :], C("lnx_g", pg), C("lnx_b", pg), ALU.mult, ALU.add, [X["yf"], cst], [X["yf"]])
                    k.tt("pool", X["yf"][:, :], X["yf"][:, :], X["bonus"][:, :], ALU.add, [X["yf"], X["bonus"]], [X["yf"]])
                    k.tt("dve", y[:, p, :], X["yf"][:, :], X["gate"][:, :], ALU.mult, [X["yf"], X["gate"]], [y])
                    yield

                for ti in range(nt):
                    t0 = ti * T
                    hU = hUs[ti % 2] if q == 0 else None
                    hR = hRs[ti % len(hRs)]
                    if q > 0:
                        k.dma("sp", ldR[ti % 2], hR[:, :, :], res_d[:, :, t0:t0 + T], reads=[res_tr[ti]], writes=[hR])
                        for j in range(6):
                            k.dma("sp", ldX, xj[j][:, :, :], xj_d[j][:, :, t0:t0 + T], reads=[xtr[ti]], writes=[xj[j]])
                        k.dma("sp", ldX, tw[:, :], lo_d[0][:, t0:t0 + T], reads=[xtr[ti]], writes=[tw])
                        k.dma("sp", ldX, al[:, :], lo_d[1][:, t0:t0 + T], reads=[xtr[ti]], writes=[al])
                        k.seal(ldX, xj + [tw, al])
                    else:
                        k.dma("sp", ldS[ti % 2], hU[:, :, :], h1_d[:, :, t0:t0 + T], reads=[dr["h1"][ti]], writes=[hU])
                    for _once in range(1 if q == 0 else 0):
                      k.copy("pool", ufp[:, :, 0:1], ufp[:, :, T:T + 1], [ufp], [ufp])
                    for _once in range(1 if q == 0 else 0):
                      emit_norm((sq, sd), hU, "c_g", ufp, 1, True)
                      k.tt("pool", delta[:, :, :], ufp[:, :, 0:T], ufp[:, :, 1:T + 1], ALU.subtract, [ufp], [delta])
                      for j in range(6):
                          for c in range(8):
                              mo = _off["mu"] + j * 8 + c
                              if j in (1, 4) or (j == 3 and c < 4):
                                  tb_ = xtmp[(j * 8 + c) % 2]
                                  k.act(tb_[:, :], delta[:, c, :], AF.Copy, [delta, cst], [tb_], scale=cst[:, mo:mo + 1])
                                  k.tt("pool", xj[j][:, c, :], tb_[:, :], ufp[:, c, 1:T + 1], ALU.add, [tb_, ufp], [xj[j]])
                              else:
                                  k.stt(xj[j][:, c, :], delta[:, c, :], cst[:, mo:mo + 1], ufp[:, c, 1:T + 1], ALU.mult, ALU.add,
                                        [delta, cst, ufp], [xj[j]])
                      bl = k.psum()
                      for kc in range(8):
                          k.mm(bl, bl[0:64, 0:T], W1[:, kc, :], xw[:, kc, :], kc == 0, kc == 7, [W1, xw])
                      for kc in range(8):
                          k.mm(bl, bl[0:64, T:2 * T], A1[:, kc, :], xa[:, kc, :], kc == 0, kc == 7, [A1, xa])
                      k.act(tw[:, :], bl[0:64, 0:T], AF.Tanh, [bl], [tw])
                      k.copy("act", al[:, :], bl[0:64, T:2 * T], [bl], [al])
                      bl.busy = False
                      for j in range(6):
                          k.dma("sp", stX, xj_d[j][:, :, t0:t0 + T], xj[j][:, :, :], reads=[xj[j]], writes=[xtr[ti]])
                      k.dma("sp", stX, lo_d[0][:, t0:t0 + T], tw[:, :], reads=[tw], writes=[xtr[ti]])
                      k.dma("sp", stX, lo_d[1][:, t0:t0 + T], al[:, :], reads=[al], writes=[xtr[ti]])
                      k.seal(stX, [xtr[ti]])
                      for b_ in xj + [tw, al]:
                          b_.r[stX.key] = (stX, stX.count)
                    for tg in range(2):
                        bank = k.psum()
                        for kc in range(8):
                            k.mm(bank, bank[:, 0:512], xv[:, kc, tg * 128:(tg + 1) * 128], Wv[:, kc, :], kc == 0, kc == 7, [xv, Wv])
                        k.copy("act" if tg == 0 else "dve", vtok[:, tg, :], bank[:, 0:512], [bank], [vtok])
                        bank.busy = False
                    for pp in range(2):
                        for _ in rr([pair_gen(2 * pp, PB[0]), pair_gen(2 * pp + 1, PB[1])]):
                            pass
                    for dc in range(8):
                        bank = k.psum()
                        for m in range(4):
                            k.mm(bank, bank[:, 0:T], Wo[:, m, dc * 128:(dc + 1) * 128], y[:, m, :], m == 0, m == 3, [Wo, y])
                        k.tt("dve", hR[:, dc, :], hR[:, dc, :], bank[:, 0:T], ALU.add, [hR, bank], [hR])
                        bank.busy = False
                    if not last:
                        k.dma("sp", stS[ti % 2], dst_d[:, :, t0:t0 + T], hR[:, :, :], reads=[hR], writes=[dst_tr[ti]])
                    else:
                        emit_norm((sq, sd), hR, "fin_g", delta, 0, True)
                        for tg in range(2):
                            for cq in range(2):
                                bank = k.psum()
                                for c4 in range(4):
                                    c = cq * 4 + c4
                                    k.op("pe", lambda pe, bank=bank, c4=c4, c=c, tg=tg: pe.transpose(
                                        out=bank[:, c4 * 128:(c4 + 1) * 128], in_=delta[:, c, tg * 128:(tg + 1) * 128], identity=mk_f[:, :]),
                                        reads=[delta, mk_f], writes=[bank])
                                k.copy("act" if cq == 0 else "dve", ot[:, tg, cq * 512:(cq + 1) * 512], bank[:, 0:512], [bank], [ot])
                                bank.busy = False
                        k.dma("sp", stS[ti % 2], out_d[t0:t0 + T, :].rearrange("(g p) d -> p g d", p=128), ot[:, :, :], reads=[ot], writes=[])
                k.barrier()

        if "A" in passes:
            pass_A()
        if "B" in passes:
            pass_B()
        if "C" in passes:
            trs = [[Buf(None, "tr%d_%d" % (i, j)) for j in range(NT)] for i in range(3)]
            pass_R(0, None, None, hC_d, trs[0], False)
            pass_R(1, hC_d, trs[0], hA_d, trs[1], False)
            pass_R(2, hA_d, trs[1], hC_d, trs[2], False)
            pass_R(3, hC_d, trs[2], None, None, True)
        k.barrier()
        print("instructions:", k.nins)
    return nc


def _fm(w):
    n = w.shape[1]
    return np.ascontiguousarray(w.reshape(8, 128, n).transpose(1, 0, 2))


def _cv(v):
    return np.ascontiguousarray(v.reshape(-1, 128).T)


def make_masks():
    m = np.zeros((128, NMASK), np.float32)
    p = np.arange(128)[:, None]
    c = np.arange(128)[None, :]
    m[:, M_ID:M_ID + 128] = (p == c)
    m[:, M_ONES:M_ONES + 128] = 1.0
    m[:, M_OBLK:M_OBLK + 128] = (p // 64 == c // 64)
    m[:, M_HG:M_HG + 128] = (p // 64 == c // 64) & (p <= c)
    s = np.arange(128)[:, None] % 64
    t = np.arange(64)[None, :]
    strict = (s < t).astype(np.float32)
    incl = (s <= t).astype(np.float32)
    m[:, M_RW:M_RW + 64] = strict
    m[:, M_RW + 64:M_RW + 128] = incl
    m[:, M_RW + 128:M_RW + 192] = strict
    m[:, M_RW + 192:M_RW + 256] = incl
    m[:, M_RW + 256:M_RW + 320] = (t < s)
    eye = (s == t).astype(np.float32)
    m[:, M_ID2:M_ID2 + 64] = eye
    m[:, M_ID2 + 64:M_ID2 + 128] = eye
    return m


def prep_inputs(inp, b):
    f = lambda a: np.asarray(a, np.float32)
    cst = np.zeros((128, NCONST), np.float32)

    def put(name, arr):
        cst[:, _off[name]:_off[name] + arr.shape[1]] = arr

    put("ab_g", _cv(f(inp["ab_norm_g"])[0]))
    put("c_g", _cv(f(inp["c_norm_g"])[0]))
    put("fin_g", _cv(f(inp["final_g"])))
    cw = f(inp["rg_conv_w"])[0]
    put("conv_w", np.ascontiguousarray(cw.reshape(4, 8, 128).transpose(2, 1, 0).reshape(128, 32)))
    put("conv_b", _cv(f(inp["rg_conv_b"])[0]))
    put("b_a", _cv(f(inp["rg_b_a"])[0]))
    put("b_x", _cv(f(inp["rg_b_x"])[0]))
    put("lam", _cv(f(inp["rg_lambda"])[0]))
    put("lb0", _cv(f(inp["hg_lb_logits"])[0]))
    put("lb1", _cv(f(inp["hg_lb_logits"])[1]))
    put("hg_g", f(inp["hg_norm_g"])[0].reshape(128, 1))
    mu = f(inp["c_mu"])[0]
    put("mu", np.ascontiguousarray(mu.reshape(6, 8, 128).transpose(2, 0, 1).reshape(128, 48)))
    put("w0", _cv(f(inp["c_w0"])[0]))
    put("a0", _cv(f(inp["c_a0"])[0]))
    put("k_k", _cv(f(inp["c_k_k"])[0]))
    put("k_a", _cv(f(inp["c_k_a"])[0]))
    put("r_k", _cv(f(inp["c_r_k"])[0].reshape(-1)))
    put("lnx_g", _cv(f(inp["c_lnx_g"])[0]))
    put("lnx_b", _cv(f(inp["c_lnx_b"])[0]))
    win = f(inp["ab_w_in"])[0]
    wout = f(inp["ab_w_out"])[0]
    m = {
        "x": np.ascontiguousarray(f(inp["x"])[b]),
        "consts": cst,
        "masks": make_masks(),
        "wAin": _fm(win[:, 0:2048]),
        "rgwa": np.ascontiguousarray(f(inp["rg_w_a"])[0].transpose(1, 0, 2)),
        "rgwx": np.ascontiguousarray(f(inp["rg_w_x"])[0].transpose(1, 0, 2)),
        "wAout": _fm(wout[0:1024]),
        "wBin": _fm(win[:, 2048:6144]),
        "wBout": _fm(wout[1024:2048]),
        "w1": _fm(f(inp["c_w1"])[0]),
        "a1": _fm(f(inp["c_a1"])[0]),
    }
    for h in range(2):
        sl = slice(h * 1024, (h + 1) * 1024)
        m["wr%d" % h] = _fm(f(inp["c_w_r"])[0][:, sl])
        m["wk%d" % h] = _fm(f(inp["c_w_k"])[0][:, sl])
        m["wv%d" % h] = _fm(f(inp["c_w_v"])[0][:, sl])
        m["wg%d" % h] = _fm(f(inp["c_w_g"])[0][:, sl])
        m["wo%d" % h] = _fm(f(inp["c_w_o"])[0][sl, :])
        m["w2_%d" % h] = np.ascontiguousarray(f(inp["c_w2"])[0][:, sl])
        m["a2_%d" % h] = np.ascontiguousarray(f(inp["c_a2"])[0][:, sl])
    return m


def kernel(**inputs):
    nc = build()
    in_maps = [prep_inputs(inputs, i % 4) for i in range(8)]
    res = run_bass_kernel_spmd(nc, in_maps, core_ids=list(range(8)))
    out = np.stack([np.asarray(res.results[i]["out"], np.float32) for i in range(4)], axis=0)
    return out
```

```python
import contextlib
import numpy as np
import concourse.bass as bass
import concourse.mybir as mybir
from concourse.bass_utils import run_bass_kernel_spmd
from concourse.alu_op_type import AluOpType as ALU

F32 = mybir.dt.float32
BF16 = mybir.dt.bfloat16
AF = mybir.ActivationFunctionType

S = 4096
D = 1024
T = 256
NT = S // T
RMS_EPS = 1e-6
GN_EPS = 64e-5
DEC = 0.6065306597126334
SAME_SYNC = True
import os
BSTOP = float(os.environ.get('BSTOP', '99'))
RSTOP = float(os.environ.get('RSTOP', '99'))

_off = {}
_n = 0
for _name, _w in [("ab_g", 8), ("c_g", 8), ("fin_g", 8), ("conv_w", 32), ("conv_b", 8), ("b_a", 8), ("b_x", 8),
                  ("lam", 8), ("lb0", 8), ("lb1", 8), ("hg_g", 1), ("mu", 48), ("w0", 16), ("a0", 16),
                  ("k_k", 16), ("k_a", 16), ("r_k", 16), ("lnx_g", 16), ("lnx_b", 16)]:
    _off[_name] = _n
    _n += _w
NCONST = _n
M_ID = 0
M_ONES = 128
M_OBLK = 256
M_HG = 384
M_RW = 512
M_ID2 = 832
NMASK = 960


class Buf:
    __slots__ = ("t", "name", "w", "r", "busy")

    def __init__(self, t, name):
        self.t = t
        self.name = name
        self.w = None
        self.r = {}
        self.busy = False

    def __getitem__(self, idx):
        return self.t[idx]


class Stream:
    def __init__(self, sem, key):
        self.sem = sem
        self.key = key
        self.count = 0


class KB:
    def __init__(self, nc, es):
        self.nc = nc
        self.es = es
        self.eng = {"pe": nc.tensor, "act": nc.scalar, "dve": nc.vector, "pool": nc.gpsimd, "sp": nc.sync}
        self.st = {k: Stream(es.enter_context(nc.semaphore(k + "_s")), k) for k in self.eng}
        self.waited = {k: {} for k in self.eng}
        self.dstreams = []
        self.banks = []
        self.bank_i = 0
        self.nins = 0

    def dma_stream(self, name):
        s = Stream(self.es.enter_context(self.nc.semaphore(name)), name)
        self.dstreams.append(s)
        return s

    def sbuf(self, name, shape, dtype, es=None):
        es = es or self.es
        self.nins += 0
        self.uid = getattr(self, "uid", 0) + 1
        name = "%s_u%d" % (name, self.uid)
        return Buf(es.enter_context(self.nc.sbuf_tensor(name, list(shape), dtype)), name)

    def init_psum(self):
        for i in range(8):
            self.banks.append(Buf(self.es.enter_context(self.nc.psum_tensor("bank%d" % i, [128, 512], F32)), "bank%d" % i))

    def psum(self):
        b = self.banks[2 + self.bank_i % 6]
        self.bank_i += 1
        assert not b.busy, "psum bank still in use: " + b.name
        b.busy = True
        return b

    def _wait(self, e, sv):
        s, v = sv
        w = self.waited[e]
        if w.get(s.key, 0) >= v:
            return
        w[s.key] = v
        self.eng[e].wait_ge(s.sem, v)

    def _deps(self, e, reads, writes):
        for b in reads:
            if b.name.startswith("bank"):
                for sv in b.r.values():
                    if sv[0].key != e:
                        self._wait(e, sv)
            if b.w is not None:
                if b.w[0].key == e:
                    if (SAME_SYNC is True and e != "pe") or (SAME_SYNC == "pool" and e == "pool"):
                        self._wait(e, b.w)
                else:
                    self._wait(e, b.w)
        for b in writes:
            if b.w is not None and b.w[0].key != e:
                self._wait(e, b.w)
            for sv in b.r.values():
                if sv[0].key != e:
                    self._wait(e, sv)

    def op(self, e, fn, reads=(), writes=()):
        self._deps(e, reads, writes)
        ins = fn(self.eng[e])
        s = self.st[e]
        s.count += 1
        ins.then_inc(s.sem, 1)
        self.nins += 1
        for b in reads:
            b.r[s.key] = (s, s.count)
        for b in writes:
            b.w = (s, s.count)
            b.r = {}

    def dma(self, q, stream, out_ap, in_ap, reads=(), writes=()):
        self._deps(q, reads, writes)
        ins = self.eng[q].dma_start(out=out_ap, in_=in_ap)
        stream.count += 16
        ins.then_inc(stream.sem, 16)
        self.nins += 1
        for b in reads:
            b.r[stream.key] = (stream, stream.count)
        for b in writes:
            b.w = (stream, stream.count)
            b.r = {}

    def seal(self, stream, bufs):
        for b in bufs:
            b.w = (stream, stream.count)

    def barrier(self):
        allst = list(self.st.values()) + self.dstreams
        for e in self.eng:
            for s in allst:
                if s.key != e and s.count > 0:
                    self._wait(e, (s, s.count))

    def mm(self, bank, out_ap, lhsT, rhs, start, stop, reads):
        self.op("pe", lambda pe: pe.matmul(out_ap, lhsT=lhsT, rhs=rhs, start=start, stop=stop), reads=reads, writes=[bank])

    def act(self, out_ap, in_ap, func, reads, writes, bias=None, scale=None):
        kw = {}
        if bias is not None:
            kw["bias"] = bias
        if scale is not None:
            kw["scale"] = scale
        self.op("act", lambda a: a.activation(out=out_ap, in_=in_ap, func=func, **kw), reads=reads, writes=writes)

    def tt(self, e, out_ap, in0, in1, op, reads, writes):
        self.op(e, lambda v: v.tensor_tensor(out=out_ap, in0=in0, in1=in1, op=op), reads=reads, writes=writes)

    def ts(self, e, out_ap, in0, s1, s2, op0, op1, reads, writes):
        if op1 is None:
            self.op(e, lambda v: v.tensor_scalar(out=out_ap, in0=in0, scalar1=s1, scalar2=None, op0=op0), reads=reads, writes=writes)
        else:
            self.op(e, lambda v: v.tensor_scalar(out=out_ap, in0=in0, scalar1=s1, scalar2=s2, op0=op0, op1=op1), reads=reads, writes=writes)

    def stt(self, out_ap, in0, scalar, in1, op0, op1, reads, writes):
        self.op("dve", lambda v: v.scalar_tensor_tensor(out=out_ap, in0=in0, scalar=scalar, in1=in1, op0=op0, op1=op1), reads=reads, writes=writes)

    def copy(self, e, out_ap, in_ap, reads, writes):
        if e == "act":
            self.op("act", lambda a: a.copy(out=out_ap, in_=in_ap), reads=reads, writes=writes)
        else:
            self.op(e, lambda v: v.tensor_copy(out=out_ap, in_=in_ap), reads=reads, writes=writes)


def build(nt=NT, passes="ABCD", debug=False):
    nc = bass.Bass("TRN2", target_bir_lowering=False)

    def din(name, shape):
        return nc.dram_tensor(name, list(shape), F32, kind="ExternalInput").ap()

    x_d = din("x", [S, D])
    consts_d = din("consts", [128, NCONST])
    masks_d = din("masks", [128, NMASK])
    wAin_d = din("wAin", [128, 8, 2048])
    rgwa_d = din("rgwa", [128, 8, 128])
    rgwx_d = din("rgwx", [128, 8, 128])
    wAout_d = din("wAout", [128, 8, 1024])
    wBin_d = din("wBin", [128, 8, 4096])
    wBout_d = din("wBout", [128, 8, 1024])
    wr_d = [din("wr%d" % h, [128, 8, 1024]) for h in range(2)]
    wk_d = [din("wk%d" % h, [128, 8, 1024]) for h in range(2)]
    wv_d = [din("wv%d" % h, [128, 8, 1024]) for h in range(2)]
    wg_d = [din("wg%d" % h, [128, 8, 1024]) for h in range(2)]
    wo_d = [din("wo%d" % h, [128, 8, 1024]) for h in range(2)]
    w1_d = din("w1", [128, 8, 64])
    a1_d = din("a1", [128, 8, 64])
    w2_d = [din("w2_%d" % h, [64, 1024]) for h in range(2)]
    a2_d = [din("a2_%d" % h, [64, 1024]) for h in range(2)]
    out_d = nc.dram_tensor("out", [S, D], F32, kind="ExternalOutput").ap()
    skind = "ExternalOutput" if debug else "Internal"
    h0_d = nc.dram_tensor("h0fm", [128, 8, S], F32, kind=skind).ap()
    hA_d = nc.dram_tensor("hAfm", [128, 8, S], F32, kind=skind).ap()
    h1_d = nc.dram_tensor("h1fm", [128, 8, S], F32, kind=skind).ap()
    hC_d = nc.dram_tensor("hCfm", [128, 8, S], F32, kind=skind).ap()
    xj_d = [nc.dram_tensor("xjfm%d" % j, [128, 8, S], BF16, kind="Internal").ap() for j in range(6)]
    lo_d = [nc.dram_tensor("lofm%d" % j, [64, S], BF16, kind="Internal").ap() for j in range(2)]

    es = contextlib.ExitStack()
    with es:
        k = KB(nc, es)
        k.init_psum()
        cst = k.sbuf("cst", [128, NCONST], F32)
        der = k.sbuf("der", [128, 64], F32)
        mk_f = k.sbuf("mk_f", [128, 128], F32)
        mk_b = k.sbuf("mk_b", [128, NMASK], BF16)
        mk_rw = k.sbuf("mk_rw", [128, 320], F32)
        mk_hg = k.sbuf("mk_hg", [128, 128], F32)
        zeros = k.sbuf("zeros", [128, 64], F32)
        onesf = k.sbuf("onesf", [128, 64], F32)
        cs = k.dma_stream("cstream")
        k.dma("sp", cs, cst[:, :], consts_d[:, :], writes=[cst])
        k.dma("sp", cs, mk_f[:, :], masks_d[:, M_ID:M_ID + 128], writes=[mk_f])
        k.dma("sp", cs, mk_rw[:, :], masks_d[:, M_RW:M_RW + 320], writes=[mk_rw])
        k.dma("sp", cs, mk_hg[:, :], masks_d[:, M_HG:M_HG + 128], writes=[mk_hg])
        cs2 = k.dma_stream("cstream2")
        k.dma("pool", cs2, mk_b[:, :], masks_d[:, :], writes=[mk_b])
        k.seal(cs, [cst, mk_f, mk_rw, mk_hg])
        k.op("pool", lambda g: g.memset(zeros[:, :], 0.0), writes=[zeros])
        k.op("pool", lambda g: g.memset(onesf[:, :], 1.0), writes=[onesf])
        ident_b = mk_b[:, M_ID:M_ID + 128]
        ones_b = mk_b[:, M_ONES:M_ONES + 128]
        oblk_b = mk_b[:, M_OBLK:M_OBLK + 128]

        def C(name, j=0, w=1):
            o = _off[name] + j
            return cst[:, o:o + w]

        k.act(der[:, 0:8], C("lam", 0, 8), AF.Exp, [cst], [der], scale=-1.0)
        k.act(der[:, 0:8], der[:, 0:8], AF.Ln, [der, onesf], [der], bias=onesf[:, 0:1])
        k.ts("dve", der[:, 8:16], der[:, 0:8], -16.0, None, ALU.mult, None, [der], [der])
        k.ts("dve", der[:, 0:8], der[:, 0:8], -8.0, None, ALU.mult, None, [der], [der])
        k.tt("dve", der[:, 16:24], C("lb0", 0, 8), C("lb1", 0, 8), ALU.subtract, [cst], [der])
        k.act(der[:, 16:24], der[:, 16:24], AF.Sigmoid, [der], [der])
        k.ts("dve", der[:, 24:32], der[:, 16:24], -1.0, 1.0, ALU.mult, ALU.add, [der], [der])
        k.ts("dve", der[:, 32:48], C("k_a", 0, 16), -1.0, 1.0, ALU.mult, ALU.add, [cst], [der])
        k.op("pool", lambda g: g.memset(der[:, 48:49], RMS_EPS), writes=[der])
        k.op("pool", lambda g: g.memset(der[:, 49:50], GN_EPS), writes=[der])
        eps_rms = der[:, 48:49]
        eps_gn = der[:, 49:50]

        ws = k.dma_stream("wstream")
        ldS = [k.dma_stream("ldU0"), k.dma_stream("ldU1")]
        ldR = [k.dma_stream("ldR0"), k.dma_stream("ldR1")]
        stS = [k.dma_stream("st0"), k.dma_stream("st1")]
        stS2 = [k.dma_stream("st2_0"), k.dma_stream("st2_1")]
        ldX = k.dma_stream("ldX")
        stX = k.dma_stream("stX")
        xtr = [Buf(None, "xtr%d" % i) for i in range(NT)]
        dr = {nm: [Buf(None, "%s_%d" % (nm, i)) for i in range(NT)] for nm in ("h0", "hA", "h1", "hC")}

        def loadw(buf, dram, nk=8, ncol=None, dcol0=0, dk0=0):
            ncol = ncol or dram.shape[2]
            for kc in range(nk):
                for c0 in range(0, ncol, 1024):
                    c1 = min(ncol, c0 + 1024)
                    k.dma("pool", ws, buf[:, kc, c0:c1], dram[:, dk0 + kc, dcol0 + c0:dcol0 + c1], writes=[buf])

        def emit_norm(pes_bufs, hU, gname, outbuf, col0, fp32_out):
            sq, sd = pes_bufs
            k.act(sq[:, :, :], hU[:, :, :], AF.Square, [hU], [sq])
            bank = k.psum()
            for c in range(8):
                k.mm(bank, bank[:, 0:T], ones_b, sq[:, c, :], c == 0, c == 7, [sq, mk_b])
            k.act(sd[:, :], bank[:, 0:T], AF.Sqrt, [bank, der], [sd], bias=eps_rms, scale=1.0 / D)
            bank.busy = False
            k.op("dve", lambda v: v.reciprocal(out=sd[:, :], in_=sd[:, :]), reads=[sd], writes=[sd])
            for c in range(8):
                k.stt(outbuf[:, c, col0:col0 + T], hU[:, c, :], C(gname, c), sd[:, :], ALU.mult, ALU.mult,
                      [hU, cst, sd], [outbuf])

        def out_proj(Wout, y, hR):
            for dc in range(8):
                bank = k.psum()
                for m in range(8):
                    k.mm(bank, bank[:, 0:T], Wout[:, m, dc * 128:(dc + 1) * 128], y[:, m, :], m == 0, m == 7, [Wout, y])
                k.tt("dve", hR[:, dc, :], hR[:, dc, :], bank[:, 0:T], ALU.add, [hR, bank], [hR])
                bank.busy = False

        def rr(gens):
            gens = list(gens)
            while gens:
                for g_ in list(gens):
                    try:
                        next(g_)
                    except StopIteration:
                        gens.remove(g_)
                yield

        def pass_A():
            with contextlib.ExitStack() as pes:
                Win = k.sbuf("A_Win", [128, 8, 2048], BF16, pes)
                Wa = k.sbuf("A_Wa", [128, 8, 128], BF16, pes)
                Wx = k.sbuf("A_Wx", [128, 8, 128], BF16, pes)
                Wout = k.sbuf("A_Wout", [128, 8, 1024], BF16, pes)
                loadw(Win, wAin_d)
                k.dma("pool", ws, Wa[:, :, :], rgwa_d[:, :, :], writes=[Wa])
                k.dma("pool", ws, Wx[:, :, :], rgwx_d[:, :, :], writes=[Wx])
                loadw(Wout, wAout_d)
                k.seal(ws, [Win, Wa, Wx, Wout])
                xts = [k.sbuf("A_xt%d" % i, [128, 2, 1024], F32, pes) for i in range(2)]
                hUs = [k.sbuf("A_hU%d" % i, [128, 8, T], F32, pes) for i in range(2)]
                sq = k.sbuf("A_sq", [128, 8, T], BF16, pes)
                sd = k.sbuf("A_sd", [128, T], F32, pes)
                u = k.sbuf("A_u", [128, 8, T], BF16, pes)
                y = k.sbuf("A_y", [128, 8, T], BF16, pes)
                xaext = [k.sbuf("A_xa%d" % c, [128, T + 3], F32, pes) for c in range(8)]
                carry = [k.sbuf("A_cy%d" % c, [128, 1], F32, pes) for c in range(8)]
                xc = [k.sbuf("A_xc%d" % i, [128, T], F32, pes) for i in range(4)]
                xcb = [k.sbuf("A_xcb%d" % i, [128, T], BF16, pes) for i in range(4)]
                sr = [k.sbuf("A_sr%d" % i, [128, T], F32, pes) for i in range(4)]
                si = [k.sbuf("A_si%d" % i, [128, T], F32, pes) for i in range(4)]
                av = [k.sbuf("A_av%d" % i, [128, T], F32, pes) for i in range(4)]
                mv = [k.sbuf("A_mv%d" % i, [128, T], F32, pes) for i in range(4)]
                uu = [k.sbuf("A_uu%d" % i, [128, T], F32, pes) for i in range(4)]
                hh = [k.sbuf("A_hh%d" % i, [128, T], F32, pes) for i in range(4)]
                sg = [k.sbuf("A_sg%d" % i, [128, T], F32, pes) for i in range(4)]
                for c in range(8):
                    k.op("pool", lambda g, c=c: g.memset(xaext[c][:, :], 0.0), writes=[xaext[c]])
                    k.op("pool", lambda g, c=c: g.memset(carry[c][:, :], 0.0), writes=[carry[c]])

                for ti in range(nt):
                    t0 = ti * T
                    xt = xts[ti % 2]
                    hU = hUs[ti % 2]
                    k.dma("sp", ldS[ti % 2], xt[:, :, :], x_d[t0:t0 + T, :].rearrange("(g p) d -> p g d", p=128), writes=[xt])
                    for cp in range(4):
                        bank = k.psum()
                        for cc in range(2):
                            c = cp * 2 + cc
                            for tg in range(2):
                                o = cc * 256 + tg * 128
                                k.op("pe", lambda pe, o=o, c=c, tg=tg, bank=bank: pe.transpose(
                                    out=bank[:, o:o + 128], in_=xt[:, tg, c * 128:(c + 1) * 128], identity=mk_f[:, :]),
                                    reads=[xt, mk_f], writes=[bank])
                        for cc in range(2):
                            c = cp * 2 + cc
                            k.copy("act" if cc == 0 else "dve", hU[:, c, :], bank[:, cc * 256:cc * 256 + 256], [bank], [hU])
                        bank.busy = False
                    k.dma("sp", stS2[ti % 2], h0_d[:, :, t0:t0 + T], hU[:, :, :], reads=[hU], writes=[dr["h0"][ti]])
                    emit_norm((sq, sd), hU, "ab_g", u, 0, False)
                    def blockA(c):
                        i2 = c % 4
                        b1 = k.psum()
                        for kc in range(8):
                            k.mm(b1, b1[:, 0:T], Win[:, kc, c * 128:(c + 1) * 128], u[:, kc, :], kc == 0, kc == 7, [Win, u])
                        for kc in range(8):
                            k.mm(b1, b1[:, T:2 * T], Win[:, kc, 1024 + c * 128:1024 + (c + 1) * 128], u[:, kc, :], kc == 0, kc == 7, [Win, u])
                        xe = xaext[c]
                        k.copy("pool", xe[:, 0:3], xe[:, T:T + 3], [xe], [xe])
                        k.copy("act", xe[:, 3:T + 3], b1[:, 0:T], [b1], [xe])
                        k.act(sg[i2][:, :], b1[:, T:2 * T], AF.Silu, [b1], [sg[i2]])
                        b1.busy = False
                        yield
                        cw = _off["conv_w"] + c * 4
                        k.ts("dve", xc[i2][:, :], xe[:, 3:T + 3], cst[:, cw + 3:cw + 4], C("conv_b", c), ALU.mult, ALU.add, [xe, cst], [xc[i2]])
                        for j in (2, 1, 0):
                            k.stt(xc[i2][:, :], xe[:, j:j + T], cst[:, cw + j:cw + j + 1], xc[i2][:, :], ALU.mult, ALU.add, [xe, cst, xc[i2]], [xc[i2]])
                        k.copy("pool", xcb[i2][:, :], xc[i2][:, :], [xc[i2]], [xcb[i2]])
                        yield
                        b2 = k.psum()
                        k.mm(b2, b2[:, 0:T], Wa[:, c, :], xcb[i2][:, :], True, True, [Wa, xcb[i2]])
                        k.mm(b2, b2[:, T:2 * T], Wx[:, c, :], xcb[i2][:, :], True, True, [Wx, xcb[i2]])
                        k.act(sr[i2][:, :], b2[:, 0:T], AF.Sigmoid, [b2, cst], [sr[i2]], bias=C("b_a", c))
                        k.act(si[i2][:, :], b2[:, T:2 * T], AF.Sigmoid, [b2, cst], [si[i2]], bias=C("b_x", c))
                        b2.busy = False
                        yield
                        k.act(av[i2][:, :], sr[i2][:, :], AF.Exp, [sr[i2], der], [av[i2]], scale=der[:, c:c + 1])
                        k.act(mv[i2][:, :], sr[i2][:, :], AF.Exp, [sr[i2], der], [mv[i2]], scale=der[:, 8 + c:9 + c])
                        k.tt("pool", uu[i2][:, :], si[i2][:, :], xc[i2][:, :], ALU.mult, [si[i2], xc[i2]], [uu[i2]])
                        yield
                        k.act(mv[i2][:, :], mv[i2][:, :], AF.Sqrt, [mv[i2], onesf], [mv[i2]], bias=onesf[:, 0:1], scale=-1.0)
                        yield
                        k.tt("dve", uu[i2][:, :], uu[i2][:, :], mv[i2][:, :], ALU.mult, [uu[i2], mv[i2]], [uu[i2]])
                        k.op("dve", lambda v, i2=i2, c=c: v.tensor_tensor_scan(
                            out=hh[i2][:, :], data0=av[i2][:, :], data1=uu[i2][:, :], initial=carry[c][:, 0:1],
                            op0=ALU.mult, op1=ALU.add), reads=[av[i2], uu[i2], carry[c]], writes=[hh[i2]])
                        k.copy("pool", carry[c][:, 0:1], hh[i2][:, T - 1:T], [hh[i2]], [carry[c]])
                        k.tt("dve", y[:, c, :], hh[i2][:, :], sg[i2][:, :], ALU.mult, [hh[i2], sg[i2]], [y])
                        yield

                    for c4 in range(2):
                        for _ in rr([blockA(c4 * 4 + i) for i in range(4)]):
                            pass
                    out_proj(Wout, y, hU)
                    k.dma("sp", stS[ti % 2], hA_d[:, :, t0:t0 + T], hU[:, :, :], reads=[hU], writes=[dr["hA"][ti]])
                k.barrier()

        def pass_B():
            with contextlib.ExitStack() as pes:
                Win = k.sbuf("B_Win", [128, 8, 4096], BF16, pes)
                Wout = k.sbuf("B_Wout", [128, 8, 1024], BF16, pes)
                loadw(Win, wBin_d)
                loadw(Wout, wBout_d)
                k.seal(ws, [Win, Wout])
                hUs = [k.sbuf("B_hU%d" % i, [128, 8, T], F32, pes) for i in range(2)]
                hRs = [k.sbuf("B_hR%d" % i, [128, 8, T], F32, pes) for i in range(2)]
                sq = k.sbuf("B_sq", [128, 8, T], BF16, pes)
                sd = k.sbuf("B_sd", [128, T], F32, pes)
                u = k.sbuf("B_u", [128, 8, T], BF16, pes)
                y = k.sbuf("B_y", [128, 8, T], BF16, pes)
                vtok = k.sbuf("B_vtok", [128, 2, 1024], BF16, pes)
                stf = [k.sbuf("B_stf%d" % h, [128, 128], F32, pes) for h in range(8)]
                stb = [k.sbuf("B_stb%d" % h, [128, 128], BF16, pes) for h in range(8)]
                NB = 4
                qf = [k.sbuf("B_qf%d" % i, [128, T], F32, pes) for i in range(NB)]
                sig = [k.sbuf("B_sig%d" % i, [128, T], F32, pes) for i in range(NB)]
                ff = [k.sbuf("B_f%d" % i, [128, T], F32, pes) for i in range(NB)]
                kf = [k.sbuf("B_k%d" % i, [128, T], F32, pes) for i in range(NB)]
                Pc = [k.sbuf("B_P%d" % i, [128, T], F32, pes) for i in range(NB)]
                Pi = [k.sbuf("B_Pi%d" % i, [128, T], F32, pes) for i in range(NB)]
                qd = [k.sbuf("B_qd%d" % i, [128, T], BF16, pes) for i in range(NB)]
                kif = [k.sbuf("B_kif%d" % i, [128, T], F32, pes) for i in range(NB)]
                kib = [k.sbuf("B_kib%d" % i, [128, T], BF16, pes) for i in range(NB)]
                keb = [k.sbuf("B_keb%d" % i, [128, T], BF16, pes) for i in range(NB)]
                scm = [k.sbuf("B_scm%d" % i, [128, 128], BF16, pes) for i in range(NB)]
                ket = [k.sbuf("B_ket%d" % i, [128, 128], BF16, pes) for i in range(NB)]
                osq = [k.sbuf("B_osq%d" % i, [128, T], BF16, pes) for i in range(NB)]
                ors = [k.sbuf("B_ors%d" % i, [128, T], F32, pes) for i in range(NB)]
                o1 = [k.sbuf("B_o1%d" % i, [128, T], F32, pes) for i in range(NB)]
                sgb = [k.sbuf("B_sg%d" % i, [128, T], F32, pes) for i in range(NB)]
                for h in range(8):
                    k.op("pool", lambda g, h=h: g.memset(stf[h][:, :], 0.0), writes=[stf[h]])
                    k.op("pool", lambda g, h=h: g.memset(stb[h][:, :], 0.0), writes=[stb[h]])
                accb = [k.banks[0], k.banks[1]]
                for ti in range(nt):
                    t0 = ti * T
                    hU = hUs[ti % 2]
                    hR = hRs[ti % 2]
                    k.dma("sp", ldS[ti % 2], hU[:, :, :], h0_d[:, :, t0:t0 + T], reads=[dr["h0"][ti]], writes=[hU])
                    k.dma("sp", ldR[ti % 2], hR[:, :, :], hA_d[:, :, t0:t0 + T], reads=[dr["hA"][ti]], writes=[hR])
                    emit_norm((sq, sd), hU, "ab_g", u, 0, False)
                    for tg in range(2):
                        for cg in range(2):
                            bank = k.psum()
                            for kc in range(8):
                                k.mm(bank, bank[:, 0:512], u[:, kc, tg * 128:(tg + 1) * 128],
                                     Win[:, kc, 2048 + cg * 512:2048 + (cg + 1) * 512], kc == 0, kc == 7, [u, Win])
                            k.copy("act" if cg == 0 else "dve", vtok[:, tg, cg * 512:(cg + 1) * 512], bank[:, 0:512], [bank], [vtok])
                            bank.busy = False
                    def headB(h):
                        i2 = h % NB
                        bo = accb[(h % 4) // 2]
                        oc = (h % 2) * 256
                        bq = k.psum()
                        for kc in range(8):
                            k.mm(bq, bq[:, 0:T], Win[:, kc, h * 128:(h + 1) * 128], u[:, kc, :], kc == 0, kc == 7, [Win, u])
                        for kc in range(8):
                            k.mm(bq, bq[:, T:2 * T], Win[:, kc, 1024 + h * 128:1024 + (h + 1) * 128], u[:, kc, :], kc == 0, kc == 7, [Win, u])
                        k.act(sig[i2][:, :], bq[:, T:2 * T], AF.Sigmoid, [bq], [sig[i2]])
                        k.copy("act", qf[i2][:, :], bq[:, 0:T], [bq], [qf[i2]])
                        bq.busy = False
                        yield
                        k.ts("dve", ff[i2][:, :], sig[i2][:, :], der[:, 24 + h:25 + h], der[:, 16 + h:17 + h], ALU.mult, ALU.add, [sig[i2], der], [ff[i2]])
                        k.ts("pool", kf[i2][:, :], ff[i2][:, :], -1.0, 1.0, ALU.mult, ALU.add, [ff[i2]], [kf[i2]])
                        for j in range(T // 64):
                            k.op("dve", lambda v, i2=i2, j=j: v.tensor_tensor_scan(
                                out=Pc[i2][:, j * 64:(j + 1) * 64], data0=ff[i2][:, j * 64:(j + 1) * 64], data1=zeros[:, 0:64],
                                initial=1.0, op0=ALU.mult, op1=ALU.add), reads=[ff[i2], zeros], writes=[Pc[i2]])
                        yield
                        k.op("dve", lambda v, i2=i2: v.reciprocal(out=Pi[i2][:, :], in_=Pc[i2][:, :]), reads=[Pc[i2]], writes=[Pi[i2]])
                        k.tt("pool", qd[i2][:, :], qf[i2][:, :], Pc[i2][:, :], ALU.mult, [qf[i2], Pc[i2]], [qd[i2]])
                        yield
                        k.tt("pool", kif[i2][:, :], kf[i2][:, :], Pi[i2][:, :], ALU.mult, [kf[i2], Pi[i2]], [kif[i2]])
                        k.copy("pool", kib[i2][:, :], kif[i2][:, :], [kif[i2]], [kib[i2]])
                        for j in range(T // 64):
                            k.ts("dve", keb[i2][:, j * 64:(j + 1) * 64], kif[i2][:, j * 64:(j + 1) * 64],
                                 Pc[i2][:, j * 64 + 63:j * 64 + 64], None, ALU.mult, None, [kif[i2], Pc[i2]], [keb[i2]])
                        yield
                        for tg in range(2):
                            c0 = tg * 128
                            bs = k.psum()
                            k.mm(bs, bs[:, 0:128], kib[i2][:, c0:c0 + 128], qd[i2][:, c0:c0 + 128], True, True, [kib[i2], qd[i2]])
                            k.mm(bs, bs[:, 128:256], keb[i2][:, c0:c0 + 128], ident_b, True, True, [keb[i2], mk_b])
                            k.tt("dve", scm[i2][:, :], bs[:, 0:128], mk_hg[:, :], ALU.mult, [bs, mk_hg], [scm[i2]])
                            k.copy("act", ket[i2][:, :], bs[:, 128:256], [bs], [ket[i2]])
                            bs.busy = False
                            yield
                            for jp in range(2):
                                cj = c0 + jp * 64
                                r0 = jp * 64
                                k.mm(bo, bo[:, oc + cj:oc + cj + 64], vtok[:, tg, h * 128:(h + 1) * 128], scm[i2][:, r0:r0 + 64], True, False, [vtok, scm[i2]])
                                k.mm(bo, bo[:, oc + cj:oc + cj + 64], stb[h][:, :], qd[i2][:, cj:cj + 64], False, True, [stb[h], qd[i2]])
                                bst = k.psum()
                                k.mm(bst, bst[:, 0:128], ket[i2][r0:r0 + 64, :], vtok[r0:r0 + 64, tg, h * 128:(h + 1) * 128], True, True, [ket[i2], vtok])
                                k.stt(stf[h][:, :], stf[h][:, :], Pc[i2][:, cj + 63:cj + 64], bst[:, 0:128], ALU.mult, ALU.add, [stf[h], Pc[i2], bst], [stf[h]])
                                bst.busy = False
                                k.copy("act", stb[h][:, :], stf[h][:, :], [stf[h]], [stb[h]])
                                yield
                        bg = k.psum()
                        for kc in range(8):
                            k.mm(bg, bg[:, 0:T], Win[:, kc, 3072 + h * 128:3072 + (h + 1) * 128], u[:, kc, :], kc == 0, kc == 7, [Win, u])
                        k.act(sgb[i2][:, :], bg[:, 0:T], AF.Silu, [bg], [sgb[i2]])
                        k.act(osq[i2][:, :], bo[:, oc:oc + T], AF.Square, [bo], [osq[i2]])
                        k.copy("act", o1[i2][:, :], bo[:, oc:oc + T], [bo], [o1[i2]])
                        k.mm(bg, bg[:, T:2 * T], ones_b, osq[i2][:, :], True, True, [mk_b, osq[i2]])
                        k.act(ors[i2][:, :], bg[:, T:2 * T], AF.Sqrt, [bg, der], [ors[i2]], bias=eps_rms, scale=1.0 / 128)
                        bg.busy = False
                        yield
                        k.op("dve", lambda v, i2=i2: v.reciprocal(out=ors[i2][:, :], in_=ors[i2][:, :]), reads=[ors[i2]], writes=[ors[i2]])
                        k.tt("pool", o1[i2][:, :], o1[i2][:, :], ors[i2][:, :], ALU.mult, [o1[i2], ors[i2]], [o1[i2]])
                        k.stt(y[:, h, :], o1[i2][:, :], C("hg_g"), sgb[i2][:, :], ALU.mult, ALU.mult, [o1[i2], cst, sgb[i2]], [y])
                        yield

                    for h4 in range(2):
                        for _ in rr([headB(h4 * 4 + i) for i in range(4)]):
                            pass
                    out_proj(Wout, y, hR)
                    k.dma("sp", stS[ti % 2], h1_d[:, :, t0:t0 + T], hR[:, :, :], reads=[hR], writes=[dr["h1"][ti]])
                k.barrier()


        def pass_R(q, res_d, res_tr, dst_d, dst_tr, last):
            hf, qo = q // 2, (q % 2) * 512
            with contextlib.ExitStack() as pes:
                Wr = k.sbuf("R_Wr", [128, 8, 512], BF16, pes)
                Wk = k.sbuf("R_Wk", [128, 8, 512], BF16, pes)
                Wv = k.sbuf("R_Wv", [128, 8, 512], BF16, pes)
                Wg = k.sbuf("R_Wg", [128, 8, 512], BF16, pes)
                W1 = k.sbuf("R_W1", [128, 8, 64], BF16, pes)
                A1 = k.sbuf("R_A1", [128, 8, 64], BF16, pes)
                W2 = k.sbuf("R_W2", [64, 512], BF16, pes)
                A2 = k.sbuf("R_A2", [64, 512], BF16, pes)
                Wo = k.sbuf("R_Wo", [128, 4, 1024], BF16, pes)
                loadw(Wr, wr_d[hf], ncol=512, dcol0=qo)
                loadw(Wk, wk_d[hf], ncol=512, dcol0=qo)
                loadw(Wv, wv_d[hf], ncol=512, dcol0=qo)
                loadw(Wg, wg_d[hf], ncol=512, dcol0=qo)
                k.dma("pool", ws, W1[:, :, :], w1_d[:, :, :], writes=[W1])
                k.dma("pool", ws, A1[:, :, :], a1_d[:, :, :], writes=[A1])
                k.dma("pool", ws, W2[:, :], w2_d[hf][:, qo:qo + 512], writes=[W2])
                k.dma("pool", ws, A2[:, :], a2_d[hf][:, qo:qo + 512], writes=[A2])
                loadw(Wo, wo_d[hf], nk=4, dk0=(q % 2) * 4)
                k.seal(ws, [Wr, Wk, Wv, Wg, W1, A1, W2, A2, Wo])
                hUs = [k.sbuf("R_hU%d" % i, [128, 8, T], F32, pes) for i in range(2)] if q == 0 else None
                hRs = [k.sbuf("R_hR%d" % i, [128, 8, T], F32, pes) for i in range(2)] if q > 0 else hUs
                sq = k.sbuf("R_sq", [128, 8, T], BF16, pes)
                sd = k.sbuf("R_sd", [128, T], F32, pes)
                ufp = k.sbuf("R_ufp", [128, 8, T + 1], F32, pes)
                delta = k.sbuf("R_delta", [128, 8, T], F32, pes)
                xj = [k.sbuf("R_x%d" % j, [128, 8, T], BF16, pes) for j in range(6)]
                y = k.sbuf("R_y", [128, 4, T], BF16, pes)
                xtmp = [k.sbuf("R_xtmp", [128, T], F32, pes) for _ in range(2)]
                vtok = k.sbuf("R_vtok", [128, 2, 512], BF16, pes)
                tw = k.sbuf("R_tw", [64, T], BF16, pes)
                al = k.sbuf("R_al", [64, T], BF16, pes)
                Tf = [k.sbuf("R_Tf%d" % p, [128, 64], F32, pes) for p in range(4)]
                Tbk = [k.sbuf("R_Tbk%d" % p, [128, 128], BF16, pes) for p in range(4)]
                ot = k.sbuf("R_ot", [128, 2, 1024], F32, pes) if last else None
                f32n = ["sw", "av", "rf", "gate", "kkp", "nk", "kf", "bb", "cum", "cm", "g", "gi", "vfm", "bonus"]
                b16n = ["kk2", "rk2", "ktb", "btb", "kend", "bend", "yb16", "ysq"]
                zl = list(Tbk)
                PB = []
                for si in range(2):
                    d_ = {}
                    d_["X"] = {n_: k.sbuf("R_" + n_, [128, T], F32, pes) for n_ in f32n}
                    d_["X"]["yf"] = d_["X"]["sw"]
                    d_["X"]["mean"] = d_["X"]["cum"]
                    d_["X"]["var"] = d_["X"]["gi"]
                    d_["t16"] = {n_: k.sbuf("R_" + n_, [128, T], BF16, pes) for n_ in b16n}
                    d_["krt"] = k.sbuf("R_krt", [128, 2, T], BF16, pes)
                    d_["tokA"] = k.sbuf("R_tokA", [128, 256], BF16, pes)
                    d_["tokB"] = k.sbuf("R_tokB", [128, 256], BF16, pes)
                    d_["KEZ"] = [k.sbuf("R_kez", [128, 256], BF16, pes) for j in range(2)]
                    zl += d_["KEZ"]
                    d_["ch"] = []
                    for tg in range(2):
                        c_ = {}
                        c_["Gz"] = [k.sbuf("R_Gz", [128, 2, 320], BF16, pes) for j in range(2)]
                        c_["Qz"] = [k.sbuf("R_Qz", [128, 128], BF16, pes) for j in range(2)]
                        c_["nUz"] = [k.sbuf("R_nUz", [128, 128], BF16, pes) for j in range(2)]
                        c_["WTd"] = [k.sbuf("R_WTd", [128, 64], BF16, pes) for j in range(2)]
                        c_["Qs"] = [k.sbuf("R_Q", [128, 128], BF16, pes) for j in range(2)]
                        c_["PPs"] = [k.sbuf("R_PP", [128, 256], BF16, pes) for j in range(2)]
                        c_["IP"] = [k.sbuf("R_IP", [128, 2, 64], BF16, pes) for j in range(2)]
                        c_["AKV"] = k.sbuf("R_akv", [128, 128], BF16, pes)
                        c_["UP"] = k.sbuf("R_up", [128, 128], F32, pes)
                        zl += c_["Gz"] + c_["Qz"] + c_["nUz"]
                        d_["ch"].append(c_)
                    PB.append(d_)
                for bl_ in zl:
                    k.op("pool", lambda g_, bl_=bl_: g_.memset(bl_[:, :, :] if len(bl_.t.shape) == 3 else bl_[:, :], 0.0), writes=[bl_])
                id2 = mk_b[:, M_ID2:M_ID2 + 128]
                k.op("pool", lambda g_: g_.memset(ufp[:, :, :], 0.0), writes=[ufp])
                for p in range(4):
                    k.op("pool", lambda g_, p=p: g_.memset(Tf[p][:, :], 0.0), writes=[Tf[p]])
                accb = [k.banks[0], k.banks[1]]
                xr, xw, xk, xv, xa, xg = xj

                def chain_gen(p, tg, S_):
                    c_ = S_["ch"][tg]
                    t16, krt, tokA = S_["t16"], S_["krt"], S_["tokA"]
                    Gz, Qz, WTd, Qs, PPs, IPs, AKV, UP = c_["Gz"], c_["Qz"], c_["WTd"], c_["Qs"], c_["PPs"], c_["IP"], c_["AKV"], c_["UP"]
                    for hp in range(2):
                        rs_ = slice(64 * hp, 64 * hp + 64)
                        bG = k.psum()
                        for jp in range(2):
                            cs_ = slice(tg * 128 + jp * 64, tg * 128 + jp * 64 + 64)
                            ro = slice(64 * jp, 64 * jp + 64)
                            k.mm(bG, bG[ro, 0:64], t16["ktb"][rs_, cs_], krt[rs_, 0, cs_], True, True, [t16["ktb"], krt])
                            k.mm(bG, bG[ro, 64:128], t16["ktb"][rs_, cs_], krt[rs_, 1, cs_], True, True, [t16["ktb"], krt])
                            k.mm(bG, bG[ro, 128:192], t16["btb"][rs_, cs_], krt[rs_, 0, cs_], True, True, [t16["btb"], krt])
                            k.mm(bG, bG[ro, 192:256], t16["btb"][rs_, cs_], krt[rs_, 1, cs_], True, True, [t16["btb"], krt])
                            k.mm(bG, bG[ro, 256:320], krt[rs_, 0, cs_], t16["btb"][rs_, cs_], True, True, [t16["btb"], krt])
                        for jp in range(2):
                            ro = slice(64 * jp, 64 * jp + 64)
                            k.tt("dve", Gz[jp][ro, hp, :], bG[ro, 0:320], mk_rw[ro, :], ALU.mult, [bG, mk_rw], [Gz[jp]])
                        bG.busy = False
                        yield
                    Qc = Qs[0]
                    for jp in range(2):
                        ro = slice(64 * jp, 64 * jp + 64)
                        for hp in range(2):
                            k.tt("pool", Qc[ro, 64 * hp:64 * hp + 64], id2[ro, 0:64], Gz[jp][ro, hp, 128:192], ALU.subtract, [mk_b, Gz[jp]], [Qc])
                    qi = 0

                    def emit_q(lvl_, qi_):
                        IPc = IPs[lvl_ % 2]
                        bQ = k.psum()
                        for hp in range(2):
                            for jp in range(2):
                                ro = slice(64 * jp, 64 * jp + 64)
                                k.mm(bQ, bQ[ro, hp * 64:hp * 64 + 64], IPc[ro, hp, :], Qs[qi_][ro, hp * 64:hp * 64 + 64], True, True, [IPc, Qs[qi_]])
                        if lvl_ < 5:
                            Qn = Qs[1 - qi_]
                            k.copy("act", Qn[:, :], bQ[:, 0:128], [bQ], [Qn])
                        else:
                            for jp in range(2):
                                ro = slice(64 * jp, 64 * jp + 64)
                                k.copy("act", Qz[jp][ro, :], bQ[ro, 0:128], [bQ], [Qz[jp]])
                        bQ.busy = False

                    for lvl in range(1, 6):
                        if lvl > 1:
                            emit_q(lvl - 1, qi)
                            qi = 1 - qi
                        IPc = IPs[lvl % 2]
                        bP = k.psum()
                        for hp in range(2):
                            for jp in range(2):
                                ro = slice(64 * jp, 64 * jp + 64)
                                if lvl == 1:
                                    Pm, PTm, Pb = Gz[jp][ro, hp, 256:320], Gz[jp][ro, hp, 128:192], Gz[jp]
                                else:
                                    PPc = PPs[lvl % 2]
                                    Pm, PTm, Pb = PPc[ro, hp * 128:hp * 128 + 64], PPc[ro, hp * 128 + 64:hp * 128 + 128], PPc
                                k.mm(bP, bP[ro, hp * 128:hp * 128 + 64], PTm, Pm, True, True, [Pb])
                                if lvl < 5:
                                    k.mm(bP, bP[ro, hp * 128 + 64:hp * 128 + 128], Pm, PTm, True, True, [Pb])
                        for hp in range(2):
                            k.tt("dve", IPc[:, hp, :], bP[:, hp * 128:hp * 128 + 64], id2[:, 0:64], ALU.add, [bP, mk_b], [IPc])
                        if lvl < 5:
                            PPn = PPs[(lvl + 1) % 2]
                            k.copy("act", PPn[:, :], bP[:, 0:256], [bP], [PPn])
                        bP.busy = False
                        yield
                    emit_q(5, qi)
                    qi = 1 - qi
                    yield
                    bA = k.psum()
                    for hp in range(2):
                        for jp in range(2):
                            ro = slice(64 * jp, 64 * jp + 64)
                            vc = slice(p * 128 + 64 * hp, p * 128 + 64 * hp + 64)
                            k.mm(bA, bA[ro, 64 * hp:64 * hp + 64], Gz[jp][ro, hp, 0:64], vtok[ro, tg, vc], True, True, [Gz[jp], vtok])
                    k.copy("act", AKV[:, :], bA[:, 0:128], [bA], [AKV])
                    bA.busy = False
                    bW = k.psum()
                    for jp in range(2):
                        k.mm(bW, bW[:, jp * 128:(jp + 1) * 128], tokA[:, tg * 128:(tg + 1) * 128], Qz[jp][:, :], True, True, [tokA, Qz[jp]])
                    for jp in range(2):
                        for hp in range(2):
                            rs_ = slice(64 * hp, 64 * hp + 64)
                            k.copy("dve", WTd[jp][rs_, :], bW[rs_, jp * 128 + 64 * hp:jp * 128 + 64 * hp + 64], [bW], [WTd[jp]])
                    bW.busy = False
                    yield
                    bX = k.psum()
                    for hp in range(2):
                        for jp in range(2):
                            ro = slice(64 * jp, 64 * jp + 64)
                            k.mm(bX, bX[ro, 64 * hp:64 * hp + 64], Qz[jp][ro, hp * 64:hp * 64 + 64], AKV[ro, 64 * hp:64 * hp + 64], True, True, [Qz[jp], AKV])
                    k.copy("act", UP[:, :], bX[:, 0:128], [bX], [UP])
                    bX.busy = False
                    yield

                def seq_gen(p, tg, S_, psY):
                    c_ = S_["ch"][tg]
                    X, krt, tokB, KEZ = S_["X"], S_["krt"], S_["tokB"], S_["KEZ"]
                    Gz, nUz, WTd, UP = c_["Gz"], c_["nUz"], c_["WTd"], c_["UP"]
                    for jp in range(2):
                        ro = slice(64 * jp, 64 * jp + 64)
                        c0 = tg * 128 + jp * 64
                        cs_ = slice(c0, c0 + 64)
                        bU = k.psum()
                        k.mm(bU, bU[ro, 0:128], WTd[jp][:, :], Tbk[p][:, :], True, True, [WTd[jp], Tbk[p]])
                        k.stt(nUz[jp][ro, :], bU[ro, 0:128], -1.0, UP[ro, :], ALU.mult, ALU.subtract, [bU, UP], [nUz[jp]])
                        bU.busy = False
                        k.mm(psY, psY[:, cs_], Tbk[p][:, :], krt[:, 1, cs_], True, False, [Tbk[p], krt])
                        for hp in range(2):
                            rs_ = slice(64 * hp, 64 * hp + 64)
                            vc = slice(p * 128 + 64 * hp, p * 128 + 64 * hp + 64)
                            k.mm(psY, psY[rs_, cs_], vtok[:, tg, vc], Gz[jp][:, hp, 64:128], False, False, [vtok, Gz[jp]])
                        yield
                        for hp in range(2):
                            rs_ = slice(64 * hp, 64 * hp + 64)
                            k.mm(psY, psY[rs_, cs_], nUz[jp][:, 64 * hp:64 * hp + 64], Gz[jp][:, hp, 192:256], False, True, [nUz[jp], Gz[jp]])
                        bS = k.psum()
                        for hp in range(2):
                            rs_ = slice(64 * hp, 64 * hp + 64)
                            vc = slice(p * 128 + 64 * hp, p * 128 + 64 * hp + 64)
                            hc = slice(tg * 128 + 64 * hp, tg * 128 + 64 * hp + 64)
                            k.mm(bS, bS[rs_, 0:64], KEZ[jp][:, hc], vtok[:, tg, vc], True, False, [KEZ[jp], vtok])
                            k.mm(bS, bS[rs_, 0:64], tokB[:, hc], nUz[jp][:, 64 * hp:64 * hp + 64], False, True, [tokB, nUz[jp]])
                        k.stt(Tf[p][:, :], Tf[p][:, :], X["g"][:, c0 + 63:c0 + 64], bS[:, 0:64], ALU.mult, ALU.add, [Tf[p], X["g"], bS], [Tf[p]])
                        bS.busy = False
                        for hp in range(2):
                            rs_ = slice(64 * hp, 64 * hp + 64)
                            k.copy("dve", Tbk[p][rs_, 64 * hp:64 * hp + 64], Tf[p][rs_, :], [Tf[p]], [Tbk[p]])
                        yield

                def pair_gen(p, S_):
                    pg = q * 4 + p
                    cols = slice(p * 128, (p + 1) * 128)
                    X, t16, krt, tokA, tokB, KEZ = S_["X"], S_["t16"], S_["krt"], S_["tokA"], S_["tokB"], S_["KEZ"]
                    b_rk = k.psum()
                    for kc in range(8):
                        k.mm(b_rk, b_rk[:, 0:T], Wr[:, kc, cols], xr[:, kc, :], kc == 0, kc == 7, [Wr, xr])
                    for kc in range(8):
                        k.mm(b_rk, b_rk[:, T:2 * T], Wk[:, kc, cols], xk[:, kc, :], kc == 0, kc == 7, [Wk, xk])
                    k.copy("act", X["rf"][:, :], b_rk[:, 0:T], [b_rk], [X["rf"]])
                    k.ts("dve", X["kkp"][:, :], b_rk[:, T:2 * T], C("k_k", pg), None, ALU.mult, None, [b_rk, cst], [X["kkp"]])
                    k.copy("act", X["kf"][:, :], b_rk[:, T:2 * T], [b_rk], [X["kf"]])
                    b_rk.busy = False
                    yield
                    b_gw = k.psum()
                    for kc in range(8):
                        k.mm(b_gw, b_gw[:, 0:T], Wg[:, kc, cols], xg[:, kc, :], kc == 0, kc == 7, [Wg, xg])
                    k.mm(b_gw, b_gw[:, T:2 * T], W2[:, cols], tw[:, :], True, True, [W2, tw])
                    k.act(X["sw"][:, :], b_gw[:, T:2 * T], AF.Sigmoid, [b_gw, cst], [X["sw"]], bias=C("w0", pg))
                    k.act(X["gate"][:, :], b_gw[:, 0:T], AF.Sigmoid, [b_gw], [X["gate"]])
                    k.tt("dve", X["gate"][:, :], b_gw[:, 0:T], X["gate"][:, :], ALU.mult, [b_gw, X["gate"]], [X["gate"]])
                    b_gw.busy = False
                    yield
                    b_av = k.psum()
                    k.mm(b_av, b_av[:, 0:T], A2[:, cols], al[:, :], True, True, [A2, al])
                    for tg in range(2):
                        k.mm(b_av, b_av[:, T + tg * 128:T + (tg + 1) * 128], vtok[:, tg, cols], ident_b, True, True, [vtok, mk_b])
                    k.act(X["av"][:, :], b_av[:, 0:T], AF.Sigmoid, [b_av, cst], [X["av"]], bias=C("a0", pg))
                    k.copy("act", X["vfm"][:, :], b_av[:, T:2 * T], [b_av], [X["vfm"]])
                    b_av.busy = False
                    yield
                    k.ts("pool", X["nk"][:, :], X["av"][:, :], C("k_a", pg), der[:, 32 + pg:33 + pg], ALU.mult, ALU.add, [X["av"], cst, der], [X["nk"]])
                    k.tt("pool", X["kf"][:, :], X["kf"][:, :], X["nk"][:, :], ALU.mult, [X["kf"], X["nk"]], [X["kf"]])
                    k.act(t16["kk2"][:, :], X["kkp"][:, :], AF.Square, [X["kkp"]], [t16["kk2"]])
                    b_n = k.psum()
                    k.mm(b_n, b_n[:, 0:T], oblk_b, t16["kk2"][:, :], True, True, [mk_b, t16["kk2"]])
                    k.act(X["nk"][:, :], b_n[:, 0:T], AF.Sqrt, [b_n], [X["nk"]])
                    k.ts("dve", X["nk"][:, :], X["nk"][:, :], 1e-12, None, ALU.max, None, [X["nk"]], [X["nk"]])
                    k.op("dve", lambda v: v.reciprocal(out=X["nk"][:, :], in_=X["nk"][:, :]), reads=[X["nk"]], writes=[X["nk"]])
                    k.tt("dve", X["kkp"][:, :], X["kkp"][:, :], X["nk"][:, :], ALU.mult, [X["kkp"], X["nk"]], [X["kkp"]])
                    k.tt("pool", X["bb"][:, :], X["kkp"][:, :], X["av"][:, :], ALU.mult, [X["kkp"], X["av"]], [X["bb"]])
                    k.stt(t16["rk2"][:, :], X["rf"][:, :], C("r_k", pg), X["kf"][:, :], ALU.mult, ALU.mult, [X["rf"], cst, X["kf"]], [t16["rk2"]])
                    k.mm(b_n, b_n[:, T:2 * T], oblk_b, t16["rk2"][:, :], True, True, [mk_b, t16["rk2"]])
                    k.tt("dve", X["bonus"][:, :], b_n[:, T:2 * T], X["vfm"][:, :], ALU.mult, [b_n, X["vfm"]], [X["bonus"]])
                    b_n.busy = False
                    yield
                    for j in range(T // 64):
                        k.op("dve", lambda v, j=j: v.tensor_tensor_scan(
                            out=X["cum"][:, j * 64:(j + 1) * 64], data0=onesf[:, 0:64], data1=X["sw"][:, j * 64:(j + 1) * 64],
                            initial=0.0, op0=ALU.mult, op1=ALU.add), reads=[onesf, X["sw"]], writes=[X["cum"]])
                    k.tt("pool", X["cm"][:, :], X["cum"][:, :], X["sw"][:, :], ALU.subtract, [X["cum"], X["sw"]], [X["cm"]])
                    k.act(X["g"][:, :], X["cum"][:, :], AF.Exp, [X["cum"]], [X["g"]], scale=-DEC)
                    k.act(X["gi"][:, :], X["cum"][:, :], AF.Exp, [X["cum"]], [X["gi"]], scale=DEC)
                    k.act(X["cm"][:, :], X["cm"][:, :], AF.Exp, [X["cm"]], [X["cm"]], scale=-DEC)
                    yield
                    k.tt("pool", krt[:, 0, :], X["kkp"][:, :], X["cm"][:, :], ALU.mult, [X["kkp"], X["cm"]], [krt])
                    k.tt("dve", krt[:, 1, :], X["rf"][:, :], X["g"][:, :], ALU.mult, [X["rf"], X["g"]], [krt])
                    k.tt("dve", X["kf"][:, :], X["kf"][:, :], X["gi"][:, :], ALU.mult, [X["kf"], X["gi"]], [X["kf"]])
                    k.tt("pool", X["bb"][:, :], X["bb"][:, :], X["gi"][:, :], ALU.mult, [X["bb"], X["gi"]], [X["bb"]])
                    k.copy("act", t16["ktb"][:, :], X["kf"][:, :], [X["kf"]], [t16["ktb"]])
                    k.copy("pool", t16["btb"][:, :], X["bb"][:, :], [X["bb"]], [t16["btb"]])
                    for j in range(T // 64):
                        sl = slice(j * 64, (j + 1) * 64)
                        ge = X["g"][:, j * 64 + 63:j * 64 + 64]
                        k.ts("dve", t16["kend"][:, sl], X["kf"][:, sl], ge, None, ALU.mult, None, [X["kf"], X["g"]], [t16["kend"]])
                        k.ts("pool", t16["bend"][:, sl], X["bb"][:, sl], ge, None, ALU.mult, None, [X["bb"], X["g"]], [t16["bend"]])
                    yield
                    b_t = k.psum()
                    for tg in range(2):
                        k.mm(b_t, b_t[:, tg * 128:(tg + 1) * 128], krt[:, 0, tg * 128:(tg + 1) * 128], ident_b, True, True, [krt, mk_b])
                    for tg in range(2):
                        k.mm(b_t, b_t[:, 256 + tg * 128:256 + (tg + 1) * 128], t16["kend"][:, tg * 128:(tg + 1) * 128], ident_b, True, True, [t16["kend"], mk_b])
                    k.copy("act", tokA[:, 0:256], b_t[:, 0:256], [b_t], [tokA])
                    for jp in range(2):
                        ro = slice(64 * jp, 64 * jp + 64)
                        k.copy("act", KEZ[jp][ro, :], b_t[ro, 256:512], [b_t], [KEZ[jp]])
                    b_t.busy = False
                    b_t2 = k.psum()
                    for tg in range(2):
                        k.mm(b_t2, b_t2[:, tg * 128:(tg + 1) * 128], t16["bend"][:, tg * 128:(tg + 1) * 128], ident_b, True, True, [t16["bend"], mk_b])
                    k.copy("dve", tokB[:, :], b_t2[:, 0:256], [b_t2], [tokB])
                    b_t2.busy = False
                    yield
                    psY = accb[p % 2]
                    yield from rr([chain_gen(p, 0, S_), chain_gen(p, 1, S_)])
                    yield from seq_gen(p, 0, S_, psY)
                    yield from seq_gen(p, 1, S_, psY)
                    k.copy("act", X["yf"][:, :], psY[:, 0:T], [psY], [X["yf"]])
                    k.act(t16["ysq"][:, :], psY[:, 0:T], AF.Square, [psY], [t16["ysq"]])
                    k.copy("pool", t16["yb16"][:, :], X["yf"][:, :], [X["yf"]], [t16["yb16"]])
                    bM = k.psum()
                    k.mm(bM, bM[:, 0:T], oblk_b, t16["yb16"][:, :], True, True, [mk_b, t16["yb16"]])
                    k.mm(bM, bM[:, T:2 * T], oblk_b, t16["ysq"][:, :], True, True, [mk_b, t16["ysq"]])
                    k.ts("dve", X["mean"][:, :], bM[:, 0:T], 1.0 / 64, None, ALU.mult, None, [bM], [X["mean"]])
                    k.tt("pool", X["var"][:, :], X["mean"][:, :], X["mean"][:, :], ALU.mult, [X["mean"]], [X["var"]])
                    k.stt(X["var"][:, :], bM[:, T:2 * T], 1.0 / 64, X["var"][:, :], ALU.mult, ALU.subtract, [bM, X["var"]], [X["var"]])
                    bM.busy = False
                    yield
                    k.act(X["var"][:, :], X["var"][:, :], AF.Sqrt, [X["var"], der], [X["var"]], bias=eps_gn)
                    k.op("dve", lambda v: v.reciprocal(out=X["var"][:, :], in_=X["var"][:, :]), reads=[X["var"]], writes=[X["var"]])
                    k.tt("pool", X["yf"][:, :], X["yf"][:, :], X["mean"][:, :], ALU.subtract, [X["yf"], X["mean"]], [X["yf"]])
                    k.tt("dve", X["yf"][:, :], X["yf"][:, :], X["var"][:, :], ALU.mult, [X["yf"], X["var"]], [X["yf"]])
                    k.ts("pool", X["yf"][:, :], X["yf"][:, :], C("lnx_g", pg), C("lnx_b", pg), ALU.mult, ALU.add, [X["yf"], cst], [X["yf"]])
                    k.tt("pool", X["yf"][:, :], X["yf"][:, :], X["bonus"][:, :], ALU.add, [X["yf"], X["bonus"]], [X["yf"]])
                    k.tt("dve", y[:, p, :], X["yf"][:, :], X["gate"][:, :], ALU.mult, [X["yf"], X["gate"]], [y])
                    yield

                for ti in range(nt):
                    t0 = ti * T
                    hU = hUs[ti % 2] if q == 0 else None
                    hR = hRs[ti % len(hRs)]
                    if q > 0:
                        k.dma("sp", ldR[ti % 2], hR[:, :, :], res_d[:, :, t0:t0 + T], reads=[res_tr[ti]], writes=[hR])
                        for j in range(6):
                            k.dma("sp", ldX, xj[j][:, :, :], xj_d[j][:, :, t0:t0 + T], reads=[xtr[ti]], writes=[xj[j]])
                        k.dma("sp", ldX, tw[:, :], lo_d[0][:, t0:t0 + T], reads=[xtr[ti]], writes=[tw])
                        k.dma("sp", ldX, al[:, :], lo_d[1][:, t0:t0 + T], reads=[xtr[ti]], writes=[al])
                        k.seal(ldX, xj + [tw, al])
                    else:
                        k.dma("sp", ldS[ti % 2], hU[:, :, :], h1_d[:, :, t0:t0 + T], reads=[dr["h1"][ti]], writes=[hU])
                    for _once in range(1 if q == 0 else 0):
                      k.copy("pool", ufp[:, :, 0:1], ufp[:, :, T:T + 1], [ufp], [ufp])
                    for _once in range(1 if q == 0 else 0):
                      emit_norm((sq, sd), hU, "c_g", ufp, 1, True)
                      k.tt("pool", delta[:, :, :], ufp[:, :, 0:T], ufp[:, :, 1:T + 1], ALU.subtract, [ufp], [delta])
                      for j in range(6):
                          for c in range(8):
                              mo = _off["mu"] + j * 8 + c
                              if j in (1, 4) or (j == 3 and c < 4):
                                  tb_ = xtmp[(j * 8 + c) % 2]
                                  k.act(tb_[:, :], delta[:, c, :], AF.Copy, [delta, cst], [tb_], scale=cst[:, mo:mo + 1])
                                  k.tt("pool", xj[j][:, c, :], tb_[:, :], ufp[:, c, 1:T + 1], ALU.add, [tb_, ufp], [xj[j]])
                              else:
                                  k.stt(xj[j][:, c, :], delta[:, c, :], cst[:, mo:mo + 1], ufp[:, c, 1:T + 1], ALU.mult, ALU.add,
                                        [delta, cst, ufp], [xj[j]])
                      bl = k.psum()
                      for kc in range(8):
                          k.mm(bl, bl[0:64, 0:T], W1[:, kc, :], xw[:, kc, :], kc == 0, kc == 7, [W1, xw])
                      for kc in range(8):
                          k.mm(bl, bl[0:64, T:2 * T], A1[:, kc, :], xa[:, kc, :], kc == 0, kc == 7, [A1, xa])
                      k.act(tw[:, :], bl[0:64, 0:T], AF.Tanh, [bl], [tw])
                      k.copy("act", al[:, :], bl[0:64, T:2 * T], [bl], [al])
                      bl.busy = False
                      for j in range(6):
                          k.dma("sp", stX, xj_d[j][:, :, t0:t0 + T], xj[j][:, :, :], reads=[xj[j]], writes=[xtr[ti]])
                      k.dma("sp", stX, lo_d[0][:, t0:t0 + T], tw[:, :], reads=[tw], writes=[xtr[ti]])
                      k.dma("sp", stX, lo_d[1][:, t0:t0 + T], al[:, :], reads=[al], writes=[xtr[ti]])
                      k.seal(stX, [xtr[ti]])
                      for b_ in xj + [tw, al]:
                          b_.r[stX.key] = (stX, stX.count)
                    for tg in range(2):
                        bank = k.psum()
                        for kc in range(8):
                            k.mm(bank, bank[:, 0:512], xv[:, kc, tg * 128:(tg + 1) * 128], Wv[:, kc, :], kc == 0, kc == 7, [xv, Wv])
                        k.copy("act" if tg == 0 else "dve", vtok[:, tg, :], bank[:, 0:512], [bank], [vtok])
                        bank.busy = False
                    for pp in range(2):
                        for _ in rr([pair_gen(2 * pp, PB[0]), pair_gen(2 * pp + 1, PB[1])]):
                            pass
                    for dc in range(8):
                        bank = k.psum()
                        for m in range(4):
                            k.mm(bank, bank[:, 0:T], Wo[:, m, dc * 128:(dc + 1) * 128], y[:, m, :], m == 0, m == 3, [Wo, y])
                        k.tt("dve", hR[:, dc, :], hR[:, dc, :], bank[:, 0:T], ALU.add, [hR, bank], [hR])
                        bank.busy = False
                    if not last:
                        k.dma("sp", stS[ti % 2], dst_d[:, :, t0:t0 + T], hR[:, :, :], reads=[hR], writes=[dst_tr[ti]])
                    else:
                        emit_norm((sq, sd), hR, "fin_g", delta, 0, True)
                        for tg in range(2):
                            for cq in range(2):
                                bank = k.psum()
                                for c4 in range(4):
                                    c = cq * 4 + c4
                                    k.op("pe", lambda pe, bank=bank, c4=c4, c=c, tg=tg: pe.transpose(
                                        out=bank[:, c4 * 128:(c4 + 1) * 128], in_=delta[:, c, tg * 128:(tg + 1) * 128], identity=mk_f[:, :]),
                                        reads=[delta, mk_f], writes=[bank])
                                k.copy("act" if cq == 0 else "dve", ot[:, tg, cq * 512:(cq + 1) * 512], bank[:, 0:512], [bank], [ot])
                                bank.busy = False
                        k.dma("sp", stS[ti % 2], out_d[t0:t0 + T, :].rearrange("(g p) d -> p g d", p=128), ot[:, :, :], reads=[ot], writes=[])
                k.barrier()

        if "A" in passes:
            pass_A()
        if "B" in passes:
            pass_B()
        if "C" in passes:
            trs = [[Buf(None, "tr%d_%d" % (i, j)) for j in range(NT)] for i in range(3)]
            pass_R(0, None, None, hC_d, trs[0], False)
            pass_R(1, hC_d, trs[0], hA_d, trs[1], False)
            pass_R(2, hA_d, trs[1], hC_d, trs[2], False)
            pass_R(3, hC_d, trs[2], None, None, True)
        k.barrier()
        print("instructions:", k.nins)
    return nc


def _fm(w):
    n = w.shape[1]
    return np.ascontiguousarray(w.reshape(8, 128, n).transpose(1, 0, 2))


def _cv(v):
    return np.ascontiguousarray(v.reshape(-1, 128).T)


def make_masks():
    m = np.zeros((128, NMASK), np.float32)
    p = np.arange(128)[:, None]
    c = np.arange(128)[None, :]
    m[:, M_ID:M_ID + 128] = (p == c)
    m[:, M_ONES:M_ONES + 128] = 1.0
    m[:, M_OBLK:M_OBLK + 128] = (p // 64 == c // 64)
    m[:, M_HG:M_HG + 128] = (p // 64 == c // 64) & (p <= c)
    s = np.arange(128)[:, None] % 64
    t = np.arange(64)[None, :]
    strict = (s < t).astype(np.float32)
    incl = (s <= t).astype(np.float32)
    m[:, M_RW:M_RW + 64] = strict
    m[:, M_RW + 64:M_RW + 128] = incl
    m[:, M_RW + 128:M_RW + 192] = strict
    m[:, M_RW + 192:M_RW + 256] = incl
    m[:, M_RW + 256:M_RW + 320] = (t < s)
    eye = (s == t).astype(np.float32)
    m[:, M_ID2:M_ID2 + 64] = eye
    m[:, M_ID2 + 64:M_ID2 + 128] = eye
    return m


def prep_inputs(inp, b):
    f = lambda a: np.asarray(a, np.float32)
    cst = np.zeros((128, NCONST), np.float32)

    def put(name, arr):
        cst[:, _off[name]:_off[name] + arr.shape[1]] = arr

    put("ab_g", _cv(f(inp["ab_norm_g"])[0]))
    put("c_g", _cv(f(inp["c_norm_g"])[0]))
    put("fin_g", _cv(f(inp["final_g"])))
    cw = f(inp["rg_conv_w"])[0]
    put("conv_w", np.ascontiguousarray(cw.reshape(4, 8, 128).transpose(2, 1, 0).reshape(128, 32)))
    put("conv_b", _cv(f(inp["rg_conv_b"])[0]))
    put("b_a", _cv(f(inp["rg_b_a"])[0]))
    put("b_x", _cv(f(inp["rg_b_x"])[0]))
    put("lam", _cv(f(inp["rg_lambda"])[0]))
    put("lb0", _cv(f(inp["hg_lb_logits"])[0]))
    put("lb1", _cv(f(inp["hg_lb_logits"])[1]))
    put("hg_g", f(inp["hg_norm_g"])[0].reshape(128, 1))
    mu = f(inp["c_mu"])[0]
    put("mu", np.ascontiguousarray(mu.reshape(6, 8, 128).transpose(2, 0, 1).reshape(128, 48)))
    put("w0", _cv(f(inp["c_w0"])[0]))
    put("a0", _cv(f(inp["c_a0"])[0]))
    put("k_k", _cv(f(inp["c_k_k"])[0]))
    put("k_a", _cv(f(inp["c_k_a"])[0]))
    put("r_k", _cv(f(inp["c_r_k"])[0].reshape(-1)))
    put("lnx_g", _cv(f(inp["c_lnx_g"])[0]))
    put("lnx_b", _cv(f(inp["c_lnx_b"])[0]))
    win = f(inp["ab_w_in"])[0]
    wout = f(inp["ab_w_out"])[0]
    m = {
        "x": np.ascontiguousarray(f(inp["x"])[b]),
        "consts": cst,
        "masks": make_masks(),
        "wAin": _fm(win[:, 0:2048]),
        "rgwa": np.ascontiguousarray(f(inp["rg_w_a"])[0].transpose(1, 0, 2)),
        "rgwx": np.ascontiguousarray(f(inp["rg_w_x"])[0].transpose(1, 0, 2)),
        "wAout": _fm(wout[0:1024]),
        "wBin": _fm(win[:, 2048:6144]),
        "wBout": _fm(wout[1024:2048]),
        "w1": _fm(f(inp["c_w1"])[0]),
        "a1": _fm(f(inp["c_a1"])[0]),
    }
    for h in range(2):
        sl = slice(h * 1024, (h + 1) * 1024)
        m["wr%d" % h] = _fm(f(inp["c_w_r"])[0][:, sl])
        m["wk%d" % h] = _fm(f(inp["c_w_k"])[0][:, sl])
        m["wv%d" % h] = _fm(f(inp["c_w_v"])[0][:, sl])
        m["wg%d" % h] = _fm(f(inp["c_w_g"])[0][:, sl])
        m["wo%d" % h] = _fm(f(inp["c_w_o"])[0][sl, :])
        m["w2_%d" % h] = np.ascontiguousarray(f(inp["c_w2"])[0][:, sl])
        m["a2_%d" % h] = np.ascontiguousarray(f(inp["c_a2"])[0][:, sl])
    return m


def kernel(**inputs):
    nc = build()
    in_maps = [prep_inputs(inputs, i % 4) for i in range(8)]
    res = run_bass_kernel_spmd(nc, in_maps, core_ids=list(range(8)))
    out = np.stack([np.asarray(res.results[i]["out"], np.float32) for i in range(4)], axis=0)
    return out
```
